# Optimizing a Trainium2 kernel written in Bass

```python
import jax, jax.numpy as jnp
from jax import lax
import numpy as np

D_MODEL = 1024
BATCH = 2
SEQ = 8192
DEPTH = 2

CHUNK = 64
LEFT_CHUNKS = 8
BAND_CHUNKS = LEFT_CHUNKS + 1
BAND = BAND_CHUNKS * CHUNK
HEAD_DIM = 64
MIX_WIDTH = D_MODEL
RWKV_WIDTH = MIX_WIDTH // 2
ATT_WIDTH = MIX_WIDTH - RWKV_WIDTH
RWKV_HEADS = RWKV_WIDTH // HEAD_DIM
ATT_HEADS = ATT_WIDTH // HEAD_DIM
DECAY_LORA = 64
AAA_LORA = 64
GATE_LORA = 128
RWKV_COLS = 3 * RWKV_WIDTH + DECAY_LORA + AAA_LORA + GATE_LORA
IN_COLS = RWKV_COLS + 3 * ATT_WIDTH
MAX_REL = 2 * CHUNK
CONV_WIDTH = 31
D_FF = 4 * D_MODEL
NORM_EPS = 1e-6
LN_EPS = 1e-5
GN_EPS = 64e-5

kernel_name = "rwkv7_chunkattn_conformer_hybrid"


def rmsnorm(x, g):
    x32 = x.astype(jnp.float32)
    y = x32 * lax.rsqrt(jnp.mean(x32 * x32, axis=-1, keepdims=True) + NORM_EPS)
    return (y * g.astype(jnp.float32)).astype(x.dtype)


def token_shift(p):
    return jnp.pad(p, ((0, 0), (1, 0), (0, 0)))[:, :-1]


def rwkv7_scan(r, decay, k, v, kk, a):
    B, T, H, N = r.shape
    def step(S, inp):
        r_t, w_t, k_t, v_t, kk_t, a_t = inp
        sa = jnp.einsum('bhij,bhj->bhi', S, -kk_t)
        S = (S * w_t[:, :, None, :]
             + sa[..., None] * (kk_t * a_t)[:, :, None, :]
             + v_t[..., None] * k_t[:, :, None, :])
        y = jnp.einsum('bhij,bhj->bhi', S, r_t)
        return S, y
    xs = tuple(jnp.moveaxis(z.astype(jnp.float32), 1, 0) for z in (r, decay, k, v, kk, a))
    S0 = jnp.zeros((B, H, N, N), jnp.float32)
    _, ys = lax.scan(step, S0, xs)
    return jnp.moveaxis(ys, 0, 1)


def rwkv7_group(p, mu, w0, w2, a0, a2, g2, k_k, k_a, r_k, lnx_w, lnx_b):
    B, T, _ = p.shape
    p = p + (token_shift(p) - p) * mu
    split_at = [RWKV_WIDTH, 2 * RWKV_WIDTH, 3 * RWKV_WIDTH,
                3 * RWKV_WIDTH + DECAY_LORA, 3 * RWKV_WIDTH + DECAY_LORA + AAA_LORA]
    r, k, v, xw, xa, xg = jnp.split(p, split_at, axis=-1)
    w = -jax.nn.softplus(-(w0 + jnp.tanh(xw) @ w2)) - 0.5
    decay = jnp.exp(-jnp.exp(w.astype(jnp.float32)))
    a = jax.nn.sigmoid(a0 + xa @ a2)
    g = jax.nn.sigmoid(xg) @ g2
    heads = lambda z: z.reshape(B, T, RWKV_HEADS, HEAD_DIM)
    kk = heads(k * k_k).astype(jnp.float32)
    kk = kk / jnp.maximum(jnp.sqrt(jnp.sum(kk * kk, axis=-1, keepdims=True)), 1e-12)
    k = k * (1 + (a - 1) * k_a)
    rh, kh, vh, ah = heads(r), heads(k), heads(v), heads(a)
    y = rwkv7_scan(rh, heads(decay), kh, vh, kk, ah)
    mean = jnp.mean(y, axis=-1, keepdims=True)
    var = jnp.mean(jnp.square(y - mean), axis=-1, keepdims=True)
    y = (y - mean) * lax.rsqrt(var + GN_EPS)
    y = y * lnx_w.reshape(RWKV_HEADS, HEAD_DIM).astype(jnp.float32) + lnx_b.reshape(RWKV_HEADS, HEAD_DIM).astype(jnp.float32)
    bonus = jnp.sum((rh * kh * r_k.reshape(RWKV_HEADS, HEAD_DIM)).astype(jnp.float32), axis=-1, keepdims=True) * vh.astype(jnp.float32)
    y = (y + bonus).reshape(B, T, RWKV_WIDTH).astype(p.dtype)
    return y * g


def chunk_band_attention(q, k, v, rel_bias):
    B, T, _ = q.shape
    nc = T // CHUNK
    shp = (B, nc, CHUNK, ATT_HEADS, HEAD_DIM)
    qc, kc, vc = q.reshape(shp), k.reshape(shp), v.reshape(shp)
    pad = ((0, 0), (LEFT_CHUNKS, 0), (0, 0), (0, 0), (0, 0))
    band_idx = jnp.arange(nc)[:, None] + jnp.arange(BAND_CHUNKS)[None, :]
    kb = jnp.pad(kc, pad)[:, band_idx].reshape(B, nc, BAND, ATT_HEADS, HEAD_DIM)
    vb = jnp.pad(vc, pad)[:, band_idx].reshape(B, nc, BAND, ATT_HEADS, HEAD_DIM)
    valid = jnp.repeat(band_idx >= LEFT_CHUNKS, CHUNK, axis=1)
    rel = jnp.arange(CHUNK)[:, None] + LEFT_CHUNKS * CHUNK - jnp.arange(BAND)[None, :]
    bias = rel_bias[:, jnp.clip(rel, -MAX_REL, MAX_REL) + MAX_REL]
    s = jnp.einsum('bcqhd,bckhd->bhcqk', qc, kb).astype(jnp.float32) * (HEAD_DIM ** -0.5)
    s = s + bias[:, None].astype(jnp.float32)
    s = jnp.where(valid[:, None, :], s, -1e30)
    probs = jax.nn.softmax(s, axis=-1).astype(v.dtype)
    o = jnp.einsum('bhcqk,bckhd->bcqhd', probs, vb)
    return o.reshape(B, T, ATT_WIDTH)


def rwkv_attention_mixer(h, w_in, mu, w0, w2, a0, a2, g2, k_k, k_a, r_k, lnx_w, lnx_b, rel_bias, w_out):
    proj = h @ w_in
    p_rwkv, q, k, v = jnp.split(proj, [RWKV_COLS, RWKV_COLS + ATT_WIDTH, RWKV_COLS + 2 * ATT_WIDTH], axis=-1)
    y_rwkv = rwkv7_group(p_rwkv, mu, w0, w2, a0, a2, g2, k_k, k_a, r_k, lnx_w, lnx_b)
    y_att = chunk_band_attention(q, k, v, rel_bias)
    return jnp.concatenate([y_rwkv, y_att], axis=-1) @ w_out


def conformer_conv(h, pw1, pw1_b, dw, dw_b, ln_w, ln_b, pw2, pw2_b):
    u = jax.nn.glu(h @ pw1 + pw1_b, axis=-1)
    u = lax.conv_general_dilated(u, dw[:, None, :].astype(u.dtype), (1,), [(CONV_WIDTH - 1, 0)],
                                 dimension_numbers=('NWC', 'WIO', 'NWC'),
                                 feature_group_count=D_MODEL) + dw_b
    u32 = u.astype(jnp.float32)
    mean = jnp.mean(u32, axis=-1, keepdims=True)
    var = jnp.mean(jnp.square(u32 - mean), axis=-1, keepdims=True)
    u = ((u32 - mean) * lax.rsqrt(var + LN_EPS) * ln_w.astype(jnp.float32) + ln_b.astype(jnp.float32)).astype(h.dtype)
    return jax.nn.silu(u) @ pw2 + pw2_b


def sq_relu_mlp(h, up, down):
    return jnp.square(jax.nn.relu(h @ up)) @ down


def setup_inputs(seed: int = 0) -> dict:
    key = jax.random.key(seed)
    ks = jax.random.split(key, 40)
    nrm = lambda i, shape, scale: jax.random.normal(ks[i], shape, jnp.float32) * scale
    gain = lambda i, n: 1.0 + 0.05 * jax.random.normal(ks[i], (n,), jnp.float32)
    return {
        "x": nrm(0, (BATCH, SEQ, D_MODEL), 1.0),
        "l0_norm_mix": gain(1, D_MODEL),
        "l0_w_in": nrm(2, (D_MODEL, IN_COLS), D_MODEL ** -0.5),
        "l0_shift_mu": jax.random.uniform(ks[3], (RWKV_COLS,), jnp.float32),
        "l0_w0": jax.random.uniform(ks[4], (RWKV_WIDTH,), jnp.float32, -6.0, 1.0),
        "l0_w2": nrm(5, (DECAY_LORA, RWKV_WIDTH), DECAY_LORA ** -0.5),
        "l0_a0": nrm(6, (RWKV_WIDTH,), 0.1),
        "l0_a2": nrm(7, (AAA_LORA, RWKV_WIDTH), AAA_LORA ** -0.5),
        "l0_g2": nrm(8, (GATE_LORA, RWKV_WIDTH), GATE_LORA ** -0.5),
        "l0_k_k": 0.85 + nrm(9, (RWKV_WIDTH,), 0.05),
        "l0_k_a": 1.0 + nrm(10, (RWKV_WIDTH,), 0.05),
        "l0_r_k": nrm(11, (RWKV_WIDTH,), 0.1),
        "l0_lnx_w": gain(12, RWKV_WIDTH),
        "l0_lnx_b": nrm(13, (RWKV_WIDTH,), 0.02),
        "l0_rel_bias": nrm(14, (ATT_HEADS, 2 * MAX_REL + 1), 0.5),
        "l0_w_out": nrm(15, (MIX_WIDTH, D_MODEL), MIX_WIDTH ** -0.5),
        "l0_norm_ffn": gain(16, D_MODEL),
        "l0_ffn_up": nrm(17, (D_MODEL, D_FF), D_MODEL ** -0.5),
        "l0_ffn_down": nrm(18, (D_FF, D_MODEL), D_FF ** -0.5),
        "l1_norm_mix": gain(19, D_MODEL),
        "l1_pw1": nrm(20, (D_MODEL, 2 * D_MODEL), D_MODEL ** -0.5),
        "l1_pw1_b": nrm(21, (2 * D_MODEL,), 0.02),
        "l1_dw": nrm(22, (CONV_WIDTH, D_MODEL), CONV_WIDTH ** -0.5),
        "l1_dw_b": nrm(23, (D_MODEL,), 0.02),
        "l1_ln_w": gain(24, D_MODEL),
        "l1_ln_b": nrm(25, (D_MODEL,), 0.02),
        "l1_pw2": nrm(26, (D_MODEL, D_MODEL), D_MODEL ** -0.5),
        "l1_pw2_b": nrm(27, (D_MODEL,), 0.02),
        "l1_norm_ffn": gain(28, D_MODEL),
        "l1_ffn_up": nrm(29, (D_MODEL, D_FF), D_MODEL ** -0.5),
        "l1_ffn_down": nrm(30, (D_FF, D_MODEL), D_FF ** -0.5),
        "final_norm": gain(31, D_MODEL),
    }


def reference(x, l0_norm_mix, l0_w_in, l0_shift_mu, l0_w0, l0_w2, l0_a0, l0_a2, l0_g2,
              l0_k_k, l0_k_a, l0_r_k, l0_lnx_w, l0_lnx_b, l0_rel_bias, l0_w_out,
              l0_norm_ffn, l0_ffn_up, l0_ffn_down,
              l1_norm_mix, l1_pw1, l1_pw1_b, l1_dw, l1_dw_b, l1_ln_w, l1_ln_b,
              l1_pw2, l1_pw2_b, l1_norm_ffn, l1_ffn_up, l1_ffn_down, final_norm):
    mix_norms = (l0_norm_mix, l1_norm_mix)
    mix_params = (
        (l0_w_in, l0_shift_mu, l0_w0, l0_w2, l0_a0, l0_a2, l0_g2, l0_k_k, l0_k_a,
         l0_r_k, l0_lnx_w, l0_lnx_b, l0_rel_bias, l0_w_out),
        (l1_pw1, l1_pw1_b, l1_dw, l1_dw_b, l1_ln_w, l1_ln_b, l1_pw2, l1_pw2_b),
    )
    ffn_params = ((l0_norm_ffn, l0_ffn_up, l0_ffn_down), (l1_norm_ffn, l1_ffn_up, l1_ffn_down))
    for layer in range(DEPTH):
        h = rmsnorm(x, mix_norms[layer])
        if layer % 2 == 0:
            x = x + rwkv_attention_mixer(h, *mix_params[layer])
        else:
            x = x + conformer_conv(h, *mix_params[layer])
        g_ffn, up, down = ffn_params[layer]
        x = x + sq_relu_mlp(rmsnorm(x, g_ffn), up, down)
    return rmsnorm(x, final_norm)
```

```python
import contextlib
import numpy as np
import concourse.bass as bass
import concourse.mybir as mybir
from concourse.bass_utils import run_bass_kernel_spmd

F32 = mybir.dt.float32
BF16 = mybir.dt.bfloat16
AF = mybir.ActivationFunctionType
ALU = mybir.AluOpType
AX = mybir.AxisListType

SCHED = True
SYNC_SAME = True


class Sem:
    def __init__(self, nc, name):
        self.h = nc.alloc_semaphore(name)
        self.count = 0
        self.last = {}


class Op:
    __slots__ = ("eng", "fn", "deps", "dsem", "dval", "signal", "sigval", "dmadeps", "epoch", "alldeps", "cost", "idx", "barrier",
                 "chain", "nbytes", "junk")

    def __init__(self, eng, fn):
        self.eng = eng
        self.fn = fn
        self.deps = []
        self.dmadeps = []
        self.dsem = None
        self.dval = 0
        self.signal = False
        self.sigval = 0
        self.epoch = 0
        self.alldeps = []
        self.cost = 100.0
        self.idx = 0
        self.barrier = False
        self.chain = None
        self.nbytes = 0
        self.junk = 0


class Prog:
    ENG = ("pe", "act", "dve", "pool", "sp")

    def __init__(self, nc):
        self.nc = nc
        self.e = {"pe": nc.tensor, "act": nc.scalar, "dve": nc.vector, "pool": nc.gpsimd, "sp": nc.sync}
        self.ops = []
        self.lastw = {}
        self.readers = {}
        self.esems = [{k: Sem(nc, "sem0_" + k) for k in self.ENG}]
        self.epoch = 0
        self.all_sems = []
        self.last_op = {}
        self.last_dma = {}
        self.junk_fn = None
        self.junk_cost = 110.0
        self.junk_frac = 0.7
        self.junk_cap = 10

    def sb(self, name, shape, dt=F32):
        return self.nc.alloc_sbuf_tensor(name, list(shape), dt).ap()

    def ps(self, name, shape, dt=F32):
        return self.nc.alloc_psum_tensor(name, list(shape), dt).ap()

    def sem(self, name):
        s = Sem(self.nc, name)
        self.all_sems.append(s)
        return s

    def barrier(self):
        lasts = dict(self.last_op)
        news = []
        for eng in self.ENG:
            o = Op(eng, lambda e: e.nop())
            for d in lasts.values():
                if d.dsem is None and not (d.eng == eng and eng == "pe"):
                    d.signal = True
                    o.deps.append(d)
            for s in self.all_sems:
                if s.count:
                    o.dmadeps.append((s, s.count))
            o.epoch = self.epoch
            o.barrier = True
            news.append(o)
        for o in news:
            self.ops.append(o)
        self.epoch += 1
        self.esems.append({k: Sem(self.nc, f"sem{self.epoch}_" + k) for k in self.ENG})
        self.last_op = {}

    def _dep(self, op, d):
        if d is None or d is op:
            return
        op.alldeps.append(d)
        if d.dsem is not None:
            op.dmadeps.append((d.dsem, d.dsem.count))
            op.alldeps.extend(d.dsem.last.values())
            return
        if d.eng == op.eng and op.dsem is None:
            if d.eng == "pe" or not SYNC_SAME:
                return
        if d.epoch < self.epoch:
            return
        d.signal = True
        op.deps.append(d)

    def op(self, eng, fn, r=(), w=(), dma=None, cost=None, nbytes=0):
        o = Op(eng, fn)
        o.epoch = self.epoch
        o.idx = len(self.ops)
        if cost is not None:
            o.cost = cost
        o.nbytes = nbytes
        if dma is not None:
            o.dsem = dma
        for k in r:
            self._dep(o, self.lastw.get(k))
        for k in w:
            self._dep(o, self.lastw.get(k))
            for rd in self.readers.get(k, ()):
                self._dep(o, rd)
        if dma is not None:
            dma.count += 16
            o.dval = dma.count
            o.chain = self.last_dma.get(eng)
            self.last_dma[eng] = o
            dma.last[eng] = o
        for k in r:
            self.readers.setdefault(k, []).append(o)
        for k in w:
            self.lastw[k] = o
            self.readers[k] = []
        self.ops.append(o)
        if dma is None:
            self.last_op[eng] = o
        return o

    @staticmethod
    def _free(ap):
        n = 1
        for d in list(ap.shape)[1:]:
            n *= int(d)
        return n

    def _ecost(self, eng, ap):
        f = self._free(ap)
        if eng == "pool":
            return 120.0 + 2.1 * f
        return 70.0 + 1.05 * f

    def dma(self, eng, out, in_, sem, r=(), w=(), **kw):
        nb = self._free(out) * int(out.shape[0]) * 4
        return self.op(eng, lambda e: e.dma_start(out=out, in_=in_, **kw), r=r, w=w, dma=sem, cost=60.0, nbytes=nb)

    def mm(self, out, lhsT, rhs, start, stop, r=(), w=(), **kw):
        n = self._free(rhs)
        mul = 4.0 if rhs.dtype == F32 else 1.0
        return self.op("pe", lambda e: e.matmul(out, lhsT=lhsT, rhs=rhs, start=start, stop=stop, **kw), r=r, w=w,
                       cost=35.0 + 0.43 * mul * max(n, 64))

    def tr(self, out, in_, ident, r=(), w=(), **kw):
        return self.op("pe", lambda e: e.transpose(out, in_, ident, **kw), r=r, w=w, cost=80.0 + 0.43 * self._free(in_))

    def act(self, out, in_, func, r=(), w=(), eng="act", **kw):
        return self.op(eng, lambda e: e.activation(out=out, in_=in_, func=func, **kw), r=r, w=w, cost=self._ecost(eng, out) + 60)

    def tt(self, eng, out, in0, in1, op, r=(), w=()):
        return self.op(eng, lambda e: e.tensor_tensor(out=out, in0=in0, in1=in1, op=op), r=r, w=w, cost=self._ecost(eng, out))

    def ts(self, eng, out, in0, s1, op0, s2=None, op1=None, r=(), w=(), **kw):
        if op1 is None:
            return self.op(eng, lambda e: e.tensor_scalar(out=out, in0=in0, scalar1=s1, scalar2=None, op0=op0, **kw), r=r, w=w,
                           cost=self._ecost(eng, out))
        return self.op(eng, lambda e: e.tensor_scalar(out=out, in0=in0, scalar1=s1, scalar2=s2, op0=op0, op1=op1, **kw), r=r, w=w,
                       cost=self._ecost(eng, out))

    def stt(self, out, in0, scalar, in1, op0, op1, r=(), w=(), eng="dve"):
        return self.op(eng, lambda e: e.scalar_tensor_tensor(out=out, in0=in0, scalar=scalar, in1=in1, op0=op0, op1=op1), r=r, w=w,
                       cost=self._ecost(eng, out))

    def cp(self, eng, out, in_, r=(), w=()):
        if eng == "act":
            return self.op(eng, lambda e: e.copy(out=out, in_=in_), r=r, w=w, cost=self._ecost(eng, out) + 60)
        return self.op(eng, lambda e: e.tensor_copy(out=out, in_=in_), r=r, w=w, cost=self._ecost(eng, out))

    def memset(self, eng, ap, val, w=()):
        return self.op(eng, lambda e: e.memset(ap, val), w=w, cost=self._ecost(eng, ap))

    def _schedule(self, seg):
        import heapq
        pos = {id(o): i for i, o in enumerate(seg)}
        npred = [0] * len(seg)
        succ = [[] for _ in seg]
        for i, o in enumerate(seg):
            ps = set()
            for d in o.alldeps:
                j = pos.get(id(d))
                if j is not None and j != i:
                    ps.add(j)
            if o.chain is not None:
                j = pos.get(id(o.chain))
                if j is not None:
                    ps.add(j)
            npred[i] = len(ps)
            for j in ps:
                succ[j].append(i)
        ready_t = [0.0] * len(seg)
        done_t = [0.0] * len(seg)
        issue_t = [0.0] * len(seg)
        free_at = {k: 0.0 for k in self.ENG}
        heaps = {k: [] for k in self.ENG}
        for i, o in enumerate(seg):
            if npred[i] == 0:
                heapq.heappush(heaps[o.eng], (0.0, i))
        order = []
        nleft = len(seg)
        while nleft:
            best = None
            for k in self.ENG:
                h = heaps[k]
                if not h:
                    continue
                rt, i = h[0]
                st = max(rt, free_at[k])
                if best is None or st < best[0] or (st == best[0] and i < best[2]):
                    best = (st, k, i)
            st, k, _ = best
            h = heaps[k]
            cand = []
            while h and h[0][0] <= st and len(cand) < 16:
                cand.append(heapq.heappop(h))
            cand.sort(key=lambda x: x[1])
            rt, i = cand[0]
            for cnd in cand[1:]:
                heapq.heappush(h, cnd)
            o = seg[i]
            order.append(o)
            nleft -= 1
            if k == "pe" and self.junk_fn is not None:
                gap = st - free_at[k]
                if gap > self.junk_cost:
                    o.junk = min(self.junk_cap, int(gap * self.junk_frac / self.junk_cost))
            if o.dsem is not None:
                free_at[k] = st + o.cost
                issue_t[i] = st + o.cost
                done_t[i] = st + 2000.0 + o.nbytes / 150.0
            else:
                free_at[k] = st + o.cost
                issue_t[i] = st + o.cost
                done_t[i] = st + o.cost
            for j in succ[i]:
                oj = seg[j]
                if oj.chain is o and o not in oj.alldeps:
                    t = issue_t[i]
                else:
                    t = done_t[i] + (40.0 if (oj.eng == o.eng and oj.eng == "pe") else 180.0)
                if t > ready_t[j]:
                    ready_t[j] = t
                npred[j] -= 1
                if npred[j] == 0:
                    heapq.heappush(heaps[oj.eng], (ready_t[j], j))
        return order, max(free_at.values())

    def reorder(self):
        segs = {}
        for o in self.ops:
            segs.setdefault(o.epoch, []).append(o)
        new = []
        tot = 0.0
        for ep in sorted(segs):
            seg = [o for o in segs[ep] if not o.barrier]
            bar = [o for o in segs[ep] if o.barrier]
            order, t = self._schedule(seg)
            tot += t
            new.extend(order)
            new.extend(bar)
        self.ops = new
        return tot

    def finalize(self, final_sems=()):
        if SCHED:
            self.reorder()
        cnt = {}
        waited = {k: {} for k in self.ENG}
        for o in self.ops:
            if o.dsem is None and o.signal:
                kk_ = (o.epoch, o.eng)
                cnt[kk_] = cnt.get(kk_, 0) + 1
                o.sigval = cnt[kk_]
        for o in self.ops:
            e = self.e[o.eng]
            need = {}
            for d in o.deps:
                s = self.esems[d.epoch][d.eng]
                need[id(s)] = (s, max(need.get(id(s), (s, 0))[1], d.sigval))
            for (s, v) in o.dmadeps:
                need[id(s)] = (s, max(need.get(id(s), (s, 0))[1], v))
            wd = waited[o.eng]
            for _ in range(o.junk):
                self.junk_fn(e)
            for sid, (s, v) in need.items():
                if wd.get(sid, 0) >= v:
                    continue
                e.wait_ge(s.h, v)
                wd[sid] = v
            ins = o.fn(e)
            if o.dsem is not None:
                ins.then_inc(o.dsem.h, 16)
            elif o.signal:
                ins.then_inc(self.esems[o.epoch][o.eng].h, 1)
        for s in final_sems:
            self.nc.sync.wait_ge(s.h, s.count)
        return cnt


D = 1024
DFF = 4096
NK = 8
NORM_EPS = 1e-6
LN_EPS = 1e-5
CW = 31


class Phase:
    def __init__(self, p):
        self.p = p
        self.stack = contextlib.ExitStack()

    def __enter__(self):
        self.stack.__enter__()
        return self

    def sb(self, name, shape, dt=F32):
        return self.stack.enter_context(self.p.nc.sbuf_tensor(name, list(shape), dt)).ap()

    def __exit__(self, *a):
        self.p.barrier()
        return self.stack.__exit__(*a)


def make_consts(p):
    c = {}
    c["identf"] = p.sb("identf", [128, 128], F32)
    c["identb"] = p.sb("identb", [128, 128], BF16)
    c["onesf"] = p.sb("onesf", [128, 128], F32)
    p.memset("pool", c["identf"][:], 1.0, w=["identf"])
    p.op("pool", lambda e: e.affine_select(out=c["identf"][:], in_=c["identf"][:], pattern=[[-1, 128]],
                                            compare_op=ALU.is_equal, fill=0.0, base=0, channel_multiplier=1),
         r=["identf"], w=["identf"])
    p.cp("dve", c["identb"][:], c["identf"][:], r=["identf"], w=["identb"])
    p.memset("pool", c["onesf"][:], 1.0, w=["onesf"])
    return c


class PsumPool:
    def __init__(self, p, n=8):
        self.t = [p.ps(f"psb{i}", [128, 512], F32) for i in range(n)]
        self.keys = [f"psb{i}" for i in range(n)]
        self.i = 0
        self.n = n

    def get(self):
        i = self.i
        self.i = (self.i + 1) % self.n
        return self.t[i], self.keys[i]

    def sub(self, idx):
        q = PsumPool.__new__(PsumPool)
        q.t = [self.t[i] for i in idx]; q.keys = [self.keys[i] for i in idx]; q.i = 0; q.n = len(idx)
        return q


def rms_to_hT(p, c, pp, X, xkey, tile, hT, hkey, col0, scr, idx):
    sq = scr["sq"][idx % 2]; ss = scr["ss"][idx % 2]; hb = scr["hb"][idx % 2]
    kq = ("sq", idx % 2); ks = ("ss", idx % 2); kh = ("hb", idx % 2)
    xt = X[:, tile, :]
    p.act(sq[:], xt, AF.Square, r=[xkey], w=[kq, ks], accum_out=ss[:, 0:1])
    p.ts("dve", ss[:, 1:2], ss[:, 0:1], 1.0 / D, ALU.mult, NORM_EPS, ALU.add, r=[ks], w=[ks])
    p.act(ss[:, 2:3], ss[:, 1:2], AF.Sqrt, r=[ks], w=[ks])
    p.op("dve", lambda e: e.reciprocal(out=ss[:, 3:4], in_=ss[:, 2:3]), r=[ks], w=[ks])
    p.ts("dve", hb[:], xt, ss[:, 3:4], ALU.mult, r=[xkey, ks], w=[kh])
    pt, pk = pp.get()
    ptb = pt.bitcast(BF16)
    for dk in range(NK):
        p.tr(ptb[:, dk * 128:(dk + 1) * 128], hb[:, dk * 128:(dk + 1) * 128], c["identb"][:], r=[kh, "identb"], w=[pk])
    p.cp("act", hT[:, :, col0:col0 + 128], ptb.rearrange("p (k t) -> p k t", k=NK), r=[pk], w=[hkey])


B_INPUTS = [("w_out", [D, D]), ("g_ffn0", [D]), ("up0", [D, DFF]), ("dn0", [DFF, D]), ("g_mix1", [D]), ("pw1", [D, 2 * D]),
            ("pw1_b", [2 * D]), ("dw", [CW, D]), ("dw_b", [D]), ("ln_w", [D]), ("ln_b", [D]), ("pw2", [D, D]), ("pw2_b", [D]),
            ("g_ffn1", [D]), ("up1", [D, DFF]), ("dn1", [DFF, D]), ("g_fin", [D]), ("hmask", [128, 1])]


def build_B(nc, NT=17):
    NTOK = NT * 128
    p = Prog(nc)
    dr = lambda name, shape, dt=F32, kind="ExternalInput": nc.dram_tensor(name, list(shape), dt, kind=kind).ap()
    T = {name: dr(name, shape) for name, shape in B_INPUTS}
    T["xin"] = dr("xin", [NTOK, D])
    T["yT"] = dr("yT", [D, NTOK], BF16)
    T["out"] = dr("out", [NTOK - 128, D], kind="ExternalOutput")
    c = make_consts(p)
    pp = PsumPool(p)
    s_st = emit_B(p, c, pp, T, NT)
    p.finalize(final_sems=[s_st])
    return p


def emit_B(p, c, pp, T, NT=17):
    NTOK = NT * 128
    NMAIN = NTOK - 128
    xin = T["xin"]; yT = T["yT"]; hmask = T["hmask"]; w_out = T["w_out"]
    g_ffn0 = T["g_ffn0"]; up0 = T["up0"]; dn0 = T["dn0"]
    g_mix1 = T["g_mix1"]; pw1 = T["pw1"]; pw1_b = T["pw1_b"]
    dw = T["dw"]; dw_b = T["dw_b"]; ln_w = T["ln_w"]; ln_b = T["ln_b"]
    pw2 = T["pw2"]; pw2_b = T["pw2_b"]
    g_ffn1 = T["g_ffn1"]; up1 = T["up1"]; dn1 = T["dn1"]
    g_fin = T["g_fin"]
    out = T["out"]

    s_ld = p.sem("s_ld"); s_w = [p.sem("s_w0"), p.sem("s_w1"), p.sem("s_w2"), p.sem("s_w3")]
    s_st = p.sem("s_st")

    X = p.sb("X", [128, NT, D], F32)
    vecs = p.sb("vecs", [128, 8 * NK + 2 * NK], F32)

    def colvec(i, src, n=NK):
        ap = vecs[:, i:i + n]
        p.dma("sp", ap, src.rearrange("(k p) -> p k", p=128), s_ld, w=["vecs"], allow_slow_non_contiguous=True)
        return ap
    V = {}
    off = 0
    for name, src, n in [("g_ffn0", g_ffn0, NK), ("g_mix1", g_mix1, NK), ("pw1_b", pw1_b, 2 * NK), ("dw_b", dw_b, NK),
                         ("ln_w", ln_w, NK), ("ln_b", ln_b, NK), ("g_ffn1", g_ffn1, NK)]:
        V[name] = colvec(off, src, n); off += n
    rows = p.sb("rows", [128, 2, D], F32)
    p.dma("sp", rows[:, 0, :], pw2_b.partition_broadcast(128), s_ld, w=["rows"])
    p.dma("sp", rows[:, 1, :], g_fin.partition_broadcast(128), s_ld, w=["rows"])
    hm = p.sb("hm", [128, 1], F32)
    p.dma("sp", hm[:], hmask, s_ld, w=["hm"])
    xv = xin.rearrange("(n p) d -> p n d", p=128)
    for n in range(NT):
        p.dma("sp" if n % 2 == 0 else "act", X[:, n, :], xv[:, n, :], s_ld, w=[("X", n)])

    scr_store = {}

    def ffn(ph, hT, tiles, g_col, up, dn, tagp):
        SL = 512
        nsl = DFF // SL
        stU = [ph.sb(f"{tagp}stU{i}", [128, NK, SL], F32) for i in range(1)] * 2
        stD = [ph.sb(f"{tagp}stD{i}", [128, SL // 128, D], F32) for i in range(1)] * 2
        WU = [ph.sb(f"{tagp}WU{i}", [128, NK, SL], BF16) for i in range(2)]
        WD = [ph.sb(f"{tagp}WD{i}", [128, SL // 128, D], BF16) for i in range(2)]
        aT = [ph.sb(f"{tagp}aT{i}", [128, SL // 128, 512], BF16) for i in range(2)]
        rl = [ph.sb(f"{tagp}rl{i}", [128, 512], F32) for i in range(2)]
        upv = up.rearrange("(k p) f -> p k f", p=128)
        dnv = dn.rearrange("(k p) d -> p k d", p=128)
        groups = []
        i = 0
        while i < len(tiles):
            groups.append(tiles[i:i + 4]); i += 4
        ai = 0; ri = 0
        ppU = pp.sub([0, 1, 2]); ppD = pp.sub([3, 4, 5, 6, 7])
        def load(s):
            b = s % 2
            p.dma("sp", stU[b][:], upv[:, :, s * SL:(s + 1) * SL], s_w[0], w=[(tagp, "stU", 0)])
            p.dma("sp", stD[b][:], dnv[:, s * (SL // 128):(s + 1) * (SL // 128), :], s_w[2], w=[(tagp, "stD", 0)])

        def cast(s):
            b = s % 2
            for dk in range(NK):
                if dk % 2 == 0:
                    p.ts("dve", WU[b][:, dk, :], stU[b][:, dk, :], g_col[:, dk:dk + 1], ALU.mult,
                         r=[(tagp, "stU", 0), "vecs"], w=[(tagp, "WU", b)])
                else:
                    p.act(WU[b][:, dk, :], stU[b][:, dk, :], AF.Copy, r=[(tagp, "stU", 0), "vecs"], w=[(tagp, "WU", b)], scale=g_col[:, dk:dk + 1])
            for f_ in range(SL // 128):
                p.cp("dve" if f_ % 2 == 0 else "act", WD[b][:, f_, :], stD[b][:, f_, :], r=[(tagp, "stD", 0)], w=[(tagp, "WD", b)])
        load(0)
        cast(0)
        for s in range(nsl):
            b = s % 2
            if s + 1 < nsl:
                load(s + 1)
            for grp in groups:
                n = len(grp) * 128
                c0 = grp[0] * 128
                a = aT[ai % 2]; ka = (tagp, "aT", ai % 2); ai += 1
                for ft in range(SL // 128):
                    pt, pk = ppU.get()
                    for dk in range(NK):
                        p.mm(pt[:, 0:n], WU[b][:, dk, ft * 128:(ft + 1) * 128], hT[:, dk, c0:c0 + n], dk == 0, dk == NK - 1,
                             r=[(tagp, "WU", b), (tagp, "hT")], w=[pk])
                    r_ = rl[ri % 2]; kr = (tagp, "rl", ri % 2); ri += 1
                    p.act(r_[:, 0:n], pt[:, 0:n], AF.Relu, r=[pk], w=[kr])
                    p.act(a[:, ft, 0:n], r_[:, 0:n], AF.Square, r=[kr], w=[ka])
                for ti, t in enumerate(grp):
                    for half in range(2):
                        pt, pk = ppD.get()
                        for ft in range(SL // 128):
                            p.mm(pt[:], a[:, ft, ti * 128:(ti + 1) * 128], WD[b][:, ft, half * 512:(half + 1) * 512],
                                 ft == 0, ft == SL // 128 - 1, r=[ka, (tagp, "WD", b)], w=[pk])
                        xs = X[:, t, half * 512:(half + 1) * 512]
                        p.tt("dve", xs, xs, pt[:], ALU.add, r=[pk, ("X", t)], w=[("X", t)])
            if s + 1 < nsl:
                cast(s + 1)

    def norm_all(ph, tiles, hT, hkey, scr):
        for i, t in enumerate(tiles):
            rms_to_hT(p, c, pp, X, ("X", t), t, hT, hkey, t * 128, scr, i)

    def mk_scr(ph, tag):
        return {"sq": [ph.sb(f"{tag}sq{i}", [128, D], BF16) for i in range(2)],
                "ss": [ph.sb(f"{tag}ss{i}", [128, 4], F32) for i in range(2)],
                "hb": [ph.sb(f"{tag}hb{i}", [128, D], BF16) for i in range(2)]}

    with Phase(p) as ph:
        hT = ph.sb("hT0", [128, NK, NTOK], BF16)
        scr = mk_scr(ph, "p1")
        with Phase(p) as ph2:
            yv = yT.rearrange("(k p) t -> p k t", p=128)
            for k in range(NK):
                p.dma("sp" if k % 2 == 0 else "act", hT[:, k, :], yv[:, k, :], s_ld, w=[("p1", "hT")])
            wst = [ph2.sb(f"wst{i}", [128, NK, 512], F32) for i in range(2)]
            woB = ph2.sb("woB", [128, NK, D], BF16)
            wov = w_out.rearrange("(k p) d -> p k d", p=128)
            for h in range(2):
                p.dma("sp", wst[h][:], wov[:, :, h * 512:(h + 1) * 512], s_w[h], w=[("wst", h)])
                for dk in range(NK):
                    p.cp("act" if dk % 2 else "dve", woB[:, dk, h * 512:(h + 1) * 512], wst[h][:, dk, :], r=[("wst", h)], w=["woB"])
            for t in range(NT):
                for half in range(2):
                    pt, pk = pp.get()
                    for k in range(NK):
                        p.mm(pt[:], hT[:, k, t * 128:(t + 1) * 128], woB[:, k, half * 512:(half + 1) * 512], k == 0, k == NK - 1,
                             r=[("p1", "hT"), "woB"], w=[pk])
                    xs = X[:, t, half * 512:(half + 1) * 512]
                    p.tt("dve", xs, xs, pt[:], ALU.add, r=[pk, ("X", t)], w=[("X", t)])
        norm_all(ph, list(range(NT)), hT, ("p1", "hT"), scr)
        with Phase(p) as ph2:
            ffn(ph2, hT, list(range(NT)), V["g_ffn0"], up0, dn0, "p1")

    with Phase(p) as ph:
        scr = mk_scr(ph, "p2")
        pw1B = ph.sb("pw1B", [128, NK, 2 * D], BF16)
        pw2B = ph.sb("pw2B", [128, NK, D], BF16)
        with Phase(p) as ph2:
            wst = [ph2.sb(f"wst2{i}", [128, NK, 512], F32) for i in range(2)]
            pw1v = pw1.rearrange("(k p) c -> p k c", p=128)
            pw2v = pw2.rearrange("(k p) c -> p k c", p=128)
            for j in range(4):
                b = j % 2
                p.dma("sp", wst[b][:], pw1v[:, :, j * 512:(j + 1) * 512], s_w[b], w=[("wst2", b)])
                for dk in range(NK):
                    if dk % 2:
                        p.act(pw1B[:, dk, j * 512:(j + 1) * 512], wst[b][:, dk, :], AF.Copy, r=[("wst2", b), "vecs"], w=["pw1B"],
                              scale=V["g_mix1"][:, dk:dk + 1])
                    else:
                        p.ts("dve", pw1B[:, dk, j * 512:(j + 1) * 512], wst[b][:, dk, :], V["g_mix1"][:, dk:dk + 1], ALU.mult,
                             r=[("wst2", b), "vecs"], w=["pw1B"])
            for j in range(2):
                b = j % 2
                p.dma("sp", wst[b][:], pw2v[:, :, j * 512:(j + 1) * 512], s_w[b], w=[("wst2", b)])
                for dk in range(NK):
                    p.cp("act" if dk % 2 else "dve", pw2B[:, dk, j * 512:(j + 1) * 512], wst[b][:, dk, :], r=[("wst2", b)], w=["pw2B"])
        dwS = ph.sb("dwS", [CW, D], F32)
        dwT = ph.sb("dwT", [128, NK, 32], F32)
        p.dma("sp", dwS[:], dw, s_ld, w=["dwS"])
        for ct in range(NK):
            pt, pk = pp.get()
            p.tr(pt[:, 0:CW], dwS[:, ct * 128:(ct + 1) * 128], c["identf"][0:CW, 0:CW], r=["dwS", "identf"], w=[pk])
            p.cp("dve", dwT[:, ct, 0:CW], pt[:, 0:CW], r=[pk], w=["dwT"])
        UH = ph.sb("UH", [128, NK, CW - 1], F32)
        hTg = ph.sb("hTg", [128, NK, 512], BF16)
        uT = [ph.sb(f"uT{i}", [128, CW - 1 + 512], F32) for i in range(2)]
        sg = [ph.sb(f"sg{i}", [128, 512], F32) for i in range(2)]
        zT = ph.sb("zT", [128, NK, 512], F32)
        zq = [ph.sb(f"zq{i}", [128, 512], F32) for i in range(2)]
        st = ph.sb("lnst", [128, 4, 512], F32)
        zn = [ph.sb(f"zn{i}", [128, 512], F32) for i in range(2)]
        z2b = [ph.sb(f"z2b{i}", [128, 512], F32) for i in range(2)]
        tpb = [ph.sb(f"tpb{i}", [128, 512], F32) for i in range(2)]
        KD = 18
        sT = ph.sb("sT", [128, NK, 512], BF16)
        groups = [[0]] + [list(range(1 + 4 * g, 1 + 4 * g + 4)) for g in range((NT - 1) // 4)]
        ui = 0
        for gi, grp in enumerate(groups):
            n = len(grp) * 128
            for i, t in enumerate(grp):
                rms_to_hT(p, c, pp, X, ("X", t), t, hTg, "hTg", i * 128, scr, i)
            for ct in range(NK):
                pa, pka = pp.get()
                pb, pkb = pp.get()
                for dk in range(NK):
                    p.mm(pa[:, 0:n], pw1B[:, dk, ct * 128:(ct + 1) * 128], hTg[:, dk, 0:n], dk == 0, dk == NK - 1, r=["pw1B", "hTg"], w=[pka])
                for dk in range(NK):
                    p.mm(pb[:, 0:n], pw1B[:, dk, D + ct * 128:D + (ct + 1) * 128], hTg[:, dk, 0:n], dk == 0, dk == NK - 1, r=["pw1B", "hTg"], w=[pkb])
                u = uT[ui % 2]; ku = ("uT", ui % 2); s_ = sg[ui % 2]; ksg = ("sg", ui % 2); ui += 1
                p.act(s_[:, 0:n], pb[:, 0:n], AF.Sigmoid, r=[pkb, "vecs"], w=[ksg], bias=V["pw1_b"][:, NK + ct:NK + ct + 1])
                if gi > 0:
                    p.cp("pool", u[:, 0:CW - 1], UH[:, ct, :], r=[("UH", ct)], w=[ku])
                p.stt(u[:, CW - 1:CW - 1 + n], pa[:, 0:n], V["pw1_b"][:, ct:ct + 1], s_[:, 0:n], ALU.add, ALU.mult, r=[pka, ksg, "vecs"], w=[ku])
                if gi == 0:
                    p.ts("dve", UH[:, ct, :], u[:, CW - 1 + n - (CW - 1):CW - 1 + n], hm[:, 0:1], ALU.mult, r=[ku, "hm"], w=[("UH", ct)])
                    continue
                p.cp("pool", UH[:, ct, :], u[:, n:n + CW - 1], r=[ku], w=[("UH", ct)])
                z = zT[:, ct, 0:n]
                p.ts("dve", z, u[:, 0:n], dwT[:, ct, 0:1], ALU.mult, V["dw_b"][:, ct:ct + 1], ALU.add, r=[ku, "dwT", "vecs"], w=[("zT", ct)])
                for k in range(1, KD):
                    p.stt(z, u[:, k:k + n], dwT[:, ct, k:k + 1], z, ALU.mult, ALU.add, r=[ku, "dwT", ("zT", ct)], w=[("zT", ct)])
                z2 = z2b[ct % 2][:, 0:n]; kz2 = ("z2b", ct % 2)
                for k in range(KD, CW):
                    if k == KD:
                        p.act(z2, u[:, k:k + n], AF.Copy, r=[ku, "dwT"], w=[kz2], scale=dwT[:, ct, k:k + 1])
                    else:
                        tp_ = tpb[k % 2][:, 0:n]; ktp = ("tpb", k % 2)
                        p.act(tp_, u[:, k:k + n], AF.Copy, r=[ku, "dwT"], w=[ktp], scale=dwT[:, ct, k:k + 1])
                        p.tt("pool", z2, z2, tp_, ALU.add, r=[kz2, ktp], w=[kz2])
                p.tt("pool", z, z, z2, ALU.add, r=[("zT", ct), kz2], w=[("zT", ct)])
            if gi == 0:
                continue
            p1, pk1 = pp.get()
            p2, pk2 = pp.get()
            for ct in range(NK):
                p.mm(p1[:, 0:n], c["onesf"][:], zT[:, ct, 0:n], ct == 0, ct == NK - 1, r=["onesf", ("zT", ct)], w=[pk1])
            for ct in range(NK):
                q = zq[ct % 2]; kq = ("zq", ct % 2)
                p.act(q[:, 0:n], zT[:, ct, 0:n], AF.Square, r=[("zT", ct)], w=[kq])
                p.mm(p2[:, 0:n], c["onesf"][:], q[:, 0:n], ct == 0, ct == NK - 1, r=["onesf", kq], w=[pk2])
            mean = st[:, 0, 0:n]; var = st[:, 1, 0:n]; tmp = st[:, 2, 0:n]; rstd = st[:, 3, 0:n]
            p.ts("dve", mean, p1[:, 0:n], 1.0 / D, ALU.mult, r=[pk1], w=["lnst"])
            p.tt("dve", tmp, mean, mean, ALU.mult, r=["lnst"], w=["lnst"])
            p.stt(var, p2[:, 0:n], 1.0 / D, tmp, ALU.mult, ALU.subtract, r=[pk2, "lnst"], w=["lnst"])
            p.ts("dve", var, var, LN_EPS, ALU.add, r=["lnst"], w=["lnst"])
            p.act(tmp, var, AF.Sqrt, r=["lnst"], w=["lnst"])
            p.op("dve", lambda e, rstd=rstd, tmp=tmp: e.reciprocal(out=rstd, in_=tmp), r=["lnst"], w=["lnst"])
            for ct in range(NK):
                zz = zn[ct % 2]; kz = ("zn", ct % 2)
                p.tt("dve", zz[:, 0:n], zT[:, ct, 0:n], mean, ALU.subtract, r=[("zT", ct), "lnst"], w=[kz])
                p.tt("pool", zz[:, 0:n], zz[:, 0:n], rstd, ALU.mult, r=[kz, "lnst"], w=[kz])
                p.act(sT[:, ct, 0:n], zz[:, 0:n], AF.Silu, r=[kz, "vecs"], w=["sT"],
                      scale=V["ln_w"][:, ct:ct + 1], bias=V["ln_b"][:, ct:ct + 1])
            for ti, t in enumerate(grp):
                for half in range(2):
                    pt, pk = pp.get()
                    for ct in range(NK):
                        p.mm(pt[:], sT[:, ct, ti * 128:(ti + 1) * 128], pw2B[:, ct, half * 512:(half + 1) * 512], ct == 0, ct == NK - 1,
                             r=["sT", "pw2B"], w=[pk])
                    xs = X[:, t, half * 512:(half + 1) * 512]
                    p.tt("dve", xs, xs, pt[:], ALU.add, r=[pk, ("X", t)], w=[("X", t)])
                    p.tt("pool", xs, xs, rows[:, 0, half * 512:(half + 1) * 512], ALU.add, r=["rows", ("X", t)], w=[("X", t)])

    main = list(range(1, NT))
    with Phase(p) as ph:
        hT = ph.sb("hT1", [128, NK, NTOK], BF16)
        scr = mk_scr(ph, "p3")
        norm_all(ph, main, hT, ("p3", "hT"), scr)
        with Phase(p) as ph2:
            ffn(ph2, hT, main, V["g_ffn1"], up1, dn1, "p3")
    with Phase(p) as ph:
        sq = [ph.sb(f"fsq{i}", [128, D], F32) for i in range(2)]
        ss = [ph.sb(f"fss{i}", [128, 4], F32) for i in range(2)]
        ov = out.rearrange("(n p) d -> p n d", p=128)
        for i, t in enumerate(main):
            b = i % 2
            xt = X[:, t, :]
            p.act(sq[b][:], xt, AF.Square, r=[("X", t)], w=[("fsq", b), ("fss", b)], accum_out=ss[b][:, 0:1])
            p.ts("dve", ss[b][:, 1:2], ss[b][:, 0:1], 1.0 / D, ALU.mult, NORM_EPS, ALU.add, r=[("fss", b)], w=[("fss", b)])
            p.act(ss[b][:, 2:3], ss[b][:, 1:2], AF.Sqrt, r=[("fss", b)], w=[("fss", b)])
            p.op("dve", lambda e, b=b: e.reciprocal(out=ss[b][:, 3:4], in_=ss[b][:, 2:3]), r=[("fss", b)], w=[("fss", b)])
            p.stt(sq[b][:], xt, ss[b][:, 3:4], rows[:, 1, :], ALU.mult, ALU.mult, r=[("X", t), ("fss", b), "rows"], w=[("fsq", b)])
            p.dma("sp", ov[:, t - 1, :], sq[b][:], s_st, r=[("fsq", b)], w=[("out", t)])
    return s_st


C = 64
TB = 512
NCH = TB // C
DEC = 0.6065306597126334
GN_EPS = 64e-5


def build_A(nc, NB=16):
    T = NB * TB
    p = Prog(nc)
    dr = lambda name, shape, dt=F32, kind="ExternalInput": nc.dram_tensor(name, list(shape), dt, kind=kind).ap()
    xb = dr("xb", [T, D])
    wsel = dr("wsel", [D, 1024])
    g_mix = dr("g_mix", [D])
    cv = dr("cv", [128, 16])
    w2 = dr("w2", [64, 128]); a2 = dr("a2", [64, 128]); g2 = dr("g2", [128, 128])
    attb = dr("attb", [2, 128, 640])
    yT = dr("yT", [256, T], BF16, kind="ExternalOutput")

    s_ld = p.sem("s_ld"); s_x = [p.sem("s_x0"), p.sem("s_x1")]; s_w = [p.sem("s_w0"), p.sem("s_w1")]
    s_st = p.sem("s_st")
    c = make_consts(p)
    pp = PsumPool(p, 7)
    psD = p.ps("psD", [128, 512], F32)

    CV = p.sb("CV", [128, 16], F32)
    p.dma("sp", CV[:], cv, s_ld, w=["CV"])
    MU = lambda ct: CV[:, ct:ct + 1]
    W0, A0, KK, KA, RK, LNW, LNB = [CV[:, 5 + i:6 + i] for i in range(7)]
    OMKA = CV[:, 12:13]
    p.ts("dve", OMKA, KA, -1.0, ALU.mult, 1.0, ALU.add, r=["CV"], w=["CV"])
    gcol = p.sb("gcol", [128, NK], F32)
    p.dma("sp", gcol[:], g_mix.rearrange("(k p) -> p k", p=128), s_ld, w=["gcol"], allow_slow_non_contiguous=True)
    W2 = p.sb("W2", [128, 128], F32); A2 = p.sb("A2", [128, 128], F32); G2 = p.sb("G2", [128, 128], F32)
    p.memset("pool", W2[:], 0.0, w=["W2"])
    p.memset("pool", A2[:], 0.0, w=["A2"])
    p.dma("sp", W2[0:64, :], w2, s_ld, w=["W2"])
    p.dma("sp", A2[64:128, :], a2, s_ld, w=["A2"])
    p.dma("sp", G2[:], g2, s_ld, w=["G2"])
    ATB = p.sb("ATB", [128, 2, 640], F32)
    for h in range(2):
        p.dma("sp", ATB[:, h, :], attb[h], s_ld, w=["ATB"])
    bones = p.sb("bones", [128, 128], F32)
    p.memset("pool", bones[:], 0.0, w=["bones"])
    p.memset("pool", bones[0:64, 0:64], 1.0, w=["bones"])
    p.memset("pool", bones[64:128, 64:128], 1.0, w=["bones"])
    rmask = p.sb("rmask", [128, TB], F32)
    p.memset("pool", rmask[:], 1.0, w=["rmask"])
    p.memset("pool", rmask.rearrange("p (c t) -> p c t", t=C)[:, :, 0:1], 0.0, w=["rmask"])
    HB = 4
    mU = p.sb("mU", [128, HB, 128], F32); mL = p.sb("mL", [128, HB, 64], F32); idn = p.sb("idn", [128, HB, 64], F32)
    for t_, name in ((mU, "mU"), (mL, "mL"), (idn, "idn")):
        p.memset("pool", t_[:], 1.0, w=[name])
    for h in range(2):
        hs = slice(64 * h, 64 * h + 64)
        p.op("pool", lambda e, hs=hs: e.affine_select(out=mU[hs, :, 0:64], in_=mU[hs, :, 0:64], pattern=[[0, HB], [1, 64]],
                                                      compare_op=ALU.is_gt, fill=0.0, base=0, channel_multiplier=-1), r=["mU"], w=["mU"])
        p.op("pool", lambda e, hs=hs: e.affine_select(out=mU[hs, :, 64:128], in_=mU[hs, :, 64:128], pattern=[[0, HB], [1, 64]],
                                                      compare_op=ALU.is_ge, fill=0.0, base=0, channel_multiplier=-1), r=["mU"], w=["mU"])
        p.op("pool", lambda e, hs=hs: e.affine_select(out=mL[hs, :, :], in_=mL[hs, :, :], pattern=[[0, HB], [-1, 64]],
                                                      compare_op=ALU.is_gt, fill=0.0, base=0, channel_multiplier=1), r=["mL"], w=["mL"])
        p.op("pool", lambda e, hs=hs: e.affine_select(out=idn[hs, :, :], in_=idn[hs, :, :], pattern=[[0, HB], [-1, 64]],
                                                      compare_op=ALU.is_equal, fill=0.0, base=0, channel_multiplier=1), r=["idn"], w=["idn"])

    WB = p.sb("WB", [128, NK, 1024], BF16)
    with Phase(p) as ph:
        wst = [ph.sb(f"wstA{i}", [128, NK, 512], F32) for i in range(2)]
        wv = wsel.rearrange("(k p) c -> p k c", p=128)
        for hf in range(2):
            p.dma("sp", wst[hf][:], wv[:, :, hf * 512:(hf + 1) * 512], s_w[hf], w=[("wstA", hf)])
            for dk in range(NK):
                p.ts("pool" if dk % 2 else "dve", WB[:, dk, hf * 512:(hf + 1) * 512], wst[hf][:, dk, :], gcol[:, dk:dk + 1], ALU.mult,
                     r=[("wstA", hf), "gcol"], w=["WB"])

    def t5(name, dt=F32, n=TB):
        return p.sb(name, [128, n], dt)
    XT = [p.sb(f"XT{i}", [128, 4, D], F32) for i in range(2)]
    sq = p.sb("sqA", [128, D], BF16); ssb = p.sb("ssA", [128, 4], F32); hb = p.sb("hbA", [128, D], BF16)
    hT = p.sb("hTA", [128, NK, TB], BF16)
    PJ = [[p.sb(f"PJ{i}_{ct}", [128, 1 + TB], F32) for ct in range(5)] for i in range(2)]
    for i in range(2):
        for ct in range(5):
            p.memset("pool", PJ[i][ct][:, 0:1], 0.0, w=[("PJ", i, ct)])
    qT = t5("qT", BF16)
    Kring = p.sb("Kring", [128, 2 * TB], BF16)
    Vring = p.sb("Vring", [128, 8, 128], BF16)
    tmp = t5("tmpA"); Pm = [t5(f"Pm{ct}") for ct in range(5)]
    txw = p.sb("txw", [128, TB], F32); sxg = t5("sxg")
    sgm = t5("sgm"); av = t5("av"); gv = t5("gv")
    kk = t5("kk"); kk2 = t5("kk2"); rn = t5("rn"); kmod = t5("kmod"); beta = t5("beta"); bon = t5("bon")
    cs = t5("cs"); csd = t5("csd"); dcs = t5("dcs")
    E1 = t5("E1"); E2 = t5("E2"); E3 = t5("E3"); E4 = t5("E4")
    AR = p.sb("AR", [128, NCH, 2, C], BF16)
    Kt = t5("Kt", BF16); Bt = t5("Bt", BF16); Khc = t5("Khc", BF16); Bhc = t5("Bhc", BF16); vB = t5("vB", BF16)
    Vt = p.sb("Vt", [128, NCH, C], BF16); Kh = p.sb("Kh", [128, NCH, C], BF16); Bh = p.sb("Bh", [128, NCH, C], BF16)
    A1 = p.sb("A1", [128, NCH, 128], BF16); A2m = p.sb("A2m", [128, NCH, 128], BF16)
    Nm = [[p.sb(f"Nm{hb_}_{i}", [128, HB, C], BF16) for i in range(2)] for hb_ in range(2)]
    Mm = [[p.sb(f"Mm{hb_}_{i}", [128, HB, C], BF16) for i in range(2)] for hb_ in range(2)]
    Pq = [[p.sb(f"Pq{hb_}_{i}", [128, HB, C], BF16) for i in range(2)] for hb_ in range(2)]
    Qq = [[p.sb(f"Qq{hb_}_{i}", [128, HB, C], BF16) for i in range(2)] for hb_ in range(2)]
    TmT = p.sb("TmT", [128, NCH, C], BF16)
    Hf = p.sb("Hf", [128, C], F32); Hb = [p.sb(f"Hb{i}", [128, C], BF16) for i in range(2)]
    p.memset("pool", Hf[:], 0.0, w=["Hf"])
    p.memset("pool", Hb[0][:], 0.0, w=[("Hb", 0)])
    Xb = p.sb("Xb", [128, C], BF16); Ub = p.sb("Ub", [128, C], BF16)
    Yf = t5("Yf"); yc = t5("yc"); ycq = t5("ycq"); rs = t5("rsA"); yo = t5("yo", BF16)
    S = p.sb("S_att", [128, 640], F32); Pe = p.sb("Pe", [128, 640], F32); Pn = p.sb("Pn", [128, 640], BF16)
    PT = p.sb("PT", [128, 5, 128], BF16)
    ast = p.sb("ast", [128, 4], F32)
    ao = t5("ao", BF16)

    xv = xb.rearrange("(n p) d -> p n d", p=128)

    def load_x(n):
        b = n % 2
        for i in range(4):
            p.dma("sp", XT[b][:, i, :], xv[:, n * 4 + i, :], s_x[b], w=[("XT", b, i)])

    hidx = [0]
    load_x(0)
    for n in range(NB):
        b = n % 2
        if n + 1 < NB:
            load_x(n + 1)
        for i in range(4):
            xt = XT[b][:, i, :]
            p.act(sq[:], xt, AF.Square, r=[("XT", b, i)], w=["sqA", "ssA"], accum_out=ssb[:, 0:1])
            p.ts("dve", ssb[:, 1:2], ssb[:, 0:1], 1.0 / D, ALU.mult, NORM_EPS, ALU.add, r=["ssA"], w=["ssA"])
            p.act(ssb[:, 2:3], ssb[:, 1:2], AF.Sqrt, r=["ssA"], w=["ssA"])
            p.op("dve", lambda e: e.reciprocal(out=ssb[:, 3:4], in_=ssb[:, 2:3]), r=["ssA"], w=["ssA"])
            p.ts("dve", hb[:], xt, ssb[:, 3:4], ALU.mult, r=[("XT", b, i), "ssA"], w=["hbA"])
            pt, pk = pp.get()
            ptb = pt.bitcast(BF16)
            for dk in range(NK):
                p.tr(ptb[:, dk * 128:(dk + 1) * 128], hb[:, dk * 128:(dk + 1) * 128], c["identb"][:], r=["hbA", "identb"], w=[pk])
            p.cp("act", hT[:, :, i * 128:(i + 1) * 128], ptb.rearrange("p (k t) -> p k t", k=NK), r=[pk], w=["hTA"])
        if n > 0:
            p.cp("pool", Kring[:, 0:TB], Kring[:, TB:2 * TB], r=["Kring"], w=["Kring"])
            p.cp("pool", Vring[:, 0:4, :], Vring[:, 4:8, :], r=["Vring"], w=["Vring"])
        for ct in range(7):
            pt, pk = pp.get()
            for dk in range(NK):
                p.mm(pt[:], WB[:, dk, ct * 128:(ct + 1) * 128], hT[:, dk, :], dk == 0, dk == NK - 1, r=["WB", "hTA"], w=[pk])
            if ct < 5:
                p.cp("act" if ct % 2 else "dve", PJ[b][ct][:, 1:1 + TB], pt[:], r=[pk], w=[("PJ", b, ct)])
            elif ct == 5:
                p.cp("act", qT[:], pt[:], r=[pk], w=["qT"])
            else:
                p.cp("dve", Kring[:, TB:2 * TB], pt[:], r=[pk], w=["Kring"])
        pt, pk = pp.get()
        for i in range(4):
            for dk in range(NK):
                p.mm(pt[:, i * 128:(i + 1) * 128], hT[:, dk, i * 128:(i + 1) * 128], WB[:, dk, 7 * 128:8 * 128], dk == 0, dk == NK - 1,
                     r=["WB", "hTA"], w=[pk])
        p.cp("act", Vring[:, 4:8, :], pt.rearrange("p (i v) -> p i v", i=4), r=[pk], w=["Vring"])
        for ct in range(5):
            cur = PJ[b][ct][:, 1:1 + TB]; prv = PJ[b][ct][:, 0:TB]
            p.tt("pool", tmp[:], prv, cur, ALU.subtract, r=[("PJ", b, ct)], w=["tmpA"])
            p.stt(Pm[ct][:], tmp[:], MU(ct), cur, ALU.mult, ALU.add, r=["tmpA", ("PJ", b, ct), "CV"], w=[("Pm", ct)])
            p.cp("pool", PJ[1 - b][ct][:, 0:1], PJ[b][ct][:, TB:TB + 1], r=[("PJ", b, ct)], w=[("PJ", 1 - b, ct)])
        r_, k_, v_ = Pm[0], Pm[1], Pm[2]
        p.act(txw[:], Pm[3][:], AF.Tanh, r=[("Pm", 3)], w=["txw"])
        p.act(sxg[:], Pm[4][:], AF.Sigmoid, r=[("Pm", 4)], w=["sxg"])
        pw_, pkw = pp.get()
        p.mm(pw_[:], W2[:], txw[:], True, True, r=["W2", "txw"], w=[pkw])
        p.act(sgm[:], pw_[:], AF.Sigmoid, r=[pkw, "CV"], w=["sgm"], bias=W0)
        pa_, pka = pp.get()
        p.mm(pa_[:], A2[:], Pm[3][:], True, True, r=["A2", ("Pm", 3)], w=[pka])
        p.act(av[:], pa_[:], AF.Sigmoid, r=[pka, "CV"], w=["av"], bias=A0)
        pg_, pkg = pp.get()
        p.mm(pg_[:], G2[:], sxg[:], True, True, r=["G2", "sxg"], w=[pkg])
        p.cp("act", gv[:], pg_[:], r=[pkg], w=["gv"])
        p.ts("dve", kk[:], k_[:], KK, ALU.mult, r=[("Pm", 1), "CV"], w=["kk"])
        p.tt("pool", kk2[:], kk[:], kk[:], ALU.mult, r=["kk"], w=["kk2"])
        pn_, pkn = pp.get()
        p.mm(pn_[:], bones[:], kk2[:], True, True, r=["bones", "kk2"], w=[pkn])
        p.act(rn[:], pn_[:], AF.Sqrt, r=[pkn], w=["rn"])
        p.ts("dve", rn[:], rn[:], 1e-12, ALU.max, r=["rn"], w=["rn"])
        p.op("dve", lambda e: e.reciprocal(out=rn[:], in_=rn[:]), r=["rn"], w=["rn"])
        p.tt("dve", kk[:], kk[:], rn[:], ALU.mult, r=["kk", "rn"], w=["kk"])
        p.ts("dve", tmp[:], av[:], KA, ALU.mult, OMKA, ALU.add, r=["av", "CV"], w=["tmpA"])
        p.tt("dve", kmod[:], k_[:], tmp[:], ALU.mult, r=[("Pm", 1), "tmpA"], w=["kmod"])
        p.tt("pool", beta[:], kk[:], av[:], ALU.mult, r=["kk", "av"], w=["beta"])
        p.stt(tmp[:], r_[:], RK, kmod[:], ALU.mult, ALU.mult, r=[("Pm", 0), "kmod", "CV"], w=["tmpA"])
        pb_, pkb = pp.get()
        p.mm(pb_[:], bones[:], tmp[:], True, True, r=["bones", "tmpA"], w=[pkb])
        p.tt("dve", bon[:], v_[:], pb_[:], ALU.mult, r=[("Pm", 2), pkb], w=["bon"])
        p.op("dve", lambda e: e.tensor_tensor_scan(out=cs[:], data0=rmask[:], data1=sgm[:], initial=0.0, op0=ALU.mult, op1=ALU.add),
             r=["rmask", "sgm"], w=["cs"])
        p.tt("pool", csd[:], cs[:], sgm[:], ALU.subtract, r=["cs", "sgm"], w=["csd"])
        cs3 = cs.rearrange("p (c t) -> p c t", t=C)
        p.tt("pool", dcs.rearrange("p (c t) -> p c t", t=C), cs3[:, :, C - 1:C].to_broadcast([128, NCH, C]), cs3, ALU.subtract,
             r=["cs"], w=["dcs"])
        p.act(E1[:], cs[:], AF.Exp, r=["cs"], w=["E1"], scale=-DEC)
        p.act(E2[:], cs[:], AF.Exp, r=["cs"], w=["E2"], scale=DEC)
        p.act(E3[:], csd[:], AF.Exp, r=["csd"], w=["E3"], scale=-DEC)
        p.act(E4[:], dcs[:], AF.Exp, r=["dcs"], w=["E4"], scale=-DEC)
        c3 = lambda ap: ap.rearrange("p (c t) -> p c t", t=C)
        p.stt(AR[:, :, 0, :], c3(kk), -1.0, c3(E3), ALU.mult, ALU.mult, r=["kk", "E3"], w=["AR"])
        p.tt("dve", AR[:, :, 1, :], c3(r_), c3(E1), ALU.mult, r=[("Pm", 0), "E1"], w=["AR"])
        p.tt("pool", Kt[:], kmod[:], E2[:], ALU.mult, r=["kmod", "E2"], w=["Kt"])
        p.tt("pool", Bt[:], beta[:], E2[:], ALU.mult, r=["beta", "E2"], w=["Bt"])
        p.tt("dve", Khc[:], kmod[:], E4[:], ALU.mult, r=["kmod", "E4"], w=["Khc"])
        p.tt("pool", Bhc[:], beta[:], E4[:], ALU.mult, r=["beta", "E4"], w=["Bhc"])
        p.cp("act", vB[:], v_[:], r=[("Pm", 2)], w=["vB"])
        for src, dst, ks, kd in ((vB, Vt, "vB", "Vt"), (Khc, Kh, "Khc", "Kh"), (Bhc, Bh, "Bhc", "Bh")):
            pt, pk = pp.get()
            ptb = pt.bitcast(BF16)
            for ch in range(NCH):
                for h in range(2):
                    hs = slice(64 * h, 64 * h + 64)
                    p.tr(ptb[hs, ch * C:(ch + 1) * C], src[hs, ch * C:(ch + 1) * C], c["identb"][hs, hs], r=[ks, "identb"], w=[pk],
                         tile_position=(64 * h, 64 * h))
            p.cp("act", dst.rearrange("p c t -> p (c t)"), ptb[:, 0:TB], r=[pk], w=[kd])
        for hb_ in range(2):
            p1, pk1 = pp.get(); p2, pk2 = pp.get(); p3, pk3 = pp.get()
            for cc in range(HB):
                ch = hb_ * HB + cc
                for h in range(2):
                    hs = slice(64 * h, 64 * h + 64)
                    tp = (64 * h, 64 * h)
                    arh = AR[hs, ch, :, :].rearrange("p a t -> p (a t)")
                    p.mm(p1[hs, cc * 128:(cc + 1) * 128], Kt[hs, ch * C:(ch + 1) * C], arh, True, True, r=["Kt", "AR"], w=[pk1], tile_position=tp)
                    p.mm(p2[hs, cc * 128:(cc + 1) * 128], Bt[hs, ch * C:(ch + 1) * C], arh, True, True, r=["Bt", "AR"], w=[pk2], tile_position=tp)
                    p.mm(p3[hs, cc * C:(cc + 1) * C], AR[hs, ch, 0, :], Bt[hs, ch * C:(ch + 1) * C], True, True, r=["AR", "Bt"], w=[pk3], tile_position=tp)
            chs = slice(hb_ * HB, (hb_ + 1) * HB)
            p.tt("dve", A1[:, chs, :], p1.rearrange("p (c t) -> p c t", c=HB), mU[:], ALU.mult, r=[pk1, "mU"], w=["A1"])
            p.tt("dve", A2m[:, chs, :], p2.rearrange("p (c t) -> p c t", c=HB), mU[:], ALU.mult, r=[pk2, "mU"], w=["A2m"])
            p.tt("dve", Mm[hb_][0][:], p3[:, 0:HB * C].rearrange("p (c t) -> p c t", c=HB), mL[:], ALU.mult, r=[pk3, "mL"], w=[("Mm", hb_, 0)])
            p.cp("pool", Nm[hb_][0][:], A2m[:, chs, 0:C], r=["A2m"], w=[("Nm", hb_, 0)])
            p.tt("pool", Pq[hb_][0][:], Nm[hb_][0][:], idn[:], ALU.add, r=[("Nm", hb_, 0), "idn"], w=[("Pq", hb_, 0)])
            p.tt("pool", Qq[hb_][0][:], Mm[hb_][0][:], idn[:], ALU.add, r=[("Mm", hb_, 0), "idn"], w=[("Qq", hb_, 0)])
        for lvl in range(1, 6):
            o_ = (lvl - 1) % 2; n_ = lvl % 2
            last = lvl == 5
            stage1 = []
            for hb_ in range(2):
                pN, pkN = pp.get()
                pM, pkM = (None, None) if last else pp.get()
                for cc in range(HB):
                    for h in range(2):
                        hs = slice(64 * h, 64 * h + 64); tp = (64 * h, 64 * h)
                        p.mm(pN[hs, cc * C:(cc + 1) * C], Mm[hb_][o_][hs, cc, :], Nm[hb_][o_][hs, cc, :], True, True,
                             r=[("Mm", hb_, o_), ("Nm", hb_, o_)], w=[pkN], tile_position=tp)
                        if not last:
                            p.mm(pM[hs, cc * C:(cc + 1) * C], Nm[hb_][o_][hs, cc, :], Mm[hb_][o_][hs, cc, :], True, True,
                                 r=[("Mm", hb_, o_), ("Nm", hb_, o_)], w=[pkM], tile_position=tp)
                stage1.append((pN, pkN, pM, pkM))
            for hb_ in range(2):
                pN, pkN, pM, pkM = stage1[hb_]
                p.cp("dve", Nm[hb_][n_][:], pN[:, 0:HB * C].rearrange("p (c t) -> p c t", c=HB), r=[pkN], w=[("Nm", hb_, n_)])
                if not last:
                    p.cp("act", Mm[hb_][n_][:], pM[:, 0:HB * C].rearrange("p (c t) -> p c t", c=HB), r=[pkM], w=[("Mm", hb_, n_)])
            stage2 = []
            for hb_ in range(2):
                pP, pkP = pp.get()
                pQ, pkQ = (None, None) if last else pp.get()
                for cc in range(HB):
                    for h in range(2):
                        hs = slice(64 * h, 64 * h + 64); tp = (64 * h, 64 * h)
                        p.mm(pP[hs, cc * C:(cc + 1) * C], Qq[hb_][o_][hs, cc, :], Nm[hb_][n_][hs, cc, :], True, True,
                             r=[("Qq", hb_, o_), ("Nm", hb_, n_)], w=[pkP], tile_position=tp)
                        if not last:
                            p.mm(pQ[hs, cc * C:(cc + 1) * C], Pq[hb_][o_][hs, cc, :], Mm[hb_][n_][hs, cc, :], True, True,
                                 r=[("Pq", hb_, o_), ("Mm", hb_, n_)], w=[pkQ], tile_position=tp)
                stage2.append((pP, pkP, pQ, pkQ))
            for hb_ in range(2):
                pP, pkP, pQ, pkQ = stage2[hb_]
                chs = slice(hb_ * HB, (hb_ + 1) * HB)
                dstP = TmT[:, chs, :] if last else Pq[hb_][n_][:]
                kdP = "TmT" if last else ("Pq", hb_, n_)
                p.tt("dve", dstP, Pq[hb_][o_][:], pP[:, 0:HB * C].rearrange("p (c t) -> p c t", c=HB), ALU.add,
                     r=[pkP, ("Pq", hb_, o_)], w=[kdP])
                if not last:
                    p.tt("dve", Qq[hb_][n_][:], Qq[hb_][o_][:], pQ[:, 0:HB * C].rearrange("p (c t) -> p c t", c=HB), ALU.add,
                         r=[pkQ, ("Qq", hb_, o_)], w=[("Qq", hb_, n_)])
        pY, pkY = psD, "psD"
        for ch in range(NCH):
            hi = hidx[0] % 2; ho = 1 - hi; hidx[0] += 1
            pX, pkX = pp.get()
            for h in range(2):
                hs = slice(64 * h, 64 * h + 64); tp = (64 * h, 64 * h)
                p.mm(pX[hs, 0:C], A1[hs, ch, 0:C], Vt[hs, ch, :], True, False, r=["A1", "Vt"], w=[pkX], tile_position=tp)
                p.mm(pX[hs, 0:C], AR[hs, ch, 0, :], Hb[hi][hs, :], False, True, r=["AR", ("Hb", hi)], w=[pkX], tile_position=tp)
            p.cp("act", Xb[:], pX[:, 0:C], r=[pkX], w=["Xb"])
            pU, pkU = pp.get()
            for h in range(2):
                hs = slice(64 * h, 64 * h + 64); tp = (64 * h, 64 * h)
                p.mm(pU[hs, 0:C], TmT[hs, ch, :], Xb[hs, :], True, True, r=["TmT", "Xb"], w=[pkU], tile_position=tp)
            p.cp("act", Ub[:], pU[:, 0:C], r=[pkU], w=["Ub"])
            pH, pkH = pp.get()
            for h in range(2):
                hs = slice(64 * h, 64 * h + 64); tp = (64 * h, 64 * h)
                p.mm(pH[hs, 0:C], Kh[hs, ch, :], Vt[hs, ch, :], True, False, r=["Kh", "Vt"], w=[pkH], tile_position=tp)
                p.mm(pH[hs, 0:C], Bh[hs, ch, :], Ub[hs, :], False, True, r=["Bh", "Ub"], w=[pkH], tile_position=tp)
            gC = E1[:, ch * C + C - 1:ch * C + C]
            p.stt(Hb[ho][:], Hf[:], gC, pH[:, 0:C], ALU.mult, ALU.add, r=["Hf", "E1", pkH], w=[("Hb", ho)])
            for h in range(2):
                hs = slice(64 * h, 64 * h + 64); tp = (64 * h, 64 * h)
                yo_ = pY[hs, ch * C:(ch + 1) * C]
                p.mm(yo_, Hb[hi][hs, :], AR[hs, ch, 1, :], True, False, r=[("Hb", hi), "AR"], w=[pkY], tile_position=tp)
                p.mm(yo_, Vt[hs, ch, :], A1[hs, ch, C:2 * C], False, False, r=["Vt", "A1"], w=[pkY], tile_position=tp)
                p.mm(yo_, Ub[hs, :], A2m[hs, ch, C:2 * C], False, True, r=["Ub", "A2m"], w=[pkY], tile_position=tp)
            p.stt(Hf[:], Hf[:], gC, pH[:, 0:C], ALU.mult, ALU.add, r=["Hf", "E1", pkH], w=["Hf"])
        p.cp("act", Yf[:], pY[:], r=[pkY], w=["Yf"])
        pm_, pkm = pp.get()
        p.mm(pm_[:], bones[:], Yf[:], True, True, r=["bones", "Yf"], w=[pkm])
        p.stt(yc[:], pm_[:], -1.0 / C, Yf[:], ALU.mult, ALU.add, r=[pkm, "Yf"], w=["yc"])
        p.tt("pool", ycq[:], yc[:], yc[:], ALU.mult, r=["yc"], w=["ycq"])
        pv_, pkv = pp.get()
        p.mm(pv_[:], bones[:], ycq[:], True, True, r=["bones", "ycq"], w=[pkv])
        p.ts("dve", rs[:], pv_[:], 1.0 / C, ALU.mult, GN_EPS, ALU.add, r=[pkv], w=["rsA"])
        p.act(rs[:], rs[:], AF.Sqrt, r=["rsA"], w=["rsA"])
        p.op("dve", lambda e: e.reciprocal(out=rs[:], in_=rs[:]), r=["rsA"], w=["rsA"])
        p.tt("dve", yc[:], yc[:], rs[:], ALU.mult, r=["yc", "rsA"], w=["yc"])
        p.ts("dve", yc[:], yc[:], LNW, ALU.mult, LNB, ALU.add, r=["yc", "CV"], w=["yc"])
        p.tt("pool", yc[:], yc[:], bon[:], ALU.add, r=["yc", "bon"], w=["yc"])
        p.tt("dve", yo[:], yc[:], gv[:], ALU.mult, r=["yc", "gv"], w=["yo"])
        p.dma("sp", yT[0:128, n * TB:(n + 1) * TB], yo[:], s_st, r=["yo"], w=[("yT", 0, n)])
        pO, pkO = psD, "psD"
        for i in range(4):
            kv0 = 0
            if n == 0:
                kv0 = TB - i * 128
            nk = 640 - kv0
            nkt = nk // 128
            for h in range(2):
                hs = slice(64 * h, 64 * h + 64)
                pS1, pkS1 = pp.get(); pS2, pkS2 = pp.get()
                lo = i * 128 + kv0
                n1 = min(nk, 512)
                p.mm(pS1[:, 0:n1], qT[hs, i * 128:(i + 1) * 128], Kring[hs, lo:lo + n1], True, True, r=["qT", "Kring"], w=[pkS1])
                p.stt(S[:, kv0:kv0 + n1], pS1[:, 0:n1], 0.125, ATB[:, h, kv0:kv0 + n1], ALU.mult, ALU.add, r=[pkS1, "ATB"], w=["S_att"])
                if nk > 512:
                    p.mm(pS2[:, 0:128], qT[hs, i * 128:(i + 1) * 128], Kring[hs, lo + 512:lo + 640], True, True, r=["qT", "Kring"], w=[pkS2])
                    p.stt(S[:, 512:640], pS2[:, 0:128], 0.125, ATB[:, h, 512:640], ALU.mult, ALU.add, r=[pkS2, "ATB"], w=["S_att"])
                p.op("dve", lambda e, kv0=kv0: e.tensor_reduce(out=ast[:, 0:1], in_=S[:, kv0:640], axis=AX.X, op=ALU.max), r=["S_att"], w=["ast"])
                p.ts("dve", ast[:, 1:2], ast[:, 0:1], -1.0, ALU.mult, r=["ast"], w=["ast"])
                p.act(Pe[:, kv0:640], S[:, kv0:640], AF.Exp, r=["S_att", "ast"], w=["Pe", "ast"], bias=ast[:, 1:2], accum_out=ast[:, 2:3])
                p.op("dve", lambda e: e.reciprocal(out=ast[:, 3:4], in_=ast[:, 2:3]), r=["ast"], w=["ast"])
                p.ts("dve", Pn[:, kv0:640], Pe[:, kv0:640], ast[:, 3:4], ALU.mult, r=["Pe", "ast"], w=["Pn"])
                ptr, pktr = pp.get()
                ptrb = ptr.bitcast(BF16)
                for kt in range(nkt):
                    c0 = kv0 + kt * 128
                    p.tr(ptrb[:, kt * 128:(kt + 1) * 128], Pn[:, c0:c0 + 128], c["identb"][:], r=["Pn", "identb"], w=[pktr])
                p.cp("act", PT[:, 0:nkt, :], ptrb[:, 0:nkt * 128].rearrange("p (k q) -> p k q", k=nkt), r=[pktr], w=["PT"])
                vt0 = i + (kv0 // 128)
                for kt in range(nkt):
                    p.mm(pO[hs, i * 128:(i + 1) * 128], Vring[:, vt0 + kt, 64 * h:64 * h + 64], PT[:, kt, :], kt == 0, kt == nkt - 1,
                         r=["Vring", "PT"], w=[pkO], tile_position=(0, 64 * h))
        p.cp("act", ao[:], pO[:], r=[pkO], w=["ao"])
        p.dma("sp", yT[128:256, n * TB:(n + 1) * TB], ao[:], s_st, r=["ao"], w=[("yT", 1, n)])
    p.finalize(final_sems=[s_st])
    return p


C = 64
TB = 512
NCH = TB // C
DEC = 0.6065306597126334
GN_EPS = 64e-5
NBLK = 16
FIRST_FULL = 11
KV_BLOCK = 10
HB = 4


def emit_A(p, ph, c, pools, psD, psO, dr_in, yscr, nblk=NBLK, first_full=FIRST_FULL, kv_block=KV_BLOCK, nhp=4):
    xb = dr_in["xb"]; wsel4 = dr_in["wsel"]; g_mix = dr_in["g_mix"]; cv4 = dr_in["cv"]
    w24 = dr_in["w2"]; a24 = dr_in["a2"]; g24 = dr_in["g2"]; attb4 = dr_in["attb"]; amask = dr_in["amask"]
    s_ld = p.sem("a_ld"); s_x = [p.sem(f"a_x{i}") for i in range(4)]; s_w = [p.sem("a_w0"), p.sem("a_w1")]
    s_st = p.sem("a_st")
    s_hs = p.sem("a_hs"); s_hl = [p.sem("a_hl0"), p.sem("a_hl1")]
    hTs = dr_in["hTs"]
    sb = ph.sb
    pp1, pp2, s3 = pools

    gcol = sb("gcol", [128, NK], F32)
    p.dma("sp", gcol[:], g_mix.rearrange("(k p) -> p k", p=128), s_ld, w=["gcol"], allow_slow_non_contiguous=True)
    bones = sb("bones", [128, 128], F32)
    p.memset("pool", bones[:], 0.0, w=["bones"])
    p.memset("pool", bones[0:64, 0:64], 1.0, w=["bones"])
    p.memset("pool", bones[64:128, 64:128], 1.0, w=["bones"])
    rmask = sb("rmask", [128, TB], F32)
    p.memset("pool", rmask[:], 1.0, w=["rmask"])
    p.memset("pool", rmask.rearrange("p (c t) -> p c t", t=C)[:, :, 0:1], 0.0, w=["rmask"])
    mU = sb("mU", [128, HB, 128], F32); mL = sb("mL", [128, HB, 64], F32); idn = sb("idn", [128, HB, 64], F32)
    for t_, name in ((mU, "mU"), (mL, "mL"), (idn, "idn")):
        p.memset("pool", t_[:], 1.0, w=[name])
    for h in range(2):
        hs = slice(64 * h, 64 * h + 64)
        p.op("pool", lambda e, hs=hs: e.affine_select(out=mU[hs, :, 0:64], in_=mU[hs, :, 0:64], pattern=[[0, HB], [1, 64]],
                                                      compare_op=ALU.is_gt, fill=0.0, base=0, channel_multiplier=-1), r=["mU"], w=["mU"])
        p.op("pool", lambda e, hs=hs: e.affine_select(out=mU[hs, :, 64:128], in_=mU[hs, :, 64:128], pattern=[[0, HB], [1, 64]],
                                                      compare_op=ALU.is_ge, fill=0.0, base=0, channel_multiplier=-1), r=["mU"], w=["mU"])
        p.op("pool", lambda e, hs=hs: e.affine_select(out=mL[hs, :, :], in_=mL[hs, :, :], pattern=[[0, HB], [-1, 64]],
                                                      compare_op=ALU.is_gt, fill=0.0, base=0, channel_multiplier=1), r=["mL"], w=["mL"])
        p.op("pool", lambda e, hs=hs: e.affine_select(out=idn[hs, :, :], in_=idn[hs, :, :], pattern=[[0, HB], [-1, 64]],
                                                      compare_op=ALU.is_equal, fill=0.0, base=0, channel_multiplier=1), r=["idn"], w=["idn"])
    AMK = sb("AMK", [128, 4, 640], F32)
    p.dma("sp", AMK[:], amask, s_ld, w=["AMK"])

    def t5(name, dt=F32, n=TB):
        return sb(name, [128, n], dt)
    CV = sb("CV", [128, 16], F32)
    MU = lambda ct: CV[:, ct:ct + 1]
    W0, A0, KK, KA, RK, LNW, LNB = [CV[:, 5 + i:6 + i] for i in range(7)]
    OMKA = CV[:, 12:13]
    W2 = sb("W2", [128, 128], F32); A2 = sb("A2", [128, 128], F32); G2 = sb("G2", [128, 128], F32)
    p.memset("pool", W2[:], 0.0, w=["W2"])
    p.memset("pool", A2[:], 0.0, w=["A2"])
    ATB = sb("ATB", [128, 2, 640], F32)
    WB = sb("WB", [128, NK, 1024], BF16)
    wst = [sb(f"wstA{i}", [128, NK, 128], F32) for i in range(2)]
    XTbig = sb("XTbig", [128, 4, D], F32)
    XT = [XTbig[:, i, :] for i in range(4)]
    hTalt = XTbig[:, 0:2, :].rearrange("p a d -> p (a d)").bitcast(BF16).rearrange("p (k t) -> p k t", k=NK)
    sq = sb("sqA", [128, D], BF16); ssb2 = [sb(f"ssA{i}", [128, 4], F32) for i in range(2)]; hb2 = [sb(f"hbA{i}", [128, D], BF16) for i in range(2)]
    hT = sb("hTA", [128, NK, TB], BF16)
    PJ = [[sb(f"PJ{i}_{ct}", [128, 1 + TB], F32) for ct in range(5)] for i in range(2)]
    qT = t5("qT", BF16)
    Kring = sb("Kring", [128, 2 * TB], BF16)
    Vring = sb("Vring", [128, 8, 128], BF16)
    tmp = t5("tmpA"); Pm = [t5(f"Pm{ct}") for ct in range(5)]
    sgm = t5("sgm"); av = t5("av"); gv2 = [t5("gv0"), t5("gv1")]
    kk = t5("kk"); kmod = t5("kmod"); beta = t5("beta"); bon2 = [t5("bon0"), t5("bon1")]
    cs = t5("cs"); csd = t5("csd"); dcs = t5("dcs")
    E12 = [t5("E1_0"), t5("E1_1")]; E2 = t5("E2"); E3 = t5("E3"); E4 = t5("E4")
    txw, ktxw = E2, "E2"; sxg, ksxg = E3, "E3"; kk2, kkk2 = csd, "csd"; rn, krn = dcs, "dcs"
    Yf, kYf = t5("Yf"), "Yf"; yc, kyc = t5("yc"), "yc"; ycq, kycq = t5("ycq"), "ycq"; rs, krs = t5("rsA"), "rsA"
    AR2 = [sb(f"AR{i}", [128, NCH, 2, C], BF16) for i in range(2)]
    for i in range(2):
        p.memset("pool", AR2[i][:], 0.0, w=[("AR", i)])
    Kt2 = [t5(f"Kt{i}", BF16) for i in range(2)]; Bt2 = [t5(f"Bt{i}", BF16) for i in range(2)]; Khc2 = [t5(f"Khc{i}", BF16) for i in range(2)]
    Bhc2 = [t5(f"Bhc{i}", BF16) for i in range(2)]; vB2 = [t5(f"vB{i}", BF16) for i in range(2)]
    Vt2 = [sb(f"Vt{i}", [128, NCH, C], BF16) for i in range(2)]; Kh2 = [sb(f"Kh{i}", [128, NCH, C], BF16) for i in range(2)]
    Bh2 = [sb(f"Bh{i}", [128, NCH, C], BF16) for i in range(2)]
    A12 = [sb(f"A1_{i}", [128, NCH, 128], BF16) for i in range(2)]; A2m2 = [sb(f"A2m_{i}", [128, NCH, 128], BF16) for i in range(2)]
    NS = [[sb(f"NS{hb_}_{i}", [128, HB, 2, C], BF16) for i in range(2)] for hb_ in range(2)]
    Mm = [[sb(f"Mm{hb_}_{i}", [128, HB, C], BF16) for i in range(2)] for hb_ in range(2)]
    TmT2 = [sb(f"TmT{i}", [128, NCH, C], BF16) for i in range(2)]
    Hf = sb("Hf", [128, C], F32); Hb = [sb(f"Hb{i}", [128, C], BF16) for i in range(2)]
    Xb = sb("Xb", [128, C], BF16); Ub = sb("Ub", [128, C], BF16)
    yo = t5("yo", BF16)
    S = sb("S_att", [128, 640], F32); Pe = sb("Pe", [128, 640], F32); Pn = sb("Pn", [128, 640], BF16)
    PT = sb("PT", [128, 5, 128], BF16)
    ast = sb("ast", [128, 4], F32)
    ao = t5("ao", BF16)

    xv = xb.rearrange("(n p) d -> p n d", p=128)
    xcnt = [0]

    def load_x_tile(n, i):
        j = xcnt[0] % 4; xcnt[0] += 1
        p.dma("sp", XT[j], xv[:, n * 4 + i, :], s_x[j], w=[("XT", j)])
        return j

    for hp in range(nhp):
        p.dma("sp", CV[:], cv4[hp], s_ld, w=["CV"])
        p.ts("dve", OMKA, KA, -1.0, ALU.mult, 1.0, ALU.add, r=["CV"], w=["CV"])
        p.dma("sp", W2[0:64, :], w24[hp], s_ld, w=["W2"])
        p.dma("sp", A2[64:128, :], a24[hp], s_ld, w=["A2"])
        p.dma("sp", G2[:], g24[hp], s_ld, w=["G2"])
        for h in range(2):
            p.dma("sp", ATB[:, h, :], attb4[hp, h], s_ld, w=["ATB"])
        wv = wsel4[hp].rearrange("(k p) c -> p k c", p=128)
        for j in range(8):
            b_ = j % 2
            p.dma("sp", wst[b_][:], wv[:, :, j * 128:(j + 1) * 128], s_w[b_], w=[("wstA", b_)])
            for dk in range(NK):
                if dk % 2:
                    p.act(WB[:, dk, j * 128:(j + 1) * 128], wst[b_][:, dk, :], AF.Copy, r=[("wstA", b_), "gcol"], w=["WB"], scale=gcol[:, dk:dk + 1])
                else:
                    p.ts("dve", WB[:, dk, j * 128:(j + 1) * 128], wst[b_][:, dk, :], gcol[:, dk:dk + 1], ALU.mult,
                         r=[("wstA", b_), "gcol"], w=["WB"])
        p.memset("pool", Hf[:], 0.0, w=["Hf"])
        p.memset("pool", Hb[0][:], 0.0, w=[("Hb", 0)])
        for ct in range(5):
            p.memset("pool", PJ[0][ct][:, 0:1], 0.0, w=[("PJ", 0, ct)])
        hidx = 0
        share = hp > 0

        def load_hT(n):
            if n % 2 == 0:
                p.dma("sp", hT[:].rearrange("p k t -> p (k t)"), hTs[n], s_hl[0], r=[("hTs", n)], w=["hTA"])
            else:
                p.dma("sp", hTalt.rearrange("p k t -> p (k t)"), hTs[n], s_hl[1], r=[("hTs", n)], w=["hTB", ("XT", 0), ("XT", 1)])
        if share:
            load_hT(0)
        else:
            nxt = [load_x_tile(0, i) for i in range(4)]
        for n in range(nblk):
            b = n % 2
            full = n >= first_full
            kvb = full or n == kv_block
            if share:
                hTc, khT = (hT, "hTA") if n % 2 == 0 else (hTalt, "hTB")
                if n + 1 < nblk:
                    load_hT(n + 1)
            else:
                hTc, khT = hT, "hTA"
                cur_x = nxt
            AR, kAR = AR2[b], ("AR", b); E1, kE1 = E12[b], ("E1", b); bon, kbon = bon2[b], ("bon", b); gv, kgv = gv2[b], ("gv", b)
            Vt, kVt = Vt2[b], ("Vt", b); Kh, kKh = Kh2[b], ("Kh", b); Bh, kBh = Bh2[b], ("Bh", b)
            A1, kA1 = A12[b], ("A1", b); A2m, kA2m = A2m2[b], ("A2m", b); TmT, kTmT = TmT2[b], ("TmT", b)
            Kt, kKt = Kt2[b], ("Kt", b); Bt, kBt = Bt2[b], ("Bt", b); Khc, kKhc = Khc2[b], ("Khc", b)
            Bhc, kBhc = Bhc2[b], ("Bhc", b); vB, kvB = vB2[b], ("vB", b)
            nxt = []
            for i in range(0 if share else 4):
                j = cur_x[i]
                xt = XT[j]
                ssb = ssb2[i % 2]; hb = hb2[i % 2]; kss = ("ssA", i % 2); khb = ("hbA", i % 2)
                p.act(sq[:], xt, AF.Square, r=[("XT", j)], w=[kss], accum_out=ssb[:, 0:1])
                p.ts("dve", ssb[:, 1:2], ssb[:, 0:1], 1.0 / D, ALU.mult, NORM_EPS, ALU.add, r=[kss], w=[kss])
                p.act(ssb[:, 2:3], ssb[:, 1:2], AF.Sqrt, r=[kss], w=[kss])
                p.op("dve", lambda e, ssb=ssb: e.reciprocal(out=ssb[:, 3:4], in_=ssb[:, 2:3]), r=[kss], w=[kss])
                p.ts("dve", hb[:], xt, ssb[:, 3:4], ALU.mult, r=[("XT", j), kss], w=[khb])
                if n + 1 < nblk:
                    nxt.append(load_x_tile(n + 1, i))
                pt, pk = pp1.get()
                ptb = pt.bitcast(BF16)
                for dk in range(NK):
                    p.tr(ptb[:, dk * 128:(dk + 1) * 128], hb[:, dk * 128:(dk + 1) * 128], c["identb"][:], r=[khb, "identb"], w=[pk])
                p.cp("act", hT[:, :, i * 128:(i + 1) * 128], ptb.rearrange("p (k t) -> p k t", k=NK), r=[pk], w=["hTA"])
            if not share and nhp > 1:
                p.dma("sp", hTs[n], hT[:].rearrange("p k t -> p (k t)"), s_hs, r=["hTA"], w=[("hTs", n)])
            if full:
                p.cp("pool", Kring[:, 0:TB], Kring[:, TB:2 * TB], r=["Kring"], w=["Kring"])
                p.cp("pool", Vring[:, 0:4, :], Vring[:, 4:8, :], r=["Vring"], w=["Vring"])
            cts = [0, 1, 2, 3, 4, 5, 6] if full else ([1, 2, 3, 6] if kvb else [1, 2, 3])
            for ct in cts:
                pt, pk = pp1.get()
                for dk in range(NK):
                    p.mm(pt[:], WB[:, dk, ct * 128:(ct + 1) * 128], hTc[:, dk, :], dk == 0, dk == NK - 1, r=["WB", khT], w=[pk])
                if ct < 5:
                    p.cp("act" if ct % 2 else "dve", PJ[b][ct][:, 1:1 + TB], pt[:], r=[pk], w=[("PJ", b, ct)])
                elif ct == 5:
                    p.cp("act", qT[:], pt[:], r=[pk], w=["qT"])
                else:
                    p.cp("dve", Kring[:, TB:2 * TB], pt[:], r=[pk], w=["Kring"])
            if kvb:
                pt, pk = pp1.get()
                for i in range(4):
                    for dk in range(NK):
                        p.mm(pt[:, i * 128:(i + 1) * 128], hTc[:, dk, i * 128:(i + 1) * 128], WB[:, dk, 7 * 128:8 * 128], dk == 0, dk == NK - 1,
                             r=["WB", khT], w=[pk])
                p.cp("act", Vring[:, 4:8, :], pt.rearrange("p (i v) -> p i v", i=4), r=[pk], w=["Vring"])
            mcts = [0, 1, 2, 3, 4] if full else [1, 2, 3]
            for ct in range(5):
                if ct in mcts:
                    cur = PJ[b][ct][:, 1:1 + TB]; prv = PJ[b][ct][:, 0:TB]
                    p.tt("pool", tmp[:], prv, cur, ALU.subtract, r=[("PJ", b, ct)], w=["tmpA"])
                    p.stt(Pm[ct][:], tmp[:], MU(ct), cur, ALU.mult, ALU.add, r=["tmpA", ("PJ", b, ct), "CV"], w=[("Pm", ct)])
                    p.cp("pool", PJ[1 - b][ct][:, 0:1], PJ[b][ct][:, TB:TB + 1], r=[("PJ", b, ct)], w=[("PJ", 1 - b, ct)])
                elif n + 1 == first_full:
                    pt, pk = pp1.get()
                    for dk in range(NK):
                        p.mm(pt[:, 0:1], WB[:, dk, ct * 128:(ct + 1) * 128], hTc[:, dk, TB - 1:TB], dk == 0, dk == NK - 1, r=["WB", khT], w=[pk])
                    p.cp("act", PJ[1 - b][ct][:, 0:1], pt[:, 0:1], r=[pk], w=[("PJ", 1 - b, ct)])
            r_, k_, v_ = Pm[0], Pm[1], Pm[2]
            p.act(txw[:], Pm[3][:], AF.Tanh, r=[("Pm", 3)], w=[ktxw])
            pw_, pkw = pp1.get()
            p.mm(pw_[:], W2[:], txw[:], True, True, r=["W2", ktxw], w=[pkw])
            p.act(sgm[:], pw_[:], AF.Sigmoid, r=[pkw, "CV"], w=["sgm"], bias=W0)
            pa_, pka = pp1.get()
            p.mm(pa_[:], A2[:], Pm[3][:], True, True, r=["A2", ("Pm", 3)], w=[pka])
            p.act(av[:], pa_[:], AF.Sigmoid, r=[pka, "CV"], w=["av"], bias=A0)
            if full:
                p.act(sxg[:], Pm[4][:], AF.Sigmoid, r=[("Pm", 4)], w=[ksxg])
                pg_, pkg = pp1.get()
                p.mm(pg_[:], G2[:], sxg[:], True, True, r=["G2", ksxg], w=[pkg])
                p.cp("act", gv[:], pg_[:], r=[pkg], w=[kgv])
            p.ts("dve", kk[:], k_[:], KK, ALU.mult, r=[("Pm", 1), "CV"], w=["kk"])
            p.tt("pool", kk2[:], kk[:], kk[:], ALU.mult, r=["kk"], w=[kkk2])
            pn_, pkn = pp1.get()
            p.mm(pn_[:], bones[:], kk2[:], True, True, r=["bones", kkk2], w=[pkn])
            p.act(rn[:], pn_[:], AF.Sqrt, r=[pkn], w=[krn])
            p.ts("dve", rn[:], rn[:], 1e-12, ALU.max, r=[krn], w=[krn])
            p.op("dve", lambda e: e.reciprocal(out=rn[:], in_=rn[:]), r=[krn], w=[krn])
            p.tt("dve", kk[:], kk[:], rn[:], ALU.mult, r=["kk", krn], w=["kk"])
            p.ts("dve", tmp[:], av[:], KA, ALU.mult, OMKA, ALU.add, r=["av", "CV"], w=["tmpA"])
            p.tt("dve", kmod[:], k_[:], tmp[:], ALU.mult, r=[("Pm", 1), "tmpA"], w=["kmod"])
            p.tt("pool", beta[:], kk[:], av[:], ALU.mult, r=["kk", "av"], w=["beta"])
            if full:
                p.stt(tmp[:], r_[:], RK, kmod[:], ALU.mult, ALU.mult, r=[("Pm", 0), "kmod", "CV"], w=["tmpA"])
                pb_, pkb = pp1.get()
                p.mm(pb_[:], bones[:], tmp[:], True, True, r=["bones", "tmpA"], w=[pkb])
                p.tt("dve", bon[:], v_[:], pb_[:], ALU.mult, r=[("Pm", 2), pkb], w=[kbon])
            p.op("dve", lambda e: e.tensor_tensor_scan(out=cs[:], data0=rmask[:], data1=sgm[:], initial=0.0, op0=ALU.mult, op1=ALU.add),
                 r=["rmask", "sgm"], w=["cs"])
            p.tt("pool", csd[:], cs[:], sgm[:], ALU.subtract, r=["cs", "sgm"], w=["csd"])
            cs3 = cs.rearrange("p (c t) -> p c t", t=C)
            p.tt("pool", dcs.rearrange("p (c t) -> p c t", t=C), cs3[:, :, C - 1:C].to_broadcast([128, NCH, C]), cs3, ALU.subtract,
                 r=["cs"], w=["dcs"])
            p.act(E1[:], cs[:], AF.Exp, r=["cs"], w=[kE1], scale=-DEC)
            p.act(E2[:], cs[:], AF.Exp, r=["cs"], w=["E2"], scale=DEC)
            p.act(E3[:], csd[:], AF.Exp, r=["csd"], w=["E3"], scale=-DEC)
            p.act(E4[:], dcs[:], AF.Exp, r=["dcs"], w=["E4"], scale=-DEC)
            c3 = lambda ap: ap.rearrange("p (c t) -> p c t", t=C)
            p.stt(AR[:, :, 0, :], c3(kk), -1.0, c3(E3), ALU.mult, ALU.mult, r=["kk", "E3"], w=[kAR])
            if full:
                p.tt("dve", AR[:, :, 1, :], c3(r_), c3(E1), ALU.mult, r=[("Pm", 0), kE1], w=[kAR])
            p.tt("pool", Kt[:], kmod[:], E2[:], ALU.mult, r=["kmod", "E2"], w=[kKt])
            p.tt("pool", Bt[:], beta[:], E2[:], ALU.mult, r=["beta", "E2"], w=[kBt])
            p.tt("dve", Khc[:], kmod[:], E4[:], ALU.mult, r=["kmod", "E4"], w=[kKhc])
            p.tt("pool", Bhc[:], beta[:], E4[:], ALU.mult, r=["beta", "E4"], w=[kBhc])
            p.cp("act", vB[:], v_[:], r=[("Pm", 2)], w=[kvB])
            for src, dst, ks, kd in ((vB, Vt, kvB, kVt), (Khc, Kh, kKhc, kKh), (Bhc, Bh, kBhc, kBh)):
                pt, pk = pp2.get()
                ptb = pt.bitcast(BF16)
                for ch in range(NCH):
                    for h in range(2):
                        hs = slice(64 * h, 64 * h + 64)
                        p.tr(ptb[hs, ch * C:(ch + 1) * C], src[hs, ch * C:(ch + 1) * C], c["identb"][hs, hs], r=[ks, "identb"], w=[pk],
                             tile_position=(64 * h, 64 * h))
                p.cp("act", dst.rearrange("p c t -> p (c t)"), ptb[:, 0:TB], r=[pk], w=[kd])
            for hb_ in range(2):
                p1, pk1 = pp2.get(); p2, pk2 = pp2.get(); p3, pk3 = pp2.get()
                for cc in range(HB):
                    ch = hb_ * HB + cc
                    for which in range(3):
                        for h in range(2):
                            hs = slice(64 * h, 64 * h + 64)
                            tp = (64 * h, 64 * h)
                            arh = AR[hs, ch, :, :].rearrange("p a t -> p (a t)")
                            if which == 0:
                                p.mm(p1[hs, cc * 128:(cc + 1) * 128], Kt[hs, ch * C:(ch + 1) * C], arh, True, True, r=[kKt, kAR], w=[pk1], tile_position=tp)
                            elif which == 1:
                                p.mm(p2[hs, cc * 128:(cc + 1) * 128], Bt[hs, ch * C:(ch + 1) * C], arh, True, True, r=[kBt, kAR], w=[pk2], tile_position=tp)
                            else:
                                p.mm(p3[hs, cc * C:(cc + 1) * C], AR[hs, ch, 0, :], Bt[hs, ch * C:(ch + 1) * C], True, True, r=[kAR, kBt], w=[pk3], tile_position=tp)
                chs = slice(hb_ * HB, (hb_ + 1) * HB)
                p.tt("dve", A1[:, chs, :], p1.rearrange("p (c t) -> p c t", c=HB), mU[:], ALU.mult, r=[pk1, "mU"], w=[kA1])
                p.tt("dve", A2m[:, chs, :], p2.rearrange("p (c t) -> p c t", c=HB), mU[:], ALU.mult, r=[pk2, "mU"], w=[kA2m])
                p.tt("dve", Mm[hb_][0][:], p3[:, 0:HB * C].rearrange("p (c t) -> p c t", c=HB), mL[:], ALU.mult, r=[pk3, "mL"], w=[("Mm", hb_, 0)])
                p.cp("pool", NS[hb_][0][:, :, 0, :], A2m[:, chs, 0:C], r=[kA2m], w=[("NS", hb_, 0)])
            for lvl in range(0, 6):
                o_ = lvl % 2; n_ = (lvl + 1) % 2
                first = lvl == 0; last = lvl == 5
                stage = []
                pBb, pkBb = (None, None) if last else pp2.get()
                for hb_ in range(2):
                    pA, pkA = pp2.get()
                    pB, pkB = (None, None) if last else (pBb[:, hb_ * HB * C:(hb_ + 1) * HB * C], pkBb)
                    for cc in range(HB):
                        for h in range(2):
                            hs = slice(64 * h, 64 * h + 64); tp = (64 * h, 64 * h)
                            if first:
                                rhs = NS[hb_][o_][hs, cc, 0, :]; wd = C
                            elif last:
                                rhs = NS[hb_][o_][hs, cc, 1, :]; wd = C
                            else:
                                rhs = NS[hb_][o_][hs, cc, :, :].rearrange("p a t -> p (a t)"); wd = 2 * C
                            p.mm(pA[hs, cc * 128:cc * 128 + wd], Mm[hb_][o_][hs, cc, :], rhs, True, True,
                                 r=[("Mm", hb_, o_), ("NS", hb_, o_)], w=[pkA], tile_position=tp)
                        if not last:
                            for h in range(2):
                                hs = slice(64 * h, 64 * h + 64); tp = (64 * h, 64 * h)
                                p.mm(pB[hs, cc * C:(cc + 1) * C], NS[hb_][o_][hs, cc, 0, :], Mm[hb_][o_][hs, cc, :], True, True,
                                     r=[("Mm", hb_, o_), ("NS", hb_, o_)], w=[pkB], tile_position=tp)
                    stage.append((pA, pkA, pB, pkB))
                for hb_ in range(2):
                    pA, pkA, pB, pkB = stage[hb_]
                    chs = slice(hb_ * HB, (hb_ + 1) * HB)
                    pA3 = pA.rearrange("p (c a t) -> p c a t", c=HB, a=2)
                    if last:
                        p.tt("dve", TmT[:, chs, :], NS[hb_][o_][:, :, 1, :], pA3[:, :, 0, :], ALU.add, r=[pkA, ("NS", hb_, o_)], w=[kTmT])
                        continue
                    p.cp("act", Mm[hb_][n_][:], pB[:, 0:HB * C].rearrange("p (c t) -> p c t", c=HB), r=[pkB], w=[("Mm", hb_, n_)])
                    p.cp("dve", NS[hb_][n_][:, :, 0, :], pA3[:, :, 0, :], r=[pkA], w=[("NS", hb_, n_)])
                    if first:
                        p.tt("pool", NS[hb_][n_][:, :, 1, :], NS[hb_][o_][:, :, 0, :], idn[:], ALU.add, r=[("NS", hb_, o_), "idn"], w=[("NS", hb_, n_)])
                    else:
                        p.tt("dve", NS[hb_][n_][:, :, 1, :], NS[hb_][o_][:, :, 1, :], pA3[:, :, 1, :], ALU.add,
                             r=[pkA, ("NS", hb_, o_)], w=[("NS", hb_, n_)])
            pY, pkY = psD, "psD"
            for ch in range(NCH):
                hi = hidx % 2; ho = 1 - hi; hidx += 1
                pX, pkX = s3[:, 0:C], "s3"
                for h in range(2):
                    hs = slice(64 * h, 64 * h + 64); tp = (64 * h, 64 * h)
                    p.mm(pX[hs, 0:C], A1[hs, ch, 0:C], Vt[hs, ch, :], True, False, r=[kA1, kVt], w=[pkX], tile_position=tp)
                for h in range(2):
                    hs = slice(64 * h, 64 * h + 64); tp = (64 * h, 64 * h)
                    p.mm(pX[hs, 0:C], AR[hs, ch, 0, :], Hb[hi][hs, :], False, True, r=[kAR, ("Hb", hi)], w=[pkX], tile_position=tp)
                p.cp("dve", Xb[:], pX[:, 0:C], r=[pkX], w=["Xb"])
                pU, pkU = s3[:, 0:C], "s3"
                for h in range(2):
                    hs = slice(64 * h, 64 * h + 64); tp = (64 * h, 64 * h)
                    p.mm(pU[hs, 0:C], TmT[hs, ch, :], Xb[hs, :], True, True, r=[kTmT, "Xb"], w=[pkU], tile_position=tp)
                p.cp("dve", Ub[:], pU[:, 0:C], r=[pkU], w=["Ub"])
                pH, pkH = (s3[:, 0:C], "s3") if full else (psD[:, 0:C], "psD")
                for h in range(2):
                    hs = slice(64 * h, 64 * h + 64); tp = (64 * h, 64 * h)
                    p.mm(pH[hs, 0:C], Kh[hs, ch, :], Vt[hs, ch, :], True, False, r=[kKh, kVt], w=[pkH], tile_position=tp)
                for h in range(2):
                    hs = slice(64 * h, 64 * h + 64); tp = (64 * h, 64 * h)
                    p.mm(pH[hs, 0:C], Bh[hs, ch, :], Ub[hs, :], False, True, r=[kBh, "Ub"], w=[pkH], tile_position=tp)
                gC = E1[:, ch * C + C - 1:ch * C + C]
                p.stt(Hb[ho][:], Hf[:], gC, pH[:, 0:C], ALU.mult, ALU.add, r=["Hf", kE1, pkH], w=[("Hb", ho)])
                if full:
                    for which in range(3):
                        for h in range(2):
                            hs = slice(64 * h, 64 * h + 64); tp = (64 * h, 64 * h)
                            yo_ = pY[hs, ch * C:(ch + 1) * C]
                            if which == 0:
                                p.mm(yo_, Hb[hi][hs, :], AR[hs, ch, 1, :], True, False, r=[("Hb", hi), kAR], w=[pkY], tile_position=tp)
                            elif which == 1:
                                p.mm(yo_, Vt[hs, ch, :], A1[hs, ch, C:2 * C], False, False, r=[kVt, kA1], w=[pkY], tile_position=tp)
                            else:
                                p.mm(yo_, Ub[hs, :], A2m[hs, ch, C:2 * C], False, True, r=["Ub", kA2m], w=[pkY], tile_position=tp)
                p.stt(Hf[:], Hf[:], gC, pH[:, 0:C], ALU.mult, ALU.add, r=["Hf", kE1, pkH], w=["Hf"])
            if not full:
                continue
            if n == first_full:
                ycol0, ysrc0, ylen = 0, TB - 128, 128
            else:
                ycol0, ysrc0, ylen = 128 + (n - first_full - 1) * TB, 0, TB
            p.cp("act", Yf[:], pY[:], r=[pkY], w=[kYf])
            pm_, pkm = pp1.get()
            p.mm(pm_[:], bones[:], Yf[:], True, True, r=["bones", kYf], w=[pkm])
            p.stt(yc[:], pm_[:], -1.0 / C, Yf[:], ALU.mult, ALU.add, r=[pkm, kYf], w=[kyc])
            p.tt("pool", ycq[:], yc[:], yc[:], ALU.mult, r=[kyc], w=[kycq])
            pv_, pkv = pp1.get()
            p.mm(pv_[:], bones[:], ycq[:], True, True, r=["bones", kycq], w=[pkv])
            p.ts("dve", rs[:], pv_[:], 1.0 / C, ALU.mult, GN_EPS, ALU.add, r=[pkv], w=[krs])
            p.act(rs[:], rs[:], AF.Sqrt, r=[krs], w=[krs])
            p.op("dve", lambda e: e.reciprocal(out=rs[:], in_=rs[:]), r=[krs], w=[krs])
            p.tt("dve", yc[:], yc[:], rs[:], ALU.mult, r=[kyc, krs], w=[kyc])
            p.ts("dve", yc[:], yc[:], LNW, ALU.mult, LNB, ALU.add, r=[kyc, "CV"], w=[kyc])
            p.tt("pool", yc[:], yc[:], bon[:], ALU.add, r=[kyc, kbon], w=[kyc])
            p.tt("dve", yo[:], yc[:], gv[:], ALU.mult, r=[kyc, kgv], w=["yo"])
            p.dma("sp", yscr[hp * 128:(hp + 1) * 128, ycol0:ycol0 + ylen], yo[:, ysrc0:ysrc0 + ylen], s_st, r=["yo"], w=[("yscr", hp, n, 0)])
            pO, pkO = psO, "psO"
            tiles = [3] if n == first_full else [0, 1, 2, 3]
            for i in tiles:
                for h in range(2):
                    hs = slice(64 * h, 64 * h + 64)
                    pS1, pkS1 = pp2.get(); pS2, pkS2 = pp2.get()
                    lo = i * 128
                    p.mm(pS1[:], qT[hs, i * 128:(i + 1) * 128], Kring[hs, lo:lo + 512], True, True, r=["qT", "Kring"], w=[pkS1])
                    p.stt(S[:, 0:512], pS1[:], 0.125, ATB[:, h, 0:512], ALU.mult, ALU.add, r=[pkS1, "ATB"], w=["S_att"])
                    p.mm(pS2[:, 0:128], qT[hs, i * 128:(i + 1) * 128], Kring[hs, lo + 512:lo + 640], True, True, r=["qT", "Kring"], w=[pkS2])
                    p.stt(S[:, 512:640], pS2[:, 0:128], 0.125, ATB[:, h, 512:640], ALU.mult, ALU.add, r=[pkS2, "ATB"], w=["S_att"])
                    if n == first_full + 1:
                        p.tt("pool", S[:], S[:], AMK[:, i, :], ALU.add, r=["S_att", "AMK"], w=["S_att"])
                    p.op("dve", lambda e: e.tensor_reduce(out=ast[:, 0:1], in_=S[:], axis=AX.X, op=ALU.max), r=["S_att"], w=["ast"])
                    p.ts("dve", ast[:, 1:2], ast[:, 0:1], -1.0, ALU.mult, r=["ast"], w=["ast"])
                    p.act(Pe[:], S[:], AF.Exp, r=["S_att", "ast"], w=["Pe", "ast"], bias=ast[:, 1:2], accum_out=ast[:, 2:3])
                    p.op("dve", lambda e: e.reciprocal(out=ast[:, 3:4], in_=ast[:, 2:3]), r=["ast"], w=["ast"])
                    p.ts("dve", Pn[:], Pe[:], ast[:, 3:4], ALU.mult, r=["Pe", "ast"], w=["Pn"])
                    ptr, pktr = pp2.get()
                    ptrb = ptr.bitcast(BF16)
                    for kt in range(5):
                        p.tr(ptrb[:, kt * 128:(kt + 1) * 128], Pn[:, kt * 128:(kt + 1) * 128], c["identb"][:], r=["Pn", "identb"], w=[pktr])
                    p.cp("act", PT[:], ptrb[:, 0:640].rearrange("p (k q) -> p k q", k=5), r=[pktr], w=["PT"])
                    for kt in range(5):
                        p.mm(pO[hs, i * 128:(i + 1) * 128], Vring[:, i + kt, 64 * h:64 * h + 64], PT[:, kt, :], kt == 0, kt == 4,
                             r=["Vring", "PT"], w=[pkO], tile_position=(0, 64 * h))
            p.cp("act", ao[:, ysrc0:ysrc0 + ylen], pO[:, ysrc0:ysrc0 + ylen], r=[pkO], w=["ao"])
            p.dma("sp", yscr[512 + hp * 128:512 + (hp + 1) * 128, ycol0:ycol0 + ylen], ao[:, ysrc0:ysrc0 + ylen], s_st, r=["ao"], w=[("yscr", hp, n, 1)])


def build_fused(nc):
    p = Prog(nc)
    dr = lambda name, shape, dt=F32, kind="ExternalInput": nc.dram_tensor(name, list(shape), dt, kind=kind).ap()
    A = dict(xb=dr("xb", [NBLK * TB, D]), wsel=dr("wsel", [4, D, 1024]), g_mix=dr("g_mix", [D]), cv=dr("cv", [4, 128, 16]),
             w2=dr("w2", [4, 64, 128]), a2=dr("a2", [4, 64, 128]), g2=dr("g2", [4, 128, 128]), attb=dr("attb", [4, 2, 128, 640]),
             amask=dr("amask", [128, 4, 640]))
    T = {name: dr(name, shape) for name, shape in B_INPUTS}
    T["out"] = dr("out", [2048, D], kind="ExternalOutput")
    yscr = nc.dram_tensor("yscr", [D, 2176], BF16).ap()
    A["hTs"] = nc.dram_tensor("hTs", [NBLK, 128, NK * TB], BF16).ap()
    T["yT"] = yscr
    T["xin"] = A["xb"][NBLK * TB - 2176:NBLK * TB, :]
    c = make_consts(p)
    banks = [p.ps(f"psb{i}", [128, 512], F32) for i in range(6)]
    psD = p.ps("psD", [128, 512], F32)
    psO = p.ps("psO", [128, 512], F32)

    def mkpool(idx):
        q = PsumPool.__new__(PsumPool)
        q.t = [banks[i] for i in idx]; q.keys = [f"psb{i}" for i in idx]; q.i = 0; q.n = len(idx)
        return q
    pools = (mkpool([0, 1]), mkpool([2, 3, 4]), banks[5])
    pp = mkpool([0, 1, 2, 3, 4, 5])
    with Phase(p) as ph:
        emit_A(p, ph, c, pools, psD, psO, A, yscr)
    pp.t.extend([psD, psO]); pp.keys.extend(["psD", "psO"]); pp.n = 8
    s_st = emit_B(p, c, pp, T, 17)
    p.finalize(final_sems=[s_st])
    return p


RW = 512
RWKV_COLS = 1792


def att_bias_tile(rel_bias_h):
    q = np.arange(128)[:, None]; kc = np.arange(640)[None, :]
    qc, qi = q // 64, q % 64
    kcb, ki = kc // 64, kc % 64
    inband = (kcb >= qc) & (kcb <= qc + 8)
    kb = (kcb - qc) * 64 + ki
    rel = qi + 512 - kb
    idx = np.clip(rel, -128, 128) + 128
    out = np.where(inband, rel_bias_h[np.clip(idx, 0, 256)], np.float32(-30000.0)).astype(np.float32)
    return out


def prep_A(inputs, b, hp, T=8192):
    f = lambda k: np.asarray(inputs[k], np.float32)
    w_in = f("l0_w_in")
    hs = slice(128 * hp, 128 * hp + 128)
    cols = np.concatenate([np.arange(0, 512)[hs], np.arange(512, 1024)[hs], np.arange(1024, 1536)[hs],
                           np.arange(1536, 1664), np.arange(1664, 1792),
                           np.arange(1792, 2304)[hs], np.arange(2304, 2816)[hs], np.arange(2816, 3328)[hs]])
    mu = f("l0_shift_mu")
    cv = np.zeros((128, 16), np.float32)
    for ct in range(5):
        cv[:, ct] = mu[cols[ct * 128:(ct + 1) * 128]]
    for i, k in enumerate(["l0_w0", "l0_a0", "l0_k_k", "l0_k_a", "l0_r_k", "l0_lnx_w", "l0_lnx_b"]):
        cv[:, 5 + i] = f(k)[hs]
    rb = f("l0_rel_bias")
    return dict(
        xb=np.ascontiguousarray(f("x")[b, :T]), wsel=np.ascontiguousarray(w_in[:, cols]), g_mix=f("l0_norm_mix"), cv=cv,
        w2=np.ascontiguousarray(f("l0_w2")[:, hs]), a2=np.ascontiguousarray(f("l0_a2")[:, hs]), g2=np.ascontiguousarray(f("l0_g2")[:, hs]),
        attb=np.stack([att_bias_tile(rb[2 * hp]), att_bias_tile(rb[2 * hp + 1])]))


def _prep_fused(inputs, c, shared):
    f = lambda k: np.asarray(inputs[k], np.float32)
    b, q = c // 4, c % 4
    x = f("x")
    n_real = (q + 1) * 2048
    xb = np.zeros((8192, 1024), np.float32)
    xb[8192 - n_real:] = x[b, :n_real]
    amask = np.zeros((128, 4, 640), np.float32)
    hm = np.ones((128, 1), np.float32)
    if q == 0:
        hm[:] = 0.0
        for i in range(4):
            amask[:, i, :512 - 128 * i] = -30000.0
    d = dict(shared)
    d.update(xb=xb, amask=amask, hmask=hm)
    return d


def _shared_inputs(inputs):
    f = lambda k: np.asarray(inputs[k], np.float32)
    per = [prep_A(inputs, 0, hp, 8) for hp in range(4)]
    sh = dict(wsel=np.stack([d["wsel"] for d in per]), cv=np.stack([d["cv"] for d in per]),
              w2=np.stack([d["w2"] for d in per]), a2=np.stack([d["a2"] for d in per]), g2=np.stack([d["g2"] for d in per]),
              attb=np.stack([d["attb"] for d in per]), g_mix=f("l0_norm_mix"),
              w_out=f("l0_w_out"), g_ffn0=f("l0_norm_ffn"), up0=f("l0_ffn_up"), dn0=f("l0_ffn_down"),
              g_mix1=f("l1_norm_mix"), pw1=f("l1_pw1"), pw1_b=f("l1_pw1_b"), dw=f("l1_dw"), dw_b=f("l1_dw_b"),
              ln_w=f("l1_ln_w"), ln_b=f("l1_ln_b"), pw2=f("l1_pw2"), pw2_b=f("l1_pw2_b"),
              g_ffn1=f("l1_norm_ffn"), up1=f("l1_ffn_up"), dn1=f("l1_ffn_down"), g_fin=f("final_norm"))
    return sh


def kernel(**inputs):
    nc = bass.Bass("TRN2", target_bir_lowering=False)
    build_fused(nc)
    shared = _shared_inputs(inputs)
    maps = [_prep_fused(inputs, c, shared) for c in range(8)]
    res = run_bass_kernel_spmd(nc, maps, core_ids=list(range(8)))
    out = np.concatenate([np.asarray(res.results[c]["out"]) for c in range(8)], axis=0)
    return out.reshape(2, 8192, 1024).astype(np.float32)
```

```python
import contextlib
import numpy as np
import concourse.bass as bass
import concourse.mybir as mybir
from concourse.bass_utils import run_bass_kernel_spmd

F32 = mybir.dt.float32
BF16 = mybir.dt.bfloat16
AF = mybir.ActivationFunctionType
ALU = mybir.AluOpType
AX = mybir.AxisListType

SCHED = True
SYNC_SAME = True


class Sem:
    def __init__(self, nc, name):
        self.h = nc.alloc_semaphore(name)
        self.count = 0
        self.last = {}


class Op:
    __slots__ = ("eng", "fn", "deps", "dsem", "dval", "signal", "sigval", "dmadeps", "epoch", "alldeps", "cost", "idx", "barrier",
                 "chain", "nbytes", "junk")

    def __init__(self, eng, fn):
        self.eng = eng
        self.fn = fn
        self.deps = []
        self.dmadeps = []
        self.dsem = None
        self.dval = 0
        self.signal = False
        self.sigval = 0
        self.epoch = 0
        self.alldeps = []
        self.cost = 100.0
        self.idx = 0
        self.barrier = False
        self.chain = None
        self.nbytes = 0
        self.junk = 0


class Prog:
    ENG = ("pe", "act", "dve", "pool", "sp")

    def __init__(self, nc):
        self.nc = nc
        self.e = {"pe": nc.tensor, "act": nc.scalar, "dve": nc.vector, "pool": nc.gpsimd, "sp": nc.sync}
        self.ops = []
        self.lastw = {}
        self.readers = {}
        self.esems = [{k: Sem(nc, "sem0_" + k) for k in self.ENG}]
        self.epoch = 0
        self.all_sems = []
        self.last_op = {}
        self.last_dma = {}
        self.junk_fn = None
        self.junk_cost = 110.0
        self.junk_frac = 0.7
        self.junk_cap = 10

    def sb(self, name, shape, dt=F32):
        return self.nc.alloc_sbuf_tensor(name, list(shape), dt).ap()

    def ps(self, name, shape, dt=F32):
        return self.nc.alloc_psum_tensor(name, list(shape), dt).ap()

    def sem(self, name):
        s = Sem(self.nc, name)
        self.all_sems.append(s)
        return s

    def barrier(self):
        lasts = dict(self.last_op)
        news = []
        for eng in self.ENG:
            o = Op(eng, lambda e: e.nop())
            for d in lasts.values():
                if d.dsem is None and not (d.eng == eng and eng == "pe"):
                    d.signal = True
                    o.deps.append(d)
            for s in self.all_sems:
                if s.count:
                    o.dmadeps.append((s, s.count))
            o.epoch = self.epoch
            o.barrier = True
            news.append(o)
        for o in news:
            self.ops.append(o)
        self.epoch += 1
        self.esems.append({k: Sem(self.nc, f"sem{self.epoch}_" + k) for k in self.ENG})
        self.last_op = {}

    def _dep(self, op, d):
        if d is None or d is op:
            return
        op.alldeps.append(d)
        if d.dsem is not None:
            op.dmadeps.append((d.dsem, d.dsem.count))
            op.alldeps.extend(d.dsem.last.values())
            return
        if d.eng == op.eng and op.dsem is None:
            if d.eng == "pe" or not SYNC_SAME:
                return
        if d.epoch < self.epoch:
            return
        d.signal = True
        op.deps.append(d)

    def op(self, eng, fn, r=(), w=(), dma=None, cost=None, nbytes=0):
        o = Op(eng, fn)
        o.epoch = self.epoch
        o.idx = len(self.ops)
        if cost is not None:
            o.cost = cost
        o.nbytes = nbytes
        if dma is not None:
            o.dsem = dma
        for k in r:
            self._dep(o, self.lastw.get(k))
        for k in w:
            self._dep(o, self.lastw.get(k))
            for rd in self.readers.get(k, ()):
                self._dep(o, rd)
        if dma is not None:
            dma.count += 16
            o.dval = dma.count
            o.chain = self.last_dma.get(eng)
            self.last_dma[eng] = o
            dma.last[eng] = o
        for k in r:
            self.readers.setdefault(k, []).append(o)
        for k in w:
            self.lastw[k] = o
            self.readers[k] = []
        self.ops.append(o)
        if dma is None:
            self.last_op[eng] = o
        return o

    @staticmethod
    def _free(ap):
        n = 1
        for d in list(ap.shape)[1:]:
            n *= int(d)
        return n

    def _ecost(self, eng, ap):
        f = self._free(ap)
        if eng == "pool":
            return 120.0 + 2.1 * f
        return 70.0 + 1.05 * f

    def dma(self, eng, out, in_, sem, r=(), w=(), **kw):
        nb = self._free(out) * int(out.shape[0]) * 4
        return self.op(eng, lambda e: e.dma_start(out=out, in_=in_, **kw), r=r, w=w, dma=sem, cost=60.0, nbytes=nb)

    def mm(self, out, lhsT, rhs, start, stop, r=(), w=(), **kw):
        n = self._free(rhs)
        mul = 4.0 if rhs.dtype == F32 else 1.0
        return self.op("pe", lambda e: e.matmul(out, lhsT=lhsT, rhs=rhs, start=start, stop=stop, **kw), r=r, w=w,
                       cost=35.0 + 0.43 * mul * max(n, 64))

    def tr(self, out, in_, ident, r=(), w=(), **kw):
        return self.op("pe", lambda e: e.transpose(out, in_, ident, **kw), r=r, w=w, cost=80.0 + 0.43 * self._free(in_))

    def act(self, out, in_, func, r=(), w=(), eng="act", **kw):
        return self.op(eng, lambda e: e.activation(out=out, in_=in_, func=func, **kw), r=r, w=w, cost=self._ecost(eng, out) + 60)

    def tt(self, eng, out, in0, in1, op, r=(), w=()):
        return self.op(eng, lambda e: e.tensor_tensor(out=out, in0=in0, in1=in1, op=op), r=r, w=w, cost=self._ecost(eng, out))

    def ts(self, eng, out, in0, s1, op0, s2=None, op1=None, r=(), w=(), **kw):
        if op1 is None:
            return self.op(eng, lambda e: e.tensor_scalar(out=out, in0=in0, scalar1=s1, scalar2=None, op0=op0, **kw), r=r, w=w,
                           cost=self._ecost(eng, out))
        return self.op(eng, lambda e: e.tensor_scalar(out=out, in0=in0, scalar1=s1, scalar2=s2, op0=op0, op1=op1, **kw), r=r, w=w,
                       cost=self._ecost(eng, out))

    def stt(self, out, in0, scalar, in1, op0, op1, r=(), w=(), eng="dve"):
        return self.op(eng, lambda e: e.scalar_tensor_tensor(out=out, in0=in0, scalar=scalar, in1=in1, op0=op0, op1=op1), r=r, w=w,
                       cost=self._ecost(eng, out))

    def cp(self, eng, out, in_, r=(), w=()):
        if eng == "act":
            return self.op(eng, lambda e: e.copy(out=out, in_=in_), r=r, w=w, cost=self._ecost(eng, out) + 60)
        return self.op(eng, lambda e: e.tensor_copy(out=out, in_=in_), r=r, w=w, cost=self._ecost(eng, out))

    def memset(self, eng, ap, val, w=()):
        return self.op(eng, lambda e: e.memset(ap, val), w=w, cost=self._ecost(eng, ap))

    def _schedule(self, seg):
        import heapq
        pos = {id(o): i for i, o in enumerate(seg)}
        npred = [0] * len(seg)
        succ = [[] for _ in seg]
        for i, o in enumerate(seg):
            ps = set()
            for d in o.alldeps:
                j = pos.get(id(d))
                if j is not None and j != i:
                    ps.add(j)
            if o.chain is not None:
                j = pos.get(id(o.chain))
                if j is not None:
                    ps.add(j)
            npred[i] = len(ps)
            for j in ps:
                succ[j].append(i)
        ready_t = [0.0] * len(seg)
        done_t = [0.0] * len(seg)
        issue_t = [0.0] * len(seg)
        free_at = {k: 0.0 for k in self.ENG}
        heaps = {k: [] for k in self.ENG}
        for i, o in enumerate(seg):
            if npred[i] == 0:
                heapq.heappush(heaps[o.eng], (0.0, i))
        order = []
        nleft = len(seg)
        while nleft:
            best = None
            for k in self.ENG:
                h = heaps[k]
                if not h:
                    continue
                rt, i = h[0]
                st = max(rt, free_at[k])
                if best is None or st < best[0] or (st == best[0] and i < best[2]):
                    best = (st, k, i)
            st, k, _ = best
            h = heaps[k]
            cand = []
            while h and h[0][0] <= st and len(cand) < 16:
                cand.append(heapq.heappop(h))
            cand.sort(key=lambda x: x[1])
            rt, i = cand[0]
            for cnd in cand[1:]:
                heapq.heappush(h, cnd)
            o = seg[i]
            order.append(o)
            nleft -= 1
            if k == "pe" and self.junk_fn is not None:
                gap = st - free_at[k]
                if gap > self.junk_cost:
                    o.junk = min(self.junk_cap, int(gap * self.junk_frac / self.junk_cost))
            if o.dsem is not None:
                free_at[k] = st + o.cost
                issue_t[i] = st + o.cost
                done_t[i] = st + 2000.0 + o.nbytes / 150.0
            else:
                free_at[k] = st + o.cost
                issue_t[i] = st + o.cost
                done_t[i] = st + o.cost
            for j in succ[i]:
                oj = seg[j]
                if oj.chain is o and o not in oj.alldeps:
                    t = issue_t[i]
                else:
                    t = done_t[i] + (40.0 if (oj.eng == o.eng and oj.eng == "pe") else 180.0)
                if t > ready_t[j]:
                    ready_t[j] = t
                npred[j] -= 1
                if npred[j] == 0:
                    heapq.heappush(heaps[oj.eng], (ready_t[j], j))
        return order, max(free_at.values())

    def reorder(self):
        segs = {}
        for o in self.ops:
            segs.setdefault(o.epoch, []).append(o)
        new = []
        tot = 0.0
        for ep in sorted(segs):
            seg = [o for o in segs[ep] if not o.barrier]
            bar = [o for o in segs[ep] if o.barrier]
            order, t = self._schedule(seg)
            tot += t
            new.extend(order)
            new.extend(bar)
        self.ops = new
        return tot

    def finalize(self, final_sems=()):
        if SCHED:
            self.reorder()
        cnt = {}
        waited = {k: {} for k in self.ENG}
        for o in self.ops:
            if o.dsem is None and o.signal:
                kk_ = (o.epoch, o.eng)
                cnt[kk_] = cnt.get(kk_, 0) + 1
                o.sigval = cnt[kk_]
        for o in self.ops:
            e = self.e[o.eng]
            need = {}
            for d in o.deps:
                s = self.esems[d.epoch][d.eng]
                need[id(s)] = (s, max(need.get(id(s), (s, 0))[1], d.sigval))
            for (s, v) in o.dmadeps:
                need[id(s)] = (s, max(need.get(id(s), (s, 0))[1], v))
            wd = waited[o.eng]
            for _ in range(o.junk):
                self.junk_fn(e)
            for sid, (s, v) in need.items():
                if wd.get(sid, 0) >= v:
                    continue
                e.wait_ge(s.h, v)
                wd[sid] = v
            ins = o.fn(e)
            if o.dsem is not None:
                ins.then_inc(o.dsem.h, 16)
            elif o.signal:
                ins.then_inc(self.esems[o.epoch][o.eng].h, 1)
        for s in final_sems:
            self.nc.sync.wait_ge(s.h, s.count)
        return cnt


D = 1024
DFF = 4096
NK = 8
NORM_EPS = 1e-6
LN_EPS = 1e-5
CW = 31


class Phase:
    def __init__(self, p):
        self.p = p
        self.stack = contextlib.ExitStack()

    def __enter__(self):
        self.stack.__enter__()
        return self

    def sb(self, name, shape, dt=F32):
        return self.stack.enter_context(self.p.nc.sbuf_tensor(name, list(shape), dt)).ap()

    def __exit__(self, *a):
        self.p.barrier()
        return self.stack.__exit__(*a)


def make_consts(p):
    c = {}
    c["identf"] = p.sb("identf", [128, 128], F32)
    c["identb"] = p.sb("identb", [128, 128], BF16)
    c["onesf"] = p.sb("onesf", [128, 128], F32)
    p.memset("pool", c["identf"][:], 1.0, w=["identf"])
    p.op("pool", lambda e: e.affine_select(out=c["identf"][:], in_=c["identf"][:], pattern=[[-1, 128]],
                                            compare_op=ALU.is_equal, fill=0.0, base=0, channel_multiplier=1),
         r=["identf"], w=["identf"])
    p.cp("dve", c["identb"][:], c["identf"][:], r=["identf"], w=["identb"])
    p.memset("pool", c["onesf"][:], 1.0, w=["onesf"])
    return c


class PsumPool:
    def __init__(self, p, n=8):
        self.t = [p.ps(f"psb{i}", [128, 512], F32) for i in range(n)]
        self.keys = [f"psb{i}" for i in range(n)]
        self.i = 0
        self.n = n

    def get(self):
        i = self.i
        self.i = (self.i + 1) % self.n
        return self.t[i], self.keys[i]

    def sub(self, idx):
        q = PsumPool.__new__(PsumPool)
        q.t = [self.t[i] for i in idx]; q.keys = [self.keys[i] for i in idx]; q.i = 0; q.n = len(idx)
        return q


def rms_to_hT(p, c, pp, X, xkey, tile, hT, hkey, col0, scr, idx):
    sq = scr["sq"][idx % 2]; ss = scr["ss"][idx % 2]; hb = scr["hb"][idx % 2]
    kq = ("sq", idx % 2); ks = ("ss", idx % 2); kh = ("hb", idx % 2)
    xt = X[:, tile, :]
    p.act(sq[:], xt, AF.Square, r=[xkey], w=[kq, ks], accum_out=ss[:, 0:1])
    p.ts("dve", ss[:, 1:2], ss[:, 0:1], 1.0 / D, ALU.mult, NORM_EPS, ALU.add, r=[ks], w=[ks])
    p.act(ss[:, 2:3], ss[:, 1:2], AF.Sqrt, r=[ks], w=[ks])
    p.op("dve", lambda e: e.reciprocal(out=ss[:, 3:4], in_=ss[:, 2:3]), r=[ks], w=[ks])
    p.ts("dve", hb[:], xt, ss[:, 3:4], ALU.mult, r=[xkey, ks], w=[kh])
    pt, pk = pp.get()
    ptb = pt.bitcast(BF16)
    for dk in range(NK):
        p.tr(ptb[:, dk * 128:(dk + 1) * 128], hb[:, dk * 128:(dk + 1) * 128], c["identb"][:], r=[kh, "identb"], w=[pk])
    p.cp("act", hT[:, :, col0:col0 + 128], ptb.rearrange("p (k t) -> p k t", k=NK), r=[pk], w=[hkey])


B_INPUTS = [("w_out", [D, D]), ("g_ffn0", [D]), ("up0", [D, DFF]), ("dn0", [DFF, D]), ("g_mix1", [D]), ("pw1", [D, 2 * D]),
            ("pw1_b", [2 * D]), ("dw", [CW, D]), ("dw_b", [D]), ("ln_w", [D]), ("ln_b", [D]), ("pw2", [D, D]), ("pw2_b", [D]),
            ("g_ffn1", [D]), ("up1", [D, DFF]), ("dn1", [DFF, D]), ("g_fin", [D]), ("hmask", [128, 1])]


def build_B(nc, NT=17):
    NTOK = NT * 128
    p = Prog(nc)
    dr = lambda name, shape, dt=F32, kind="ExternalInput": nc.dram_tensor(name, list(shape), dt, kind=kind).ap()
    T = {name: dr(name, shape) for name, shape in B_INPUTS}
    T["xin"] = dr("xin", [NTOK, D])
    T["yT"] = dr("yT", [D, NTOK], BF16)
    T["out"] = dr("out", [NTOK - 128, D], kind="ExternalOutput")
    c = make_consts(p)
    pp = PsumPool(p)
    s_st = emit_B(p, c, pp, T, NT)
    p.finalize(final_sems=[s_st])
    return p


def emit_B(p, c, pp, T, NT=17):
    NTOK = NT * 128
    NMAIN = NTOK - 128
    xin = T["xin"]; yT = T["yT"]; hmask = T["hmask"]; w_out = T["w_out"]
    g_ffn0 = T["g_ffn0"]; up0 = T["up0"]; dn0 = T["dn0"]
    g_mix1 = T["g_mix1"]; pw1 = T["pw1"]; pw1_b = T["pw1_b"]
    dw = T["dw"]; dw_b = T["dw_b"]; ln_w = T["ln_w"]; ln_b = T["ln_b"]
    pw2 = T["pw2"]; pw2_b = T["pw2_b"]
    g_ffn1 = T["g_ffn1"]; up1 = T["up1"]; dn1 = T["dn1"]
    g_fin = T["g_fin"]
    out = T["out"]

    s_ld = p.sem("s_ld"); s_w = [p.sem("s_w0"), p.sem("s_w1"), p.sem("s_w2"), p.sem("s_w3")]
    s_st = p.sem("s_st")

    X = p.sb("X", [128, NT, D], F32)
    vecs = p.sb("vecs", [128, 8 * NK + 2 * NK], F32)

    def colvec(i, src, n=NK):
        ap = vecs[:, i:i + n]
        p.dma("sp", ap, src.rearrange("(k p) -> p k", p=128), s_ld, w=["vecs"], allow_slow_non_contiguous=True)
        return ap
    V = {}
    off = 0
    for name, src, n in [("g_ffn0", g_ffn0, NK), ("g_mix1", g_mix1, NK), ("pw1_b", pw1_b, 2 * NK), ("dw_b", dw_b, NK),
                         ("ln_w", ln_w, NK), ("ln_b", ln_b, NK), ("g_ffn1", g_ffn1, NK)]:
        V[name] = colvec(off, src, n); off += n
    rows = p.sb("rows", [128, 2, D], F32)
    p.dma("sp", rows[:, 0, :], pw2_b.partition_broadcast(128), s_ld, w=["rows"])
    p.dma("sp", rows[:, 1, :], g_fin.partition_broadcast(128), s_ld, w=["rows"])
    hm = p.sb("hm", [128, 1], F32)
    p.dma("sp", hm[:], hmask, s_ld, w=["hm"])
    xv = xin.rearrange("(n p) d -> p n d", p=128)
    for n in range(NT):
        p.dma("sp" if n % 2 == 0 else "act", X[:, n, :], xv[:, n, :], s_ld, w=[("X", n)])

    scr_store = {}

    def ffn(ph, hT, tiles, g_col, up, dn, tagp):
        SL = 512
        nsl = DFF // SL
        stU = [ph.sb(f"{tagp}stU{i}", [128, NK, SL], F32) for i in range(1)] * 2
        stD = [ph.sb(f"{tagp}stD{i}", [128, SL // 128, D], F32) for i in range(1)] * 2
        WU = [ph.sb(f"{tagp}WU{i}", [128, NK, SL], BF16) for i in range(2)]
        WD = [ph.sb(f"{tagp}WD{i}", [128, SL // 128, D], BF16) for i in range(2)]
        aT = [ph.sb(f"{tagp}aT{i}", [128, SL // 128, 512], BF16) for i in range(2)]
        rl = [ph.sb(f"{tagp}rl{i}", [128, 512], F32) for i in range(2)]
        upv = up.rearrange("(k p) f -> p k f", p=128)
        dnv = dn.rearrange("(k p) d -> p k d", p=128)
        groups = []
        i = 0
        while i < len(tiles):
            groups.append(tiles[i:i + 4]); i += 4
        ai = 0; ri = 0
        ppU = pp.sub([0, 1, 2]); ppD = pp.sub([3, 4, 5, 6, 7])
        def load(s):
            b = s % 2
            p.dma("sp", stU[b][:], upv[:, :, s * SL:(s + 1) * SL], s_w[0], w=[(tagp, "stU", 0)])
            p.dma("sp", stD[b][:], dnv[:, s * (SL // 128):(s + 1) * (SL // 128), :], s_w[2], w=[(tagp, "stD", 0)])

        def cast(s):
            b = s % 2
            for dk in range(NK):
                if dk % 2 == 0:
                    p.ts("dve", WU[b][:, dk, :], stU[b][:, dk, :], g_col[:, dk:dk + 1], ALU.mult,
                         r=[(tagp, "stU", 0), "vecs"], w=[(tagp, "WU", b)])
                else:
                    p.act(WU[b][:, dk, :], stU[b][:, dk, :], AF.Copy, r=[(tagp, "stU", 0), "vecs"], w=[(tagp, "WU", b)], scale=g_col[:, dk:dk + 1])
            for f_ in range(SL // 128):
                p.cp("dve" if f_ % 2 == 0 else "act", WD[b][:, f_, :], stD[b][:, f_, :], r=[(tagp, "stD", 0)], w=[(tagp, "WD", b)])
        load(0)
        cast(0)
        for s in range(nsl):
            b = s % 2
            if s + 1 < nsl:
                load(s + 1)
            for grp in groups:
                n = len(grp) * 128
                c0 = grp[0] * 128
                a = aT[ai % 2]; ka = (tagp, "aT", ai % 2); ai += 1
                for ft in range(SL // 128):
                    pt, pk = ppU.get()
                    for dk in range(NK):
                        p.mm(pt[:, 0:n], WU[b][:, dk, ft * 128:(ft + 1) * 128], hT[:, dk, c0:c0 + n], dk == 0, dk == NK - 1,
                             r=[(tagp, "WU", b), (tagp, "hT")], w=[pk])
                    r_ = rl[ri % 2]; kr = (tagp, "rl", ri % 2); ri += 1
                    p.act(r_[:, 0:n], pt[:, 0:n], AF.Relu, r=[pk], w=[kr])
                    p.act(a[:, ft, 0:n], r_[:, 0:n], AF.Square, r=[kr], w=[ka])
                for ti, t in enumerate(grp):
                    for half in range(2):
                        pt, pk = ppD.get()
                        for ft in range(SL // 128):
                            p.mm(pt[:], a[:, ft, ti * 128:(ti + 1) * 128], WD[b][:, ft, half * 512:(half + 1) * 512],
                                 ft == 0, ft == SL // 128 - 1, r=[ka, (tagp, "WD", b)], w=[pk])
                        xs = X[:, t, half * 512:(half + 1) * 512]
                        p.tt("dve", xs, xs, pt[:], ALU.add, r=[pk, ("X", t)], w=[("X", t)])
            if s + 1 < nsl:
                cast(s + 1)

    def norm_all(ph, tiles, hT, hkey, scr):
        for i, t in enumerate(tiles):
            rms_to_hT(p, c, pp, X, ("X", t), t, hT, hkey, t * 128, scr, i)

    def mk_scr(ph, tag):
        return {"sq": [ph.sb(f"{tag}sq{i}", [128, D], BF16) for i in range(2)],
                "ss": [ph.sb(f"{tag}ss{i}", [128, 4], F32) for i in range(2)],
                "hb": [ph.sb(f"{tag}hb{i}", [128, D], BF16) for i in range(2)]}

    with Phase(p) as ph:
        hT = ph.sb("hT0", [128, NK, NTOK], BF16)
        scr = mk_scr(ph, "p1")
        with Phase(p) as ph2:
            yv = yT.rearrange("(k p) t -> p k t", p=128)
            for k in range(NK):
                p.dma("sp" if k % 2 == 0 else "act", hT[:, k, :], yv[:, k, :], s_ld, w=[("p1", "hT")])
            wst = [ph2.sb(f"wst{i}", [128, NK, 512], F32) for i in range(2)]
            woB = ph2.sb("woB", [128, NK, D], BF16)
            wov = w_out.rearrange("(k p) d -> p k d", p=128)
            for h in range(2):
                p.dma("sp", wst[h][:], wov[:, :, h * 512:(h + 1) * 512], s_w[h], w=[("wst", h)])
                for dk in range(NK):
                    p.cp("act" if dk % 2 else "dve", woB[:, dk, h * 512:(h + 1) * 512], wst[h][:, dk, :], r=[("wst", h)], w=["woB"])
            for t in range(NT):
                for half in range(2):
                    pt, pk = pp.get()
                    for k in range(NK):
                        p.mm(pt[:], hT[:, k, t * 128:(t + 1) * 128], woB[:, k, half * 512:(half + 1) * 512], k == 0, k == NK - 1,
                             r=[("p1", "hT"), "woB"], w=[pk])
                    xs = X[:, t, half * 512:(half + 1) * 512]
                    p.tt("dve", xs, xs, pt[:], ALU.add, r=[pk, ("X", t)], w=[("X", t)])
        norm_all(ph, list(range(NT)), hT, ("p1", "hT"), scr)
        with Phase(p) as ph2:
            ffn(ph2, hT, list(range(NT)), V["g_ffn0"], up0, dn0, "p1")

    with Phase(p) as ph:
        scr = mk_scr(ph, "p2")
        pw1B = ph.sb("pw1B", [128, NK, 2 * D], BF16)
        pw2B = ph.sb("pw2B", [128, NK, D], BF16)
        with Phase(p) as ph2:
            wst = [ph2.sb(f"wst2{i}", [128, NK, 512], F32) for i in range(2)]
            pw1v = pw1.rearrange("(k p) c -> p k c", p=128)
            pw2v = pw2.rearrange("(k p) c -> p k c", p=128)
            for j in range(4):
                b = j % 2
                p.dma("sp", wst[b][:], pw1v[:, :, j * 512:(j + 1) * 512], s_w[b], w=[("wst2", b)])
                for dk in range(NK):
                    if dk % 2:
                        p.act(pw1B[:, dk, j * 512:(j + 1) * 512], wst[b][:, dk, :], AF.Copy, r=[("wst2", b), "vecs"], w=["pw1B"],
                              scale=V["g_mix1"][:, dk:dk + 1])
                    else:
                        p.ts("dve", pw1B[:, dk, j * 512:(j + 1) * 512], wst[b][:, dk, :], V["g_mix1"][:, dk:dk + 1], ALU.mult,
                             r=[("wst2", b), "vecs"], w=["pw1B"])
            for j in range(2):
                b = j % 2
                p.dma("sp", wst[b][:], pw2v[:, :, j * 512:(j + 1) * 512], s_w[b], w=[("wst2", b)])
                for dk in range(NK):
                    p.cp("act" if dk % 2 else "dve", pw2B[:, dk, j * 512:(j + 1) * 512], wst[b][:, dk, :], r=[("wst2", b)], w=["pw2B"])
        dwS = ph.sb("dwS", [CW, D], F32)
        dwT = ph.sb("dwT", [128, NK, 32], F32)
        p.dma("sp", dwS[:], dw, s_ld, w=["dwS"])
        for ct in range(NK):
            pt, pk = pp.get()
            p.tr(pt[:, 0:CW], dwS[:, ct * 128:(ct + 1) * 128], c["identf"][0:CW, 0:CW], r=["dwS", "identf"], w=[pk])
            p.cp("dve", dwT[:, ct, 0:CW], pt[:, 0:CW], r=[pk], w=["dwT"])
        UH = ph.sb("UH", [128, NK, CW - 1], F32)
        hTg = ph.sb("hTg", [128, NK, 512], BF16)
        uT = [ph.sb(f"uT{i}", [128, CW - 1 + 512], F32) for i in range(2)]
        sg = [ph.sb(f"sg{i}", [128, 512], F32) for i in range(2)]
        zT = ph.sb("zT", [128, NK, 512], F32)
        zq = [ph.sb(f"zq{i}", [128, 512], F32) for i in range(2)]
        st = ph.sb("lnst", [128, 4, 512], F32)
        zn = [ph.sb(f"zn{i}", [128, 512], F32) for i in range(2)]
        z2b = [ph.sb(f"z2b{i}", [128, 512], F32) for i in range(2)]
        tpb = [ph.sb(f"tpb{i}", [128, 512], F32) for i in range(2)]
        KD = 21
        sT = ph.sb("sT", [128, NK, 512], BF16)
        groups = [[0]] + [list(range(1 + 4 * g, 1 + 4 * g + 4)) for g in range((NT - 1) // 4)]
        ui = 0
        for gi, grp in enumerate(groups):
            n = len(grp) * 128
            for i, t in enumerate(grp):
                rms_to_hT(p, c, pp, X, ("X", t), t, hTg, "hTg", i * 128, scr, i)
            for ct in range(NK):
                pa, pka = pp.get()
                pb, pkb = pp.get()
                for dk in range(NK):
                    p.mm(pa[:, 0:n], pw1B[:, dk, ct * 128:(ct + 1) * 128], hTg[:, dk, 0:n], dk == 0, dk == NK - 1, r=["pw1B", "hTg"], w=[pka])
                for dk in range(NK):
                    p.mm(pb[:, 0:n], pw1B[:, dk, D + ct * 128:D + (ct + 1) * 128], hTg[:, dk, 0:n], dk == 0, dk == NK - 1, r=["pw1B", "hTg"], w=[pkb])
                u = uT[ui % 2]; ku = ("uT", ui % 2); s_ = sg[ui % 2]; ksg = ("sg", ui % 2); ui += 1
                p.act(s_[:, 0:n], pb[:, 0:n], AF.Sigmoid, r=[pkb, "vecs"], w=[ksg], bias=V["pw1_b"][:, NK + ct:NK + ct + 1])
                if gi > 0:
                    p.cp("pool", u[:, 0:CW - 1], UH[:, ct, :], r=[("UH", ct)], w=[ku])
                p.stt(u[:, CW - 1:CW - 1 + n], pa[:, 0:n], V["pw1_b"][:, ct:ct + 1], s_[:, 0:n], ALU.add, ALU.mult, r=[pka, ksg, "vecs"], w=[ku])
                if gi == 0:
                    p.ts("dve", UH[:, ct, :], u[:, CW - 1 + n - (CW - 1):CW - 1 + n], hm[:, 0:1], ALU.mult, r=[ku, "hm"], w=[("UH", ct)])
                    continue
                p.cp("pool", UH[:, ct, :], u[:, n:n + CW - 1], r=[ku], w=[("UH", ct)])
                z = zT[:, ct, 0:n]
                p.ts("dve", z, u[:, 0:n], dwT[:, ct, 0:1], ALU.mult, V["dw_b"][:, ct:ct + 1], ALU.add, r=[ku, "dwT", "vecs"], w=[("zT", ct)])
                for k in range(1, KD):
                    p.stt(z, u[:, k:k + n], dwT[:, ct, k:k + 1], z, ALU.mult, ALU.add, r=[ku, "dwT", ("zT", ct)], w=[("zT", ct)])
                z2 = z2b[ct % 2][:, 0:n]; kz2 = ("z2b", ct % 2)
                for k in range(KD, CW):
                    if k == KD:
                        p.act(z2, u[:, k:k + n], AF.Copy, r=[ku, "dwT"], w=[kz2], scale=dwT[:, ct, k:k + 1])
                    else:
                        tp_ = tpb[k % 2][:, 0:n]; ktp = ("tpb", k % 2)
                        p.act(tp_, u[:, k:k + n], AF.Copy, r=[ku, "dwT"], w=[ktp], scale=dwT[:, ct, k:k + 1])
                        p.tt("pool", z2, z2, tp_, ALU.add, r=[kz2, ktp], w=[kz2])
                p.tt("pool", z, z, z2, ALU.add, r=[("zT", ct), kz2], w=[("zT", ct)])
            if gi == 0:
                continue
            p1, pk1 = pp.get()
            p2, pk2 = pp.get()
            for ct in range(NK):
                p.mm(p1[:, 0:n], c["onesf"][:], zT[:, ct, 0:n], ct == 0, ct == NK - 1, r=["onesf", ("zT", ct)], w=[pk1])
            for ct in range(NK):
                q = zq[ct % 2]; kq = ("zq", ct % 2)
                p.act(q[:, 0:n], zT[:, ct, 0:n], AF.Square, r=[("zT", ct)], w=[kq])
                p.mm(p2[:, 0:n], c["onesf"][:], q[:, 0:n], ct == 0, ct == NK - 1, r=["onesf", kq], w=[pk2])
            mean = st[:, 0, 0:n]; var = st[:, 1, 0:n]; tmp = st[:, 2, 0:n]; rstd = st[:, 3, 0:n]
            p.ts("dve", mean, p1[:, 0:n], 1.0 / D, ALU.mult, r=[pk1], w=["lnst"])
            p.tt("dve", tmp, mean, mean, ALU.mult, r=["lnst"], w=["lnst"])
            p.stt(var, p2[:, 0:n], 1.0 / D, tmp, ALU.mult, ALU.subtract, r=[pk2, "lnst"], w=["lnst"])
            p.ts("dve", var, var, LN_EPS, ALU.add, r=["lnst"], w=["lnst"])
            p.act(tmp, var, AF.Sqrt, r=["lnst"], w=["lnst"])
            p.op("dve", lambda e, rstd=rstd, tmp=tmp: e.reciprocal(out=rstd, in_=tmp), r=["lnst"], w=["lnst"])
            for ct in range(NK):
                zz = zn[ct % 2]; kz = ("zn", ct % 2)
                p.tt("dve", zz[:, 0:n], zT[:, ct, 0:n], mean, ALU.subtract, r=[("zT", ct), "lnst"], w=[kz])
                p.tt("pool", zz[:, 0:n], zz[:, 0:n], rstd, ALU.mult, r=[kz, "lnst"], w=[kz])
                p.act(sT[:, ct, 0:n], zz[:, 0:n], AF.Silu, r=[kz, "vecs"], w=["sT"],
                      scale=V["ln_w"][:, ct:ct + 1], bias=V["ln_b"][:, ct:ct + 1])
            for ti, t in enumerate(grp):
                for half in range(2):
                    pt, pk = pp.get()
                    for ct in range(NK):
                        p.mm(pt[:], sT[:, ct, ti * 128:(ti + 1) * 128], pw2B[:, ct, half * 512:(half + 1) * 512], ct == 0, ct == NK - 1,
                             r=["sT", "pw2B"], w=[pk])
                    xs = X[:, t, half * 512:(half + 1) * 512]
                    p.tt("dve", xs, xs, pt[:], ALU.add, r=[pk, ("X", t)], w=[("X", t)])
                    p.tt("pool", xs, xs, rows[:, 0, half * 512:(half + 1) * 512], ALU.add, r=["rows", ("X", t)], w=[("X", t)])

    main = list(range(1, NT))
    with Phase(p) as ph:
        hT = ph.sb("hT1", [128, NK, NTOK], BF16)
        scr = mk_scr(ph, "p3")
        norm_all(ph, main, hT, ("p3", "hT"), scr)
        with Phase(p) as ph2:
            ffn(ph2, hT, main, V["g_ffn1"], up1, dn1, "p3")
    with Phase(p) as ph:
        sq = [ph.sb(f"fsq{i}", [128, D], F32) for i in range(2)]
        ss = [ph.sb(f"fss{i}", [128, 4], F32) for i in range(2)]
        ov = out.rearrange("(n p) d -> p n d", p=128)
        for i, t in enumerate(main):
            b = i % 2
            xt = X[:, t, :]
            p.act(sq[b][:], xt, AF.Square, r=[("X", t)], w=[("fsq", b), ("fss", b)], accum_out=ss[b][:, 0:1])
            p.ts("dve", ss[b][:, 1:2], ss[b][:, 0:1], 1.0 / D, ALU.mult, NORM_EPS, ALU.add, r=[("fss", b)], w=[("fss", b)])
            p.act(ss[b][:, 2:3], ss[b][:, 1:2], AF.Sqrt, r=[("fss", b)], w=[("fss", b)])
            p.op("dve", lambda e, b=b: e.reciprocal(out=ss[b][:, 3:4], in_=ss[b][:, 2:3]), r=[("fss", b)], w=[("fss", b)])
            p.stt(sq[b][:], xt, ss[b][:, 3:4], rows[:, 1, :], ALU.mult, ALU.mult, r=[("X", t), ("fss", b), "rows"], w=[("fsq", b)])
            p.dma("sp", ov[:, t - 1, :], sq[b][:], s_st, r=[("fsq", b)], w=[("out", t)])
    return s_st


C = 64
TB = 512
NCH = TB // C
DEC = 0.6065306597126334
GN_EPS = 64e-5


def build_A(nc, NB=16):
    T = NB * TB
    p = Prog(nc)
    dr = lambda name, shape, dt=F32, kind="ExternalInput": nc.dram_tensor(name, list(shape), dt, kind=kind).ap()
    xb = dr("xb", [T, D])
    wsel = dr("wsel", [D, 1024])
    g_mix = dr("g_mix", [D])
    cv = dr("cv", [128, 16])
    w2 = dr("w2", [64, 128]); a2 = dr("a2", [64, 128]); g2 = dr("g2", [128, 128])
    attb = dr("attb", [2, 128, 640])
    yT = dr("yT", [256, T], BF16, kind="ExternalOutput")

    s_ld = p.sem("s_ld"); s_x = [p.sem("s_x0"), p.sem("s_x1")]; s_w = [p.sem("s_w0"), p.sem("s_w1")]
    s_st = p.sem("s_st")
    c = make_consts(p)
    pp = PsumPool(p, 7)
    psD = p.ps("psD", [128, 512], F32)

    CV = p.sb("CV", [128, 16], F32)
    p.dma("sp", CV[:], cv, s_ld, w=["CV"])
    MU = lambda ct: CV[:, ct:ct + 1]
    W0, A0, KK, KA, RK, LNW, LNB = [CV[:, 5 + i:6 + i] for i in range(7)]
    OMKA = CV[:, 12:13]
    p.ts("dve", OMKA, KA, -1.0, ALU.mult, 1.0, ALU.add, r=["CV"], w=["CV"])
    gcol = p.sb("gcol", [128, NK], F32)
    p.dma("sp", gcol[:], g_mix.rearrange("(k p) -> p k", p=128), s_ld, w=["gcol"], allow_slow_non_contiguous=True)
    W2 = p.sb("W2", [128, 128], F32); A2 = p.sb("A2", [128, 128], F32); G2 = p.sb("G2", [128, 128], F32)
    p.memset("pool", W2[:], 0.0, w=["W2"])
    p.memset("pool", A2[:], 0.0, w=["A2"])
    p.dma("sp", W2[0:64, :], w2, s_ld, w=["W2"])
    p.dma("sp", A2[64:128, :], a2, s_ld, w=["A2"])
    p.dma("sp", G2[:], g2, s_ld, w=["G2"])
    ATB = p.sb("ATB", [128, 2, 640], F32)
    for h in range(2):
        p.dma("sp", ATB[:, h, :], attb[h], s_ld, w=["ATB"])
    bones = p.sb("bones", [128, 128], F32)
    p.memset("pool", bones[:], 0.0, w=["bones"])
    p.memset("pool", bones[0:64, 0:64], 1.0, w=["bones"])
    p.memset("pool", bones[64:128, 64:128], 1.0, w=["bones"])
    rmask = p.sb("rmask", [128, TB], F32)
    p.memset("pool", rmask[:], 1.0, w=["rmask"])
    p.memset("pool", rmask.rearrange("p (c t) -> p c t", t=C)[:, :, 0:1], 0.0, w=["rmask"])
    HB = 4
    mU = p.sb("mU", [128, HB, 128], F32); mL = p.sb("mL", [128, HB, 64], F32); idn = p.sb("idn", [128, HB, 64], F32)
    for t_, name in ((mU, "mU"), (mL, "mL"), (idn, "idn")):
        p.memset("pool", t_[:], 1.0, w=[name])
    for h in range(2):
        hs = slice(64 * h, 64 * h + 64)
        p.op("pool", lambda e, hs=hs: e.affine_select(out=mU[hs, :, 0:64], in_=mU[hs, :, 0:64], pattern=[[0, HB], [1, 64]],
                                                      compare_op=ALU.is_gt, fill=0.0, base=0, channel_multiplier=-1), r=["mU"], w=["mU"])
        p.op("pool", lambda e, hs=hs: e.affine_select(out=mU[hs, :, 64:128], in_=mU[hs, :, 64:128], pattern=[[0, HB], [1, 64]],
                                                      compare_op=ALU.is_ge, fill=0.0, base=0, channel_multiplier=-1), r=["mU"], w=["mU"])
        p.op("pool", lambda e, hs=hs: e.affine_select(out=mL[hs, :, :], in_=mL[hs, :, :], pattern=[[0, HB], [-1, 64]],
                                                      compare_op=ALU.is_gt, fill=0.0, base=0, channel_multiplier=1), r=["mL"], w=["mL"])
        p.op("pool", lambda e, hs=hs: e.affine_select(out=idn[hs, :, :], in_=idn[hs, :, :], pattern=[[0, HB], [-1, 64]],
                                                      compare_op=ALU.is_equal, fill=0.0, base=0, channel_multiplier=1), r=["idn"], w=["idn"])

    WB = p.sb("WB", [128, NK, 1024], BF16)
    with Phase(p) as ph:
        wst = [ph.sb(f"wstA{i}", [128, NK, 512], F32) for i in range(2)]
        wv = wsel.rearrange("(k p) c -> p k c", p=128)
        for hf in range(2):
            p.dma("sp", wst[hf][:], wv[:, :, hf * 512:(hf + 1) * 512], s_w[hf], w=[("wstA", hf)])
            for dk in range(NK):
                p.ts("pool" if dk % 2 else "dve", WB[:, dk, hf * 512:(hf + 1) * 512], wst[hf][:, dk, :], gcol[:, dk:dk + 1], ALU.mult,
                     r=[("wstA", hf), "gcol"], w=["WB"])

    def t5(name, dt=F32, n=TB):
        return p.sb(name, [128, n], dt)
    XT = [p.sb(f"XT{i}", [128, 4, D], F32) for i in range(2)]
    sq = p.sb("sqA", [128, D], BF16); ssb = p.sb("ssA", [128, 4], F32); hb = p.sb("hbA", [128, D], BF16)
    hT = p.sb("hTA", [128, NK, TB], BF16)
    PJ = [[p.sb(f"PJ{i}_{ct}", [128, 1 + TB], F32) for ct in range(5)] for i in range(2)]
    for i in range(2):
        for ct in range(5):
            p.memset("pool", PJ[i][ct][:, 0:1], 0.0, w=[("PJ", i, ct)])
    qT = t5("qT", BF16)
    Kring = p.sb("Kring", [128, 2 * TB], BF16)
    Vring = p.sb("Vring", [128, 8, 128], BF16)
    tmp = t5("tmpA"); Pm = [t5(f"Pm{ct}") for ct in range(5)]
    txw = p.sb("txw", [128, TB], F32); sxg = t5("sxg")
    sgm = t5("sgm"); av = t5("av"); gv = t5("gv")
    kk = t5("kk"); kk2 = t5("kk2"); rn = t5("rn"); kmod = t5("kmod"); beta = t5("beta"); bon = t5("bon")
    cs = t5("cs"); csd = t5("csd"); dcs = t5("dcs")
    E1 = t5("E1"); E2 = t5("E2"); E3 = t5("E3"); E4 = t5("E4")
    AR = p.sb("AR", [128, NCH, 2, C], BF16)
    Kt = t5("Kt", BF16); Bt = t5("Bt", BF16); Khc = t5("Khc", BF16); Bhc = t5("Bhc", BF16); vB = t5("vB", BF16)
    Vt = p.sb("Vt", [128, NCH, C], BF16); Kh = p.sb("Kh", [128, NCH, C], BF16); Bh = p.sb("Bh", [128, NCH, C], BF16)
    A1 = p.sb("A1", [128, NCH, 128], BF16); A2m = p.sb("A2m", [128, NCH, 128], BF16)
    Nm = [[p.sb(f"Nm{hb_}_{i}", [128, HB, C], BF16) for i in range(2)] for hb_ in range(2)]
    Mm = [[p.sb(f"Mm{hb_}_{i}", [128, HB, C], BF16) for i in range(2)] for hb_ in range(2)]
    Pq = [[p.sb(f"Pq{hb_}_{i}", [128, HB, C], BF16) for i in range(2)] for hb_ in range(2)]
    Qq = [[p.sb(f"Qq{hb_}_{i}", [128, HB, C], BF16) for i in range(2)] for hb_ in range(2)]
    TmT = p.sb("TmT", [128, NCH, C], BF16)
    Hf = p.sb("Hf", [128, C], F32); Hb = [p.sb(f"Hb{i}", [128, C], BF16) for i in range(2)]
    p.memset("pool", Hf[:], 0.0, w=["Hf"])
    p.memset("pool", Hb[0][:], 0.0, w=[("Hb", 0)])
    Xb = p.sb("Xb", [128, C], BF16); Ub = p.sb("Ub", [128, C], BF16)
    Yf = t5("Yf"); yc = t5("yc"); ycq = t5("ycq"); rs = t5("rsA"); yo = t5("yo", BF16)
    S = p.sb("S_att", [128, 640], F32); Pe = p.sb("Pe", [128, 640], F32); Pn = p.sb("Pn", [128, 640], BF16)
    PT = p.sb("PT", [128, 5, 128], BF16)
    ast = p.sb("ast", [128, 4], F32)
    ao = t5("ao", BF16)

    xv = xb.rearrange("(n p) d -> p n d", p=128)

    def load_x(n):
        b = n % 2
        for i in range(4):
            p.dma("sp", XT[b][:, i, :], xv[:, n * 4 + i, :], s_x[b], w=[("XT", b, i)])

    hidx = [0]
    load_x(0)
    for n in range(NB):
        b = n % 2
        if n + 1 < NB:
            load_x(n + 1)
        for i in range(4):
            xt = XT[b][:, i, :]
            p.act(sq[:], xt, AF.Square, r=[("XT", b, i)], w=["sqA", "ssA"], accum_out=ssb[:, 0:1])
            p.ts("dve", ssb[:, 1:2], ssb[:, 0:1], 1.0 / D, ALU.mult, NORM_EPS, ALU.add, r=["ssA"], w=["ssA"])
            p.act(ssb[:, 2:3], ssb[:, 1:2], AF.Sqrt, r=["ssA"], w=["ssA"])
            p.op("dve", lambda e: e.reciprocal(out=ssb[:, 3:4], in_=ssb[:, 2:3]), r=["ssA"], w=["ssA"])
            p.ts("dve", hb[:], xt, ssb[:, 3:4], ALU.mult, r=[("XT", b, i), "ssA"], w=["hbA"])
            pt, pk = pp.get()
            ptb = pt.bitcast(BF16)
            for dk in range(NK):
                p.tr(ptb[:, dk * 128:(dk + 1) * 128], hb[:, dk * 128:(dk + 1) * 128], c["identb"][:], r=["hbA", "identb"], w=[pk])
            p.cp("act", hT[:, :, i * 128:(i + 1) * 128], ptb.rearrange("p (k t) -> p k t", k=NK), r=[pk], w=["hTA"])
        if n > 0:
            p.cp("pool", Kring[:, 0:TB], Kring[:, TB:2 * TB], r=["Kring"], w=["Kring"])
            p.cp("pool", Vring[:, 0:4, :], Vring[:, 4:8, :], r=["Vring"], w=["Vring"])
        for ct in range(7):
            pt, pk = pp.get()
            for dk in range(NK):
                p.mm(pt[:], WB[:, dk, ct * 128:(ct + 1) * 128], hT[:, dk, :], dk == 0, dk == NK - 1, r=["WB", "hTA"], w=[pk])
            if ct < 5:
                p.cp("act" if ct % 2 else "dve", PJ[b][ct][:, 1:1 + TB], pt[:], r=[pk], w=[("PJ", b, ct)])
            elif ct == 5:
                p.cp("act", qT[:], pt[:], r=[pk], w=["qT"])
            else:
                p.cp("dve", Kring[:, TB:2 * TB], pt[:], r=[pk], w=["Kring"])
        pt, pk = pp.get()
        for i in range(4):
            for dk in range(NK):
                p.mm(pt[:, i * 128:(i + 1) * 128], hT[:, dk, i * 128:(i + 1) * 128], WB[:, dk, 7 * 128:8 * 128], dk == 0, dk == NK - 1,
                     r=["WB", "hTA"], w=[pk])
        p.cp("act", Vring[:, 4:8, :], pt.rearrange("p (i v) -> p i v", i=4), r=[pk], w=["Vring"])
        for ct in range(5):
            cur = PJ[b][ct][:, 1:1 + TB]; prv = PJ[b][ct][:, 0:TB]
            p.tt("pool", tmp[:], prv, cur, ALU.subtract, r=[("PJ", b, ct)], w=["tmpA"])
            p.stt(Pm[ct][:], tmp[:], MU(ct), cur, ALU.mult, ALU.add, r=["tmpA", ("PJ", b, ct), "CV"], w=[("Pm", ct)])
            p.cp("pool", PJ[1 - b][ct][:, 0:1], PJ[b][ct][:, TB:TB + 1], r=[("PJ", b, ct)], w=[("PJ", 1 - b, ct)])
        r_, k_, v_ = Pm[0], Pm[1], Pm[2]
        p.act(txw[:], Pm[3][:], AF.Tanh, r=[("Pm", 3)], w=["txw"])
        p.act(sxg[:], Pm[4][:], AF.Sigmoid, r=[("Pm", 4)], w=["sxg"])
        pw_, pkw = pp.get()
        p.mm(pw_[:], W2[:], txw[:], True, True, r=["W2", "txw"], w=[pkw])
        p.act(sgm[:], pw_[:], AF.Sigmoid, r=[pkw, "CV"], w=["sgm"], bias=W0)
        pa_, pka = pp.get()
        p.mm(pa_[:], A2[:], Pm[3][:], True, True, r=["A2", ("Pm", 3)], w=[pka])
        p.act(av[:], pa_[:], AF.Sigmoid, r=[pka, "CV"], w=["av"], bias=A0)
        pg_, pkg = pp.get()
        p.mm(pg_[:], G2[:], sxg[:], True, True, r=["G2", "sxg"], w=[pkg])
        p.cp("act", gv[:], pg_[:], r=[pkg], w=["gv"])
        p.ts("dve", kk[:], k_[:], KK, ALU.mult, r=[("Pm", 1), "CV"], w=["kk"])
        p.tt("pool", kk2[:], kk[:], kk[:], ALU.mult, r=["kk"], w=["kk2"])
        pn_, pkn = pp.get()
        p.mm(pn_[:], bones[:], kk2[:], True, True, r=["bones", "kk2"], w=[pkn])
        p.act(rn[:], pn_[:], AF.Sqrt, r=[pkn], w=["rn"])
        p.ts("dve", rn[:], rn[:], 1e-12, ALU.max, r=["rn"], w=["rn"])
        p.op("dve", lambda e: e.reciprocal(out=rn[:], in_=rn[:]), r=["rn"], w=["rn"])
        p.tt("dve", kk[:], kk[:], rn[:], ALU.mult, r=["kk", "rn"], w=["kk"])
        p.ts("dve", tmp[:], av[:], KA, ALU.mult, OMKA, ALU.add, r=["av", "CV"], w=["tmpA"])
        p.tt("dve", kmod[:], k_[:], tmp[:], ALU.mult, r=[("Pm", 1), "tmpA"], w=["kmod"])
        p.tt("pool", beta[:], kk[:], av[:], ALU.mult, r=["kk", "av"], w=["beta"])
        p.stt(tmp[:], r_[:], RK, kmod[:], ALU.mult, ALU.mult, r=[("Pm", 0), "kmod", "CV"], w=["tmpA"])
        pb_, pkb = pp.get()
        p.mm(pb_[:], bones[:], tmp[:], True, True, r=["bones", "tmpA"], w=[pkb])
        p.tt("dve", bon[:], v_[:], pb_[:], ALU.mult, r=[("Pm", 2), pkb], w=["bon"])
        p.op("dve", lambda e: e.tensor_tensor_scan(out=cs[:], data0=rmask[:], data1=sgm[:], initial=0.0, op0=ALU.mult, op1=ALU.add),
             r=["rmask", "sgm"], w=["cs"])
        p.tt("pool", csd[:], cs[:], sgm[:], ALU.subtract, r=["cs", "sgm"], w=["csd"])
        cs3 = cs.rearrange("p (c t) -> p c t", t=C)
        p.tt("pool", dcs.rearrange("p (c t) -> p c t", t=C), cs3[:, :, C - 1:C].to_broadcast([128, NCH, C]), cs3, ALU.subtract,
             r=["cs"], w=["dcs"])
        p.act(E1[:], cs[:], AF.Exp, r=["cs"], w=["E1"], scale=-DEC)
        p.act(E2[:], cs[:], AF.Exp, r=["cs"], w=["E2"], scale=DEC)
        p.act(E3[:], csd[:], AF.Exp, r=["csd"], w=["E3"], scale=-DEC)
        p.act(E4[:], dcs[:], AF.Exp, r=["dcs"], w=["E4"], scale=-DEC)
        c3 = lambda ap: ap.rearrange("p (c t) -> p c t", t=C)
        p.stt(AR[:, :, 0, :], c3(kk), -1.0, c3(E3), ALU.mult, ALU.mult, r=["kk", "E3"], w=["AR"])
        p.tt("dve", AR[:, :, 1, :], c3(r_), c3(E1), ALU.mult, r=[("Pm", 0), "E1"], w=["AR"])
        p.tt("pool", Kt[:], kmod[:], E2[:], ALU.mult, r=["kmod", "E2"], w=["Kt"])
        p.tt("pool", Bt[:], beta[:], E2[:], ALU.mult, r=["beta", "E2"], w=["Bt"])
        p.tt("dve", Khc[:], kmod[:], E4[:], ALU.mult, r=["kmod", "E4"], w=["Khc"])
        p.tt("pool", Bhc[:], beta[:], E4[:], ALU.mult, r=["beta", "E4"], w=["Bhc"])
        p.cp("act", vB[:], v_[:], r=[("Pm", 2)], w=["vB"])
        for src, dst, ks, kd in ((vB, Vt, "vB", "Vt"), (Khc, Kh, "Khc", "Kh"), (Bhc, Bh, "Bhc", "Bh")):
            pt, pk = pp.get()
            ptb = pt.bitcast(BF16)
            for ch in range(NCH):
                for h in range(2):
                    hs = slice(64 * h, 64 * h + 64)
                    p.tr(ptb[hs, ch * C:(ch + 1) * C], src[hs, ch * C:(ch + 1) * C], c["identb"][hs, hs], r=[ks, "identb"], w=[pk],
                         tile_position=(64 * h, 64 * h))
            p.cp("act", dst.rearrange("p c t -> p (c t)"), ptb[:, 0:TB], r=[pk], w=[kd])
        for hb_ in range(2):
            p1, pk1 = pp.get(); p2, pk2 = pp.get(); p3, pk3 = pp.get()
            for cc in range(HB):
                ch = hb_ * HB + cc
                for h in range(2):
                    hs = slice(64 * h, 64 * h + 64)
                    tp = (64 * h, 64 * h)
                    arh = AR[hs, ch, :, :].rearrange("p a t -> p (a t)")
                    p.mm(p1[hs, cc * 128:(cc + 1) * 128], Kt[hs, ch * C:(ch + 1) * C], arh, True, True, r=["Kt", "AR"], w=[pk1], tile_position=tp)
                    p.mm(p2[hs, cc * 128:(cc + 1) * 128], Bt[hs, ch * C:(ch + 1) * C], arh, True, True, r=["Bt", "AR"], w=[pk2], tile_position=tp)
                    p.mm(p3[hs, cc * C:(cc + 1) * C], AR[hs, ch, 0, :], Bt[hs, ch * C:(ch + 1) * C], True, True, r=["AR", "Bt"], w=[pk3], tile_position=tp)
            chs = slice(hb_ * HB, (hb_ + 1) * HB)
            p.tt("dve", A1[:, chs, :], p1.rearrange("p (c t) -> p c t", c=HB), mU[:], ALU.mult, r=[pk1, "mU"], w=["A1"])
            p.tt("dve", A2m[:, chs, :], p2.rearrange("p (c t) -> p c t", c=HB), mU[:], ALU.mult, r=[pk2, "mU"], w=["A2m"])
            p.tt("dve", Mm[hb_][0][:], p3[:, 0:HB * C].rearrange("p (c t) -> p c t", c=HB), mL[:], ALU.mult, r=[pk3, "mL"], w=[("Mm", hb_, 0)])
            p.cp("pool", Nm[hb_][0][:], A2m[:, chs, 0:C], r=["A2m"], w=[("Nm", hb_, 0)])
            p.tt("pool", Pq[hb_][0][:], Nm[hb_][0][:], idn[:], ALU.add, r=[("Nm", hb_, 0), "idn"], w=[("Pq", hb_, 0)])
            p.tt("pool", Qq[hb_][0][:], Mm[hb_][0][:], idn[:], ALU.add, r=[("Mm", hb_, 0), "idn"], w=[("Qq", hb_, 0)])
        for lvl in range(1, 6):
            o_ = (lvl - 1) % 2; n_ = lvl % 2
            last = lvl == 5
            stage1 = []
            for hb_ in range(2):
                pN, pkN = pp.get()
                pM, pkM = (None, None) if last else pp.get()
                for cc in range(HB):
                    for h in range(2):
                        hs = slice(64 * h, 64 * h + 64); tp = (64 * h, 64 * h)
                        p.mm(pN[hs, cc * C:(cc + 1) * C], Mm[hb_][o_][hs, cc, :], Nm[hb_][o_][hs, cc, :], True, True,
                             r=[("Mm", hb_, o_), ("Nm", hb_, o_)], w=[pkN], tile_position=tp)
                        if not last:
                            p.mm(pM[hs, cc * C:(cc + 1) * C], Nm[hb_][o_][hs, cc, :], Mm[hb_][o_][hs, cc, :], True, True,
                                 r=[("Mm", hb_, o_), ("Nm", hb_, o_)], w=[pkM], tile_position=tp)
                stage1.append((pN, pkN, pM, pkM))
            for hb_ in range(2):
                pN, pkN, pM, pkM = stage1[hb_]
                p.cp("dve", Nm[hb_][n_][:], pN[:, 0:HB * C].rearrange("p (c t) -> p c t", c=HB), r=[pkN], w=[("Nm", hb_, n_)])
                if not last:
                    p.cp("act", Mm[hb_][n_][:], pM[:, 0:HB * C].rearrange("p (c t) -> p c t", c=HB), r=[pkM], w=[("Mm", hb_, n_)])
            stage2 = []
            for hb_ in range(2):
                pP, pkP = pp.get()
                pQ, pkQ = (None, None) if last else pp.get()
                for cc in range(HB):
                    for h in range(2):
                        hs = slice(64 * h, 64 * h + 64); tp = (64 * h, 64 * h)
                        p.mm(pP[hs, cc * C:(cc + 1) * C], Qq[hb_][o_][hs, cc, :], Nm[hb_][n_][hs, cc, :], True, True,
                             r=[("Qq", hb_, o_), ("Nm", hb_, n_)], w=[pkP], tile_position=tp)
                        if not last:
                            p.mm(pQ[hs, cc * C:(cc + 1) * C], Pq[hb_][o_][hs, cc, :], Mm[hb_][n_][hs, cc, :], True, True,
                                 r=[("Pq", hb_, o_), ("Mm", hb_, n_)], w=[pkQ], tile_position=tp)
                stage2.append((pP, pkP, pQ, pkQ))
            for hb_ in range(2):
                pP, pkP, pQ, pkQ = stage2[hb_]
                chs = slice(hb_ * HB, (hb_ + 1) * HB)
                dstP = TmT[:, chs, :] if last else Pq[hb_][n_][:]
                kdP = "TmT" if last else ("Pq", hb_, n_)
                p.tt("dve", dstP, Pq[hb_][o_][:], pP[:, 0:HB * C].rearrange("p (c t) -> p c t", c=HB), ALU.add,
                     r=[pkP, ("Pq", hb_, o_)], w=[kdP])
                if not last:
                    p.tt("dve", Qq[hb_][n_][:], Qq[hb_][o_][:], pQ[:, 0:HB * C].rearrange("p (c t) -> p c t", c=HB), ALU.add,
                         r=[pkQ, ("Qq", hb_, o_)], w=[("Qq", hb_, n_)])
        pY, pkY = psD, "psD"
        for ch in range(NCH):
            hi = hidx[0] % 2; ho = 1 - hi; hidx[0] += 1
            pX, pkX = pp.get()
            for h in range(2):
                hs = slice(64 * h, 64 * h + 64); tp = (64 * h, 64 * h)
                p.mm(pX[hs, 0:C], A1[hs, ch, 0:C], Vt[hs, ch, :], True, False, r=["A1", "Vt"], w=[pkX], tile_position=tp)
                p.mm(pX[hs, 0:C], AR[hs, ch, 0, :], Hb[hi][hs, :], False, True, r=["AR", ("Hb", hi)], w=[pkX], tile_position=tp)
            p.cp("act", Xb[:], pX[:, 0:C], r=[pkX], w=["Xb"])
            pU, pkU = pp.get()
            for h in range(2):
                hs = slice(64 * h, 64 * h + 64); tp = (64 * h, 64 * h)
                p.mm(pU[hs, 0:C], TmT[hs, ch, :], Xb[hs, :], True, True, r=["TmT", "Xb"], w=[pkU], tile_position=tp)
            p.cp("act", Ub[:], pU[:, 0:C], r=[pkU], w=["Ub"])
            pH, pkH = pp.get()
            for h in range(2):
                hs = slice(64 * h, 64 * h + 64); tp = (64 * h, 64 * h)
                p.mm(pH[hs, 0:C], Kh[hs, ch, :], Vt[hs, ch, :], True, False, r=["Kh", "Vt"], w=[pkH], tile_position=tp)
                p.mm(pH[hs, 0:C], Bh[hs, ch, :], Ub[hs, :], False, True, r=["Bh", "Ub"], w=[pkH], tile_position=tp)
            gC = E1[:, ch * C + C - 1:ch * C + C]
            p.stt(Hb[ho][:], Hf[:], gC, pH[:, 0:C], ALU.mult, ALU.add, r=["Hf", "E1", pkH], w=[("Hb", ho)])
            for h in range(2):
                hs = slice(64 * h, 64 * h + 64); tp = (64 * h, 64 * h)
                yo_ = pY[hs, ch * C:(ch + 1) * C]
                p.mm(yo_, Hb[hi][hs, :], AR[hs, ch, 1, :], True, False, r=[("Hb", hi), "AR"], w=[pkY], tile_position=tp)
                p.mm(yo_, Vt[hs, ch, :], A1[hs, ch, C:2 * C], False, False, r=["Vt", "A1"], w=[pkY], tile_position=tp)
                p.mm(yo_, Ub[hs, :], A2m[hs, ch, C:2 * C], False, True, r=["Ub", "A2m"], w=[pkY], tile_position=tp)
            p.stt(Hf[:], Hf[:], gC, pH[:, 0:C], ALU.mult, ALU.add, r=["Hf", "E1", pkH], w=["Hf"])
        p.cp("act", Yf[:], pY[:], r=[pkY], w=["Yf"])
        pm_, pkm = pp.get()
        p.mm(pm_[:], bones[:], Yf[:], True, True, r=["bones", "Yf"], w=[pkm])
        p.stt(yc[:], pm_[:], -1.0 / C, Yf[:], ALU.mult, ALU.add, r=[pkm, "Yf"], w=["yc"])
        p.tt("pool", ycq[:], yc[:], yc[:], ALU.mult, r=["yc"], w=["ycq"])
        pv_, pkv = pp.get()
        p.mm(pv_[:], bones[:], ycq[:], True, True, r=["bones", "ycq"], w=[pkv])
        p.ts("dve", rs[:], pv_[:], 1.0 / C, ALU.mult, GN_EPS, ALU.add, r=[pkv], w=["rsA"])
        p.act(rs[:], rs[:], AF.Sqrt, r=["rsA"], w=["rsA"])
        p.op("dve", lambda e: e.reciprocal(out=rs[:], in_=rs[:]), r=["rsA"], w=["rsA"])
        p.tt("dve", yc[:], yc[:], rs[:], ALU.mult, r=["yc", "rsA"], w=["yc"])
        p.ts("dve", yc[:], yc[:], LNW, ALU.mult, LNB, ALU.add, r=["yc", "CV"], w=["yc"])
        p.tt("pool", yc[:], yc[:], bon[:], ALU.add, r=["yc", "bon"], w=["yc"])
        p.tt("dve", yo[:], yc[:], gv[:], ALU.mult, r=["yc", "gv"], w=["yo"])
        p.dma("sp", yT[0:128, n * TB:(n + 1) * TB], yo[:], s_st, r=["yo"], w=[("yT", 0, n)])
        pO, pkO = psD, "psD"
        for i in range(4):
            kv0 = 0
            if n == 0:
                kv0 = TB - i * 128
            nk = 640 - kv0
            nkt = nk // 128
            for h in range(2):
                hs = slice(64 * h, 64 * h + 64)
                pS1, pkS1 = pp.get(); pS2, pkS2 = pp.get()
                lo = i * 128 + kv0
                n1 = min(nk, 512)
                p.mm(pS1[:, 0:n1], qT[hs, i * 128:(i + 1) * 128], Kring[hs, lo:lo + n1], True, True, r=["qT", "Kring"], w=[pkS1])
                p.stt(S[:, kv0:kv0 + n1], pS1[:, 0:n1], 0.125, ATB[:, h, kv0:kv0 + n1], ALU.mult, ALU.add, r=[pkS1, "ATB"], w=["S_att"])
                if nk > 512:
                    p.mm(pS2[:, 0:128], qT[hs, i * 128:(i + 1) * 128], Kring[hs, lo + 512:lo + 640], True, True, r=["qT", "Kring"], w=[pkS2])
                    p.stt(S[:, 512:640], pS2[:, 0:128], 0.125, ATB[:, h, 512:640], ALU.mult, ALU.add, r=[pkS2, "ATB"], w=["S_att"])
                p.op("dve", lambda e, kv0=kv0: e.tensor_reduce(out=ast[:, 0:1], in_=S[:, kv0:640], axis=AX.X, op=ALU.max), r=["S_att"], w=["ast"])
                p.ts("dve", ast[:, 1:2], ast[:, 0:1], -1.0, ALU.mult, r=["ast"], w=["ast"])
                p.act(Pe[:, kv0:640], S[:, kv0:640], AF.Exp, r=["S_att", "ast"], w=["Pe", "ast"], bias=ast[:, 1:2], accum_out=ast[:, 2:3])
                p.op("dve", lambda e: e.reciprocal(out=ast[:, 3:4], in_=ast[:, 2:3]), r=["ast"], w=["ast"])
                p.ts("dve", Pn[:, kv0:640], Pe[:, kv0:640], ast[:, 3:4], ALU.mult, r=["Pe", "ast"], w=["Pn"])
                ptr, pktr = pp.get()
                ptrb = ptr.bitcast(BF16)
                for kt in range(nkt):
                    c0 = kv0 + kt * 128
                    p.tr(ptrb[:, kt * 128:(kt + 1) * 128], Pn[:, c0:c0 + 128], c["identb"][:], r=["Pn", "identb"], w=[pktr])
                p.cp("act", PT[:, 0:nkt, :], ptrb[:, 0:nkt * 128].rearrange("p (k q) -> p k q", k=nkt), r=[pktr], w=["PT"])
                vt0 = i + (kv0 // 128)
                for kt in range(nkt):
                    p.mm(pO[hs, i * 128:(i + 1) * 128], Vring[:, vt0 + kt, 64 * h:64 * h + 64], PT[:, kt, :], kt == 0, kt == nkt - 1,
                         r=["Vring", "PT"], w=[pkO], tile_position=(0, 64 * h))
        p.cp("act", ao[:], pO[:], r=[pkO], w=["ao"])
        p.dma("sp", yT[128:256, n * TB:(n + 1) * TB], ao[:], s_st, r=["ao"], w=[("yT", 1, n)])
    p.finalize(final_sems=[s_st])
    return p


C = 64
TB = 512
NCH = TB // C
DEC = 0.6065306597126334
GN_EPS = 64e-5
NBLK = 16
FIRST_FULL = 11
KV_BLOCK = 10
HB = 4


def emit_A(p, ph, c, pools, psD, psO, dr_in, yscr, nblk=NBLK, first_full=FIRST_FULL, kv_block=KV_BLOCK, nhp=4):
    xb = dr_in["xb"]; wsel4 = dr_in["wsel"]; g_mix = dr_in["g_mix"]; cv4 = dr_in["cv"]
    w24 = dr_in["w2"]; a24 = dr_in["a2"]; g24 = dr_in["g2"]; attb4 = dr_in["attb"]; amask = dr_in["amask"]
    s_ld = p.sem("a_ld"); s_x = [p.sem(f"a_x{i}") for i in range(4)]; s_w = [p.sem("a_w0"), p.sem("a_w1")]
    s_st = p.sem("a_st")
    s_hs = p.sem("a_hs"); s_hl = [p.sem("a_hl0"), p.sem("a_hl1")]
    hTs = dr_in["hTs"]
    sb = ph.sb
    pp1, pp2, s3 = pools

    gcol = sb("gcol", [128, NK], F32)
    p.dma("sp", gcol[:], g_mix.rearrange("(k p) -> p k", p=128), s_ld, w=["gcol"], allow_slow_non_contiguous=True)
    bones = sb("bones", [128, 128], F32)
    p.memset("pool", bones[:], 0.0, w=["bones"])
    p.memset("pool", bones[0:64, 0:64], 1.0, w=["bones"])
    p.memset("pool", bones[64:128, 64:128], 1.0, w=["bones"])
    rmask = sb("rmask", [128, TB], F32)
    p.memset("pool", rmask[:], 1.0, w=["rmask"])
    p.memset("pool", rmask.rearrange("p (c t) -> p c t", t=C)[:, :, 0:1], 0.0, w=["rmask"])
    mU = sb("mU", [128, HB, 128], F32); mL = sb("mL", [128, HB, 64], F32); idn = sb("idn", [128, HB, 64], F32)
    for t_, name in ((mU, "mU"), (mL, "mL"), (idn, "idn")):
        p.memset("pool", t_[:], 1.0, w=[name])
    for h in range(2):
        hs = slice(64 * h, 64 * h + 64)
        p.op("pool", lambda e, hs=hs: e.affine_select(out=mU[hs, :, 0:64], in_=mU[hs, :, 0:64], pattern=[[0, HB], [1, 64]],
                                                      compare_op=ALU.is_gt, fill=0.0, base=0, channel_multiplier=-1), r=["mU"], w=["mU"])
        p.op("pool", lambda e, hs=hs: e.affine_select(out=mU[hs, :, 64:128], in_=mU[hs, :, 64:128], pattern=[[0, HB], [1, 64]],
                                                      compare_op=ALU.is_ge, fill=0.0, base=0, channel_multiplier=-1), r=["mU"], w=["mU"])
        p.op("pool", lambda e, hs=hs: e.affine_select(out=mL[hs, :, :], in_=mL[hs, :, :], pattern=[[0, HB], [-1, 64]],
                                                      compare_op=ALU.is_gt, fill=0.0, base=0, channel_multiplier=1), r=["mL"], w=["mL"])
        p.op("pool", lambda e, hs=hs: e.affine_select(out=idn[hs, :, :], in_=idn[hs, :, :], pattern=[[0, HB], [-1, 64]],
                                                      compare_op=ALU.is_equal, fill=0.0, base=0, channel_multiplier=1), r=["idn"], w=["idn"])
    AMK = sb("AMK", [128, 4, 640], F32)
    p.dma("sp", AMK[:], amask, s_ld, w=["AMK"])

    def t5(name, dt=F32, n=TB):
        return sb(name, [128, n], dt)
    CV = sb("CV", [128, 16], F32)
    MU = lambda ct: CV[:, ct:ct + 1]
    W0, A0, KK, KA, RK, LNW, LNB = [CV[:, 5 + i:6 + i] for i in range(7)]
    OMKA = CV[:, 12:13]
    W2 = sb("W2", [128, 128], F32); A2 = sb("A2", [128, 128], F32); G2 = sb("G2", [128, 128], F32)
    p.memset("pool", W2[:], 0.0, w=["W2"])
    p.memset("pool", A2[:], 0.0, w=["A2"])
    ATB = sb("ATB", [128, 2, 640], F32)
    WB = sb("WB", [128, NK, 1024], BF16)
    wst = [sb(f"wstA{i}", [128, NK, 128], F32) for i in range(2)]
    XTbig = sb("XTbig", [128, 4, D], F32)
    XT = [XTbig[:, i, :] for i in range(4)]
    hTalt = XTbig[:, 0:2, :].rearrange("p a d -> p (a d)").bitcast(BF16).rearrange("p (k t) -> p k t", k=NK)
    sq = sb("sqA", [128, D], BF16); ssb2 = [sb(f"ssA{i}", [128, 4], F32) for i in range(2)]; hb2 = [sb(f"hbA{i}", [128, D], BF16) for i in range(2)]
    hT = sb("hTA", [128, NK, TB], BF16)
    PJ = [[sb(f"PJ{i}_{ct}", [128, 1 + TB], F32) for ct in range(5)] for i in range(2)]
    qT = t5("qT", BF16)
    Kring = sb("Kring", [128, 2 * TB], BF16)
    Vring = sb("Vring", [128, 8, 128], BF16)
    tmp = t5("tmpA"); Pm = [t5(f"Pm{ct}") for ct in range(5)]
    sgm = t5("sgm"); av = t5("av"); gv2 = [t5("gv0"), t5("gv1")]
    kk = t5("kk"); kmod = t5("kmod"); beta = t5("beta"); bon2 = [t5("bon0"), t5("bon1")]
    cs = t5("cs"); csd = t5("csd"); dcs = t5("dcs")
    E12 = [t5("E1_0"), t5("E1_1")]; E2 = t5("E2"); E3 = t5("E3"); E4 = t5("E4")
    txw, ktxw = E2, "E2"; sxg, ksxg = E3, "E3"; kk2, kkk2 = csd, "csd"; rn, krn = dcs, "dcs"
    Yf, kYf = t5("Yf"), "Yf"; yc, kyc = t5("yc"), "yc"; ycq, kycq = t5("ycq"), "ycq"; rs, krs = t5("rsA"), "rsA"
    AR2 = [sb(f"AR{i}", [128, NCH, 2, C], BF16) for i in range(2)]
    for i in range(2):
        p.memset("pool", AR2[i][:], 0.0, w=[("AR", i)])
    Kt2 = [t5(f"Kt{i}", BF16) for i in range(2)]; Bt2 = [t5(f"Bt{i}", BF16) for i in range(2)]; Khc2 = [t5(f"Khc{i}", BF16) for i in range(2)]
    Bhc2 = [t5(f"Bhc{i}", BF16) for i in range(2)]; vB2 = [t5(f"vB{i}", BF16) for i in range(2)]
    Vt2 = [sb(f"Vt{i}", [128, NCH, C], BF16) for i in range(2)]; Kh2 = [sb(f"Kh{i}", [128, NCH, C], BF16) for i in range(2)]
    Bh2 = [sb(f"Bh{i}", [128, NCH, C], BF16) for i in range(2)]
    A12 = [sb(f"A1_{i}", [128, NCH, 128], BF16) for i in range(2)]; A2m2 = [sb(f"A2m_{i}", [128, NCH, 128], BF16) for i in range(2)]
    NS = [[sb(f"NS{hb_}_{i}", [128, HB, 2, C], BF16) for i in range(2)] for hb_ in range(2)]
    Mm = [[sb(f"Mm{hb_}_{i}", [128, HB, C], BF16) for i in range(2)] for hb_ in range(2)]
    TmT2 = [sb(f"TmT{i}", [128, NCH, C], BF16) for i in range(2)]
    Hf = sb("Hf", [128, C], F32); Hb = [sb(f"Hb{i}", [128, C], BF16) for i in range(2)]
    Xb = sb("Xb", [128, C], BF16); Ub = sb("Ub", [128, C], BF16)
    yo = t5("yo", BF16)
    S = sb("S_att", [128, 640], F32); Pe = sb("Pe", [128, 640], F32); Pn = sb("Pn", [128, 640], BF16)
    PT = sb("PT", [128, 5, 128], BF16)
    ast = sb("ast", [128, 4], F32)
    ao = t5("ao", BF16)

    xv = xb.rearrange("(n p) d -> p n d", p=128)
    xcnt = [0]

    def load_x_tile(n, i):
        j = xcnt[0] % 4; xcnt[0] += 1
        p.dma("sp", XT[j], xv[:, n * 4 + i, :], s_x[j], w=[("XT", j)])
        return j

    for hp in range(nhp):
        p.dma("sp", CV[:], cv4[hp], s_ld, w=["CV"])
        p.ts("dve", OMKA, KA, -1.0, ALU.mult, 1.0, ALU.add, r=["CV"], w=["CV"])
        p.dma("sp", W2[0:64, :], w24[hp], s_ld, w=["W2"])
        p.dma("sp", A2[64:128, :], a24[hp], s_ld, w=["A2"])
        p.dma("sp", G2[:], g24[hp], s_ld, w=["G2"])
        for h in range(2):
            p.dma("sp", ATB[:, h, :], attb4[hp, h], s_ld, w=["ATB"])
        wv = wsel4[hp].rearrange("(k p) c -> p k c", p=128)
        for j in range(8):
            b_ = j % 2
            p.dma("sp", wst[b_][:], wv[:, :, j * 128:(j + 1) * 128], s_w[b_], w=[("wstA", b_)])
            for dk in range(NK):
                if dk % 2:
                    p.act(WB[:, dk, j * 128:(j + 1) * 128], wst[b_][:, dk, :], AF.Copy, r=[("wstA", b_), "gcol"], w=["WB"], scale=gcol[:, dk:dk + 1])
                else:
                    p.ts("dve", WB[:, dk, j * 128:(j + 1) * 128], wst[b_][:, dk, :], gcol[:, dk:dk + 1], ALU.mult,
                         r=[("wstA", b_), "gcol"], w=["WB"])
        p.memset("pool", Hf[:], 0.0, w=["Hf"])
        p.memset("pool", Hb[0][:], 0.0, w=[("Hb", 0)])
        for ct in range(5):
            p.memset("pool", PJ[0][ct][:, 0:1], 0.0, w=[("PJ", 0, ct)])
        hidx = 0
        share = hp > 0

        def load_hT(n):
            if n % 2 == 0:
                p.dma("sp", hT[:].rearrange("p k t -> p (k t)"), hTs[n], s_hl[0], r=[("hTs", n)], w=["hTA"])
            else:
                p.dma("sp", hTalt.rearrange("p k t -> p (k t)"), hTs[n], s_hl[1], r=[("hTs", n)], w=["hTB", ("XT", 0), ("XT", 1)])
        if share:
            load_hT(0)
        else:
            nxt = [load_x_tile(0, i) for i in range(4)]
        for n in range(nblk):
            b = n % 2
            full = n >= first_full
            kvb = full or n == kv_block
            if share:
                hTc, khT = (hT, "hTA") if n % 2 == 0 else (hTalt, "hTB")
                if n + 1 < nblk:
                    load_hT(n + 1)
            else:
                hTc, khT = hT, "hTA"
                cur_x = nxt
            AR, kAR = AR2[b], ("AR", b); E1, kE1 = E12[b], ("E1", b); bon, kbon = bon2[b], ("bon", b); gv, kgv = gv2[b], ("gv", b)
            Vt, kVt = Vt2[b], ("Vt", b); Kh, kKh = Kh2[b], ("Kh", b); Bh, kBh = Bh2[b], ("Bh", b)
            A1, kA1 = A12[b], ("A1", b); A2m, kA2m = A2m2[b], ("A2m", b); TmT, kTmT = TmT2[b], ("TmT", b)
            Kt, kKt = Kt2[b], ("Kt", b); Bt, kBt = Bt2[b], ("Bt", b); Khc, kKhc = Khc2[b], ("Khc", b)
            Bhc, kBhc = Bhc2[b], ("Bhc", b); vB, kvB = vB2[b], ("vB", b)
            nxt = []
            for i in range(0 if share else 4):
                j = cur_x[i]
                xt = XT[j]
                ssb = ssb2[i % 2]; hb = hb2[i % 2]; kss = ("ssA", i % 2); khb = ("hbA", i % 2)
                p.act(sq[:], xt, AF.Square, r=[("XT", j)], w=[kss], accum_out=ssb[:, 0:1])
                p.ts("dve", ssb[:, 1:2], ssb[:, 0:1], 1.0 / D, ALU.mult, NORM_EPS, ALU.add, r=[kss], w=[kss])
                p.act(ssb[:, 2:3], ssb[:, 1:2], AF.Sqrt, r=[kss], w=[kss])
                p.op("dve", lambda e, ssb=ssb: e.reciprocal(out=ssb[:, 3:4], in_=ssb[:, 2:3]), r=[kss], w=[kss])
                p.ts("dve", hb[:], xt, ssb[:, 3:4], ALU.mult, r=[("XT", j), kss], w=[khb])
                if n + 1 < nblk:
                    nxt.append(load_x_tile(n + 1, i))
                pt, pk = pp1.get()
                ptb = pt.bitcast(BF16)
                for dk in range(NK):
                    p.tr(ptb[:, dk * 128:(dk + 1) * 128], hb[:, dk * 128:(dk + 1) * 128], c["identb"][:], r=[khb, "identb"], w=[pk])
                p.cp("act", hT[:, :, i * 128:(i + 1) * 128], ptb.rearrange("p (k t) -> p k t", k=NK), r=[pk], w=["hTA"])
            if not share and nhp > 1:
                p.dma("sp", hTs[n], hT[:].rearrange("p k t -> p (k t)"), s_hs, r=["hTA"], w=[("hTs", n)])
            if full:
                p.cp("pool", Kring[:, 0:TB], Kring[:, TB:2 * TB], r=["Kring"], w=["Kring"])
                p.cp("pool", Vring[:, 0:4, :], Vring[:, 4:8, :], r=["Vring"], w=["Vring"])
            cts = [0, 1, 2, 3, 4, 5, 6] if full else ([1, 2, 3, 6] if kvb else [1, 2, 3])
            for ct in cts:
                pt, pk = pp1.get()
                for dk in range(NK):
                    p.mm(pt[:], WB[:, dk, ct * 128:(ct + 1) * 128], hTc[:, dk, :], dk == 0, dk == NK - 1, r=["WB", khT], w=[pk])
                if ct < 5:
                    p.cp("act" if ct % 2 else "dve", PJ[b][ct][:, 1:1 + TB], pt[:], r=[pk], w=[("PJ", b, ct)])
                elif ct == 5:
                    p.cp("act", qT[:], pt[:], r=[pk], w=["qT"])
                else:
                    p.cp("dve", Kring[:, TB:2 * TB], pt[:], r=[pk], w=["Kring"])
            if kvb:
                pt, pk = pp1.get()
                for i in range(4):
                    for dk in range(NK):
                        p.mm(pt[:, i * 128:(i + 1) * 128], hTc[:, dk, i * 128:(i + 1) * 128], WB[:, dk, 7 * 128:8 * 128], dk == 0, dk == NK - 1,
                             r=["WB", khT], w=[pk])
                p.cp("act", Vring[:, 4:8, :], pt.rearrange("p (i v) -> p i v", i=4), r=[pk], w=["Vring"])
            mcts = [0, 1, 2, 3, 4] if full else [1, 2, 3]
            for ct in range(5):
                if ct in mcts:
                    cur = PJ[b][ct][:, 1:1 + TB]; prv = PJ[b][ct][:, 0:TB]
                    p.tt("pool", tmp[:], prv, cur, ALU.subtract, r=[("PJ", b, ct)], w=["tmpA"])
                    p.stt(Pm[ct][:], tmp[:], MU(ct), cur, ALU.mult, ALU.add, r=["tmpA", ("PJ", b, ct), "CV"], w=[("Pm", ct)])
                    p.cp("pool", PJ[1 - b][ct][:, 0:1], PJ[b][ct][:, TB:TB + 1], r=[("PJ", b, ct)], w=[("PJ", 1 - b, ct)])
                elif n + 1 == first_full:
                    pt, pk = pp1.get()
                    for dk in range(NK):
                        p.mm(pt[:, 0:1], WB[:, dk, ct * 128:(ct + 1) * 128], hTc[:, dk, TB - 1:TB], dk == 0, dk == NK - 1, r=["WB", khT], w=[pk])
                    p.cp("act", PJ[1 - b][ct][:, 0:1], pt[:, 0:1], r=[pk], w=[("PJ", 1 - b, ct)])
            r_, k_, v_ = Pm[0], Pm[1], Pm[2]
            p.act(txw[:], Pm[3][:], AF.Tanh, r=[("Pm", 3)], w=[ktxw])
            pw_, pkw = pp1.get()
            p.mm(pw_[:], W2[:], txw[:], True, True, r=["W2", ktxw], w=[pkw])
            p.act(sgm[:], pw_[:], AF.Sigmoid, r=[pkw, "CV"], w=["sgm"], bias=W0)
            pa_, pka = pp1.get()
            p.mm(pa_[:], A2[:], Pm[3][:], True, True, r=["A2", ("Pm", 3)], w=[pka])
            p.act(av[:], pa_[:], AF.Sigmoid, r=[pka, "CV"], w=["av"], bias=A0)
            if full:
                p.act(sxg[:], Pm[4][:], AF.Sigmoid, r=[("Pm", 4)], w=[ksxg])
                pg_, pkg = pp1.get()
                p.mm(pg_[:], G2[:], sxg[:], True, True, r=["G2", ksxg], w=[pkg])
                p.cp("act", gv[:], pg_[:], r=[pkg], w=[kgv])
            p.ts("dve", kk[:], k_[:], KK, ALU.mult, r=[("Pm", 1), "CV"], w=["kk"])
            p.tt("pool", kk2[:], kk[:], kk[:], ALU.mult, r=["kk"], w=[kkk2])
            pn_, pkn = pp1.get()
            p.mm(pn_[:], bones[:], kk2[:], True, True, r=["bones", kkk2], w=[pkn])
            p.act(rn[:], pn_[:], AF.Sqrt, r=[pkn], w=[krn])
            p.ts("dve", rn[:], rn[:], 1e-12, ALU.max, r=[krn], w=[krn])
            p.op("dve", lambda e: e.reciprocal(out=rn[:], in_=rn[:]), r=[krn], w=[krn])
            p.tt("dve", kk[:], kk[:], rn[:], ALU.mult, r=["kk", krn], w=["kk"])
            p.ts("dve", tmp[:], av[:], KA, ALU.mult, OMKA, ALU.add, r=["av", "CV"], w=["tmpA"])
            p.tt("dve", kmod[:], k_[:], tmp[:], ALU.mult, r=[("Pm", 1), "tmpA"], w=["kmod"])
            p.tt("pool", beta[:], kk[:], av[:], ALU.mult, r=["kk", "av"], w=["beta"])
            if full:
                p.stt(tmp[:], r_[:], RK, kmod[:], ALU.mult, ALU.mult, r=[("Pm", 0), "kmod", "CV"], w=["tmpA"])
                pb_, pkb = pp1.get()
                p.mm(pb_[:], bones[:], tmp[:], True, True, r=["bones", "tmpA"], w=[pkb])
                p.tt("dve", bon[:], v_[:], pb_[:], ALU.mult, r=[("Pm", 2), pkb], w=[kbon])
            p.op("dve", lambda e: e.tensor_tensor_scan(out=cs[:], data0=rmask[:], data1=sgm[:], initial=0.0, op0=ALU.mult, op1=ALU.add),
                 r=["rmask", "sgm"], w=["cs"])
            p.tt("pool", csd[:], cs[:], sgm[:], ALU.subtract, r=["cs", "sgm"], w=["csd"])
            cs3 = cs.rearrange("p (c t) -> p c t", t=C)
            p.tt("pool", dcs.rearrange("p (c t) -> p c t", t=C), cs3[:, :, C - 1:C].to_broadcast([128, NCH, C]), cs3, ALU.subtract,
                 r=["cs"], w=["dcs"])
            p.act(E1[:], cs[:], AF.Exp, r=["cs"], w=[kE1], scale=-DEC)
            p.act(E2[:], cs[:], AF.Exp, r=["cs"], w=["E2"], scale=DEC)
            p.act(E3[:], csd[:], AF.Exp, r=["csd"], w=["E3"], scale=-DEC)
            p.act(E4[:], dcs[:], AF.Exp, r=["dcs"], w=["E4"], scale=-DEC)
            c3 = lambda ap: ap.rearrange("p (c t) -> p c t", t=C)
            p.stt(AR[:, :, 0, :], c3(kk), -1.0, c3(E3), ALU.mult, ALU.mult, r=["kk", "E3"], w=[kAR])
            if full:
                p.tt("dve", AR[:, :, 1, :], c3(r_), c3(E1), ALU.mult, r=[("Pm", 0), kE1], w=[kAR])
            p.tt("pool", Kt[:], kmod[:], E2[:], ALU.mult, r=["kmod", "E2"], w=[kKt])
            p.tt("pool", Bt[:], beta[:], E2[:], ALU.mult, r=["beta", "E2"], w=[kBt])
            p.tt("dve", Khc[:], kmod[:], E4[:], ALU.mult, r=["kmod", "E4"], w=[kKhc])
            p.tt("pool", Bhc[:], beta[:], E4[:], ALU.mult, r=["beta", "E4"], w=[kBhc])
            p.cp("act", vB[:], v_[:], r=[("Pm", 2)], w=[kvB])
            for src, dst, ks, kd in ((vB, Vt, kvB, kVt), (Khc, Kh, kKhc, kKh), (Bhc, Bh, kBhc, kBh)):
                pt, pk = pp2.get()
                ptb = pt.bitcast(BF16)
                for ch in range(NCH):
                    for h in range(2):
                        hs = slice(64 * h, 64 * h + 64)
                        p.tr(ptb[hs, ch * C:(ch + 1) * C], src[hs, ch * C:(ch + 1) * C], c["identb"][hs, hs], r=[ks, "identb"], w=[pk],
                             tile_position=(64 * h, 64 * h))
                p.cp("act", dst.rearrange("p c t -> p (c t)"), ptb[:, 0:TB], r=[pk], w=[kd])
            for hb_ in range(2):
                p1, pk1 = pp2.get(); p2, pk2 = pp2.get(); p3, pk3 = pp2.get()
                for cc in range(HB):
                    ch = hb_ * HB + cc
                    for which in range(3):
                        for h in range(2):
                            hs = slice(64 * h, 64 * h + 64)
                            tp = (64 * h, 64 * h)
                            arh = AR[hs, ch, :, :].rearrange("p a t -> p (a t)")
                            if which == 0:
                                p.mm(p1[hs, cc * 128:(cc + 1) * 128], Kt[hs, ch * C:(ch + 1) * C], arh, True, True, r=[kKt, kAR], w=[pk1], tile_position=tp)
                            elif which == 1:
                                p.mm(p2[hs, cc * 128:(cc + 1) * 128], Bt[hs, ch * C:(ch + 1) * C], arh, True, True, r=[kBt, kAR], w=[pk2], tile_position=tp)
                            else:
                                p.mm(p3[hs, cc * C:(cc + 1) * C], AR[hs, ch, 0, :], Bt[hs, ch * C:(ch + 1) * C], True, True, r=[kAR, kBt], w=[pk3], tile_position=tp)
                chs = slice(hb_ * HB, (hb_ + 1) * HB)
                p.tt("dve", A1[:, chs, :], p1.rearrange("p (c t) -> p c t", c=HB), mU[:], ALU.mult, r=[pk1, "mU"], w=[kA1])
                p.tt("dve", A2m[:, chs, :], p2.rearrange("p (c t) -> p c t", c=HB), mU[:], ALU.mult, r=[pk2, "mU"], w=[kA2m])
                p.tt("dve", Mm[hb_][0][:], p3[:, 0:HB * C].rearrange("p (c t) -> p c t", c=HB), mL[:], ALU.mult, r=[pk3, "mL"], w=[("Mm", hb_, 0)])
                p.cp("pool", NS[hb_][0][:, :, 0, :], A2m[:, chs, 0:C], r=[kA2m], w=[("NS", hb_, 0)])
            for lvl in range(0, 6):
                o_ = lvl % 2; n_ = (lvl + 1) % 2
                first = lvl == 0; last = lvl == 5
                stage = []
                pBb, pkBb = (None, None) if last else pp2.get()
                for hb_ in range(2):
                    pA, pkA = pp2.get()
                    pB, pkB = (None, None) if last else (pBb[:, hb_ * HB * C:(hb_ + 1) * HB * C], pkBb)
                    for cc in range(HB):
                        for h in range(2):
                            hs = slice(64 * h, 64 * h + 64); tp = (64 * h, 64 * h)
                            if first:
                                rhs = NS[hb_][o_][hs, cc, 0, :]; wd = C
                            elif last:
                                rhs = NS[hb_][o_][hs, cc, 1, :]; wd = C
                            else:
                                rhs = NS[hb_][o_][hs, cc, :, :].rearrange("p a t -> p (a t)"); wd = 2 * C
                            p.mm(pA[hs, cc * 128:cc * 128 + wd], Mm[hb_][o_][hs, cc, :], rhs, True, True,
                                 r=[("Mm", hb_, o_), ("NS", hb_, o_)], w=[pkA], tile_position=tp)
                        if not last:
                            for h in range(2):
                                hs = slice(64 * h, 64 * h + 64); tp = (64 * h, 64 * h)
                                p.mm(pB[hs, cc * C:(cc + 1) * C], NS[hb_][o_][hs, cc, 0, :], Mm[hb_][o_][hs, cc, :], True, True,
                                     r=[("Mm", hb_, o_), ("NS", hb_, o_)], w=[pkB], tile_position=tp)
                    stage.append((pA, pkA, pB, pkB))
                for hb_ in range(2):
                    pA, pkA, pB, pkB = stage[hb_]
                    chs = slice(hb_ * HB, (hb_ + 1) * HB)
                    pA3 = pA.rearrange("p (c a t) -> p c a t", c=HB, a=2)
                    if last:
                        p.tt("dve", TmT[:, chs, :], NS[hb_][o_][:, :, 1, :], pA3[:, :, 0, :], ALU.add, r=[pkA, ("NS", hb_, o_)], w=[kTmT])
                        continue
                    p.cp("act", Mm[hb_][n_][:], pB[:, 0:HB * C].rearrange("p (c t) -> p c t", c=HB), r=[pkB], w=[("Mm", hb_, n_)])
                    p.cp("dve", NS[hb_][n_][:, :, 0, :], pA3[:, :, 0, :], r=[pkA], w=[("NS", hb_, n_)])
                    if first:
                        p.tt("pool", NS[hb_][n_][:, :, 1, :], NS[hb_][o_][:, :, 0, :], idn[:], ALU.add, r=[("NS", hb_, o_), "idn"], w=[("NS", hb_, n_)])
                    else:
                        p.tt("dve", NS[hb_][n_][:, :, 1, :], NS[hb_][o_][:, :, 1, :], pA3[:, :, 1, :], ALU.add,
                             r=[pkA, ("NS", hb_, o_)], w=[("NS", hb_, n_)])
            pY, pkY = psD, "psD"
            for ch in range(NCH):
                hi = hidx % 2; ho = 1 - hi; hidx += 1
                pX, pkX = s3[:, 0:C], "s3"
                for h in range(2):
                    hs = slice(64 * h, 64 * h + 64); tp = (64 * h, 64 * h)
                    p.mm(pX[hs, 0:C], A1[hs, ch, 0:C], Vt[hs, ch, :], True, False, r=[kA1, kVt], w=[pkX], tile_position=tp)
                for h in range(2):
                    hs = slice(64 * h, 64 * h + 64); tp = (64 * h, 64 * h)
                    p.mm(pX[hs, 0:C], AR[hs, ch, 0, :], Hb[hi][hs, :], False, True, r=[kAR, ("Hb", hi)], w=[pkX], tile_position=tp)
                p.cp("dve", Xb[:], pX[:, 0:C], r=[pkX], w=["Xb"])
                pU, pkU = s3[:, 0:C], "s3"
                for h in range(2):
                    hs = slice(64 * h, 64 * h + 64); tp = (64 * h, 64 * h)
                    p.mm(pU[hs, 0:C], TmT[hs, ch, :], Xb[hs, :], True, True, r=[kTmT, "Xb"], w=[pkU], tile_position=tp)
                p.cp("dve", Ub[:], pU[:, 0:C], r=[pkU], w=["Ub"])
                pH, pkH = (s3[:, 0:C], "s3") if full else (psD[:, 0:C], "psD")
                for h in range(2):
                    hs = slice(64 * h, 64 * h + 64); tp = (64 * h, 64 * h)
                    p.mm(pH[hs, 0:C], Kh[hs, ch, :], Vt[hs, ch, :], True, False, r=[kKh, kVt], w=[pkH], tile_position=tp)
                for h in range(2):
                    hs = slice(64 * h, 64 * h + 64); tp = (64 * h, 64 * h)
                    p.mm(pH[hs, 0:C], Bh[hs, ch, :], Ub[hs, :], False, True, r=[kBh, "Ub"], w=[pkH], tile_position=tp)
                gC = E1[:, ch * C + C - 1:ch * C + C]
                p.stt(Hb[ho][:], Hf[:], gC, pH[:, 0:C], ALU.mult, ALU.add, r=["Hf", kE1, pkH], w=[("Hb", ho)])
                if full:
                    for which in range(3):
                        for h in range(2):
                            hs = slice(64 * h, 64 * h + 64); tp = (64 * h, 64 * h)
                            yo_ = pY[hs, ch * C:(ch + 1) * C]
                            if which == 0:
                                p.mm(yo_, Hb[hi][hs, :], AR[hs, ch, 1, :], True, False, r=[("Hb", hi), kAR], w=[pkY], tile_position=tp)
                            elif which == 1:
                                p.mm(yo_, Vt[hs, ch, :], A1[hs, ch, C:2 * C], False, False, r=[kVt, kA1], w=[pkY], tile_position=tp)
                            else:
                                p.mm(yo_, Ub[hs, :], A2m[hs, ch, C:2 * C], False, True, r=["Ub", kA2m], w=[pkY], tile_position=tp)
                p.stt(Hf[:], Hf[:], gC, pH[:, 0:C], ALU.mult, ALU.add, r=["Hf", kE1, pkH], w=["Hf"])
            if not full:
                continue
            if n == first_full:
                ycol0, ysrc0, ylen = 0, TB - 128, 128
            else:
                ycol0, ysrc0, ylen = 128 + (n - first_full - 1) * TB, 0, TB
            p.cp("act", Yf[:], pY[:], r=[pkY], w=[kYf])
            pm_, pkm = pp1.get()
            p.mm(pm_[:], bones[:], Yf[:], True, True, r=["bones", kYf], w=[pkm])
            p.stt(yc[:], pm_[:], -1.0 / C, Yf[:], ALU.mult, ALU.add, r=[pkm, kYf], w=[kyc])
            p.tt("pool", ycq[:], yc[:], yc[:], ALU.mult, r=[kyc], w=[kycq])
            pv_, pkv = pp1.get()
            p.mm(pv_[:], bones[:], ycq[:], True, True, r=["bones", kycq], w=[pkv])
            p.ts("dve", rs[:], pv_[:], 1.0 / C, ALU.mult, GN_EPS, ALU.add, r=[pkv], w=[krs])
            p.act(rs[:], rs[:], AF.Sqrt, r=[krs], w=[krs])
            p.op("dve", lambda e: e.reciprocal(out=rs[:], in_=rs[:]), r=[krs], w=[krs])
            p.tt("dve", yc[:], yc[:], rs[:], ALU.mult, r=[kyc, krs], w=[kyc])
            p.ts("dve", yc[:], yc[:], LNW, ALU.mult, LNB, ALU.add, r=[kyc, "CV"], w=[kyc])
            p.tt("pool", yc[:], yc[:], bon[:], ALU.add, r=[kyc, kbon], w=[kyc])
            p.tt("dve", yo[:], yc[:], gv[:], ALU.mult, r=[kyc, kgv], w=["yo"])
            p.dma("sp", yscr[hp * 128:(hp + 1) * 128, ycol0:ycol0 + ylen], yo[:, ysrc0:ysrc0 + ylen], s_st, r=["yo"], w=[("yscr", hp, n, 0)])
            pO, pkO = psO, "psO"
            tiles = [3] if n == first_full else [0, 1, 2, 3]
            for i in tiles:
                for h in range(2):
                    hs = slice(64 * h, 64 * h + 64)
                    pS1, pkS1 = pp2.get(); pS2, pkS2 = pp2.get()
                    lo = i * 128
                    p.mm(pS1[:], qT[hs, i * 128:(i + 1) * 128], Kring[hs, lo:lo + 512], True, True, r=["qT", "Kring"], w=[pkS1])
                    p.stt(S[:, 0:512], pS1[:], 0.125, ATB[:, h, 0:512], ALU.mult, ALU.add, r=[pkS1, "ATB"], w=["S_att"])
                    p.mm(pS2[:, 0:128], qT[hs, i * 128:(i + 1) * 128], Kring[hs, lo + 512:lo + 640], True, True, r=["qT", "Kring"], w=[pkS2])
                    p.stt(S[:, 512:640], pS2[:, 0:128], 0.125, ATB[:, h, 512:640], ALU.mult, ALU.add, r=[pkS2, "ATB"], w=["S_att"])
                    if n == first_full + 1:
                        p.tt("pool", S[:], S[:], AMK[:, i, :], ALU.add, r=["S_att", "AMK"], w=["S_att"])
                    p.op("dve", lambda e: e.tensor_reduce(out=ast[:, 0:1], in_=S[:], axis=AX.X, op=ALU.max), r=["S_att"], w=["ast"])
                    p.ts("dve", ast[:, 1:2], ast[:, 0:1], -1.0, ALU.mult, r=["ast"], w=["ast"])
                    p.act(Pe[:], S[:], AF.Exp, r=["S_att", "ast"], w=["Pe", "ast"], bias=ast[:, 1:2], accum_out=ast[:, 2:3])
                    p.op("dve", lambda e: e.reciprocal(out=ast[:, 3:4], in_=ast[:, 2:3]), r=["ast"], w=["ast"])
                    p.ts("dve", Pn[:], Pe[:], ast[:, 3:4], ALU.mult, r=["Pe", "ast"], w=["Pn"])
                    ptr, pktr = pp2.get()
                    ptrb = ptr.bitcast(BF16)
                    for kt in range(5):
                        p.tr(ptrb[:, kt * 128:(kt + 1) * 128], Pn[:, kt * 128:(kt + 1) * 128], c["identb"][:], r=["Pn", "identb"], w=[pktr])
                    p.cp("act", PT[:], ptrb[:, 0:640].rearrange("p (k q) -> p k q", k=5), r=[pktr], w=["PT"])
                    for kt in range(5):
                        p.mm(pO[hs, i * 128:(i + 1) * 128], Vring[:, i + kt, 64 * h:64 * h + 64], PT[:, kt, :], kt == 0, kt == 4,
                             r=["Vring", "PT"], w=[pkO], tile_position=(0, 64 * h))
            p.cp("act", ao[:, ysrc0:ysrc0 + ylen], pO[:, ysrc0:ysrc0 + ylen], r=[pkO], w=["ao"])
            p.dma("sp", yscr[512 + hp * 128:512 + (hp + 1) * 128, ycol0:ycol0 + ylen], ao[:, ysrc0:ysrc0 + ylen], s_st, r=["ao"], w=[("yscr", hp, n, 1)])


def build_fused(nc):
    p = Prog(nc)
    dr = lambda name, shape, dt=F32, kind="ExternalInput": nc.dram_tensor(name, list(shape), dt, kind=kind).ap()
    A = dict(xb=dr("xb", [NBLK * TB, D]), wsel=dr("wsel", [4, D, 1024]), g_mix=dr("g_mix", [D]), cv=dr("cv", [4, 128, 16]),
             w2=dr("w2", [4, 64, 128]), a2=dr("a2", [4, 64, 128]), g2=dr("g2", [4, 128, 128]), attb=dr("attb", [4, 2, 128, 640]),
             amask=dr("amask", [128, 4, 640]))
    T = {name: dr(name, shape) for name, shape in B_INPUTS}
    T["out"] = dr("out", [2048, D], kind="ExternalOutput")
    yscr = nc.dram_tensor("yscr", [D, 2176], BF16).ap()
    A["hTs"] = nc.dram_tensor("hTs", [NBLK, 128, NK * TB], BF16).ap()
    T["yT"] = yscr
    T["xin"] = A["xb"][NBLK * TB - 2176:NBLK * TB, :]
    c = make_consts(p)
    banks = [p.ps(f"psb{i}", [128, 512], F32) for i in range(6)]
    psD = p.ps("psD", [128, 512], F32)
    psO = p.ps("psO", [128, 512], F32)

    def mkpool(idx):
        q = PsumPool.__new__(PsumPool)
        q.t = [banks[i] for i in idx]; q.keys = [f"psb{i}" for i in idx]; q.i = 0; q.n = len(idx)
        return q
    pools = (mkpool([0, 1]), mkpool([2, 3, 4]), banks[5])
    pp = mkpool([0, 1, 2, 3, 4, 5])
    with Phase(p) as ph:
        emit_A(p, ph, c, pools, psD, psO, A, yscr)
    pp.t.extend([psD, psO]); pp.keys.extend(["psD", "psO"]); pp.n = 8
    s_st = emit_B(p, c, pp, T, 17)
    p.finalize(final_sems=[s_st])
    return p


RW = 512
RWKV_COLS = 1792


def att_bias_tile(rel_bias_h):
    q = np.arange(128)[:, None]; kc = np.arange(640)[None, :]
    qc, qi = q // 64, q % 64
    kcb, ki = kc // 64, kc % 64
    inband = (kcb >= qc) & (kcb <= qc + 8)
    kb = (kcb - qc) * 64 + ki
    rel = qi + 512 - kb
    idx = np.clip(rel, -128, 128) + 128
    out = np.where(inband, rel_bias_h[np.clip(idx, 0, 256)], np.float32(-30000.0)).astype(np.float32)
    return out


def prep_A(inputs, b, hp, T=8192):
    f = lambda k: np.asarray(inputs[k], np.float32)
    w_in = f("l0_w_in")
    hs = slice(128 * hp, 128 * hp + 128)
    cols = np.concatenate([np.arange(0, 512)[hs], np.arange(512, 1024)[hs], np.arange(1024, 1536)[hs],
                           np.arange(1536, 1664), np.arange(1664, 1792),
                           np.arange(1792, 2304)[hs], np.arange(2304, 2816)[hs], np.arange(2816, 3328)[hs]])
    mu = f("l0_shift_mu")
    cv = np.zeros((128, 16), np.float32)
    for ct in range(5):
        cv[:, ct] = mu[cols[ct * 128:(ct + 1) * 128]]
    for i, k in enumerate(["l0_w0", "l0_a0", "l0_k_k", "l0_k_a", "l0_r_k", "l0_lnx_w", "l0_lnx_b"]):
        cv[:, 5 + i] = f(k)[hs]
    rb = f("l0_rel_bias")
    return dict(
        xb=np.ascontiguousarray(f("x")[b, :T]), wsel=np.ascontiguousarray(w_in[:, cols]), g_mix=f("l0_norm_mix"), cv=cv,
        w2=np.ascontiguousarray(f("l0_w2")[:, hs]), a2=np.ascontiguousarray(f("l0_a2")[:, hs]), g2=np.ascontiguousarray(f("l0_g2")[:, hs]),
        attb=np.stack([att_bias_tile(rb[2 * hp]), att_bias_tile(rb[2 * hp + 1])]))


def _prep_fused(inputs, c, shared):
    f = lambda k: np.asarray(inputs[k], np.float32)
    b, q = c // 4, c % 4
    x = f("x")
    n_real = (q + 1) * 2048
    xb = np.zeros((8192, 1024), np.float32)
    xb[8192 - n_real:] = x[b, :n_real]
    amask = np.zeros((128, 4, 640), np.float32)
    hm = np.ones((128, 1), np.float32)
    if q == 0:
        hm[:] = 0.0
        for i in range(4):
            amask[:, i, :512 - 128 * i] = -30000.0
    d = dict(shared)
    d.update(xb=xb, amask=amask, hmask=hm)
    return d


def _shared_inputs(inputs):
    f = lambda k: np.asarray(inputs[k], np.float32)
    per = [prep_A(inputs, 0, hp, 8) for hp in range(4)]
    sh = dict(wsel=np.stack([d["wsel"] for d in per]), cv=np.stack([d["cv"] for d in per]),
              w2=np.stack([d["w2"] for d in per]), a2=np.stack([d["a2"] for d in per]), g2=np.stack([d["g2"] for d in per]),
              attb=np.stack([d["attb"] for d in per]), g_mix=f("l0_norm_mix"),
              w_out=f("l0_w_out"), g_ffn0=f("l0_norm_ffn"), up0=f("l0_ffn_up"), dn0=f("l0_ffn_down"),
              g_mix1=f("l1_norm_mix"), pw1=f("l1_pw1"), pw1_b=f("l1_pw1_b"), dw=f("l1_dw"), dw_b=f("l1_dw_b"),
              ln_w=f("l1_ln_w"), ln_b=f("l1_ln_b"), pw2=f("l1_pw2"), pw2_b=f("l1_pw2_b"),
              g_ffn1=f("l1_norm_ffn"), up1=f("l1_ffn_up"), dn1=f("l1_ffn_down"), g_fin=f("final_norm"))
    return sh


def kernel(**inputs):
    nc = bass.Bass("TRN2", target_bir_lowering=False)
    build_fused(nc)
    shared = _shared_inputs(inputs)
    maps = [_prep_fused(inputs, c, shared) for c in range(8)]
    res = run_bass_kernel_spmd(nc, maps, core_ids=list(range(8)))
    out = np.concatenate([np.asarray(res.results[c]["out"]) for c in range(8)], axis=0)
    return out.reshape(2, 8192, 1024).astype(np.float32)
```

```python
import contextlib
import numpy as np
import concourse.bass as bass
import concourse.mybir as mybir
from concourse.bass_utils import run_bass_kernel_spmd

F32 = mybir.dt.float32
BF16 = mybir.dt.bfloat16
AF = mybir.ActivationFunctionType
ALU = mybir.AluOpType
AX = mybir.AxisListType

SCHED = True
SYNC_SAME = True


class Sem:
    def __init__(self, nc, name):
        self.h = nc.alloc_semaphore(name)
        self.count = 0
        self.last = {}


class Op:
    __slots__ = ("eng", "fn", "deps", "dsem", "dval", "signal", "sigval", "dmadeps", "epoch", "alldeps", "cost", "idx", "barrier",
                 "chain", "nbytes", "junk")

    def __init__(self, eng, fn):
        self.eng = eng
        self.fn = fn
        self.deps = []
        self.dmadeps = []
        self.dsem = None
        self.dval = 0
        self.signal = False
        self.sigval = 0
        self.epoch = 0
        self.alldeps = []
        self.cost = 100.0
        self.idx = 0
        self.barrier = False
        self.chain = None
        self.nbytes = 0
        self.junk = 0


class Prog:
    ENG = ("pe", "act", "dve", "pool", "sp")

    def __init__(self, nc):
        self.nc = nc
        self.e = {"pe": nc.tensor, "act": nc.scalar, "dve": nc.vector, "pool": nc.gpsimd, "sp": nc.sync}
        self.ops = []
        self.lastw = {}
        self.readers = {}
        self.esems = [{k: Sem(nc, "sem0_" + k) for k in self.ENG}]
        self.epoch = 0
        self.all_sems = []
        self.last_op = {}
        self.last_dma = {}
        self.junk_fn = None
        self.junk_cost = 110.0
        self.junk_frac = 0.7
        self.junk_cap = 10

    def sb(self, name, shape, dt=F32):
        return self.nc.alloc_sbuf_tensor(name, list(shape), dt).ap()

    def ps(self, name, shape, dt=F32):
        return self.nc.alloc_psum_tensor(name, list(shape), dt).ap()

    def sem(self, name):
        s = Sem(self.nc, name)
        self.all_sems.append(s)
        return s

    def barrier(self):
        lasts = dict(self.last_op)
        news = []
        for eng in self.ENG:
            o = Op(eng, lambda e: e.nop())
            for d in lasts.values():
                if d.dsem is None and not (d.eng == eng and eng == "pe"):
                    d.signal = True
                    o.deps.append(d)
            for s in self.all_sems:
                if s.count:
                    o.dmadeps.append((s, s.count))
            o.epoch = self.epoch
            o.barrier = True
            news.append(o)
        for o in news:
            self.ops.append(o)
        self.epoch += 1
        self.esems.append({k: Sem(self.nc, f"sem{self.epoch}_" + k) for k in self.ENG})
        self.last_op = {}

    def _dep(self, op, d):
        if d is None or d is op:
            return
        op.alldeps.append(d)
        if d.dsem is not None:
            op.dmadeps.append((d.dsem, d.dsem.count))
            op.alldeps.extend(d.dsem.last.values())
            return
        if d.eng == op.eng and op.dsem is None:
            if d.eng == "pe" or not SYNC_SAME:
                return
        if d.epoch < self.epoch:
            return
        d.signal = True
        op.deps.append(d)

    def op(self, eng, fn, r=(), w=(), dma=None, cost=None, nbytes=0):
        o = Op(eng, fn)
        o.epoch = self.epoch
        o.idx = len(self.ops)
        if cost is not None:
            o.cost = cost
        o.nbytes = nbytes
        if dma is not None:
            o.dsem = dma
        for k in r:
            self._dep(o, self.lastw.get(k))
        for k in w:
            self._dep(o, self.lastw.get(k))
            for rd in self.readers.get(k, ()):
                self._dep(o, rd)
        if dma is not None:
            dma.count += 16
            o.dval = dma.count
            o.chain = self.last_dma.get(eng)
            self.last_dma[eng] = o
            dma.last[eng] = o
        for k in r:
            self.readers.setdefault(k, []).append(o)
        for k in w:
            self.lastw[k] = o
            self.readers[k] = []
        self.ops.append(o)
        if dma is None:
            self.last_op[eng] = o
        return o

    @staticmethod
    def _free(ap):
        n = 1
        for d in list(ap.shape)[1:]:
            n *= int(d)
        return n

    def _ecost(self, eng, ap):
        f = self._free(ap)
        if eng == "pool":
            return 120.0 + 2.1 * f
        return 70.0 + 1.05 * f

    def dma(self, eng, out, in_, sem, r=(), w=(), **kw):
        nb = self._free(out) * int(out.shape[0]) * 4
        return self.op(eng, lambda e: e.dma_start(out=out, in_=in_, **kw), r=r, w=w, dma=sem, cost=60.0, nbytes=nb)

    def mm(self, out, lhsT, rhs, start, stop, r=(), w=(), **kw):
        n = self._free(rhs)
        mul = 4.0 if rhs.dtype == F32 else 1.0
        return self.op("pe", lambda e: e.matmul(out, lhsT=lhsT, rhs=rhs, start=start, stop=stop, **kw), r=r, w=w,
                       cost=35.0 + 0.43 * mul * max(n, 64))

    def tr(self, out, in_, ident, r=(), w=(), **kw):
        return self.op("pe", lambda e: e.transpose(out, in_, ident, **kw), r=r, w=w, cost=80.0 + 0.43 * self._free(in_))

    def act(self, out, in_, func, r=(), w=(), eng="act", **kw):
        return self.op(eng, lambda e: e.activation(out=out, in_=in_, func=func, **kw), r=r, w=w, cost=self._ecost(eng, out) + 60)

    def tt(self, eng, out, in0, in1, op, r=(), w=()):
        return self.op(eng, lambda e: e.tensor_tensor(out=out, in0=in0, in1=in1, op=op), r=r, w=w, cost=self._ecost(eng, out))

    def ts(self, eng, out, in0, s1, op0, s2=None, op1=None, r=(), w=(), **kw):
        if op1 is None:
            return self.op(eng, lambda e: e.tensor_scalar(out=out, in0=in0, scalar1=s1, scalar2=None, op0=op0, **kw), r=r, w=w,
                           cost=self._ecost(eng, out))
        return self.op(eng, lambda e: e.tensor_scalar(out=out, in0=in0, scalar1=s1, scalar2=s2, op0=op0, op1=op1, **kw), r=r, w=w,
                       cost=self._ecost(eng, out))

    def stt(self, out, in0, scalar, in1, op0, op1, r=(), w=(), eng="dve"):
        return self.op(eng, lambda e: e.scalar_tensor_tensor(out=out, in0=in0, scalar=scalar, in1=in1, op0=op0, op1=op1), r=r, w=w,
                       cost=self._ecost(eng, out))

    def cp(self, eng, out, in_, r=(), w=()):
        if eng == "act":
            return self.op(eng, lambda e: e.copy(out=out, in_=in_), r=r, w=w, cost=self._ecost(eng, out) + 60)
        return self.op(eng, lambda e: e.tensor_copy(out=out, in_=in_), r=r, w=w, cost=self._ecost(eng, out))

    def memset(self, eng, ap, val, w=()):
        return self.op(eng, lambda e: e.memset(ap, val), w=w, cost=self._ecost(eng, ap))

    def _schedule(self, seg):
        import heapq
        pos = {id(o): i for i, o in enumerate(seg)}
        npred = [0] * len(seg)
        succ = [[] for _ in seg]
        for i, o in enumerate(seg):
            ps = set()
            for d in o.alldeps:
                j = pos.get(id(d))
                if j is not None and j != i:
                    ps.add(j)
            if o.chain is not None:
                j = pos.get(id(o.chain))
                if j is not None:
                    ps.add(j)
            npred[i] = len(ps)
            for j in ps:
                succ[j].append(i)
        ready_t = [0.0] * len(seg)
        done_t = [0.0] * len(seg)
        issue_t = [0.0] * len(seg)
        free_at = {k: 0.0 for k in self.ENG}
        heaps = {k: [] for k in self.ENG}
        for i, o in enumerate(seg):
            if npred[i] == 0:
                heapq.heappush(heaps[o.eng], (0.0, i))
        order = []
        nleft = len(seg)
        while nleft:
            best = None
            for k in self.ENG:
                h = heaps[k]
                if not h:
                    continue
                rt, i = h[0]
                st = max(rt, free_at[k])
                if best is None or st < best[0] or (st == best[0] and i < best[2]):
                    best = (st, k, i)
            st, k, _ = best
            h = heaps[k]
            cand = []
            while h and h[0][0] <= st and len(cand) < 16:
                cand.append(heapq.heappop(h))
            cand.sort(key=lambda x: x[1])
            rt, i = cand[0]
            for cnd in cand[1:]:
                heapq.heappush(h, cnd)
            o = seg[i]
            order.append(o)
            nleft -= 1
            if k == "pe" and self.junk_fn is not None:
                gap = st - free_at[k]
                if gap > self.junk_cost:
                    o.junk = min(self.junk_cap, int(gap * self.junk_frac / self.junk_cost))
            if o.dsem is not None:
                free_at[k] = st + o.cost
                issue_t[i] = st + o.cost
                done_t[i] = st + 2000.0 + o.nbytes / 150.0
            else:
                free_at[k] = st + o.cost
                issue_t[i] = st + o.cost
                done_t[i] = st + o.cost
            for j in succ[i]:
                oj = seg[j]
                if oj.chain is o and o not in oj.alldeps:
                    t = issue_t[i]
                else:
                    t = done_t[i] + (40.0 if (oj.eng == o.eng and oj.eng == "pe") else 180.0)
                if t > ready_t[j]:
                    ready_t[j] = t
                npred[j] -= 1
                if npred[j] == 0:
                    heapq.heappush(heaps[oj.eng], (ready_t[j], j))
        return order, max(free_at.values())

    def reorder(self):
        segs = {}
        for o in self.ops:
            segs.setdefault(o.epoch, []).append(o)
        new = []
        tot = 0.0
        for ep in sorted(segs):
            seg = [o for o in segs[ep] if not o.barrier]
            bar = [o for o in segs[ep] if o.barrier]
            order, t = self._schedule(seg)
            tot += t
            new.extend(order)
            new.extend(bar)
        self.ops = new
        return tot

    def finalize(self, final_sems=()):
        if SCHED:
            self.reorder()
        cnt = {}
        waited = {k: {} for k in self.ENG}
        for o in self.ops:
            if o.dsem is None and o.signal:
                kk_ = (o.epoch, o.eng)
                cnt[kk_] = cnt.get(kk_, 0) + 1
                o.sigval = cnt[kk_]
        for o in self.ops:
            e = self.e[o.eng]
            need = {}
            for d in o.deps:
                s = self.esems[d.epoch][d.eng]
                need[id(s)] = (s, max(need.get(id(s), (s, 0))[1], d.sigval))
            for (s, v) in o.dmadeps:
                need[id(s)] = (s, max(need.get(id(s), (s, 0))[1], v))
            wd = waited[o.eng]
            for _ in range(o.junk):
                self.junk_fn(e)
            for sid, (s, v) in need.items():
                if wd.get(sid, 0) >= v:
                    continue
                e.wait_ge(s.h, v)
                wd[sid] = v
            ins = o.fn(e)
            if o.dsem is not None:
                ins.then_inc(o.dsem.h, 16)
            elif o.signal:
                ins.then_inc(self.esems[o.epoch][o.eng].h, 1)
        for s in final_sems:
            self.nc.sync.wait_ge(s.h, s.count)
        return cnt


D = 1024
DFF = 4096
NK = 8
NORM_EPS = 1e-6
LN_EPS = 1e-5
CW = 31


class Phase:
    def __init__(self, p):
        self.p = p
        self.stack = contextlib.ExitStack()

    def __enter__(self):
        self.stack.__enter__()
        return self

    def sb(self, name, shape, dt=F32):
        return self.stack.enter_context(self.p.nc.sbuf_tensor(name, list(shape), dt)).ap()

    def __exit__(self, *a):
        self.p.barrier()
        return self.stack.__exit__(*a)


def make_consts(p):
    c = {}
    c["identf"] = p.sb("identf", [128, 128], F32)
    c["identb"] = p.sb("identb", [128, 128], BF16)
    c["onesf"] = p.sb("onesf", [128, 128], F32)
    p.memset("pool", c["identf"][:], 1.0, w=["identf"])
    p.op("pool", lambda e: e.affine_select(out=c["identf"][:], in_=c["identf"][:], pattern=[[-1, 128]],
                                            compare_op=ALU.is_equal, fill=0.0, base=0, channel_multiplier=1),
         r=["identf"], w=["identf"])
    p.cp("dve", c["identb"][:], c["identf"][:], r=["identf"], w=["identb"])
    p.memset("pool", c["onesf"][:], 1.0, w=["onesf"])
    return c


class PsumPool:
    def __init__(self, p, n=8):
        self.t = [p.ps(f"psb{i}", [128, 512], F32) for i in range(n)]
        self.keys = [f"psb{i}" for i in range(n)]
        self.i = 0
        self.n = n

    def get(self):
        i = self.i
        self.i = (self.i + 1) % self.n
        return self.t[i], self.keys[i]

    def sub(self, idx):
        q = PsumPool.__new__(PsumPool)
        q.t = [self.t[i] for i in idx]; q.keys = [self.keys[i] for i in idx]; q.i = 0; q.n = len(idx)
        return q


def rms_to_hT(p, c, pp, X, xkey, tile, hT, hkey, col0, scr, idx):
    sq = scr["sq"][idx % 2]; ss = scr["ss"][idx % 2]; hb = scr["hb"][idx % 2]
    kq = ("sq", idx % 2); ks = ("ss", idx % 2); kh = ("hb", idx % 2)
    xt = X[:, tile, :]
    p.act(sq[:], xt, AF.Square, r=[xkey], w=[kq, ks], accum_out=ss[:, 0:1])
    p.ts("dve", ss[:, 1:2], ss[:, 0:1], 1.0 / D, ALU.mult, NORM_EPS, ALU.add, r=[ks], w=[ks])
    p.act(ss[:, 2:3], ss[:, 1:2], AF.Sqrt, r=[ks], w=[ks])
    p.op("dve", lambda e: e.reciprocal(out=ss[:, 3:4], in_=ss[:, 2:3]), r=[ks], w=[ks])
    p.ts("dve", hb[:], xt, ss[:, 3:4], ALU.mult, r=[xkey, ks], w=[kh])
    pt, pk = pp.get()
    ptb = pt.bitcast(BF16)
    for dk in range(NK):
        p.tr(ptb[:, dk * 128:(dk + 1) * 128], hb[:, dk * 128:(dk + 1) * 128], c["identb"][:], r=[kh, "identb"], w=[pk])
    p.cp("act", hT[:, :, col0:col0 + 128], ptb.rearrange("p (k t) -> p k t", k=NK), r=[pk], w=[hkey])


B_INPUTS = [("w_out", [D, D]), ("g_ffn0", [D]), ("up0", [D, DFF]), ("dn0", [DFF, D]), ("g_mix1", [D]), ("pw1", [D, 2 * D]),
            ("pw1_b", [2 * D]), ("dw", [CW, D]), ("dw_b", [D]), ("ln_w", [D]), ("ln_b", [D]), ("pw2", [D, D]), ("pw2_b", [D]),
            ("g_ffn1", [D]), ("up1", [D, DFF]), ("dn1", [DFF, D]), ("g_fin", [D]), ("hmask", [128, 1])]


def build_B(nc, NT=17):
    NTOK = NT * 128
    p = Prog(nc)
    dr = lambda name, shape, dt=F32, kind="ExternalInput": nc.dram_tensor(name, list(shape), dt, kind=kind).ap()
    T = {name: dr(name, shape) for name, shape in B_INPUTS}
    T["xin"] = dr("xin", [NTOK, D])
    T["yT"] = dr("yT", [D, NTOK], BF16)
    T["out"] = dr("out", [NTOK - 128, D], kind="ExternalOutput")
    c = make_consts(p)
    pp = PsumPool(p)
    s_st = emit_B(p, c, pp, T, NT)
    p.finalize(final_sems=[s_st])
    return p


def emit_B(p, c, pp, T, NT=17):
    NTOK = NT * 128
    NMAIN = NTOK - 128
    xin = T["xin"]; yT = T["yT"]; hmask = T["hmask"]; w_out = T["w_out"]
    g_ffn0 = T["g_ffn0"]; up0 = T["up0"]; dn0 = T["dn0"]
    g_mix1 = T["g_mix1"]; pw1 = T["pw1"]; pw1_b = T["pw1_b"]
    dw = T["dw"]; dw_b = T["dw_b"]; ln_w = T["ln_w"]; ln_b = T["ln_b"]
    pw2 = T["pw2"]; pw2_b = T["pw2_b"]
    g_ffn1 = T["g_ffn1"]; up1 = T["up1"]; dn1 = T["dn1"]
    g_fin = T["g_fin"]
    out = T["out"]

    s_ld = p.sem("s_ld"); s_w = [p.sem("s_w0"), p.sem("s_w1"), p.sem("s_w2"), p.sem("s_w3")]
    s_st = p.sem("s_st")

    X = p.sb("X", [128, NT, D], F32)
    vecs = p.sb("vecs", [128, 8 * NK + 2 * NK], F32)

    def colvec(i, src, n=NK):
        ap = vecs[:, i:i + n]
        p.dma("sp", ap, src.rearrange("(k p) -> p k", p=128), s_ld, w=["vecs"], allow_slow_non_contiguous=True)
        return ap
    V = {}
    off = 0
    for name, src, n in [("g_ffn0", g_ffn0, NK), ("g_mix1", g_mix1, NK), ("pw1_b", pw1_b, 2 * NK), ("dw_b", dw_b, NK),
                         ("ln_w", ln_w, NK), ("ln_b", ln_b, NK), ("g_ffn1", g_ffn1, NK)]:
        V[name] = colvec(off, src, n); off += n
    rows = p.sb("rows", [128, 2, D], F32)
    p.dma("sp", rows[:, 0, :], pw2_b.partition_broadcast(128), s_ld, w=["rows"])
    p.dma("sp", rows[:, 1, :], g_fin.partition_broadcast(128), s_ld, w=["rows"])
    hm = p.sb("hm", [128, 1], F32)
    p.dma("sp", hm[:], hmask, s_ld, w=["hm"])
    xv = xin.rearrange("(n p) d -> p n d", p=128)
    for n in range(NT):
        p.dma("sp" if n % 2 == 0 else "act", X[:, n, :], xv[:, n, :], s_ld, w=[("X", n)])

    scr_store = {}

    def ffn(ph, hT, tiles, g_col, up, dn, tagp):
        SL = 512
        nsl = DFF // SL
        stU = [ph.sb(f"{tagp}stU{i}", [128, NK, SL], F32) for i in range(1)] * 2
        stD = [ph.sb(f"{tagp}stD{i}", [128, SL // 128, D], F32) for i in range(1)] * 2
        WU = [ph.sb(f"{tagp}WU{i}", [128, NK, SL], BF16) for i in range(2)]
        WD = [ph.sb(f"{tagp}WD{i}", [128, SL // 128, D], BF16) for i in range(2)]
        aT = [ph.sb(f"{tagp}aT{i}", [128, SL // 128, 512], BF16) for i in range(2)]
        rl = [ph.sb(f"{tagp}rl{i}", [128, 512], F32) for i in range(2)]
        upv = up.rearrange("(k p) f -> p k f", p=128)
        dnv = dn.rearrange("(k p) d -> p k d", p=128)
        groups = []
        i = 0
        while i < len(tiles):
            groups.append(tiles[i:i + 4]); i += 4
        ai = 0; ri = 0
        ppU = pp.sub([0, 1, 2]); ppD = pp.sub([3, 4, 5, 6, 7])
        def load(s):
            b = s % 2
            p.dma("sp", stU[b][:], upv[:, :, s * SL:(s + 1) * SL], s_w[0], w=[(tagp, "stU", 0)])
            p.dma("sp", stD[b][:], dnv[:, s * (SL // 128):(s + 1) * (SL // 128), :], s_w[2], w=[(tagp, "stD", 0)])

        def cast(s):
            b = s % 2
            for dk in range(NK):
                if dk % 2 == 0:
                    p.ts("dve", WU[b][:, dk, :], stU[b][:, dk, :], g_col[:, dk:dk + 1], ALU.mult,
                         r=[(tagp, "stU", 0), "vecs"], w=[(tagp, "WU", b)])
                else:
                    p.act(WU[b][:, dk, :], stU[b][:, dk, :], AF.Copy, r=[(tagp, "stU", 0), "vecs"], w=[(tagp, "WU", b)], scale=g_col[:, dk:dk + 1])
            for f_ in range(SL // 128):
                p.cp("dve" if f_ % 2 == 0 else "act", WD[b][:, f_, :], stD[b][:, f_, :], r=[(tagp, "stD", 0)], w=[(tagp, "WD", b)])
        load(0)
        cast(0)
        for s in range(nsl):
            b = s % 2
            if s + 1 < nsl:
                load(s + 1)
            for grp in groups:
                n = len(grp) * 128
                c0 = grp[0] * 128
                a = aT[ai % 2]; ka = (tagp, "aT", ai % 2); ai += 1
                for ft in range(SL // 128):
                    pt, pk = ppU.get()
                    for dk in range(NK):
                        p.mm(pt[:, 0:n], WU[b][:, dk, ft * 128:(ft + 1) * 128], hT[:, dk, c0:c0 + n], dk == 0, dk == NK - 1,
                             r=[(tagp, "WU", b), (tagp, "hT")], w=[pk])
                    r_ = rl[ri % 2]; kr = (tagp, "rl", ri % 2); ri += 1
                    p.act(r_[:, 0:n], pt[:, 0:n], AF.Relu, r=[pk], w=[kr])
                    p.act(a[:, ft, 0:n], r_[:, 0:n], AF.Square, r=[kr], w=[ka])
                for ti, t in enumerate(grp):
                    for half in range(2):
                        pt, pk = ppD.get()
                        for ft in range(SL // 128):
                            p.mm(pt[:], a[:, ft, ti * 128:(ti + 1) * 128], WD[b][:, ft, half * 512:(half + 1) * 512],
                                 ft == 0, ft == SL // 128 - 1, r=[ka, (tagp, "WD", b)], w=[pk])
                        xs = X[:, t, half * 512:(half + 1) * 512]
                        p.tt("dve", xs, xs, pt[:], ALU.add, r=[pk, ("X", t)], w=[("X", t)])
            if s + 1 < nsl:
                cast(s + 1)

    def norm_all(ph, tiles, hT, hkey, scr):
        for i, t in enumerate(tiles):
            rms_to_hT(p, c, pp, X, ("X", t), t, hT, hkey, t * 128, scr, i)

    def mk_scr(ph, tag):
        return {"sq": [ph.sb(f"{tag}sq{i}", [128, D], BF16) for i in range(2)],
                "ss": [ph.sb(f"{tag}ss{i}", [128, 4], F32) for i in range(2)],
                "hb": [ph.sb(f"{tag}hb{i}", [128, D], BF16) for i in range(2)]}

    with Phase(p) as ph:
        hT = ph.sb("hT0", [128, NK, NTOK], BF16)
        scr = mk_scr(ph, "p1")
        with Phase(p) as ph2:
            yv = yT.rearrange("(k p) t -> p k t", p=128)
            for k in range(NK):
                p.dma("sp" if k % 2 == 0 else "act", hT[:, k, :], yv[:, k, :], s_ld, w=[("p1", "hT")])
            wst = [ph2.sb(f"wst{i}", [128, NK, 512], F32) for i in range(2)]
            woB = ph2.sb("woB", [128, NK, D], BF16)
            wov = w_out.rearrange("(k p) d -> p k d", p=128)
            for h in range(2):
                p.dma("sp", wst[h][:], wov[:, :, h * 512:(h + 1) * 512], s_w[h], w=[("wst", h)])
                for dk in range(NK):
                    p.cp("act" if dk % 2 else "dve", woB[:, dk, h * 512:(h + 1) * 512], wst[h][:, dk, :], r=[("wst", h)], w=["woB"])
            for t in range(NT):
                for half in range(2):
                    pt, pk = pp.get()
                    for k in range(NK):
                        p.mm(pt[:], hT[:, k, t * 128:(t + 1) * 128], woB[:, k, half * 512:(half + 1) * 512], k == 0, k == NK - 1,
                             r=[("p1", "hT"), "woB"], w=[pk])
                    xs = X[:, t, half * 512:(half + 1) * 512]
                    p.tt("dve", xs, xs, pt[:], ALU.add, r=[pk, ("X", t)], w=[("X", t)])
        norm_all(ph, list(range(NT)), hT, ("p1", "hT"), scr)
        with Phase(p) as ph2:
            ffn(ph2, hT, list(range(NT)), V["g_ffn0"], up0, dn0, "p1")

    with Phase(p) as ph:
        scr = mk_scr(ph, "p2")
        pw1B = ph.sb("pw1B", [128, NK, 2 * D], BF16)
        pw2B = ph.sb("pw2B", [128, NK, D], BF16)
        with Phase(p) as ph2:
            wst = [ph2.sb(f"wst2{i}", [128, NK, 512], F32) for i in range(2)]
            pw1v = pw1.rearrange("(k p) c -> p k c", p=128)
            pw2v = pw2.rearrange("(k p) c -> p k c", p=128)
            for j in range(4):
                b = j % 2
                p.dma("sp", wst[b][:], pw1v[:, :, j * 512:(j + 1) * 512], s_w[b], w=[("wst2", b)])
                for dk in range(NK):
                    if dk % 2:
                        p.act(pw1B[:, dk, j * 512:(j + 1) * 512], wst[b][:, dk, :], AF.Copy, r=[("wst2", b), "vecs"], w=["pw1B"],
                              scale=V["g_mix1"][:, dk:dk + 1])
                    else:
                        p.ts("dve", pw1B[:, dk, j * 512:(j + 1) * 512], wst[b][:, dk, :], V["g_mix1"][:, dk:dk + 1], ALU.mult,
                             r=[("wst2", b), "vecs"], w=["pw1B"])
            for j in range(2):
                b = j % 2
                p.dma("sp", wst[b][:], pw2v[:, :, j * 512:(j + 1) * 512], s_w[b], w=[("wst2", b)])
                for dk in range(NK):
                    p.cp("act" if dk % 2 else "dve", pw2B[:, dk, j * 512:(j + 1) * 512], wst[b][:, dk, :], r=[("wst2", b)], w=["pw2B"])
        dwS = ph.sb("dwS", [CW, D], F32)
        dwT = ph.sb("dwT", [128, NK, 32], F32)
        p.dma("sp", dwS[:], dw, s_ld, w=["dwS"])
        for ct in range(NK):
            pt, pk = pp.get()
            p.tr(pt[:, 0:CW], dwS[:, ct * 128:(ct + 1) * 128], c["identf"][0:CW, 0:CW], r=["dwS", "identf"], w=[pk])
            p.cp("dve", dwT[:, ct, 0:CW], pt[:, 0:CW], r=[pk], w=["dwT"])
        UH = ph.sb("UH", [128, NK, CW - 1], F32)
        hTg = ph.sb("hTg", [128, NK, 512], BF16)
        uT = [ph.sb(f"uT{i}", [128, CW - 1 + 512], F32) for i in range(2)]
        sg = [ph.sb(f"sg{i}", [128, 512], F32) for i in range(2)]
        zT = ph.sb("zT", [128, NK, 512], F32)
        zq = [ph.sb(f"zq{i}", [128, 512], F32) for i in range(2)]
        st = ph.sb("lnst", [128, 4, 512], F32)
        zn = [ph.sb(f"zn{i}", [128, 512], F32) for i in range(2)]
        z2b = [ph.sb(f"z2b{i}", [128, 512], F32) for i in range(2)]
        tpb = [ph.sb(f"tpb{i}", [128, 512], F32) for i in range(2)]
        KD = 21
        sT = ph.sb("sT", [128, NK, 512], BF16)
        groups = [[0]] + [list(range(1 + 4 * g, 1 + 4 * g + 4)) for g in range((NT - 1) // 4)]
        ui = 0
        for gi, grp in enumerate(groups):
            n = len(grp) * 128
            for i, t in enumerate(grp):
                rms_to_hT(p, c, pp, X, ("X", t), t, hTg, "hTg", i * 128, scr, i)
            for ct in range(NK):
                pa, pka = pp.get()
                pb, pkb = pp.get()
                for dk in range(NK):
                    p.mm(pa[:, 0:n], pw1B[:, dk, ct * 128:(ct + 1) * 128], hTg[:, dk, 0:n], dk == 0, dk == NK - 1, r=["pw1B", "hTg"], w=[pka])
                for dk in range(NK):
                    p.mm(pb[:, 0:n], pw1B[:, dk, D + ct * 128:D + (ct + 1) * 128], hTg[:, dk, 0:n], dk == 0, dk == NK - 1, r=["pw1B", "hTg"], w=[pkb])
                u = uT[ui % 2]; ku = ("uT", ui % 2); s_ = sg[ui % 2]; ksg = ("sg", ui % 2); ui += 1
                p.act(s_[:, 0:n], pb[:, 0:n], AF.Sigmoid, r=[pkb, "vecs"], w=[ksg], bias=V["pw1_b"][:, NK + ct:NK + ct + 1])
                if gi > 0:
                    p.cp("pool", u[:, 0:CW - 1], UH[:, ct, :], r=[("UH", ct)], w=[ku])
                p.stt(u[:, CW - 1:CW - 1 + n], pa[:, 0:n], V["pw1_b"][:, ct:ct + 1], s_[:, 0:n], ALU.add, ALU.mult, r=[pka, ksg, "vecs"], w=[ku])
                if gi == 0:
                    p.ts("dve", UH[:, ct, :], u[:, CW - 1 + n - (CW - 1):CW - 1 + n], hm[:, 0:1], ALU.mult, r=[ku, "hm"], w=[("UH", ct)])
                    continue
                p.cp("pool", UH[:, ct, :], u[:, n:n + CW - 1], r=[ku], w=[("UH", ct)])
                z = zT[:, ct, 0:n]
                p.ts("dve", z, u[:, 0:n], dwT[:, ct, 0:1], ALU.mult, V["dw_b"][:, ct:ct + 1], ALU.add, r=[ku, "dwT", "vecs"], w=[("zT", ct)])
                for k in range(1, KD):
                    p.stt(z, u[:, k:k + n], dwT[:, ct, k:k + 1], z, ALU.mult, ALU.add, r=[ku, "dwT", ("zT", ct)], w=[("zT", ct)])
                z2 = z2b[ct % 2][:, 0:n]; kz2 = ("z2b", ct % 2)
                for k in range(KD, CW):
                    if k == KD:
                        p.act(z2, u[:, k:k + n], AF.Copy, r=[ku, "dwT"], w=[kz2], scale=dwT[:, ct, k:k + 1])
                    else:
                        tp_ = tpb[k % 2][:, 0:n]; ktp = ("tpb", k % 2)
                        p.act(tp_, u[:, k:k + n], AF.Copy, r=[ku, "dwT"], w=[ktp], scale=dwT[:, ct, k:k + 1])
                        p.tt("pool", z2, z2, tp_, ALU.add, r=[kz2, ktp], w=[kz2])
                p.tt("pool", z, z, z2, ALU.add, r=[("zT", ct), kz2], w=[("zT", ct)])
            if gi == 0:
                continue
            p1, pk1 = pp.get()
            p2, pk2 = pp.get()
            for ct in range(NK):
                p.mm(p1[:, 0:n], c["onesf"][:], zT[:, ct, 0:n], ct == 0, ct == NK - 1, r=["onesf", ("zT", ct)], w=[pk1])
            for ct in range(NK):
                q = zq[ct % 2]; kq = ("zq", ct % 2)
                p.act(q[:, 0:n], zT[:, ct, 0:n], AF.Square, r=[("zT", ct)], w=[kq])
                p.mm(p2[:, 0:n], c["onesf"][:], q[:, 0:n], ct == 0, ct == NK - 1, r=["onesf", kq], w=[pk2])
            mean = st[:, 0, 0:n]; var = st[:, 1, 0:n]; tmp = st[:, 2, 0:n]; rstd = st[:, 3, 0:n]
            p.ts("dve", mean, p1[:, 0:n], 1.0 / D, ALU.mult, r=[pk1], w=["lnst"])
            p.tt("dve", tmp, mean, mean, ALU.mult, r=["lnst"], w=["lnst"])
            p.stt(var, p2[:, 0:n], 1.0 / D, tmp, ALU.mult, ALU.subtract, r=[pk2, "lnst"], w=["lnst"])
            p.ts("dve", var, var, LN_EPS, ALU.add, r=["lnst"], w=["lnst"])
            p.act(tmp, var, AF.Sqrt, r=["lnst"], w=["lnst"])
            p.op("dve", lambda e, rstd=rstd, tmp=tmp: e.reciprocal(out=rstd, in_=tmp), r=["lnst"], w=["lnst"])
            for ct in range(NK):
                zz = zn[ct % 2]; kz = ("zn", ct % 2)
                p.tt("dve", zz[:, 0:n], zT[:, ct, 0:n], mean, ALU.subtract, r=[("zT", ct), "lnst"], w=[kz])
                p.tt("pool", zz[:, 0:n], zz[:, 0:n], rstd, ALU.mult, r=[kz, "lnst"], w=[kz])
                p.act(sT[:, ct, 0:n], zz[:, 0:n], AF.Silu, r=[kz, "vecs"], w=["sT"],
                      scale=V["ln_w"][:, ct:ct + 1], bias=V["ln_b"][:, ct:ct + 1])
            for ti, t in enumerate(grp):
                for half in range(2):
                    pt, pk = pp.get()
                    for ct in range(NK):
                        p.mm(pt[:], sT[:, ct, ti * 128:(ti + 1) * 128], pw2B[:, ct, half * 512:(half + 1) * 512], ct == 0, ct == NK - 1,
                             r=["sT", "pw2B"], w=[pk])
                    xs = X[:, t, half * 512:(half + 1) * 512]
                    p.tt("dve", xs, xs, pt[:], ALU.add, r=[pk, ("X", t)], w=[("X", t)])
                    p.tt("pool", xs, xs, rows[:, 0, half * 512:(half + 1) * 512], ALU.add, r=["rows", ("X", t)], w=[("X", t)])

    main = list(range(1, NT))
    with Phase(p) as ph:
        hT = ph.sb("hT1", [128, NK, NTOK], BF16)
        scr = mk_scr(ph, "p3")
        norm_all(ph, main, hT, ("p3", "hT"), scr)
        with Phase(p) as ph2:
            ffn(ph2, hT, main, V["g_ffn1"], up1, dn1, "p3")
    with Phase(p) as ph:
        sq = [ph.sb(f"fsq{i}", [128, D], F32) for i in range(2)]
        ss = [ph.sb(f"fss{i}", [128, 4], F32) for i in range(2)]
        ov = out.rearrange("(n p) d -> p n d", p=128)
        for i, t in enumerate(main):
            b = i % 2
            xt = X[:, t, :]
            p.act(sq[b][:], xt, AF.Square, r=[("X", t)], w=[("fsq", b), ("fss", b)], accum_out=ss[b][:, 0:1])
            p.ts("dve", ss[b][:, 1:2], ss[b][:, 0:1], 1.0 / D, ALU.mult, NORM_EPS, ALU.add, r=[("fss", b)], w=[("fss", b)])
            p.act(ss[b][:, 2:3], ss[b][:, 1:2], AF.Sqrt, r=[("fss", b)], w=[("fss", b)])
            p.op("dve", lambda e, b=b: e.reciprocal(out=ss[b][:, 3:4], in_=ss[b][:, 2:3]), r=[("fss", b)], w=[("fss", b)])
            p.stt(sq[b][:], xt, ss[b][:, 3:4], rows[:, 1, :], ALU.mult, ALU.mult, r=[("X", t), ("fss", b), "rows"], w=[("fsq", b)])
            p.dma("sp", ov[:, t - 1, :], sq[b][:], s_st, r=[("fsq", b)], w=[("out", t)])
    return s_st


C = 64
TB = 512
NCH = TB // C
DEC = 0.6065306597126334
GN_EPS = 64e-5


def build_A(nc, NB=16):
    T = NB * TB
    p = Prog(nc)
    dr = lambda name, shape, dt=F32, kind="ExternalInput": nc.dram_tensor(name, list(shape), dt, kind=kind).ap()
    xb = dr("xb", [T, D])
    wsel = dr("wsel", [D, 1024])
    g_mix = dr("g_mix", [D])
    cv = dr("cv", [128, 16])
    w2 = dr("w2", [64, 128]); a2 = dr("a2", [64, 128]); g2 = dr("g2", [128, 128])
    attb = dr("attb", [2, 128, 640])
    yT = dr("yT", [256, T], BF16, kind="ExternalOutput")

    s_ld = p.sem("s_ld"); s_x = [p.sem("s_x0"), p.sem("s_x1")]; s_w = [p.sem("s_w0"), p.sem("s_w1")]
    s_st = p.sem("s_st")
    c = make_consts(p)
    pp = PsumPool(p, 7)
    psD = p.ps("psD", [128, 512], F32)

    CV = p.sb("CV", [128, 16], F32)
    p.dma("sp", CV[:], cv, s_ld, w=["CV"])
    MU = lambda ct: CV[:, ct:ct + 1]
    W0, A0, KK, KA, RK, LNW, LNB = [CV[:, 5 + i:6 + i] for i in range(7)]
    OMKA = CV[:, 12:13]
    p.ts("dve", OMKA, KA, -1.0, ALU.mult, 1.0, ALU.add, r=["CV"], w=["CV"])
    gcol = p.sb("gcol", [128, NK], F32)
    p.dma("sp", gcol[:], g_mix.rearrange("(k p) -> p k", p=128), s_ld, w=["gcol"], allow_slow_non_contiguous=True)
    W2 = p.sb("W2", [128, 128], F32); A2 = p.sb("A2", [128, 128], F32); G2 = p.sb("G2", [128, 128], F32)
    p.memset("pool", W2[:], 0.0, w=["W2"])
    p.memset("pool", A2[:], 0.0, w=["A2"])
    p.dma("sp", W2[0:64, :], w2, s_ld, w=["W2"])
    p.dma("sp", A2[64:128, :], a2, s_ld, w=["A2"])
    p.dma("sp", G2[:], g2, s_ld, w=["G2"])
    ATB = p.sb("ATB", [128, 2, 640], F32)
    for h in range(2):
        p.dma("sp", ATB[:, h, :], attb[h], s_ld, w=["ATB"])
    bones = p.sb("bones", [128, 128], F32)
    p.memset("pool", bones[:], 0.0, w=["bones"])
    p.memset("pool", bones[0:64, 0:64], 1.0, w=["bones"])
    p.memset("pool", bones[64:128, 64:128], 1.0, w=["bones"])
    rmask = p.sb("rmask", [128, TB], F32)
    p.memset("pool", rmask[:], 1.0, w=["rmask"])
    p.memset("pool", rmask.rearrange("p (c t) -> p c t", t=C)[:, :, 0:1], 0.0, w=["rmask"])
    HB = 4
    mU = p.sb("mU", [128, HB, 128], F32); mL = p.sb("mL", [128, HB, 64], F32); idn = p.sb("idn", [128, HB, 64], F32)
    for t_, name in ((mU, "mU"), (mL, "mL"), (idn, "idn")):
        p.memset("pool", t_[:], 1.0, w=[name])
    for h in range(2):
        hs = slice(64 * h, 64 * h + 64)
        p.op("pool", lambda e, hs=hs: e.affine_select(out=mU[hs, :, 0:64], in_=mU[hs, :, 0:64], pattern=[[0, HB], [1, 64]],
                                                      compare_op=ALU.is_gt, fill=0.0, base=0, channel_multiplier=-1), r=["mU"], w=["mU"])
        p.op("pool", lambda e, hs=hs: e.affine_select(out=mU[hs, :, 64:128], in_=mU[hs, :, 64:128], pattern=[[0, HB], [1, 64]],
                                                      compare_op=ALU.is_ge, fill=0.0, base=0, channel_multiplier=-1), r=["mU"], w=["mU"])
        p.op("pool", lambda e, hs=hs: e.affine_select(out=mL[hs, :, :], in_=mL[hs, :, :], pattern=[[0, HB], [-1, 64]],
                                                      compare_op=ALU.is_gt, fill=0.0, base=0, channel_multiplier=1), r=["mL"], w=["mL"])
        p.op("pool", lambda e, hs=hs: e.affine_select(out=idn[hs, :, :], in_=idn[hs, :, :], pattern=[[0, HB], [-1, 64]],
                                                      compare_op=ALU.is_equal, fill=0.0, base=0, channel_multiplier=1), r=["idn"], w=["idn"])

    WB = p.sb("WB", [128, NK, 1024], BF16)
    with Phase(p) as ph:
        wst = [ph.sb(f"wstA{i}", [128, NK, 512], F32) for i in range(2)]
        wv = wsel.rearrange("(k p) c -> p k c", p=128)
        for hf in range(2):
            p.dma("sp", wst[hf][:], wv[:, :, hf * 512:(hf + 1) * 512], s_w[hf], w=[("wstA", hf)])
            for dk in range(NK):
                p.ts("pool" if dk % 2 else "dve", WB[:, dk, hf * 512:(hf + 1) * 512], wst[hf][:, dk, :], gcol[:, dk:dk + 1], ALU.mult,
                     r=[("wstA", hf), "gcol"], w=["WB"])

    def t5(name, dt=F32, n=TB):
        return p.sb(name, [128, n], dt)
    XT = [p.sb(f"XT{i}", [128, 4, D], F32) for i in range(2)]
    sq = p.sb("sqA", [128, D], BF16); ssb = p.sb("ssA", [128, 4], F32); hb = p.sb("hbA", [128, D], BF16)
    hT = p.sb("hTA", [128, NK, TB], BF16)
    PJ = [[p.sb(f"PJ{i}_{ct}", [128, 1 + TB], F32) for ct in range(5)] for i in range(2)]
    for i in range(2):
        for ct in range(5):
            p.memset("pool", PJ[i][ct][:, 0:1], 0.0, w=[("PJ", i, ct)])
    qT = t5("qT", BF16)
    Kring = p.sb("Kring", [128, 2 * TB], BF16)
    Vring = p.sb("Vring", [128, 8, 128], BF16)
    tmp = t5("tmpA"); Pm = [t5(f"Pm{ct}") for ct in range(5)]
    txw = p.sb("txw", [128, TB], F32); sxg = t5("sxg")
    sgm = t5("sgm"); av = t5("av"); gv = t5("gv")
    kk = t5("kk"); kk2 = t5("kk2"); rn = t5("rn"); kmod = t5("kmod"); beta = t5("beta"); bon = t5("bon")
    cs = t5("cs"); csd = t5("csd"); dcs = t5("dcs")
    E1 = t5("E1"); E2 = t5("E2"); E3 = t5("E3"); E4 = t5("E4")
    AR = p.sb("AR", [128, NCH, 2, C], BF16)
    Kt = t5("Kt", BF16); Bt = t5("Bt", BF16); Khc = t5("Khc", BF16); Bhc = t5("Bhc", BF16); vB = t5("vB", BF16)
    Vt = p.sb("Vt", [128, NCH, C], BF16); Kh = p.sb("Kh", [128, NCH, C], BF16); Bh = p.sb("Bh", [128, NCH, C], BF16)
    A1 = p.sb("A1", [128, NCH, 128], BF16); A2m = p.sb("A2m", [128, NCH, 128], BF16)
    Nm = [[p.sb(f"Nm{hb_}_{i}", [128, HB, C], BF16) for i in range(2)] for hb_ in range(2)]
    Mm = [[p.sb(f"Mm{hb_}_{i}", [128, HB, C], BF16) for i in range(2)] for hb_ in range(2)]
    Pq = [[p.sb(f"Pq{hb_}_{i}", [128, HB, C], BF16) for i in range(2)] for hb_ in range(2)]
    Qq = [[p.sb(f"Qq{hb_}_{i}", [128, HB, C], BF16) for i in range(2)] for hb_ in range(2)]
    TmT = p.sb("TmT", [128, NCH, C], BF16)
    Hf = p.sb("Hf", [128, C], F32); Hb = [p.sb(f"Hb{i}", [128, C], BF16) for i in range(2)]
    p.memset("pool", Hf[:], 0.0, w=["Hf"])
    p.memset("pool", Hb[0][:], 0.0, w=[("Hb", 0)])
    Xb = p.sb("Xb", [128, C], BF16); Ub = p.sb("Ub", [128, C], BF16)
    Yf = t5("Yf"); yc = t5("yc"); ycq = t5("ycq"); rs = t5("rsA"); yo = t5("yo", BF16)
    S = p.sb("S_att", [128, 640], F32); Pe = p.sb("Pe", [128, 640], F32); Pn = p.sb("Pn", [128, 640], BF16)
    PT = p.sb("PT", [128, 5, 128], BF16)
    ast = p.sb("ast", [128, 4], F32)
    ao = t5("ao", BF16)

    xv = xb.rearrange("(n p) d -> p n d", p=128)

    def load_x(n):
        b = n % 2
        for i in range(4):
            p.dma("sp", XT[b][:, i, :], xv[:, n * 4 + i, :], s_x[b], w=[("XT", b, i)])

    hidx = [0]
    load_x(0)
    for n in range(NB):
        b = n % 2
        if n + 1 < NB:
            load_x(n + 1)
        for i in range(4):
            xt = XT[b][:, i, :]
            p.act(sq[:], xt, AF.Square, r=[("XT", b, i)], w=["sqA", "ssA"], accum_out=ssb[:, 0:1])
            p.ts("dve", ssb[:, 1:2], ssb[:, 0:1], 1.0 / D, ALU.mult, NORM_EPS, ALU.add, r=["ssA"], w=["ssA"])
            p.act(ssb[:, 2:3], ssb[:, 1:2], AF.Sqrt, r=["ssA"], w=["ssA"])
            p.op("dve", lambda e: e.reciprocal(out=ssb[:, 3:4], in_=ssb[:, 2:3]), r=["ssA"], w=["ssA"])
            p.ts("dve", hb[:], xt, ssb[:, 3:4], ALU.mult, r=[("XT", b, i), "ssA"], w=["hbA"])
            pt, pk = pp.get()
            ptb = pt.bitcast(BF16)
            for dk in range(NK):
                p.tr(ptb[:, dk * 128:(dk + 1) * 128], hb[:, dk * 128:(dk + 1) * 128], c["identb"][:], r=["hbA", "identb"], w=[pk])
            p.cp("act", hT[:, :, i * 128:(i + 1) * 128], ptb.rearrange("p (k t) -> p k t", k=NK), r=[pk], w=["hTA"])
        if n > 0:
            p.cp("pool", Kring[:, 0:TB], Kring[:, TB:2 * TB], r=["Kring"], w=["Kring"])
            p.cp("pool", Vring[:, 0:4, :], Vring[:, 4:8, :], r=["Vring"], w=["Vring"])
        for ct in range(7):
            pt, pk = pp.get()
            for dk in range(NK):
                p.mm(pt[:], WB[:, dk, ct * 128:(ct + 1) * 128], hT[:, dk, :], dk == 0, dk == NK - 1, r=["WB", "hTA"], w=[pk])
            if ct < 5:
                p.cp("act" if ct % 2 else "dve", PJ[b][ct][:, 1:1 + TB], pt[:], r=[pk], w=[("PJ", b, ct)])
            elif ct == 5:
                p.cp("act", qT[:], pt[:], r=[pk], w=["qT"])
            else:
                p.cp("dve", Kring[:, TB:2 * TB], pt[:], r=[pk], w=["Kring"])
        pt, pk = pp.get()
        for i in range(4):
            for dk in range(NK):
                p.mm(pt[:, i * 128:(i + 1) * 128], hT[:, dk, i * 128:(i + 1) * 128], WB[:, dk, 7 * 128:8 * 128], dk == 0, dk == NK - 1,
                     r=["WB", "hTA"], w=[pk])
        p.cp("act", Vring[:, 4:8, :], pt.rearrange("p (i v) -> p i v", i=4), r=[pk], w=["Vring"])
        for ct in range(5):
            cur = PJ[b][ct][:, 1:1 + TB]; prv = PJ[b][ct][:, 0:TB]
            p.tt("pool", tmp[:], prv, cur, ALU.subtract, r=[("PJ", b, ct)], w=["tmpA"])
            p.stt(Pm[ct][:], tmp[:], MU(ct), cur, ALU.mult, ALU.add, r=["tmpA", ("PJ", b, ct), "CV"], w=[("Pm", ct)])
            p.cp("pool", PJ[1 - b][ct][:, 0:1], PJ[b][ct][:, TB:TB + 1], r=[("PJ", b, ct)], w=[("PJ", 1 - b, ct)])
        r_, k_, v_ = Pm[0], Pm[1], Pm[2]
        p.act(txw[:], Pm[3][:], AF.Tanh, r=[("Pm", 3)], w=["txw"])
        p.act(sxg[:], Pm[4][:], AF.Sigmoid, r=[("Pm", 4)], w=["sxg"])
        pw_, pkw = pp.get()
        p.mm(pw_[:], W2[:], txw[:], True, True, r=["W2", "txw"], w=[pkw])
        p.act(sgm[:], pw_[:], AF.Sigmoid, r=[pkw, "CV"], w=["sgm"], bias=W0)
        pa_, pka = pp.get()
        p.mm(pa_[:], A2[:], Pm[3][:], True, True, r=["A2", ("Pm", 3)], w=[pka])
        p.act(av[:], pa_[:], AF.Sigmoid, r=[pka, "CV"], w=["av"], bias=A0)
        pg_, pkg = pp.get()
        p.mm(pg_[:], G2[:], sxg[:], True, True, r=["G2", "sxg"], w=[pkg])
        p.cp("act", gv[:], pg_[:], r=[pkg], w=["gv"])
        p.ts("dve", kk[:], k_[:], KK, ALU.mult, r=[("Pm", 1), "CV"], w=["kk"])
        p.tt("pool", kk2[:], kk[:], kk[:], ALU.mult, r=["kk"], w=["kk2"])
        pn_, pkn = pp.get()
        p.mm(pn_[:], bones[:], kk2[:], True, True, r=["bones", "kk2"], w=[pkn])
        p.act(rn[:], pn_[:], AF.Sqrt, r=[pkn], w=["rn"])
        p.ts("dve", rn[:], rn[:], 1e-12, ALU.max, r=["rn"], w=["rn"])
        p.op("dve", lambda e: e.reciprocal(out=rn[:], in_=rn[:]), r=["rn"], w=["rn"])
        p.tt("dve", kk[:], kk[:], rn[:], ALU.mult, r=["kk", "rn"], w=["kk"])
        p.ts("dve", tmp[:], av[:], KA, ALU.mult, OMKA, ALU.add, r=["av", "CV"], w=["tmpA"])
        p.tt("dve", kmod[:], k_[:], tmp[:], ALU.mult, r=[("Pm", 1), "tmpA"], w=["kmod"])
        p.tt("pool", beta[:], kk[:], av[:], ALU.mult, r=["kk", "av"], w=["beta"])
        p.stt(tmp[:], r_[:], RK, kmod[:], ALU.mult, ALU.mult, r=[("Pm", 0), "kmod", "CV"], w=["tmpA"])
        pb_, pkb = pp.get()
        p.mm(pb_[:], bones[:], tmp[:], True, True, r=["bones", "tmpA"], w=[pkb])
        p.tt("dve", bon[:], v_[:], pb_[:], ALU.mult, r=[("Pm", 2), pkb], w=["bon"])
        p.op("dve", lambda e: e.tensor_tensor_scan(out=cs[:], data0=rmask[:], data1=sgm[:], initial=0.0, op0=ALU.mult, op1=ALU.add),
             r=["rmask", "sgm"], w=["cs"])
        p.tt("pool", csd[:], cs[:], sgm[:], ALU.subtract, r=["cs", "sgm"], w=["csd"])
        cs3 = cs.rearrange("p (c t) -> p c t", t=C)
        p.tt("pool", dcs.rearrange("p (c t) -> p c t", t=C), cs3[:, :, C - 1:C].to_broadcast([128, NCH, C]), cs3, ALU.subtract,
             r=["cs"], w=["dcs"])
        p.act(E1[:], cs[:], AF.Exp, r=["cs"], w=["E1"], scale=-DEC)
        p.act(E2[:], cs[:], AF.Exp, r=["cs"], w=["E2"], scale=DEC)
        p.act(E3[:], csd[:], AF.Exp, r=["csd"], w=["E3"], scale=-DEC)
        p.act(E4[:], dcs[:], AF.Exp, r=["dcs"], w=["E4"], scale=-DEC)
        c3 = lambda ap: ap.rearrange("p (c t) -> p c t", t=C)
        p.stt(AR[:, :, 0, :], c3(kk), -1.0, c3(E3), ALU.mult, ALU.mult, r=["kk", "E3"], w=["AR"])
        p.tt("dve", AR[:, :, 1, :], c3(r_), c3(E1), ALU.mult, r=[("Pm", 0), "E1"], w=["AR"])
        p.tt("pool", Kt[:], kmod[:], E2[:], ALU.mult, r=["kmod", "E2"], w=["Kt"])
        p.tt("pool", Bt[:], beta[:], E2[:], ALU.mult, r=["beta", "E2"], w=["Bt"])
        p.tt("dve", Khc[:], kmod[:], E4[:], ALU.mult, r=["kmod", "E4"], w=["Khc"])
        p.tt("pool", Bhc[:], beta[:], E4[:], ALU.mult, r=["beta", "E4"], w=["Bhc"])
        p.cp("act", vB[:], v_[:], r=[("Pm", 2)], w=["vB"])
        for src, dst, ks, kd in ((vB, Vt, "vB", "Vt"), (Khc, Kh, "Khc", "Kh"), (Bhc, Bh, "Bhc", "Bh")):
            pt, pk = pp.get()
            ptb = pt.bitcast(BF16)
            for ch in range(NCH):
                for h in range(2):
                    hs = slice(64 * h, 64 * h + 64)
                    p.tr(ptb[hs, ch * C:(ch + 1) * C], src[hs, ch * C:(ch + 1) * C], c["identb"][hs, hs], r=[ks, "identb"], w=[pk],
                         tile_position=(64 * h, 64 * h))
            p.cp("act", dst.rearrange("p c t -> p (c t)"), ptb[:, 0:TB], r=[pk], w=[kd])
        for hb_ in range(2):
            p1, pk1 = pp.get(); p2, pk2 = pp.get(); p3, pk3 = pp.get()
            for cc in range(HB):
                ch = hb_ * HB + cc
                for h in range(2):
                    hs = slice(64 * h, 64 * h + 64)
                    tp = (64 * h, 64 * h)
                    arh = AR[hs, ch, :, :].rearrange("p a t -> p (a t)")
                    p.mm(p1[hs, cc * 128:(cc + 1) * 128], Kt[hs, ch * C:(ch + 1) * C], arh, True, True, r=["Kt", "AR"], w=[pk1], tile_position=tp)
                    p.mm(p2[hs, cc * 128:(cc + 1) * 128], Bt[hs, ch * C:(ch + 1) * C], arh, True, True, r=["Bt", "AR"], w=[pk2], tile_position=tp)
                    p.mm(p3[hs, cc * C:(cc + 1) * C], AR[hs, ch, 0, :], Bt[hs, ch * C:(ch + 1) * C], True, True, r=["AR", "Bt"], w=[pk3], tile_position=tp)
            chs = slice(hb_ * HB, (hb_ + 1) * HB)
            p.tt("dve", A1[:, chs, :], p1.rearrange("p (c t) -> p c t", c=HB), mU[:], ALU.mult, r=[pk1, "mU"], w=["A1"])
            p.tt("dve", A2m[:, chs, :], p2.rearrange("p (c t) -> p c t", c=HB), mU[:], ALU.mult, r=[pk2, "mU"], w=["A2m"])
            p.tt("dve", Mm[hb_][0][:], p3[:, 0:HB * C].rearrange("p (c t) -> p c t", c=HB), mL[:], ALU.mult, r=[pk3, "mL"], w=[("Mm", hb_, 0)])
            p.cp("pool", Nm[hb_][0][:], A2m[:, chs, 0:C], r=["A2m"], w=[("Nm", hb_, 0)])
            p.tt("pool", Pq[hb_][0][:], Nm[hb_][0][:], idn[:], ALU.add, r=[("Nm", hb_, 0), "idn"], w=[("Pq", hb_, 0)])
            p.tt("pool", Qq[hb_][0][:], Mm[hb_][0][:], idn[:], ALU.add, r=[("Mm", hb_, 0), "idn"], w=[("Qq", hb_, 0)])
        for lvl in range(1, 6):
            o_ = (lvl - 1) % 2; n_ = lvl % 2
            last = lvl == 5
            stage1 = []
            for hb_ in range(2):
                pN, pkN = pp.get()
                pM, pkM = (None, None) if last else pp.get()
                for cc in range(HB):
                    for h in range(2):
                        hs = slice(64 * h, 64 * h + 64); tp = (64 * h, 64 * h)
                        p.mm(pN[hs, cc * C:(cc + 1) * C], Mm[hb_][o_][hs, cc, :], Nm[hb_][o_][hs, cc, :], True, True,
                             r=[("Mm", hb_, o_), ("Nm", hb_, o_)], w=[pkN], tile_position=tp)
                        if not last:
                            p.mm(pM[hs, cc * C:(cc + 1) * C], Nm[hb_][o_][hs, cc, :], Mm[hb_][o_][hs, cc, :], True, True,
                                 r=[("Mm", hb_, o_), ("Nm", hb_, o_)], w=[pkM], tile_position=tp)
                stage1.append((pN, pkN, pM, pkM))
            for hb_ in range(2):
                pN, pkN, pM, pkM = stage1[hb_]
                p.cp("dve", Nm[hb_][n_][:], pN[:, 0:HB * C].rearrange("p (c t) -> p c t", c=HB), r=[pkN], w=[("Nm", hb_, n_)])
                if not last:
                    p.cp("act", Mm[hb_][n_][:], pM[:, 0:HB * C].rearrange("p (c t) -> p c t", c=HB), r=[pkM], w=[("Mm", hb_, n_)])
            stage2 = []
            for hb_ in range(2):
                pP, pkP = pp.get()
                pQ, pkQ = (None, None) if last else pp.get()
                for cc in range(HB):
                    for h in range(2):
                        hs = slice(64 * h, 64 * h + 64); tp = (64 * h, 64 * h)
                        p.mm(pP[hs, cc * C:(cc + 1) * C], Qq[hb_][o_][hs, cc, :], Nm[hb_][n_][hs, cc, :], True, True,
                             r=[("Qq", hb_, o_), ("Nm", hb_, n_)], w=[pkP], tile_position=tp)
                        if not last:
                            p.mm(pQ[hs, cc * C:(cc + 1) * C], Pq[hb_][o_][hs, cc, :], Mm[hb_][n_][hs, cc, :], True, True,
                                 r=[("Pq", hb_, o_), ("Mm", hb_, n_)], w=[pkQ], tile_position=tp)
                stage2.append((pP, pkP, pQ, pkQ))
            for hb_ in range(2):
                pP, pkP, pQ, pkQ = stage2[hb_]
                chs = slice(hb_ * HB, (hb_ + 1) * HB)
                dstP = TmT[:, chs, :] if last else Pq[hb_][n_][:]
                kdP = "TmT" if last else ("Pq", hb_, n_)
                p.tt("dve", dstP, Pq[hb_][o_][:], pP[:, 0:HB * C].rearrange("p (c t) -> p c t", c=HB), ALU.add,
                     r=[pkP, ("Pq", hb_, o_)], w=[kdP])
                if not last:
                    p.tt("dve", Qq[hb_][n_][:], Qq[hb_][o_][:], pQ[:, 0:HB * C].rearrange("p (c t) -> p c t", c=HB), ALU.add,
                         r=[pkQ, ("Qq", hb_, o_)], w=[("Qq", hb_, n_)])
        pY, pkY = psD, "psD"
        for ch in range(NCH):
            hi = hidx[0] % 2; ho = 1 - hi; hidx[0] += 1
            pX, pkX = pp.get()
            for h in range(2):
                hs = slice(64 * h, 64 * h + 64); tp = (64 * h, 64 * h)
                p.mm(pX[hs, 0:C], A1[hs, ch, 0:C], Vt[hs, ch, :], True, False, r=["A1", "Vt"], w=[pkX], tile_position=tp)
                p.mm(pX[hs, 0:C], AR[hs, ch, 0, :], Hb[hi][hs, :], False, True, r=["AR", ("Hb", hi)], w=[pkX], tile_position=tp)
            p.cp("act", Xb[:], pX[:, 0:C], r=[pkX], w=["Xb"])
            pU, pkU = pp.get()
            for h in range(2):
                hs = slice(64 * h, 64 * h + 64); tp = (64 * h, 64 * h)
                p.mm(pU[hs, 0:C], TmT[hs, ch, :], Xb[hs, :], True, True, r=["TmT", "Xb"], w=[pkU], tile_position=tp)
            p.cp("act", Ub[:], pU[:, 0:C], r=[pkU], w=["Ub"])
            pH, pkH = pp.get()
            for h in range(2):
                hs = slice(64 * h, 64 * h + 64); tp = (64 * h, 64 * h)
                p.mm(pH[hs, 0:C], Kh[hs, ch, :], Vt[hs, ch, :], True, False, r=["Kh", "Vt"], w=[pkH], tile_position=tp)
                p.mm(pH[hs, 0:C], Bh[hs, ch, :], Ub[hs, :], False, True, r=["Bh", "Ub"], w=[pkH], tile_position=tp)
            gC = E1[:, ch * C + C - 1:ch * C + C]
            p.stt(Hb[ho][:], Hf[:], gC, pH[:, 0:C], ALU.mult, ALU.add, r=["Hf", "E1", pkH], w=[("Hb", ho)])
            for h in range(2):
                hs = slice(64 * h, 64 * h + 64); tp = (64 * h, 64 * h)
                yo_ = pY[hs, ch * C:(ch + 1) * C]
                p.mm(yo_, Hb[hi][hs, :], AR[hs, ch, 1, :], True, False, r=[("Hb", hi), "AR"], w=[pkY], tile_position=tp)
                p.mm(yo_, Vt[hs, ch, :], A1[hs, ch, C:2 * C], False, False, r=["Vt", "A1"], w=[pkY], tile_position=tp)
                p.mm(yo_, Ub[hs, :], A2m[hs, ch, C:2 * C], False, True, r=["Ub", "A2m"], w=[pkY], tile_position=tp)
            p.stt(Hf[:], Hf[:], gC, pH[:, 0:C], ALU.mult, ALU.add, r=["Hf", "E1", pkH], w=["Hf"])
        p.cp("act", Yf[:], pY[:], r=[pkY], w=["Yf"])
        pm_, pkm = pp.get()
        p.mm(pm_[:], bones[:], Yf[:], True, True, r=["bones", "Yf"], w=[pkm])
        p.stt(yc[:], pm_[:], -1.0 / C, Yf[:], ALU.mult, ALU.add, r=[pkm, "Yf"], w=["yc"])
        p.tt("pool", ycq[:], yc[:], yc[:], ALU.mult, r=["yc"], w=["ycq"])
        pv_, pkv = pp.get()
        p.mm(pv_[:], bones[:], ycq[:], True, True, r=["bones", "ycq"], w=[pkv])
        p.ts("dve", rs[:], pv_[:], 1.0 / C, ALU.mult, GN_EPS, ALU.add, r=[pkv], w=["rsA"])
        p.act(rs[:], rs[:], AF.Sqrt, r=["rsA"], w=["rsA"])
        p.op("dve", lambda e: e.reciprocal(out=rs[:], in_=rs[:]), r=["rsA"], w=["rsA"])
        p.tt("dve", yc[:], yc[:], rs[:], ALU.mult, r=["yc", "rsA"], w=["yc"])
        p.ts("dve", yc[:], yc[:], LNW, ALU.mult, LNB, ALU.add, r=["yc", "CV"], w=["yc"])
        p.tt("pool", yc[:], yc[:], bon[:], ALU.add, r=["yc", "bon"], w=["yc"])
        p.tt("dve", yo[:], yc[:], gv[:], ALU.mult, r=["yc", "gv"], w=["yo"])
        p.dma("sp", yT[0:128, n * TB:(n + 1) * TB], yo[:], s_st, r=["yo"], w=[("yT", 0, n)])
        pO, pkO = psD, "psD"
        for i in range(4):
            kv0 = 0
            if n == 0:
                kv0 = TB - i * 128
            nk = 640 - kv0
            nkt = nk // 128
            for h in range(2):
                hs = slice(64 * h, 64 * h + 64)
                pS1, pkS1 = pp.get(); pS2, pkS2 = pp.get()
                lo = i * 128 + kv0
                n1 = min(nk, 512)
                p.mm(pS1[:, 0:n1], qT[hs, i * 128:(i + 1) * 128], Kring[hs, lo:lo + n1], True, True, r=["qT", "Kring"], w=[pkS1])
                p.stt(S[:, kv0:kv0 + n1], pS1[:, 0:n1], 0.125, ATB[:, h, kv0:kv0 + n1], ALU.mult, ALU.add, r=[pkS1, "ATB"], w=["S_att"])
                if nk > 512:
                    p.mm(pS2[:, 0:128], qT[hs, i * 128:(i + 1) * 128], Kring[hs, lo + 512:lo + 640], True, True, r=["qT", "Kring"], w=[pkS2])
                    p.stt(S[:, 512:640], pS2[:, 0:128], 0.125, ATB[:, h, 512:640], ALU.mult, ALU.add, r=[pkS2, "ATB"], w=["S_att"])
                p.op("dve", lambda e, kv0=kv0: e.tensor_reduce(out=ast[:, 0:1], in_=S[:, kv0:640], axis=AX.X, op=ALU.max), r=["S_att"], w=["ast"])
                p.ts("dve", ast[:, 1:2], ast[:, 0:1], -1.0, ALU.mult, r=["ast"], w=["ast"])
                p.act(Pe[:, kv0:640], S[:, kv0:640], AF.Exp, r=["S_att", "ast"], w=["Pe", "ast"], bias=ast[:, 1:2], accum_out=ast[:, 2:3])
                p.op("dve", lambda e: e.reciprocal(out=ast[:, 3:4], in_=ast[:, 2:3]), r=["ast"], w=["ast"])
                p.ts("dve", Pn[:, kv0:640], Pe[:, kv0:640], ast[:, 3:4], ALU.mult, r=["Pe", "ast"], w=["Pn"])
                ptr, pktr = pp.get()
                ptrb = ptr.bitcast(BF16)
                for kt in range(nkt):
                    c0 = kv0 + kt * 128
                    p.tr(ptrb[:, kt * 128:(kt + 1) * 128], Pn[:, c0:c0 + 128], c["identb"][:], r=["Pn", "identb"], w=[pktr])
                p.cp("act", PT[:, 0:nkt, :], ptrb[:, 0:nkt * 128].rearrange("p (k q) -> p k q", k=nkt), r=[pktr], w=["PT"])
                vt0 = i + (kv0 // 128)
                for kt in range(nkt):
                    p.mm(pO[hs, i * 128:(i + 1) * 128], Vring[:, vt0 + kt, 64 * h:64 * h + 64], PT[:, kt, :], kt == 0, kt == nkt - 1,
                         r=["Vring", "PT"], w=[pkO], tile_position=(0, 64 * h))
        p.cp("act", ao[:], pO[:], r=[pkO], w=["ao"])
        p.dma("sp", yT[128:256, n * TB:(n + 1) * TB], ao[:], s_st, r=["ao"], w=[("yT", 1, n)])
    p.finalize(final_sems=[s_st])
    return p


C = 64
TB = 512
NCH = TB // C
DEC = 0.6065306597126334
GN_EPS = 64e-5
NBLK = 16
FIRST_FULL = 11
KV_BLOCK = 10
HB = 4


def emit_A(p, ph, c, pools, psD, psO, dr_in, yscr, nblk=NBLK, first_full=FIRST_FULL, kv_block=KV_BLOCK, nhp=4):
    xb = dr_in["xb"]; wsel4 = dr_in["wsel"]; g_mix = dr_in["g_mix"]; cv4 = dr_in["cv"]
    w24 = dr_in["w2"]; a24 = dr_in["a2"]; g24 = dr_in["g2"]; attb4 = dr_in["attb"]; amask = dr_in["amask"]
    s_ld = p.sem("a_ld"); s_x = [p.sem(f"a_x{i}") for i in range(4)]; s_w = [p.sem("a_w0"), p.sem("a_w1")]
    s_st = p.sem("a_st")
    s_hs = p.sem("a_hs"); s_hl = [p.sem("a_hl0"), p.sem("a_hl1")]
    hTs = dr_in["hTs"]
    sb = ph.sb
    pp1, pp2, s3 = pools

    gcol = sb("gcol", [128, NK], F32)
    p.dma("sp", gcol[:], g_mix.rearrange("(k p) -> p k", p=128), s_ld, w=["gcol"], allow_slow_non_contiguous=True)
    bones = sb("bones", [128, 128], F32)
    p.memset("pool", bones[:], 0.0, w=["bones"])
    p.memset("pool", bones[0:64, 0:64], 1.0, w=["bones"])
    p.memset("pool", bones[64:128, 64:128], 1.0, w=["bones"])
    rmask = sb("rmask", [128, TB], F32)
    p.memset("pool", rmask[:], 1.0, w=["rmask"])
    p.memset("pool", rmask.rearrange("p (c t) -> p c t", t=C)[:, :, 0:1], 0.0, w=["rmask"])
    mU = sb("mU", [128, HB, 128], F32); mL = sb("mL", [128, HB, 64], F32); idn = sb("idn", [128, HB, 64], F32)
    for t_, name in ((mU, "mU"), (mL, "mL"), (idn, "idn")):
        p.memset("pool", t_[:], 1.0, w=[name])
    for h in range(2):
        hs = slice(64 * h, 64 * h + 64)
        p.op("pool", lambda e, hs=hs: e.affine_select(out=mU[hs, :, 0:64], in_=mU[hs, :, 0:64], pattern=[[0, HB], [1, 64]],
                                                      compare_op=ALU.is_gt, fill=0.0, base=0, channel_multiplier=-1), r=["mU"], w=["mU"])
        p.op("pool", lambda e, hs=hs: e.affine_select(out=mU[hs, :, 64:128], in_=mU[hs, :, 64:128], pattern=[[0, HB], [1, 64]],
                                                      compare_op=ALU.is_ge, fill=0.0, base=0, channel_multiplier=-1), r=["mU"], w=["mU"])
        p.op("pool", lambda e, hs=hs: e.affine_select(out=mL[hs, :, :], in_=mL[hs, :, :], pattern=[[0, HB], [-1, 64]],
                                                      compare_op=ALU.is_gt, fill=0.0, base=0, channel_multiplier=1), r=["mL"], w=["mL"])
        p.op("pool", lambda e, hs=hs: e.affine_select(out=idn[hs, :, :], in_=idn[hs, :, :], pattern=[[0, HB], [-1, 64]],
                                                      compare_op=ALU.is_equal, fill=0.0, base=0, channel_multiplier=1), r=["idn"], w=["idn"])
    AMK = sb("AMK", [128, 4, 640], F32)
    p.dma("sp", AMK[:], amask, s_ld, w=["AMK"])

    def t5(name, dt=F32, n=TB):
        return sb(name, [128, n], dt)
    CV = sb("CV", [128, 16], F32)
    MU = lambda ct: CV[:, ct:ct + 1]
    W0, A0, KK, KA, RK, LNW, LNB = [CV[:, 5 + i:6 + i] for i in range(7)]
    OMKA = CV[:, 12:13]
    W2 = sb("W2", [128, 128], F32); A2 = sb("A2", [128, 128], F32); G2 = sb("G2", [128, 128], F32)
    p.memset("pool", W2[:], 0.0, w=["W2"])
    p.memset("pool", A2[:], 0.0, w=["A2"])
    ATB = sb("ATB", [128, 2, 640], F32)
    WB = sb("WB", [128, NK, 1024], BF16)
    wst = [sb(f"wstA{i}", [128, NK, 128], F32) for i in range(2)]
    XTbig = sb("XTbig", [128, 4, D], F32)
    XT = [XTbig[:, i, :] for i in range(4)]
    hTalt = XTbig[:, 0:2, :].rearrange("p a d -> p (a d)").bitcast(BF16).rearrange("p (k t) -> p k t", k=NK)
    sq = sb("sqA", [128, D], BF16); ssb2 = [sb(f"ssA{i}", [128, 4], F32) for i in range(2)]; hb2 = [sb(f"hbA{i}", [128, D], BF16) for i in range(2)]
    hT = sb("hTA", [128, NK, TB], BF16)
    PJ = [[sb(f"PJ{i}_{ct}", [128, 1 + TB], F32) for ct in range(5)] for i in range(2)]
    qT = t5("qT", BF16)
    Kring = sb("Kring", [128, 2 * TB], BF16)
    Vring = sb("Vring", [128, 8, 128], BF16)
    tmp = t5("tmpA"); Pm = [t5(f"Pm{ct}") for ct in range(5)]
    sgm = t5("sgm"); av = t5("av"); gv2 = [t5("gv0"), t5("gv1")]
    kk = t5("kk"); kmod = t5("kmod"); beta = t5("beta"); bon2 = [t5("bon0"), t5("bon1")]
    cs = t5("cs"); csd = t5("csd"); dcs = t5("dcs")
    E12 = [t5("E1_0"), t5("E1_1")]; E2 = t5("E2"); E3 = t5("E3"); E4 = t5("E4")
    txw, ktxw = E2, "E2"; sxg, ksxg = E3, "E3"; kk2, kkk2 = csd, "csd"; rn, krn = dcs, "dcs"
    Yf, kYf = t5("Yf"), "Yf"; yc, kyc = t5("yc"), "yc"; ycq, kycq = t5("ycq"), "ycq"; rs, krs = t5("rsA"), "rsA"
    AR2 = [sb(f"AR{i}", [128, NCH, 2, C], BF16) for i in range(2)]
    for i in range(2):
        p.memset("pool", AR2[i][:], 0.0, w=[("AR", i)])
    Kt2 = [t5(f"Kt{i}", BF16) for i in range(2)]; Bt2 = [t5(f"Bt{i}", BF16) for i in range(2)]; Khc2 = [t5(f"Khc{i}", BF16) for i in range(2)]
    Bhc2 = [t5(f"Bhc{i}", BF16) for i in range(2)]; vB2 = [t5(f"vB{i}", BF16) for i in range(2)]
    Vt2 = [sb(f"Vt{i}", [128, NCH, C], BF16) for i in range(2)]; Kh2 = [sb(f"Kh{i}", [128, NCH, C], BF16) for i in range(2)]
    Bh2 = [sb(f"Bh{i}", [128, NCH, C], BF16) for i in range(2)]
    A12 = [sb(f"A1_{i}", [128, NCH, 128], BF16) for i in range(2)]; A2m2 = [sb(f"A2m_{i}", [128, NCH, 128], BF16) for i in range(2)]
    NS = [[sb(f"NS{hb_}_{i}", [128, HB, 2, C], BF16) for i in range(2)] for hb_ in range(2)]
    Mm = [[sb(f"Mm{hb_}_{i}", [128, HB, C], BF16) for i in range(2)] for hb_ in range(2)]
    TmT2 = [sb(f"TmT{i}", [128, NCH, C], BF16) for i in range(2)]
    Hf = sb("Hf", [128, C], F32); Hb = [sb(f"Hb{i}", [128, C], BF16) for i in range(2)]
    Xb = sb("Xb", [128, C], BF16); Ub = sb("Ub", [128, C], BF16)
    yo = t5("yo", BF16)
    S = sb("S_att", [128, 640], F32); Pe = sb("Pe", [128, 640], F32); Pn = sb("Pn", [128, 640], BF16)
    PT = sb("PT", [128, 5, 128], BF16)
    ast = sb("ast", [128, 4], F32)
    ao = t5("ao", BF16)

    xv = xb.rearrange("(n p) d -> p n d", p=128)
    xcnt = [0]

    def load_x_tile(n, i):
        j = xcnt[0] % 4; xcnt[0] += 1
        p.dma("sp", XT[j], xv[:, n * 4 + i, :], s_x[j], w=[("XT", j)])
        return j

    for hp in range(nhp):
        p.dma("sp", CV[:], cv4[hp], s_ld, w=["CV"])
        p.dma("sp", W2[0:64, :], w24[hp], s_ld, w=["W2"])
        p.dma("sp", A2[64:128, :], a24[hp], s_ld, w=["A2"])
        p.dma("sp", G2[:], g24[hp], s_ld, w=["G2"])
        for h in range(2):
            p.dma("sp", ATB[:, h, :], attb4[hp, h], s_ld, w=["ATB"])
        p.ts("dve", OMKA, KA, -1.0, ALU.mult, 1.0, ALU.add, r=["CV", "W2", "A2", "G2", "ATB"], w=["CV"])
        wv = wsel4[hp].rearrange("(k p) c -> p k c", p=128)
        for j in range(8):
            b_ = j % 2
            p.dma("sp", wst[b_][:], wv[:, :, j * 128:(j + 1) * 128], s_w[b_], w=[("wstA", b_)])
            for dk in range(NK):
                if dk % 2:
                    p.act(WB[:, dk, j * 128:(j + 1) * 128], wst[b_][:, dk, :], AF.Copy, r=[("wstA", b_), "gcol"], w=["WB"], scale=gcol[:, dk:dk + 1])
                else:
                    p.ts("dve", WB[:, dk, j * 128:(j + 1) * 128], wst[b_][:, dk, :], gcol[:, dk:dk + 1], ALU.mult,
                         r=[("wstA", b_), "gcol"], w=["WB"])
        p.memset("pool", Hf[:], 0.0, w=["Hf"])
        p.memset("pool", Hb[0][:], 0.0, w=[("Hb", 0)])
        for ct in range(5):
            p.memset("pool", PJ[0][ct][:, 0:1], 0.0, w=[("PJ", 0, ct)])
        hidx = 0
        share = hp > 0

        def load_hT(n):
            if n % 2 == 0:
                p.dma("sp", hT[:].rearrange("p k t -> p (k t)"), hTs[n], s_hl[0], r=[("hTs", n)], w=["hTA"])
            else:
                p.dma("sp", hTalt.rearrange("p k t -> p (k t)"), hTs[n], s_hl[1], r=[("hTs", n)], w=["hTB", ("XT", 0), ("XT", 1)])
        if share:
            load_hT(0)
        else:
            nxt = [load_x_tile(0, i) for i in range(4)]
        for n in range(nblk):
            b = n % 2
            full = n >= first_full
            kvb = full or n == kv_block
            if share:
                hTc, khT = (hT, "hTA") if n % 2 == 0 else (hTalt, "hTB")
                if n + 1 < nblk:
                    load_hT(n + 1)
            else:
                hTc, khT = hT, "hTA"
                cur_x = nxt
            AR, kAR = AR2[b], ("AR", b); E1, kE1 = E12[b], ("E1", b); bon, kbon = bon2[b], ("bon", b); gv, kgv = gv2[b], ("gv", b)
            Vt, kVt = Vt2[b], ("Vt", b); Kh, kKh = Kh2[b], ("Kh", b); Bh, kBh = Bh2[b], ("Bh", b)
            A1, kA1 = A12[b], ("A1", b); A2m, kA2m = A2m2[b], ("A2m", b); TmT, kTmT = TmT2[b], ("TmT", b)
            Kt, kKt = Kt2[b], ("Kt", b); Bt, kBt = Bt2[b], ("Bt", b); Khc, kKhc = Khc2[b], ("Khc", b)
            Bhc, kBhc = Bhc2[b], ("Bhc", b); vB, kvB = vB2[b], ("vB", b)
            nxt = []
            for i in range(0 if share else 4):
                j = cur_x[i]
                xt = XT[j]
                ssb = ssb2[i % 2]; hb = hb2[i % 2]; kss = ("ssA", i % 2); khb = ("hbA", i % 2)
                p.act(sq[:], xt, AF.Square, r=[("XT", j)], w=[kss], accum_out=ssb[:, 0:1])
                p.ts("dve", ssb[:, 1:2], ssb[:, 0:1], 1.0 / D, ALU.mult, NORM_EPS, ALU.add, r=[kss], w=[kss])
                p.act(ssb[:, 2:3], ssb[:, 1:2], AF.Sqrt, r=[kss], w=[kss])
                p.op("dve", lambda e, ssb=ssb: e.reciprocal(out=ssb[:, 3:4], in_=ssb[:, 2:3]), r=[kss], w=[kss])
                p.ts("dve", hb[:], xt, ssb[:, 3:4], ALU.mult, r=[("XT", j), kss], w=[khb])
                if n + 1 < nblk:
                    nxt.append(load_x_tile(n + 1, i))
                pt, pk = pp1.get()
                ptb = pt.bitcast(BF16)
                for dk in range(NK):
                    p.tr(ptb[:, dk * 128:(dk + 1) * 128], hb[:, dk * 128:(dk + 1) * 128], c["identb"][:], r=[khb, "identb"], w=[pk])
                p.cp("act", hT[:, :, i * 128:(i + 1) * 128], ptb.rearrange("p (k t) -> p k t", k=NK), r=[pk], w=["hTA"])
            if not share and nhp > 1:
                p.dma("sp", hTs[n], hT[:].rearrange("p k t -> p (k t)"), s_hs, r=["hTA"], w=[("hTs", n)])
            if full:
                p.cp("pool", Kring[:, 0:TB], Kring[:, TB:2 * TB], r=["Kring"], w=["Kring"])
                p.cp("pool", Vring[:, 0:4, :], Vring[:, 4:8, :], r=["Vring"], w=["Vring"])
            cts = [0, 1, 2, 3, 4, 5, 6] if full else ([1, 2, 3, 6] if kvb else [1, 2, 3])
            for ct in cts:
                pt, pk = pp1.get()
                for dk in range(NK):
                    p.mm(pt[:], WB[:, dk, ct * 128:(ct + 1) * 128], hTc[:, dk, :], dk == 0, dk == NK - 1, r=["WB", khT], w=[pk])
                if ct < 5:
                    p.cp("act" if ct % 2 else "dve", PJ[b][ct][:, 1:1 + TB], pt[:], r=[pk], w=[("PJ", b, ct)])
                elif ct == 5:
                    p.cp("act", qT[:], pt[:], r=[pk], w=["qT"])
                else:
                    p.cp("dve", Kring[:, TB:2 * TB], pt[:], r=[pk], w=["Kring"])
            if kvb:
                pt, pk = pp1.get()
                for i in range(4):
                    for dk in range(NK):
                        p.mm(pt[:, i * 128:(i + 1) * 128], hTc[:, dk, i * 128:(i + 1) * 128], WB[:, dk, 7 * 128:8 * 128], dk == 0, dk == NK - 1,
                             r=["WB", khT], w=[pk])
                p.cp("act", Vring[:, 4:8, :], pt.rearrange("p (i v) -> p i v", i=4), r=[pk], w=["Vring"])
            mcts = [0, 1, 2, 3, 4] if full else [1, 2, 3]
            for ct in range(5):
                if ct in mcts:
                    cur = PJ[b][ct][:, 1:1 + TB]; prv = PJ[b][ct][:, 0:TB]
                    p.tt("pool", tmp[:], prv, cur, ALU.subtract, r=[("PJ", b, ct)], w=["tmpA"])
                    p.stt(Pm[ct][:], tmp[:], MU(ct), cur, ALU.mult, ALU.add, r=["tmpA", ("PJ", b, ct), "CV"], w=[("Pm", ct)])
                    p.cp("pool", PJ[1 - b][ct][:, 0:1], PJ[b][ct][:, TB:TB + 1], r=[("PJ", b, ct)], w=[("PJ", 1 - b, ct)])
                elif n + 1 == first_full:
                    pt, pk = pp1.get()
                    for dk in range(NK):
                        p.mm(pt[:, 0:1], WB[:, dk, ct * 128:(ct + 1) * 128], hTc[:, dk, TB - 1:TB], dk == 0, dk == NK - 1, r=["WB", khT], w=[pk])
                    p.cp("act", PJ[1 - b][ct][:, 0:1], pt[:, 0:1], r=[pk], w=[("PJ", 1 - b, ct)])
            r_, k_, v_ = Pm[0], Pm[1], Pm[2]
            p.act(txw[:], Pm[3][:], AF.Tanh, r=[("Pm", 3)], w=[ktxw])
            pw_, pkw = pp1.get()
            p.mm(pw_[:], W2[:], txw[:], True, True, r=["W2", ktxw], w=[pkw])
            p.act(sgm[:], pw_[:], AF.Sigmoid, r=[pkw, "CV"], w=["sgm"], bias=W0)
            pa_, pka = pp1.get()
            p.mm(pa_[:], A2[:], Pm[3][:], True, True, r=["A2", ("Pm", 3)], w=[pka])
            p.act(av[:], pa_[:], AF.Sigmoid, r=[pka, "CV"], w=["av"], bias=A0)
            if full:
                p.act(sxg[:], Pm[4][:], AF.Sigmoid, r=[("Pm", 4)], w=[ksxg])
                pg_, pkg = pp1.get()
                p.mm(pg_[:], G2[:], sxg[:], True, True, r=["G2", ksxg], w=[pkg])
                p.cp("act", gv[:], pg_[:], r=[pkg], w=[kgv])
            p.ts("dve", kk[:], k_[:], KK, ALU.mult, r=[("Pm", 1), "CV"], w=["kk"])
            p.tt("pool", kk2[:], kk[:], kk[:], ALU.mult, r=["kk"], w=[kkk2])
            pn_, pkn = pp1.get()
            p.mm(pn_[:], bones[:], kk2[:], True, True, r=["bones", kkk2], w=[pkn])
            p.act(rn[:], pn_[:], AF.Sqrt, r=[pkn], w=[krn])
            p.ts("dve", rn[:], rn[:], 1e-12, ALU.max, r=[krn], w=[krn])
            p.op("dve", lambda e: e.reciprocal(out=rn[:], in_=rn[:]), r=[krn], w=[krn])
            p.tt("dve", kk[:], kk[:], rn[:], ALU.mult, r=["kk", krn], w=["kk"])
            p.ts("dve", tmp[:], av[:], KA, ALU.mult, OMKA, ALU.add, r=["av", "CV"], w=["tmpA"])
            p.tt("dve", kmod[:], k_[:], tmp[:], ALU.mult, r=[("Pm", 1), "tmpA"], w=["kmod"])
            p.tt("pool", beta[:], kk[:], av[:], ALU.mult, r=["kk", "av"], w=["beta"])
            if full:
                p.stt(tmp[:], r_[:], RK, kmod[:], ALU.mult, ALU.mult, r=[("Pm", 0), "kmod", "CV"], w=["tmpA"])
                pb_, pkb = pp1.get()
                p.mm(pb_[:], bones[:], tmp[:], True, True, r=["bones", "tmpA"], w=[pkb])
                p.tt("dve", bon[:], v_[:], pb_[:], ALU.mult, r=[("Pm", 2), pkb], w=[kbon])
            p.op("dve", lambda e: e.tensor_tensor_scan(out=cs[:], data0=rmask[:], data1=sgm[:], initial=0.0, op0=ALU.mult, op1=ALU.add),
                 r=["rmask", "sgm"], w=["cs"])
            p.tt("pool", csd[:], cs[:], sgm[:], ALU.subtract, r=["cs", "sgm"], w=["csd"])
            cs3 = cs.rearrange("p (c t) -> p c t", t=C)
            p.tt("pool", dcs.rearrange("p (c t) -> p c t", t=C), cs3[:, :, C - 1:C].to_broadcast([128, NCH, C]), cs3, ALU.subtract,
                 r=["cs"], w=["dcs"])
            p.act(E1[:], cs[:], AF.Exp, r=["cs"], w=[kE1], scale=-DEC)
            p.act(E2[:], cs[:], AF.Exp, r=["cs"], w=["E2"], scale=DEC)
            p.act(E3[:], csd[:], AF.Exp, r=["csd"], w=["E3"], scale=-DEC)
            p.act(E4[:], dcs[:], AF.Exp, r=["dcs"], w=["E4"], scale=-DEC)
            c3 = lambda ap: ap.rearrange("p (c t) -> p c t", t=C)
            p.stt(AR[:, :, 0, :], c3(kk), -1.0, c3(E3), ALU.mult, ALU.mult, r=["kk", "E3"], w=[kAR])
            if full:
                p.tt("dve", AR[:, :, 1, :], c3(r_), c3(E1), ALU.mult, r=[("Pm", 0), kE1], w=[kAR])
            p.tt("pool", Kt[:], kmod[:], E2[:], ALU.mult, r=["kmod", "E2"], w=[kKt])
            p.tt("pool", Bt[:], beta[:], E2[:], ALU.mult, r=["beta", "E2"], w=[kBt])
            p.tt("dve", Khc[:], kmod[:], E4[:], ALU.mult, r=["kmod", "E4"], w=[kKhc])
            p.tt("pool", Bhc[:], beta[:], E4[:], ALU.mult, r=["beta", "E4"], w=[kBhc])
            p.cp("act", vB[:], v_[:], r=[("Pm", 2)], w=[kvB])
            for src, dst, ks, kd in ((vB, Vt, kvB, kVt), (Khc, Kh, kKhc, kKh), (Bhc, Bh, kBhc, kBh)):
                pt, pk = pp2.get()
                ptb = pt.bitcast(BF16)
                for ch in range(NCH):
                    for h in range(2):
                        hs = slice(64 * h, 64 * h + 64)
                        p.tr(ptb[hs, ch * C:(ch + 1) * C], src[hs, ch * C:(ch + 1) * C], c["identb"][hs, hs], r=[ks, "identb"], w=[pk],
                             tile_position=(64 * h, 64 * h))
                p.cp("act", dst.rearrange("p c t -> p (c t)"), ptb[:, 0:TB], r=[pk], w=[kd])
            for hb_ in range(2):
                p1, pk1 = pp2.get(); p2, pk2 = pp2.get(); p3, pk3 = pp2.get()
                for cc in range(HB):
                    ch = hb_ * HB + cc
                    for which in range(3):
                        for h in range(2):
                            hs = slice(64 * h, 64 * h + 64)
                            tp = (64 * h, 64 * h)
                            arh = AR[hs, ch, :, :].rearrange("p a t -> p (a t)")
                            if which == 0:
                                p.mm(p1[hs, cc * 128:(cc + 1) * 128], Kt[hs, ch * C:(ch + 1) * C], arh, True, True, r=[kKt, kAR], w=[pk1], tile_position=tp)
                            elif which == 1:
                                p.mm(p2[hs, cc * 128:(cc + 1) * 128], Bt[hs, ch * C:(ch + 1) * C], arh, True, True, r=[kBt, kAR], w=[pk2], tile_position=tp)
                            else:
                                p.mm(p3[hs, cc * C:(cc + 1) * C], AR[hs, ch, 0, :], Bt[hs, ch * C:(ch + 1) * C], True, True, r=[kAR, kBt], w=[pk3], tile_position=tp)
                chs = slice(hb_ * HB, (hb_ + 1) * HB)
                p.tt("dve", A1[:, chs, :], p1.rearrange("p (c t) -> p c t", c=HB), mU[:], ALU.mult, r=[pk1, "mU"], w=[kA1])
                p.tt("dve", A2m[:, chs, :], p2.rearrange("p (c t) -> p c t", c=HB), mU[:], ALU.mult, r=[pk2, "mU"], w=[kA2m])
                p.tt("dve", Mm[hb_][0][:], p3[:, 0:HB * C].rearrange("p (c t) -> p c t", c=HB), mL[:], ALU.mult, r=[pk3, "mL"], w=[("Mm", hb_, 0)])
                p.cp("pool", NS[hb_][0][:, :, 0, :], A2m[:, chs, 0:C], r=[kA2m], w=[("NS", hb_, 0)])
            for lvl in range(0, 6):
                o_ = lvl % 2; n_ = (lvl + 1) % 2
                first = lvl == 0; last = lvl == 5
                stage = []
                pBb, pkBb = (None, None) if last else pp2.get()
                for hb_ in range(2):
                    pA, pkA = pp2.get()
                    pB, pkB = (None, None) if last else (pBb[:, hb_ * HB * C:(hb_ + 1) * HB * C], pkBb)
                    for cc in range(HB):
                        for h in range(2):
                            hs = slice(64 * h, 64 * h + 64); tp = (64 * h, 64 * h)
                            if first:
                                rhs = NS[hb_][o_][hs, cc, 0, :]; wd = C
                            elif last:
                                rhs = NS[hb_][o_][hs, cc, 1, :]; wd = C
                            else:
                                rhs = NS[hb_][o_][hs, cc, :, :].rearrange("p a t -> p (a t)"); wd = 2 * C
                            p.mm(pA[hs, cc * 128:cc * 128 + wd], Mm[hb_][o_][hs, cc, :], rhs, True, True,
                                 r=[("Mm", hb_, o_), ("NS", hb_, o_)], w=[pkA], tile_position=tp)
                        if not last:
                            for h in range(2):
                                hs = slice(64 * h, 64 * h + 64); tp = (64 * h, 64 * h)
                                p.mm(pB[hs, cc * C:(cc + 1) * C], NS[hb_][o_][hs, cc, 0, :], Mm[hb_][o_][hs, cc, :], True, True,
                                     r=[("Mm", hb_, o_), ("NS", hb_, o_)], w=[pkB], tile_position=tp)
                    stage.append((pA, pkA, pB, pkB))
                for hb_ in range(2):
                    pA, pkA, pB, pkB = stage[hb_]
                    chs = slice(hb_ * HB, (hb_ + 1) * HB)
                    pA3 = pA.rearrange("p (c a t) -> p c a t", c=HB, a=2)
                    if last:
                        p.tt("dve", TmT[:, chs, :], NS[hb_][o_][:, :, 1, :], pA3[:, :, 0, :], ALU.add, r=[pkA, ("NS", hb_, o_)], w=[kTmT])
                        continue
                    p.cp("act", Mm[hb_][n_][:], pB[:, 0:HB * C].rearrange("p (c t) -> p c t", c=HB), r=[pkB], w=[("Mm", hb_, n_)])
                    p.cp("dve", NS[hb_][n_][:, :, 0, :], pA3[:, :, 0, :], r=[pkA], w=[("NS", hb_, n_)])
                    if first:
                        p.tt("pool", NS[hb_][n_][:, :, 1, :], NS[hb_][o_][:, :, 0, :], idn[:], ALU.add, r=[("NS", hb_, o_), "idn"], w=[("NS", hb_, n_)])
                    else:
                        p.tt("dve", NS[hb_][n_][:, :, 1, :], NS[hb_][o_][:, :, 1, :], pA3[:, :, 1, :], ALU.add,
                             r=[pkA, ("NS", hb_, o_)], w=[("NS", hb_, n_)])
            pY, pkY = psD, "psD"
            for ch in range(NCH):
                hi = hidx % 2; ho = 1 - hi; hidx += 1
                pX, pkX = s3[:, 0:C], "s3"
                for h in range(2):
                    hs = slice(64 * h, 64 * h + 64); tp = (64 * h, 64 * h)
                    p.mm(pX[hs, 0:C], A1[hs, ch, 0:C], Vt[hs, ch, :], True, False, r=[kA1, kVt], w=[pkX], tile_position=tp)
                for h in range(2):
                    hs = slice(64 * h, 64 * h + 64); tp = (64 * h, 64 * h)
                    p.mm(pX[hs, 0:C], AR[hs, ch, 0, :], Hb[hi][hs, :], False, True, r=[kAR, ("Hb", hi)], w=[pkX], tile_position=tp)
                p.cp("act", Xb[:], pX[:, 0:C], r=[pkX], w=["Xb"])
                pU, pkU = s3[:, 0:C], "s3"
                for h in range(2):
                    hs = slice(64 * h, 64 * h + 64); tp = (64 * h, 64 * h)
                    p.mm(pU[hs, 0:C], TmT[hs, ch, :], Xb[hs, :], True, True, r=[kTmT, "Xb"], w=[pkU], tile_position=tp)
                p.cp("act", Ub[:], pU[:, 0:C], r=[pkU], w=["Ub"])
                pH, pkH = (s3[:, 0:C], "s3") if full else (psD[:, 0:C], "psD")
                for h in range(2):
                    hs = slice(64 * h, 64 * h + 64); tp = (64 * h, 64 * h)
                    p.mm(pH[hs, 0:C], Kh[hs, ch, :], Vt[hs, ch, :], True, False, r=[kKh, kVt], w=[pkH], tile_position=tp)
                for h in range(2):
                    hs = slice(64 * h, 64 * h + 64); tp = (64 * h, 64 * h)
                    p.mm(pH[hs, 0:C], Bh[hs, ch, :], Ub[hs, :], False, True, r=[kBh, "Ub"], w=[pkH], tile_position=tp)
                gC = E1[:, ch * C + C - 1:ch * C + C]
                p.stt(Hb[ho][:], Hf[:], gC, pH[:, 0:C], ALU.mult, ALU.add, r=["Hf", kE1, pkH], w=[("Hb", ho)])
                if full:
                    for which in range(3):
                        for h in range(2):
                            hs = slice(64 * h, 64 * h + 64); tp = (64 * h, 64 * h)
                            yo_ = pY[hs, ch * C:(ch + 1) * C]
                            if which == 0:
                                p.mm(yo_, Hb[hi][hs, :], AR[hs, ch, 1, :], True, False, r=[("Hb", hi), kAR], w=[pkY], tile_position=tp)
                            elif which == 1:
                                p.mm(yo_, Vt[hs, ch, :], A1[hs, ch, C:2 * C], False, False, r=[kVt, kA1], w=[pkY], tile_position=tp)
                            else:
                                p.mm(yo_, Ub[hs, :], A2m[hs, ch, C:2 * C], False, True, r=["Ub", kA2m], w=[pkY], tile_position=tp)
                p.stt(Hf[:], Hf[:], gC, pH[:, 0:C], ALU.mult, ALU.add, r=["Hf", kE1, pkH], w=["Hf"])
            if not full:
                continue
            if n == first_full:
                ycol0, ysrc0, ylen = 0, TB - 128, 128
            else:
                ycol0, ysrc0, ylen = 128 + (n - first_full - 1) * TB, 0, TB
            p.cp("act", Yf[:], pY[:], r=[pkY], w=[kYf])
            pm_, pkm = pp1.get()
            p.mm(pm_[:], bones[:], Yf[:], True, True, r=["bones", kYf], w=[pkm])
            p.stt(yc[:], pm_[:], -1.0 / C, Yf[:], ALU.mult, ALU.add, r=[pkm, kYf], w=[kyc])
            p.tt("pool", ycq[:], yc[:], yc[:], ALU.mult, r=[kyc], w=[kycq])
            pv_, pkv = pp1.get()
            p.mm(pv_[:], bones[:], ycq[:], True, True, r=["bones", kycq], w=[pkv])
            p.ts("dve", rs[:], pv_[:], 1.0 / C, ALU.mult, GN_EPS, ALU.add, r=[pkv], w=[krs])
            p.act(rs[:], rs[:], AF.Sqrt, r=[krs], w=[krs])
            p.op("dve", lambda e: e.reciprocal(out=rs[:], in_=rs[:]), r=[krs], w=[krs])
            p.tt("dve", yc[:], yc[:], rs[:], ALU.mult, r=[kyc, krs], w=[kyc])
            p.ts("dve", yc[:], yc[:], LNW, ALU.mult, LNB, ALU.add, r=[kyc, "CV"], w=[kyc])
            p.tt("pool", yc[:], yc[:], bon[:], ALU.add, r=[kyc, kbon], w=[kyc])
            p.tt("dve", yo[:], yc[:], gv[:], ALU.mult, r=[kyc, kgv], w=["yo"])
            p.dma("sp", yscr[hp * 128:(hp + 1) * 128, ycol0:ycol0 + ylen], yo[:, ysrc0:ysrc0 + ylen], s_st, r=["yo"], w=[("yscr", hp, n, 0)])
            pO, pkO = psO, "psO"
            tiles = [3] if n == first_full else [0, 1, 2, 3]
            for i in tiles:
                for h in range(2):
                    hs = slice(64 * h, 64 * h + 64)
                    pS1, pkS1 = pp2.get(); pS2, pkS2 = pp2.get()
                    lo = i * 128
                    p.mm(pS1[:], qT[hs, i * 128:(i + 1) * 128], Kring[hs, lo:lo + 512], True, True, r=["qT", "Kring"], w=[pkS1])
                    p.stt(S[:, 0:512], pS1[:], 0.125, ATB[:, h, 0:512], ALU.mult, ALU.add, r=[pkS1, "ATB"], w=["S_att"])
                    p.mm(pS2[:, 0:128], qT[hs, i * 128:(i + 1) * 128], Kring[hs, lo + 512:lo + 640], True, True, r=["qT", "Kring"], w=[pkS2])
                    p.stt(S[:, 512:640], pS2[:, 0:128], 0.125, ATB[:, h, 512:640], ALU.mult, ALU.add, r=[pkS2, "ATB"], w=["S_att"])
                    if n == first_full + 1:
                        p.tt("pool", S[:], S[:], AMK[:, i, :], ALU.add, r=["S_att", "AMK"], w=["S_att"])
                    p.op("dve", lambda e: e.tensor_reduce(out=ast[:, 0:1], in_=S[:], axis=AX.X, op=ALU.max), r=["S_att"], w=["ast"])
                    p.ts("dve", ast[:, 1:2], ast[:, 0:1], -1.0, ALU.mult, r=["ast"], w=["ast"])
                    p.act(Pe[:], S[:], AF.Exp, r=["S_att", "ast"], w=["Pe", "ast"], bias=ast[:, 1:2], accum_out=ast[:, 2:3])
                    p.op("dve", lambda e: e.reciprocal(out=ast[:, 3:4], in_=ast[:, 2:3]), r=["ast"], w=["ast"])
                    p.ts("dve", Pn[:], Pe[:], ast[:, 3:4], ALU.mult, r=["Pe", "ast"], w=["Pn"])
                    ptr, pktr = pp2.get()
                    ptrb = ptr.bitcast(BF16)
                    for kt in range(5):
                        p.tr(ptrb[:, kt * 128:(kt + 1) * 128], Pn[:, kt * 128:(kt + 1) * 128], c["identb"][:], r=["Pn", "identb"], w=[pktr])
                    p.cp("act", PT[:], ptrb[:, 0:640].rearrange("p (k q) -> p k q", k=5), r=[pktr], w=["PT"])
                    for kt in range(5):
                        p.mm(pO[hs, i * 128:(i + 1) * 128], Vring[:, i + kt, 64 * h:64 * h + 64], PT[:, kt, :], kt == 0, kt == 4,
                             r=["Vring", "PT"], w=[pkO], tile_position=(0, 64 * h))
            p.cp("act", ao[:, ysrc0:ysrc0 + ylen], pO[:, ysrc0:ysrc0 + ylen], r=[pkO], w=["ao"])
            p.dma("sp", yscr[512 + hp * 128:512 + (hp + 1) * 128, ycol0:ycol0 + ylen], ao[:, ysrc0:ysrc0 + ylen], s_st, r=["ao"], w=[("yscr", hp, n, 1)])


def build_fused(nc):
    p = Prog(nc)
    dr = lambda name, shape, dt=F32, kind="ExternalInput": nc.dram_tensor(name, list(shape), dt, kind=kind).ap()
    A = dict(xb=dr("xb", [NBLK * TB, D]), wsel=dr("wsel", [4, D, 1024]), g_mix=dr("g_mix", [D]), cv=dr("cv", [4, 128, 16]),
             w2=dr("w2", [4, 64, 128]), a2=dr("a2", [4, 64, 128]), g2=dr("g2", [4, 128, 128]), attb=dr("attb", [4, 2, 128, 640]),
             amask=dr("amask", [128, 4, 640]))
    T = {name: dr(name, shape) for name, shape in B_INPUTS}
    T["out"] = dr("out", [2048, D], kind="ExternalOutput")
    yscr = nc.dram_tensor("yscr", [D, 2176], BF16).ap()
    A["hTs"] = nc.dram_tensor("hTs", [NBLK, 128, NK * TB], BF16).ap()
    T["yT"] = yscr
    T["xin"] = A["xb"][NBLK * TB - 2176:NBLK * TB, :]
    c = make_consts(p)
    banks = [p.ps(f"psb{i}", [128, 512], F32) for i in range(6)]
    psD = p.ps("psD", [128, 512], F32)
    psO = p.ps("psO", [128, 512], F32)

    def mkpool(idx):
        q = PsumPool.__new__(PsumPool)
        q.t = [banks[i] for i in idx]; q.keys = [f"psb{i}" for i in idx]; q.i = 0; q.n = len(idx)
        return q
    pools = (mkpool([0, 1]), mkpool([2, 3, 4]), banks[5])
    pp = mkpool([0, 1, 2, 3, 4, 5])
    with Phase(p) as ph:
        emit_A(p, ph, c, pools, psD, psO, A, yscr)
    pp.t.extend([psD, psO]); pp.keys.extend(["psD", "psO"]); pp.n = 8
    s_st = emit_B(p, c, pp, T, 17)
    p.finalize(final_sems=[s_st])
    return p


RW = 512
RWKV_COLS = 1792


def att_bias_tile(rel_bias_h):
    q = np.arange(128)[:, None]; kc = np.arange(640)[None, :]
    qc, qi = q // 64, q % 64
    kcb, ki = kc // 64, kc % 64
    inband = (kcb >= qc) & (kcb <= qc + 8)
    kb = (kcb - qc) * 64 + ki
    rel = qi + 512 - kb
    idx = np.clip(rel, -128, 128) + 128
    out = np.where(inband, rel_bias_h[np.clip(idx, 0, 256)], np.float32(-30000.0)).astype(np.float32)
    return out


def prep_A(inputs, b, hp, T=8192):
    f = lambda k: np.asarray(inputs[k], np.float32)
    w_in = f("l0_w_in")
    hs = slice(128 * hp, 128 * hp + 128)
    cols = np.concatenate([np.arange(0, 512)[hs], np.arange(512, 1024)[hs], np.arange(1024, 1536)[hs],
                           np.arange(1536, 1664), np.arange(1664, 1792),
                           np.arange(1792, 2304)[hs], np.arange(2304, 2816)[hs], np.arange(2816, 3328)[hs]])
    mu = f("l0_shift_mu")
    cv = np.zeros((128, 16), np.float32)
    for ct in range(5):
        cv[:, ct] = mu[cols[ct * 128:(ct + 1) * 128]]
    for i, k in enumerate(["l0_w0", "l0_a0", "l0_k_k", "l0_k_a", "l0_r_k", "l0_lnx_w", "l0_lnx_b"]):
        cv[:, 5 + i] = f(k)[hs]
    rb = f("l0_rel_bias")
    return dict(
        xb=np.ascontiguousarray(f("x")[b, :T]), wsel=np.ascontiguousarray(w_in[:, cols]), g_mix=f("l0_norm_mix"), cv=cv,
        w2=np.ascontiguousarray(f("l0_w2")[:, hs]), a2=np.ascontiguousarray(f("l0_a2")[:, hs]), g2=np.ascontiguousarray(f("l0_g2")[:, hs]),
        attb=np.stack([att_bias_tile(rb[2 * hp]), att_bias_tile(rb[2 * hp + 1])]))


def _prep_fused(inputs, c, shared):
    f = lambda k: np.asarray(inputs[k], np.float32)
    b, q = c // 4, c % 4
    x = f("x")
    n_real = (q + 1) * 2048
    xb = np.zeros((8192, 1024), np.float32)
    xb[8192 - n_real:] = x[b, :n_real]
    amask = np.zeros((128, 4, 640), np.float32)
    hm = np.ones((128, 1), np.float32)
    if q == 0:
        hm[:] = 0.0
        for i in range(4):
            amask[:, i, :512 - 128 * i] = -30000.0
    d = dict(shared)
    d.update(xb=xb, amask=amask, hmask=hm)
    return d


def _shared_inputs(inputs):
    f = lambda k: np.asarray(inputs[k], np.float32)
    per = [prep_A(inputs, 0, hp, 8) for hp in range(4)]
    sh = dict(wsel=np.stack([d["wsel"] for d in per]), cv=np.stack([d["cv"] for d in per]),
              w2=np.stack([d["w2"] for d in per]), a2=np.stack([d["a2"] for d in per]), g2=np.stack([d["g2"] for d in per]),
              attb=np.stack([d["attb"] for d in per]), g_mix=f("l0_norm_mix"),
              w_out=f("l0_w_out"), g_ffn0=f("l0_norm_ffn"), up0=f("l0_ffn_up"), dn0=f("l0_ffn_down"),
              g_mix1=f("l1_norm_mix"), pw1=f("l1_pw1"), pw1_b=f("l1_pw1_b"), dw=f("l1_dw"), dw_b=f("l1_dw_b"),
              ln_w=f("l1_ln_w"), ln_b=f("l1_ln_b"), pw2=f("l1_pw2"), pw2_b=f("l1_pw2_b"),
              g_ffn1=f("l1_norm_ffn"), up1=f("l1_ffn_up"), dn1=f("l1_ffn_down"), g_fin=f("final_norm"))
    return sh


def kernel(**inputs):
    nc = bass.Bass("TRN2", target_bir_lowering=False)
    build_fused(nc)
    shared = _shared_inputs(inputs)
    maps = [_prep_fused(inputs, c, shared) for c in range(8)]
    res = run_bass_kernel_spmd(nc, maps, core_ids=list(range(8)))
    out = np.concatenate([np.asarray(res.results[c]["out"]) for c in range(8)], axis=0)
    return out.reshape(2, 8192, 1024).astype(np.float32)
```

```python
import contextlib
import numpy as np
import concourse.bass as bass
import concourse.mybir as mybir
from concourse.bass_utils import run_bass_kernel_spmd

F32 = mybir.dt.float32
BF16 = mybir.dt.bfloat16
AF = mybir.ActivationFunctionType
ALU = mybir.AluOpType
AX = mybir.AxisListType

SCHED = True
SYNC_SAME = True


class Sem:
    def __init__(self, nc, name):
        self.h = nc.alloc_semaphore(name)
        self.count = 0
        self.last = {}


class Op:
    __slots__ = ("eng", "fn", "deps", "dsem", "dval", "signal", "sigval", "dmadeps", "epoch", "alldeps", "cost", "idx", "barrier",
                 "chain", "nbytes", "junk")

    def __init__(self, eng, fn):
        self.eng = eng
        self.fn = fn
        self.deps = []
        self.dmadeps = []
        self.dsem = None
        self.dval = 0
        self.signal = False
        self.sigval = 0
        self.epoch = 0
        self.alldeps = []
        self.cost = 100.0
        self.idx = 0
        self.barrier = False
        self.chain = None
        self.nbytes = 0
        self.junk = 0


class Prog:
    ENG = ("pe", "act", "dve", "pool", "sp")

    def __init__(self, nc):
        self.nc = nc
        self.e = {"pe": nc.tensor, "act": nc.scalar, "dve": nc.vector, "pool": nc.gpsimd, "sp": nc.sync}
        self.ops = []
        self.lastw = {}
        self.readers = {}
        self.esems = [{k: Sem(nc, "sem0_" + k) for k in self.ENG}]
        self.epoch = 0
        self.all_sems = []
        self.last_op = {}
        self.last_dma = {}
        self.junk_fn = None
        self.junk_cost = 110.0
        self.junk_frac = 0.7
        self.junk_cap = 10

    def sb(self, name, shape, dt=F32):
        return self.nc.alloc_sbuf_tensor(name, list(shape), dt).ap()

    def ps(self, name, shape, dt=F32):
        return self.nc.alloc_psum_tensor(name, list(shape), dt).ap()

    def sem(self, name):
        s = Sem(self.nc, name)
        self.all_sems.append(s)
        return s

    def barrier(self):
        lasts = dict(self.last_op)
        news = []
        for eng in self.ENG:
            o = Op(eng, lambda e: e.nop())
            for d in lasts.values():
                if d.dsem is None and not (d.eng == eng and eng == "pe"):
                    d.signal = True
                    o.deps.append(d)
            for s in self.all_sems:
                if s.count:
                    o.dmadeps.append((s, s.count))
            o.epoch = self.epoch
            o.barrier = True
            news.append(o)
        for o in news:
            self.ops.append(o)
        self.epoch += 1
        self.esems.append({k: Sem(self.nc, f"sem{self.epoch}_" + k) for k in self.ENG})
        self.last_op = {}

    def _dep(self, op, d):
        if d is None or d is op:
            return
        op.alldeps.append(d)
        if d.dsem is not None:
            op.dmadeps.append((d.dsem, d.dsem.count))
            op.alldeps.extend(d.dsem.last.values())
            return
        if d.eng == op.eng and op.dsem is None:
            if d.eng == "pe" or not SYNC_SAME:
                return
        if d.epoch < self.epoch:
            return
        d.signal = True
        op.deps.append(d)

    def op(self, eng, fn, r=(), w=(), dma=None, cost=None, nbytes=0):
        o = Op(eng, fn)
        o.epoch = self.epoch
        o.idx = len(self.ops)
        if cost is not None:
            o.cost = cost
        o.nbytes = nbytes
        if dma is not None:
            o.dsem = dma
        for k in r:
            self._dep(o, self.lastw.get(k))
        for k in w:
            self._dep(o, self.lastw.get(k))
            for rd in self.readers.get(k, ()):
                self._dep(o, rd)
        if dma is not None:
            dma.count += 16
            o.dval = dma.count
            o.chain = self.last_dma.get(eng)
            self.last_dma[eng] = o
            dma.last[eng] = o
        for k in r:
            self.readers.setdefault(k, []).append(o)
        for k in w:
            self.lastw[k] = o
            self.readers[k] = []
        self.ops.append(o)
        if dma is None:
            self.last_op[eng] = o
        return o

    @staticmethod
    def _free(ap):
        n = 1
        for d in list(ap.shape)[1:]:
            n *= int(d)
        return n

    def _ecost(self, eng, ap):
        f = self._free(ap)
        if eng == "pool":
            return 120.0 + 2.1 * f
        return 70.0 + 1.05 * f

    def dma(self, eng, out, in_, sem, r=(), w=(), **kw):
        nb = self._free(out) * int(out.shape[0]) * 4
        return self.op(eng, lambda e: e.dma_start(out=out, in_=in_, **kw), r=r, w=w, dma=sem, cost=60.0, nbytes=nb)

    def mm(self, out, lhsT, rhs, start, stop, r=(), w=(), **kw):
        n = self._free(rhs)
        mul = 4.0 if rhs.dtype == F32 else 1.0
        return self.op("pe", lambda e: e.matmul(out, lhsT=lhsT, rhs=rhs, start=start, stop=stop, **kw), r=r, w=w,
                       cost=35.0 + 0.43 * mul * max(n, 64))

    def tr(self, out, in_, ident, r=(), w=(), **kw):
        return self.op("pe", lambda e: e.transpose(out, in_, ident, **kw), r=r, w=w, cost=80.0 + 0.43 * self._free(in_))

    def act(self, out, in_, func, r=(), w=(), eng="act", **kw):
        return self.op(eng, lambda e: e.activation(out=out, in_=in_, func=func, **kw), r=r, w=w, cost=self._ecost(eng, out) + 60)

    def tt(self, eng, out, in0, in1, op, r=(), w=()):
        return self.op(eng, lambda e: e.tensor_tensor(out=out, in0=in0, in1=in1, op=op), r=r, w=w, cost=self._ecost(eng, out))

    def ts(self, eng, out, in0, s1, op0, s2=None, op1=None, r=(), w=(), **kw):
        if op1 is None:
            return self.op(eng, lambda e: e.tensor_scalar(out=out, in0=in0, scalar1=s1, scalar2=None, op0=op0, **kw), r=r, w=w,
                           cost=self._ecost(eng, out))
        return self.op(eng, lambda e: e.tensor_scalar(out=out, in0=in0, scalar1=s1, scalar2=s2, op0=op0, op1=op1, **kw), r=r, w=w,
                       cost=self._ecost(eng, out))

    def stt(self, out, in0, scalar, in1, op0, op1, r=(), w=(), eng="dve"):
        return self.op(eng, lambda e: e.scalar_tensor_tensor(out=out, in0=in0, scalar=scalar, in1=in1, op0=op0, op1=op1), r=r, w=w,
                       cost=self._ecost(eng, out))

    def cp(self, eng, out, in_, r=(), w=()):
        if eng == "act":
            return self.op(eng, lambda e: e.copy(out=out, in_=in_), r=r, w=w, cost=self._ecost(eng, out) + 60)
        return self.op(eng, lambda e: e.tensor_copy(out=out, in_=in_), r=r, w=w, cost=self._ecost(eng, out))

    def memset(self, eng, ap, val, w=()):
        return self.op(eng, lambda e: e.memset(ap, val), w=w, cost=self._ecost(eng, ap))

    def _schedule(self, seg):
        import heapq
        pos = {id(o): i for i, o in enumerate(seg)}
        npred = [0] * len(seg)
        succ = [[] for _ in seg]
        for i, o in enumerate(seg):
            ps = set()
            for d in o.alldeps:
                j = pos.get(id(d))
                if j is not None and j != i:
                    ps.add(j)
            if o.chain is not None:
                j = pos.get(id(o.chain))
                if j is not None:
                    ps.add(j)
            npred[i] = len(ps)
            for j in ps:
                succ[j].append(i)
        ready_t = [0.0] * len(seg)
        done_t = [0.0] * len(seg)
        issue_t = [0.0] * len(seg)
        free_at = {k: 0.0 for k in self.ENG}
        heaps = {k: [] for k in self.ENG}
        for i, o in enumerate(seg):
            if npred[i] == 0:
                heapq.heappush(heaps[o.eng], (0.0, i))
        order = []
        nleft = len(seg)
        while nleft:
            best = None
            for k in self.ENG:
                h = heaps[k]
                if not h:
                    continue
                rt, i = h[0]
                st = max(rt, free_at[k])
                if best is None or st < best[0] or (st == best[0] and i < best[2]):
                    best = (st, k, i)
            st, k, _ = best
            h = heaps[k]
            cand = []
            while h and h[0][0] <= st and len(cand) < 16:
                cand.append(heapq.heappop(h))
            cand.sort(key=lambda x: x[1])
            rt, i = cand[0]
            for cnd in cand[1:]:
                heapq.heappush(h, cnd)
            o = seg[i]
            order.append(o)
            nleft -= 1
            if k == "pe" and self.junk_fn is not None:
                gap = st - free_at[k]
                if gap > self.junk_cost:
                    o.junk = min(self.junk_cap, int(gap * self.junk_frac / self.junk_cost))
            if o.dsem is not None:
                free_at[k] = st + o.cost
                issue_t[i] = st + o.cost
                done_t[i] = st + 2000.0 + o.nbytes / 150.0
            else:
                free_at[k] = st + o.cost
                issue_t[i] = st + o.cost
                done_t[i] = st + o.cost
            for j in succ[i]:
                oj = seg[j]
                if oj.chain is o and o not in oj.alldeps:
                    t = issue_t[i]
                else:
                    t = done_t[i] + (40.0 if (oj.eng == o.eng and oj.eng == "pe") else 180.0)
                if t > ready_t[j]:
                    ready_t[j] = t
                npred[j] -= 1
                if npred[j] == 0:
                    heapq.heappush(heaps[oj.eng], (ready_t[j], j))
        return order, max(free_at.values())

    def reorder(self):
        segs = {}
        for o in self.ops:
            segs.setdefault(o.epoch, []).append(o)
        new = []
        tot = 0.0
        for ep in sorted(segs):
            seg = [o for o in segs[ep] if not o.barrier]
            bar = [o for o in segs[ep] if o.barrier]
            order, t = self._schedule(seg)
            tot += t
            new.extend(order)
            new.extend(bar)
        self.ops = new
        return tot

    def finalize(self, final_sems=()):
        if SCHED:
            self.reorder()
        cnt = {}
        waited = {k: {} for k in self.ENG}
        for o in self.ops:
            if o.dsem is None and o.signal:
                kk_ = (o.epoch, o.eng)
                cnt[kk_] = cnt.get(kk_, 0) + 1
                o.sigval = cnt[kk_]
        for o in self.ops:
            e = self.e[o.eng]
            need = {}
            for d in o.deps:
                s = self.esems[d.epoch][d.eng]
                need[id(s)] = (s, max(need.get(id(s), (s, 0))[1], d.sigval))
            for (s, v) in o.dmadeps:
                need[id(s)] = (s, max(need.get(id(s), (s, 0))[1], v))
            wd = waited[o.eng]
            for _ in range(o.junk):
                self.junk_fn(e)
            for sid, (s, v) in need.items():
                if wd.get(sid, 0) >= v:
                    continue
                e.wait_ge(s.h, v)
                wd[sid] = v
            ins = o.fn(e)
            if o.dsem is not None:
                ins.then_inc(o.dsem.h, 16)
            elif o.signal:
                ins.then_inc(self.esems[o.epoch][o.eng].h, 1)
        for s in final_sems:
            self.nc.sync.wait_ge(s.h, s.count)
        return cnt


D = 1024
DFF = 4096
NK = 8
NORM_EPS = 1e-6
LN_EPS = 1e-5
CW = 31


class Phase:
    def __init__(self, p):
        self.p = p
        self.stack = contextlib.ExitStack()

    def __enter__(self):
        self.stack.__enter__()
        return self

    def sb(self, name, shape, dt=F32):
        return self.stack.enter_context(self.p.nc.sbuf_tensor(name, list(shape), dt)).ap()

    def __exit__(self, *a):
        self.p.barrier()
        return self.stack.__exit__(*a)


def make_consts(p):
    c = {}
    c["identf"] = p.sb("identf", [128, 128], F32)
    c["identb"] = p.sb("identb", [128, 128], BF16)
    c["onesf"] = p.sb("onesf", [128, 128], F32)
    p.memset("pool", c["identf"][:], 1.0, w=["identf"])
    p.op("pool", lambda e: e.affine_select(out=c["identf"][:], in_=c["identf"][:], pattern=[[-1, 128]],
                                            compare_op=ALU.is_equal, fill=0.0, base=0, channel_multiplier=1),
         r=["identf"], w=["identf"])
    p.cp("dve", c["identb"][:], c["identf"][:], r=["identf"], w=["identb"])
    p.memset("pool", c["onesf"][:], 1.0, w=["onesf"])
    return c


class PsumPool:
    def __init__(self, p, n=8):
        self.t = [p.ps(f"psb{i}", [128, 512], F32) for i in range(n)]
        self.keys = [f"psb{i}" for i in range(n)]
        self.i = 0
        self.n = n

    def get(self):
        i = self.i
        self.i = (self.i + 1) % self.n
        return self.t[i], self.keys[i]

    def sub(self, idx):
        q = PsumPool.__new__(PsumPool)
        q.t = [self.t[i] for i in idx]; q.keys = [self.keys[i] for i in idx]; q.i = 0; q.n = len(idx)
        return q


def rms_to_hT(p, c, pp, X, xkey, tile, hT, hkey, col0, scr, idx):
    sq = scr["sq"][idx % 2]; ss = scr["ss"][idx % 2]; hb = scr["hb"][idx % 2]
    kq = ("sq", idx % 2); ks = ("ss", idx % 2); kh = ("hb", idx % 2)
    xt = X[:, tile, :]
    p.act(sq[:], xt, AF.Square, r=[xkey], w=[kq, ks], accum_out=ss[:, 0:1])
    p.ts("dve", ss[:, 1:2], ss[:, 0:1], 1.0 / D, ALU.mult, NORM_EPS, ALU.add, r=[ks], w=[ks])
    p.act(ss[:, 2:3], ss[:, 1:2], AF.Sqrt, r=[ks], w=[ks])
    p.op("dve", lambda e: e.reciprocal(out=ss[:, 3:4], in_=ss[:, 2:3]), r=[ks], w=[ks])
    p.ts("dve", hb[:], xt, ss[:, 3:4], ALU.mult, r=[xkey, ks], w=[kh])
    pt, pk = pp.get()
    ptb = pt.bitcast(BF16)
    for dk in range(NK):
        p.tr(ptb[:, dk * 128:(dk + 1) * 128], hb[:, dk * 128:(dk + 1) * 128], c["identb"][:], r=[kh, "identb"], w=[pk])
    p.cp("act", hT[:, :, col0:col0 + 128], ptb.rearrange("p (k t) -> p k t", k=NK), r=[pk], w=[hkey])


B_INPUTS = [("w_out", [D, D]), ("g_ffn0", [D]), ("up0", [D, DFF]), ("dn0", [DFF, D]), ("g_mix1", [D]), ("pw1", [D, 2 * D]),
            ("pw1_b", [2 * D]), ("dw", [CW, D]), ("dw_b", [D]), ("ln_w", [D]), ("ln_b", [D]), ("pw2", [D, D]), ("pw2_b", [D]),
            ("g_ffn1", [D]), ("up1", [D, DFF]), ("dn1", [DFF, D]), ("g_fin", [D]), ("hmask", [128, 1])]


def build_B(nc, NT=17):
    NTOK = NT * 128
    p = Prog(nc)
    dr = lambda name, shape, dt=F32, kind="ExternalInput": nc.dram_tensor(name, list(shape), dt, kind=kind).ap()
    T = {name: dr(name, shape) for name, shape in B_INPUTS}
    T["xin"] = dr("xin", [NTOK, D])
    T["yT"] = dr("yT", [D, NTOK], BF16)
    T["out"] = dr("out", [NTOK - 128, D], kind="ExternalOutput")
    c = make_consts(p)
    pp = PsumPool(p)
    s_st = emit_B(p, c, pp, T, NT)
    p.finalize(final_sems=[s_st])
    return p


def emit_B(p, c, pp, T, NT=17):
    NTOK = NT * 128
    NMAIN = NTOK - 128
    xin = T["xin"]; yT = T["yT"]; hmask = T["hmask"]; w_out = T["w_out"]
    g_ffn0 = T["g_ffn0"]; up0 = T["up0"]; dn0 = T["dn0"]
    g_mix1 = T["g_mix1"]; pw1 = T["pw1"]; pw1_b = T["pw1_b"]
    dw = T["dw"]; dw_b = T["dw_b"]; ln_w = T["ln_w"]; ln_b = T["ln_b"]
    pw2 = T["pw2"]; pw2_b = T["pw2_b"]
    g_ffn1 = T["g_ffn1"]; up1 = T["up1"]; dn1 = T["dn1"]
    g_fin = T["g_fin"]
    out = T["out"]

    s_ld = p.sem("s_ld"); s_w = [p.sem("s_w0"), p.sem("s_w1"), p.sem("s_w2"), p.sem("s_w3")]
    s_st = p.sem("s_st")

    X = p.sb("X", [128, NT, D], F32)
    vecs = p.sb("vecs", [128, 8 * NK + 2 * NK], F32)

    def colvec(i, src, n=NK):
        ap = vecs[:, i:i + n]
        p.dma("sp", ap, src.rearrange("(k p) -> p k", p=128), s_ld, w=["vecs"], allow_slow_non_contiguous=True)
        return ap
    V = {}
    off = 0
    for name, src, n in [("g_ffn0", g_ffn0, NK), ("g_mix1", g_mix1, NK), ("pw1_b", pw1_b, 2 * NK), ("dw_b", dw_b, NK),
                         ("ln_w", ln_w, NK), ("ln_b", ln_b, NK), ("g_ffn1", g_ffn1, NK)]:
        V[name] = colvec(off, src, n); off += n
    rows = p.sb("rows", [128, 2, D], F32)
    p.dma("sp", rows[:, 0, :], pw2_b.partition_broadcast(128), s_ld, w=["rows"])
    p.dma("sp", rows[:, 1, :], g_fin.partition_broadcast(128), s_ld, w=["rows"])
    hm = p.sb("hm", [128, 1], F32)
    p.dma("sp", hm[:], hmask, s_ld, w=["hm"])
    xv = xin.rearrange("(n p) d -> p n d", p=128)
    for n in range(NT):
        p.dma("sp" if n % 2 == 0 else "act", X[:, n, :], xv[:, n, :], s_ld, w=[("X", n)])

    scr_store = {}

    def ffn(ph, hT, tiles, g_col, up, dn, tagp):
        SL = 512
        nsl = DFF // SL
        stU = [ph.sb(f"{tagp}stU{i}", [128, NK, SL], F32) for i in range(1)] * 2
        stD = [ph.sb(f"{tagp}stD{i}", [128, SL // 128, D], F32) for i in range(1)] * 2
        WU = [ph.sb(f"{tagp}WU{i}", [128, NK, SL], BF16) for i in range(2)]
        WD = [ph.sb(f"{tagp}WD{i}", [128, SL // 128, D], BF16) for i in range(2)]
        aT = [ph.sb(f"{tagp}aT{i}", [128, SL // 128, 512], BF16) for i in range(2)]
        rl = [ph.sb(f"{tagp}rl{i}", [128, 512], F32) for i in range(2)]
        upv = up.rearrange("(k p) f -> p k f", p=128)
        dnv = dn.rearrange("(k p) d -> p k d", p=128)
        groups = []
        i = 0
        while i < len(tiles):
            groups.append(tiles[i:i + 4]); i += 4
        ai = 0; ri = 0
        ppU = pp.sub([0, 1, 2]); ppD = pp.sub([3, 4, 5, 6, 7])
        def load(s):
            b = s % 2
            p.dma("sp", stU[b][:], upv[:, :, s * SL:(s + 1) * SL], s_w[0], w=[(tagp, "stU", 0)])
            p.dma("sp", stD[b][:], dnv[:, s * (SL // 128):(s + 1) * (SL // 128), :], s_w[2], w=[(tagp, "stD", 0)])

        def cast(s):
            b = s % 2
            for dk in range(NK):
                if dk % 2 == 0:
                    p.ts("dve", WU[b][:, dk, :], stU[b][:, dk, :], g_col[:, dk:dk + 1], ALU.mult,
                         r=[(tagp, "stU", 0), "vecs"], w=[(tagp, "WU", b)])
                else:
                    p.act(WU[b][:, dk, :], stU[b][:, dk, :], AF.Copy, r=[(tagp, "stU", 0), "vecs"], w=[(tagp, "WU", b)], scale=g_col[:, dk:dk + 1])
            for f_ in range(SL // 128):
                p.cp("dve" if f_ % 2 == 0 else "act", WD[b][:, f_, :], stD[b][:, f_, :], r=[(tagp, "stD", 0)], w=[(tagp, "WD", b)])
        load(0)
        cast(0)
        for s in range(nsl):
            b = s % 2
            if s + 1 < nsl:
                load(s + 1)
            for grp in groups:
                n = len(grp) * 128
                c0 = grp[0] * 128
                a = aT[ai % 2]; ka = (tagp, "aT", ai % 2); ai += 1
                for ft in range(SL // 128):
                    pt, pk = ppU.get()
                    for dk in range(NK):
                        p.mm(pt[:, 0:n], WU[b][:, dk, ft * 128:(ft + 1) * 128], hT[:, dk, c0:c0 + n], dk == 0, dk == NK - 1,
                             r=[(tagp, "WU", b), (tagp, "hT")], w=[pk])
                    r_ = rl[ri % 2]; kr = (tagp, "rl", ri % 2); ri += 1
                    p.act(r_[:, 0:n], pt[:, 0:n], AF.Relu, r=[pk], w=[kr])
                    p.act(a[:, ft, 0:n], r_[:, 0:n], AF.Square, r=[kr], w=[ka])
                for ti, t in enumerate(grp):
                    for half in range(2):
                        pt, pk = ppD.get()
                        for ft in range(SL // 128):
                            p.mm(pt[:], a[:, ft, ti * 128:(ti + 1) * 128], WD[b][:, ft, half * 512:(half + 1) * 512],
                                 ft == 0, ft == SL // 128 - 1, r=[ka, (tagp, "WD", b)], w=[pk])
                        xs = X[:, t, half * 512:(half + 1) * 512]
                        p.tt("dve", xs, xs, pt[:], ALU.add, r=[pk, ("X", t)], w=[("X", t)])
            if s + 1 < nsl:
                cast(s + 1)

    def norm_all(ph, tiles, hT, hkey, scr):
        for i, t in enumerate(tiles):
            rms_to_hT(p, c, pp, X, ("X", t), t, hT, hkey, t * 128, scr, i)

    def mk_scr(ph, tag):
        return {"sq": [ph.sb(f"{tag}sq{i}", [128, D], BF16) for i in range(2)],
                "ss": [ph.sb(f"{tag}ss{i}", [128, 4], F32) for i in range(2)],
                "hb": [ph.sb(f"{tag}hb{i}", [128, D], BF16) for i in range(2)]}

    with Phase(p) as ph:
        hT = ph.sb("hT0", [128, NK, NTOK], BF16)
        scr = mk_scr(ph, "p1")
        with Phase(p) as ph2:
            yv = yT.rearrange("(k p) t -> p k t", p=128)
            for k in range(NK):
                p.dma("sp" if k % 2 == 0 else "act", hT[:, k, :], yv[:, k, :], s_ld, w=[("p1", "hT")])
            wst = [ph2.sb(f"wst{i}", [128, NK, 512], F32) for i in range(2)]
            woB = ph2.sb("woB", [128, NK, D], BF16)
            wov = w_out.rearrange("(k p) d -> p k d", p=128)
            for h in range(2):
                p.dma("sp", wst[h][:], wov[:, :, h * 512:(h + 1) * 512], s_w[h], w=[("wst", h)])
                for dk in range(NK):
                    p.cp("act" if dk % 2 else "dve", woB[:, dk, h * 512:(h + 1) * 512], wst[h][:, dk, :], r=[("wst", h)], w=["woB"])
            for t in range(NT):
                for half in range(2):
                    pt, pk = pp.get()
                    for k in range(NK):
                        p.mm(pt[:], hT[:, k, t * 128:(t + 1) * 128], woB[:, k, half * 512:(half + 1) * 512], k == 0, k == NK - 1,
                             r=[("p1", "hT"), "woB"], w=[pk])
                    xs = X[:, t, half * 512:(half + 1) * 512]
                    p.tt("dve", xs, xs, pt[:], ALU.add, r=[pk, ("X", t)], w=[("X", t)])
        norm_all(ph, list(range(NT)), hT, ("p1", "hT"), scr)
        with Phase(p) as ph2:
            ffn(ph2, hT, list(range(NT)), V["g_ffn0"], up0, dn0, "p1")

    with Phase(p) as ph:
        scr = mk_scr(ph, "p2")
        pw1B = ph.sb("pw1B", [128, NK, 2 * D], BF16)
        pw2B = ph.sb("pw2B", [128, NK, D], BF16)
        with Phase(p) as ph2:
            wst = [ph2.sb(f"wst2{i}", [128, NK, 512], F32) for i in range(2)]
            pw1v = pw1.rearrange("(k p) c -> p k c", p=128)
            pw2v = pw2.rearrange("(k p) c -> p k c", p=128)
            for j in range(4):
                b = j % 2
                p.dma("sp", wst[b][:], pw1v[:, :, j * 512:(j + 1) * 512], s_w[b], w=[("wst2", b)])
                for dk in range(NK):
                    if dk % 2:
                        p.act(pw1B[:, dk, j * 512:(j + 1) * 512], wst[b][:, dk, :], AF.Copy, r=[("wst2", b), "vecs"], w=["pw1B"],
                              scale=V["g_mix1"][:, dk:dk + 1])
                    else:
                        p.ts("dve", pw1B[:, dk, j * 512:(j + 1) * 512], wst[b][:, dk, :], V["g_mix1"][:, dk:dk + 1], ALU.mult,
                             r=[("wst2", b), "vecs"], w=["pw1B"])
            for j in range(2):
                b = j % 2
                p.dma("sp", wst[b][:], pw2v[:, :, j * 512:(j + 1) * 512], s_w[b], w=[("wst2", b)])
                for dk in range(NK):
                    p.cp("act" if dk % 2 else "dve", pw2B[:, dk, j * 512:(j + 1) * 512], wst[b][:, dk, :], r=[("wst2", b)], w=["pw2B"])
        dwS = ph.sb("dwS", [CW, D], F32)
        dwT = ph.sb("dwT", [128, NK, 32], F32)
        p.dma("sp", dwS[:], dw, s_ld, w=["dwS"])
        for ct in range(NK):
            pt, pk = pp.get()
            p.tr(pt[:, 0:CW], dwS[:, ct * 128:(ct + 1) * 128], c["identf"][0:CW, 0:CW], r=["dwS", "identf"], w=[pk])
            p.cp("dve", dwT[:, ct, 0:CW], pt[:, 0:CW], r=[pk], w=["dwT"])
        UH = ph.sb("UH", [128, NK, CW - 1], F32)
        hTg = ph.sb("hTg", [128, NK, 512], BF16)
        uT = [ph.sb(f"uT{i}", [128, CW - 1 + 512], F32) for i in range(2)]
        sg = [ph.sb(f"sg{i}", [128, 512], F32) for i in range(2)]
        zT = ph.sb("zT", [128, NK, 512], F32)
        zq = [ph.sb(f"zq{i}", [128, 512], F32) for i in range(2)]
        st = ph.sb("lnst", [128, 4, 512], F32)
        zn = [ph.sb(f"zn{i}", [128, 512], F32) for i in range(2)]
        z2b = [ph.sb(f"z2b{i}", [128, 512], F32) for i in range(2)]
        tpb = [ph.sb(f"tpb{i}", [128, 512], F32) for i in range(2)]
        KD = 23
        sT = ph.sb("sT", [128, NK, 512], BF16)
        groups = [[0]] + [list(range(1 + 4 * g, 1 + 4 * g + 4)) for g in range((NT - 1) // 4)]
        ui = 0
        for gi, grp in enumerate(groups):
            n = len(grp) * 128
            for i, t in enumerate(grp):
                rms_to_hT(p, c, pp, X, ("X", t), t, hTg, "hTg", i * 128, scr, i)
            for ct in range(NK):
                pa, pka = pp.get()
                pb, pkb = pp.get()
                for dk in range(NK):
                    p.mm(pa[:, 0:n], pw1B[:, dk, ct * 128:(ct + 1) * 128], hTg[:, dk, 0:n], dk == 0, dk == NK - 1, r=["pw1B", "hTg"], w=[pka])
                for dk in range(NK):
                    p.mm(pb[:, 0:n], pw1B[:, dk, D + ct * 128:D + (ct + 1) * 128], hTg[:, dk, 0:n], dk == 0, dk == NK - 1, r=["pw1B", "hTg"], w=[pkb])
                u = uT[ui % 2]; ku = ("uT", ui % 2); s_ = sg[ui % 2]; ksg = ("sg", ui % 2); ui += 1
                p.act(s_[:, 0:n], pb[:, 0:n], AF.Sigmoid, r=[pkb, "vecs"], w=[ksg], bias=V["pw1_b"][:, NK + ct:NK + ct + 1])
                if gi > 0:
                    p.cp("pool", u[:, 0:CW - 1], UH[:, ct, :], r=[("UH", ct)], w=[ku])
                p.stt(u[:, CW - 1:CW - 1 + n], pa[:, 0:n], V["pw1_b"][:, ct:ct + 1], s_[:, 0:n], ALU.add, ALU.mult, r=[pka, ksg, "vecs"], w=[ku])
                if gi == 0:
                    p.ts("dve", UH[:, ct, :], u[:, CW - 1 + n - (CW - 1):CW - 1 + n], hm[:, 0:1], ALU.mult, r=[ku, "hm"], w=[("UH", ct)])
                    continue
                p.cp("pool", UH[:, ct, :], u[:, n:n + CW - 1], r=[ku], w=[("UH", ct)])
                z = zT[:, ct, 0:n]
                p.ts("dve", z, u[:, 0:n], dwT[:, ct, 0:1], ALU.mult, V["dw_b"][:, ct:ct + 1], ALU.add, r=[ku, "dwT", "vecs"], w=[("zT", ct)])
                for k in range(1, KD):
                    p.stt(z, u[:, k:k + n], dwT[:, ct, k:k + 1], z, ALU.mult, ALU.add, r=[ku, "dwT", ("zT", ct)], w=[("zT", ct)])
                z2 = z2b[ct % 2][:, 0:n]; kz2 = ("z2b", ct % 2)
                for k in range(KD, CW):
                    if k == KD:
                        p.act(z2, u[:, k:k + n], AF.Copy, r=[ku, "dwT"], w=[kz2], scale=dwT[:, ct, k:k + 1])
                    else:
                        tp_ = tpb[k % 2][:, 0:n]; ktp = ("tpb", k % 2)
                        p.act(tp_, u[:, k:k + n], AF.Copy, r=[ku, "dwT"], w=[ktp], scale=dwT[:, ct, k:k + 1])
                        p.tt("pool", z2, z2, tp_, ALU.add, r=[kz2, ktp], w=[kz2])
                p.tt("pool", z, z, z2, ALU.add, r=[("zT", ct), kz2], w=[("zT", ct)])
            if gi == 0:
                continue
            p1, pk1 = pp.get()
            p2, pk2 = pp.get()
            for ct in range(NK):
                p.mm(p1[:, 0:n], c["onesf"][:], zT[:, ct, 0:n], ct == 0, ct == NK - 1, r=["onesf", ("zT", ct)], w=[pk1])
            for ct in range(NK):
                q = zq[ct % 2]; kq = ("zq", ct % 2)
                p.act(q[:, 0:n], zT[:, ct, 0:n], AF.Square, r=[("zT", ct)], w=[kq])
                p.mm(p2[:, 0:n], c["onesf"][:], q[:, 0:n], ct == 0, ct == NK - 1, r=["onesf", kq], w=[pk2])
            mean = st[:, 0, 0:n]; var = st[:, 1, 0:n]; tmp = st[:, 2, 0:n]; rstd = st[:, 3, 0:n]
            p.ts("dve", mean, p1[:, 0:n], 1.0 / D, ALU.mult, r=[pk1], w=["lnst"])
            p.tt("dve", tmp, mean, mean, ALU.mult, r=["lnst"], w=["lnst"])
            p.stt(var, p2[:, 0:n], 1.0 / D, tmp, ALU.mult, ALU.subtract, r=[pk2, "lnst"], w=["lnst"])
            p.ts("dve", var, var, LN_EPS, ALU.add, r=["lnst"], w=["lnst"])
            p.act(tmp, var, AF.Sqrt, r=["lnst"], w=["lnst"])
            p.op("dve", lambda e, rstd=rstd, tmp=tmp: e.reciprocal(out=rstd, in_=tmp), r=["lnst"], w=["lnst"])
            for ct in range(NK):
                zz = zn[ct % 2]; kz = ("zn", ct % 2)
                p.tt("dve", zz[:, 0:n], zT[:, ct, 0:n], mean, ALU.subtract, r=[("zT", ct), "lnst"], w=[kz])
                p.tt("pool", zz[:, 0:n], zz[:, 0:n], rstd, ALU.mult, r=[kz, "lnst"], w=[kz])
                p.act(sT[:, ct, 0:n], zz[:, 0:n], AF.Silu, r=[kz, "vecs"], w=["sT"],
                      scale=V["ln_w"][:, ct:ct + 1], bias=V["ln_b"][:, ct:ct + 1])
            for ti, t in enumerate(grp):
                for half in range(2):
                    pt, pk = pp.get()
                    for ct in range(NK):
                        p.mm(pt[:], sT[:, ct, ti * 128:(ti + 1) * 128], pw2B[:, ct, half * 512:(half + 1) * 512], ct == 0, ct == NK - 1,
                             r=["sT", "pw2B"], w=[pk])
                    xs = X[:, t, half * 512:(half + 1) * 512]
                    p.tt("dve", xs, xs, pt[:], ALU.add, r=[pk, ("X", t)], w=[("X", t)])
                    p.tt("pool", xs, xs, rows[:, 0, half * 512:(half + 1) * 512], ALU.add, r=["rows", ("X", t)], w=[("X", t)])

    main = list(range(1, NT))
    with Phase(p) as ph:
        hT = ph.sb("hT1", [128, NK, NTOK], BF16)
        scr = mk_scr(ph, "p3")
        norm_all(ph, main, hT, ("p3", "hT"), scr)
        with Phase(p) as ph2:
            ffn(ph2, hT, main, V["g_ffn1"], up1, dn1, "p3")
    with Phase(p) as ph:
        sq = [ph.sb(f"fsq{i}", [128, D], F32) for i in range(2)]
        ss = [ph.sb(f"fss{i}", [128, 4], F32) for i in range(2)]
        ov = out.rearrange("(n p) d -> p n d", p=128)
        for i, t in enumerate(main):
            b = i % 2
            xt = X[:, t, :]
            p.act(sq[b][:], xt, AF.Square, r=[("X", t)], w=[("fsq", b), ("fss", b)], accum_out=ss[b][:, 0:1])
            p.ts("dve", ss[b][:, 1:2], ss[b][:, 0:1], 1.0 / D, ALU.mult, NORM_EPS, ALU.add, r=[("fss", b)], w=[("fss", b)])
            p.act(ss[b][:, 2:3], ss[b][:, 1:2], AF.Sqrt, r=[("fss", b)], w=[("fss", b)])
            p.op("dve", lambda e, b=b: e.reciprocal(out=ss[b][:, 3:4], in_=ss[b][:, 2:3]), r=[("fss", b)], w=[("fss", b)])
            p.stt(sq[b][:], xt, ss[b][:, 3:4], rows[:, 1, :], ALU.mult, ALU.mult, r=[("X", t), ("fss", b), "rows"], w=[("fsq", b)])
            p.dma("sp", ov[:, t - 1, :], sq[b][:], s_st, r=[("fsq", b)], w=[("out", t)])
    return s_st


C = 64
TB = 512
NCH = TB // C
DEC = 0.6065306597126334
GN_EPS = 64e-5


def build_A(nc, NB=16):
    T = NB * TB
    p = Prog(nc)
    dr = lambda name, shape, dt=F32, kind="ExternalInput": nc.dram_tensor(name, list(shape), dt, kind=kind).ap()
    xb = dr("xb", [T, D])
    wsel = dr("wsel", [D, 1024])
    g_mix = dr("g_mix", [D])
    cv = dr("cv", [128, 16])
    w2 = dr("w2", [64, 128]); a2 = dr("a2", [64, 128]); g2 = dr("g2", [128, 128])
    attb = dr("attb", [2, 128, 640])
    yT = dr("yT", [256, T], BF16, kind="ExternalOutput")

    s_ld = p.sem("s_ld"); s_x = [p.sem("s_x0"), p.sem("s_x1")]; s_w = [p.sem("s_w0"), p.sem("s_w1")]
    s_st = p.sem("s_st")
    c = make_consts(p)
    pp = PsumPool(p, 7)
    psD = p.ps("psD", [128, 512], F32)

    CV = p.sb("CV", [128, 16], F32)
    p.dma("sp", CV[:], cv, s_ld, w=["CV"])
    MU = lambda ct: CV[:, ct:ct + 1]
    W0, A0, KK, KA, RK, LNW, LNB = [CV[:, 5 + i:6 + i] for i in range(7)]
    OMKA = CV[:, 12:13]
    p.ts("dve", OMKA, KA, -1.0, ALU.mult, 1.0, ALU.add, r=["CV"], w=["CV"])
    gcol = p.sb("gcol", [128, NK], F32)
    p.dma("sp", gcol[:], g_mix.rearrange("(k p) -> p k", p=128), s_ld, w=["gcol"], allow_slow_non_contiguous=True)
    W2 = p.sb("W2", [128, 128], F32); A2 = p.sb("A2", [128, 128], F32); G2 = p.sb("G2", [128, 128], F32)
    p.memset("pool", W2[:], 0.0, w=["W2"])
    p.memset("pool", A2[:], 0.0, w=["A2"])
    p.dma("sp", W2[0:64, :], w2, s_ld, w=["W2"])
    p.dma("sp", A2[64:128, :], a2, s_ld, w=["A2"])
    p.dma("sp", G2[:], g2, s_ld, w=["G2"])
    ATB = p.sb("ATB", [128, 2, 640], F32)
    for h in range(2):
        p.dma("sp", ATB[:, h, :], attb[h], s_ld, w=["ATB"])
    bones = p.sb("bones", [128, 128], F32)
    p.memset("pool", bones[:], 0.0, w=["bones"])
    p.memset("pool", bones[0:64, 0:64], 1.0, w=["bones"])
    p.memset("pool", bones[64:128, 64:128], 1.0, w=["bones"])
    rmask = p.sb("rmask", [128, TB], F32)
    p.memset("pool", rmask[:], 1.0, w=["rmask"])
    p.memset("pool", rmask.rearrange("p (c t) -> p c t", t=C)[:, :, 0:1], 0.0, w=["rmask"])
    HB = 4
    mU = p.sb("mU", [128, HB, 128], F32); mL = p.sb("mL", [128, HB, 64], F32); idn = p.sb("idn", [128, HB, 64], F32)
    for t_, name in ((mU, "mU"), (mL, "mL"), (idn, "idn")):
        p.memset("pool", t_[:], 1.0, w=[name])
    for h in range(2):
        hs = slice(64 * h, 64 * h + 64)
        p.op("pool", lambda e, hs=hs: e.affine_select(out=mU[hs, :, 0:64], in_=mU[hs, :, 0:64], pattern=[[0, HB], [1, 64]],
                                                      compare_op=ALU.is_gt, fill=0.0, base=0, channel_multiplier=-1), r=["mU"], w=["mU"])
        p.op("pool", lambda e, hs=hs: e.affine_select(out=mU[hs, :, 64:128], in_=mU[hs, :, 64:128], pattern=[[0, HB], [1, 64]],
                                                      compare_op=ALU.is_ge, fill=0.0, base=0, channel_multiplier=-1), r=["mU"], w=["mU"])
        p.op("pool", lambda e, hs=hs: e.affine_select(out=mL[hs, :, :], in_=mL[hs, :, :], pattern=[[0, HB], [-1, 64]],
                                                      compare_op=ALU.is_gt, fill=0.0, base=0, channel_multiplier=1), r=["mL"], w=["mL"])
        p.op("pool", lambda e, hs=hs: e.affine_select(out=idn[hs, :, :], in_=idn[hs, :, :], pattern=[[0, HB], [-1, 64]],
                                                      compare_op=ALU.is_equal, fill=0.0, base=0, channel_multiplier=1), r=["idn"], w=["idn"])

    WB = p.sb("WB", [128, NK, 1024], BF16)
    with Phase(p) as ph:
        wst = [ph.sb(f"wstA{i}", [128, NK, 512], F32) for i in range(2)]
        wv = wsel.rearrange("(k p) c -> p k c", p=128)
        for hf in range(2):
            p.dma("sp", wst[hf][:], wv[:, :, hf * 512:(hf + 1) * 512], s_w[hf], w=[("wstA", hf)])
            for dk in range(NK):
                p.ts("pool" if dk % 2 else "dve", WB[:, dk, hf * 512:(hf + 1) * 512], wst[hf][:, dk, :], gcol[:, dk:dk + 1], ALU.mult,
                     r=[("wstA", hf), "gcol"], w=["WB"])

    def t5(name, dt=F32, n=TB):
        return p.sb(name, [128, n], dt)
    XT = [p.sb(f"XT{i}", [128, 4, D], F32) for i in range(2)]
    sq = p.sb("sqA", [128, D], BF16); ssb = p.sb("ssA", [128, 4], F32); hb = p.sb("hbA", [128, D], BF16)
    hT = p.sb("hTA", [128, NK, TB], BF16)
    PJ = [[p.sb(f"PJ{i}_{ct}", [128, 1 + TB], F32) for ct in range(5)] for i in range(2)]
    for i in range(2):
        for ct in range(5):
            p.memset("pool", PJ[i][ct][:, 0:1], 0.0, w=[("PJ", i, ct)])
    qT = t5("qT", BF16)
    Kring = p.sb("Kring", [128, 2 * TB], BF16)
    Vring = p.sb("Vring", [128, 8, 128], BF16)
    tmp = t5("tmpA"); Pm = [t5(f"Pm{ct}") for ct in range(5)]
    txw = p.sb("txw", [128, TB], F32); sxg = t5("sxg")
    sgm = t5("sgm"); av = t5("av"); gv = t5("gv")
    kk = t5("kk"); kk2 = t5("kk2"); rn = t5("rn"); kmod = t5("kmod"); beta = t5("beta"); bon = t5("bon")
    cs = t5("cs"); csd = t5("csd"); dcs = t5("dcs")
    E1 = t5("E1"); E2 = t5("E2"); E3 = t5("E3"); E4 = t5("E4")
    AR = p.sb("AR", [128, NCH, 2, C], BF16)
    Kt = t5("Kt", BF16); Bt = t5("Bt", BF16); Khc = t5("Khc", BF16); Bhc = t5("Bhc", BF16); vB = t5("vB", BF16)
    Vt = p.sb("Vt", [128, NCH, C], BF16); Kh = p.sb("Kh", [128, NCH, C], BF16); Bh = p.sb("Bh", [128, NCH, C], BF16)
    A1 = p.sb("A1", [128, NCH, 128], BF16); A2m = p.sb("A2m", [128, NCH, 128], BF16)
    Nm = [[p.sb(f"Nm{hb_}_{i}", [128, HB, C], BF16) for i in range(2)] for hb_ in range(2)]
    Mm = [[p.sb(f"Mm{hb_}_{i}", [128, HB, C], BF16) for i in range(2)] for hb_ in range(2)]
    Pq = [[p.sb(f"Pq{hb_}_{i}", [128, HB, C], BF16) for i in range(2)] for hb_ in range(2)]
    Qq = [[p.sb(f"Qq{hb_}_{i}", [128, HB, C], BF16) for i in range(2)] for hb_ in range(2)]
    TmT = p.sb("TmT", [128, NCH, C], BF16)
    Hf = p.sb("Hf", [128, C], F32); Hb = [p.sb(f"Hb{i}", [128, C], BF16) for i in range(2)]
    p.memset("pool", Hf[:], 0.0, w=["Hf"])
    p.memset("pool", Hb[0][:], 0.0, w=[("Hb", 0)])
    Xb = p.sb("Xb", [128, C], BF16); Ub = p.sb("Ub", [128, C], BF16)
    Yf = t5("Yf"); yc = t5("yc"); ycq = t5("ycq"); rs = t5("rsA"); yo = t5("yo", BF16)
    S = p.sb("S_att", [128, 640], F32); Pe = p.sb("Pe", [128, 640], F32); Pn = p.sb("Pn", [128, 640], BF16)
    PT = p.sb("PT", [128, 5, 128], BF16)
    ast = p.sb("ast", [128, 4], F32)
    ao = t5("ao", BF16)

    xv = xb.rearrange("(n p) d -> p n d", p=128)

    def load_x(n):
        b = n % 2
        for i in range(4):
            p.dma("sp", XT[b][:, i, :], xv[:, n * 4 + i, :], s_x[b], w=[("XT", b, i)])

    hidx = [0]
    load_x(0)
    for n in range(NB):
        b = n % 2
        if n + 1 < NB:
            load_x(n + 1)
        for i in range(4):
            xt = XT[b][:, i, :]
            p.act(sq[:], xt, AF.Square, r=[("XT", b, i)], w=["sqA", "ssA"], accum_out=ssb[:, 0:1])
            p.ts("dve", ssb[:, 1:2], ssb[:, 0:1], 1.0 / D, ALU.mult, NORM_EPS, ALU.add, r=["ssA"], w=["ssA"])
            p.act(ssb[:, 2:3], ssb[:, 1:2], AF.Sqrt, r=["ssA"], w=["ssA"])
            p.op("dve", lambda e: e.reciprocal(out=ssb[:, 3:4], in_=ssb[:, 2:3]), r=["ssA"], w=["ssA"])
            p.ts("dve", hb[:], xt, ssb[:, 3:4], ALU.mult, r=[("XT", b, i), "ssA"], w=["hbA"])
            pt, pk = pp.get()
            ptb = pt.bitcast(BF16)
            for dk in range(NK):
                p.tr(ptb[:, dk * 128:(dk + 1) * 128], hb[:, dk * 128:(dk + 1) * 128], c["identb"][:], r=["hbA", "identb"], w=[pk])
            p.cp("act", hT[:, :, i * 128:(i + 1) * 128], ptb.rearrange("p (k t) -> p k t", k=NK), r=[pk], w=["hTA"])
        if n > 0:
            p.cp("pool", Kring[:, 0:TB], Kring[:, TB:2 * TB], r=["Kring"], w=["Kring"])
            p.cp("pool", Vring[:, 0:4, :], Vring[:, 4:8, :], r=["Vring"], w=["Vring"])
        for ct in range(7):
            pt, pk = pp.get()
            for dk in range(NK):
                p.mm(pt[:], WB[:, dk, ct * 128:(ct + 1) * 128], hT[:, dk, :], dk == 0, dk == NK - 1, r=["WB", "hTA"], w=[pk])
            if ct < 5:
                p.cp("act" if ct % 2 else "dve", PJ[b][ct][:, 1:1 + TB], pt[:], r=[pk], w=[("PJ", b, ct)])
            elif ct == 5:
                p.cp("act", qT[:], pt[:], r=[pk], w=["qT"])
            else:
                p.cp("dve", Kring[:, TB:2 * TB], pt[:], r=[pk], w=["Kring"])
        pt, pk = pp.get()
        for i in range(4):
            for dk in range(NK):
                p.mm(pt[:, i * 128:(i + 1) * 128], hT[:, dk, i * 128:(i + 1) * 128], WB[:, dk, 7 * 128:8 * 128], dk == 0, dk == NK - 1,
                     r=["WB", "hTA"], w=[pk])
        p.cp("act", Vring[:, 4:8, :], pt.rearrange("p (i v) -> p i v", i=4), r=[pk], w=["Vring"])
        for ct in range(5):
            cur = PJ[b][ct][:, 1:1 + TB]; prv = PJ[b][ct][:, 0:TB]
            p.tt("pool", tmp[:], prv, cur, ALU.subtract, r=[("PJ", b, ct)], w=["tmpA"])
            p.stt(Pm[ct][:], tmp[:], MU(ct), cur, ALU.mult, ALU.add, r=["tmpA", ("PJ", b, ct), "CV"], w=[("Pm", ct)])
            p.cp("pool", PJ[1 - b][ct][:, 0:1], PJ[b][ct][:, TB:TB + 1], r=[("PJ", b, ct)], w=[("PJ", 1 - b, ct)])
        r_, k_, v_ = Pm[0], Pm[1], Pm[2]
        p.act(txw[:], Pm[3][:], AF.Tanh, r=[("Pm", 3)], w=["txw"])
        p.act(sxg[:], Pm[4][:], AF.Sigmoid, r=[("Pm", 4)], w=["sxg"])
        pw_, pkw = pp.get()
        p.mm(pw_[:], W2[:], txw[:], True, True, r=["W2", "txw"], w=[pkw])
        p.act(sgm[:], pw_[:], AF.Sigmoid, r=[pkw, "CV"], w=["sgm"], bias=W0)
        pa_, pka = pp.get()
        p.mm(pa_[:], A2[:], Pm[3][:], True, True, r=["A2", ("Pm", 3)], w=[pka])
        p.act(av[:], pa_[:], AF.Sigmoid, r=[pka, "CV"], w=["av"], bias=A0)
        pg_, pkg = pp.get()
        p.mm(pg_[:], G2[:], sxg[:], True, True, r=["G2", "sxg"], w=[pkg])
        p.cp("act", gv[:], pg_[:], r=[pkg], w=["gv"])
        p.ts("dve", kk[:], k_[:], KK, ALU.mult, r=[("Pm", 1), "CV"], w=["kk"])
        p.tt("pool", kk2[:], kk[:], kk[:], ALU.mult, r=["kk"], w=["kk2"])
        pn_, pkn = pp.get()
        p.mm(pn_[:], bones[:], kk2[:], True, True, r=["bones", "kk2"], w=[pkn])
        p.act(rn[:], pn_[:], AF.Sqrt, r=[pkn], w=["rn"])
        p.ts("dve", rn[:], rn[:], 1e-12, ALU.max, r=["rn"], w=["rn"])
        p.op("dve", lambda e: e.reciprocal(out=rn[:], in_=rn[:]), r=["rn"], w=["rn"])
        p.tt("dve", kk[:], kk[:], rn[:], ALU.mult, r=["kk", "rn"], w=["kk"])
        p.ts("dve", tmp[:], av[:], KA, ALU.mult, OMKA, ALU.add, r=["av", "CV"], w=["tmpA"])
        p.tt("dve", kmod[:], k_[:], tmp[:], ALU.mult, r=[("Pm", 1), "tmpA"], w=["kmod"])
        p.tt("pool", beta[:], kk[:], av[:], ALU.mult, r=["kk", "av"], w=["beta"])
        p.stt(tmp[:], r_[:], RK, kmod[:], ALU.mult, ALU.mult, r=[("Pm", 0), "kmod", "CV"], w=["tmpA"])
        pb_, pkb = pp.get()
        p.mm(pb_[:], bones[:], tmp[:], True, True, r=["bones", "tmpA"], w=[pkb])
        p.tt("dve", bon[:], v_[:], pb_[:], ALU.mult, r=[("Pm", 2), pkb], w=["bon"])
        p.op("dve", lambda e: e.tensor_tensor_scan(out=cs[:], data0=rmask[:], data1=sgm[:], initial=0.0, op0=ALU.mult, op1=ALU.add),
             r=["rmask", "sgm"], w=["cs"])
        p.tt("pool", csd[:], cs[:], sgm[:], ALU.subtract, r=["cs", "sgm"], w=["csd"])
        cs3 = cs.rearrange("p (c t) -> p c t", t=C)
        p.tt("pool", dcs.rearrange("p (c t) -> p c t", t=C), cs3[:, :, C - 1:C].to_broadcast([128, NCH, C]), cs3, ALU.subtract,
             r=["cs"], w=["dcs"])
        p.act(E1[:], cs[:], AF.Exp, r=["cs"], w=["E1"], scale=-DEC)
        p.act(E2[:], cs[:], AF.Exp, r=["cs"], w=["E2"], scale=DEC)
        p.act(E3[:], csd[:], AF.Exp, r=["csd"], w=["E3"], scale=-DEC)
        p.act(E4[:], dcs[:], AF.Exp, r=["dcs"], w=["E4"], scale=-DEC)
        c3 = lambda ap: ap.rearrange("p (c t) -> p c t", t=C)
        p.stt(AR[:, :, 0, :], c3(kk), -1.0, c3(E3), ALU.mult, ALU.mult, r=["kk", "E3"], w=["AR"])
        p.tt("dve", AR[:, :, 1, :], c3(r_), c3(E1), ALU.mult, r=[("Pm", 0), "E1"], w=["AR"])
        p.tt("pool", Kt[:], kmod[:], E2[:], ALU.mult, r=["kmod", "E2"], w=["Kt"])
        p.tt("pool", Bt[:], beta[:], E2[:], ALU.mult, r=["beta", "E2"], w=["Bt"])
        p.tt("dve", Khc[:], kmod[:], E4[:], ALU.mult, r=["kmod", "E4"], w=["Khc"])
        p.tt("pool", Bhc[:], beta[:], E4[:], ALU.mult, r=["beta", "E4"], w=["Bhc"])
        p.cp("act", vB[:], v_[:], r=[("Pm", 2)], w=["vB"])
        for src, dst, ks, kd in ((vB, Vt, "vB", "Vt"), (Khc, Kh, "Khc", "Kh"), (Bhc, Bh, "Bhc", "Bh")):
            pt, pk = pp.get()
            ptb = pt.bitcast(BF16)
            for ch in range(NCH):
                for h in range(2):
                    hs = slice(64 * h, 64 * h + 64)
                    p.tr(ptb[hs, ch * C:(ch + 1) * C], src[hs, ch * C:(ch + 1) * C], c["identb"][hs, hs], r=[ks, "identb"], w=[pk],
                         tile_position=(64 * h, 64 * h))
            p.cp("act", dst.rearrange("p c t -> p (c t)"), ptb[:, 0:TB], r=[pk], w=[kd])
        for hb_ in range(2):
            p1, pk1 = pp.get(); p2, pk2 = pp.get(); p3, pk3 = pp.get()
            for cc in range(HB):
                ch = hb_ * HB + cc
                for h in range(2):
                    hs = slice(64 * h, 64 * h + 64)
                    tp = (64 * h, 64 * h)
                    arh = AR[hs, ch, :, :].rearrange("p a t -> p (a t)")
                    p.mm(p1[hs, cc * 128:(cc + 1) * 128], Kt[hs, ch * C:(ch + 1) * C], arh, True, True, r=["Kt", "AR"], w=[pk1], tile_position=tp)
                    p.mm(p2[hs, cc * 128:(cc + 1) * 128], Bt[hs, ch * C:(ch + 1) * C], arh, True, True, r=["Bt", "AR"], w=[pk2], tile_position=tp)
                    p.mm(p3[hs, cc * C:(cc + 1) * C], AR[hs, ch, 0, :], Bt[hs, ch * C:(ch + 1) * C], True, True, r=["AR", "Bt"], w=[pk3], tile_position=tp)
            chs = slice(hb_ * HB, (hb_ + 1) * HB)
            p.tt("dve", A1[:, chs, :], p1.rearrange("p (c t) -> p c t", c=HB), mU[:], ALU.mult, r=[pk1, "mU"], w=["A1"])
            p.tt("dve", A2m[:, chs, :], p2.rearrange("p (c t) -> p c t", c=HB), mU[:], ALU.mult, r=[pk2, "mU"], w=["A2m"])
            p.tt("dve", Mm[hb_][0][:], p3[:, 0:HB * C].rearrange("p (c t) -> p c t", c=HB), mL[:], ALU.mult, r=[pk3, "mL"], w=[("Mm", hb_, 0)])
            p.cp("pool", Nm[hb_][0][:], A2m[:, chs, 0:C], r=["A2m"], w=[("Nm", hb_, 0)])
            p.tt("pool", Pq[hb_][0][:], Nm[hb_][0][:], idn[:], ALU.add, r=[("Nm", hb_, 0), "idn"], w=[("Pq", hb_, 0)])
            p.tt("pool", Qq[hb_][0][:], Mm[hb_][0][:], idn[:], ALU.add, r=[("Mm", hb_, 0), "idn"], w=[("Qq", hb_, 0)])
        for lvl in range(1, 6):
            o_ = (lvl - 1) % 2; n_ = lvl % 2
            last = lvl == 5
            stage1 = []
            for hb_ in range(2):
                pN, pkN = pp.get()
                pM, pkM = (None, None) if last else pp.get()
                for cc in range(HB):
                    for h in range(2):
                        hs = slice(64 * h, 64 * h + 64); tp = (64 * h, 64 * h)
                        p.mm(pN[hs, cc * C:(cc + 1) * C], Mm[hb_][o_][hs, cc, :], Nm[hb_][o_][hs, cc, :], True, True,
                             r=[("Mm", hb_, o_), ("Nm", hb_, o_)], w=[pkN], tile_position=tp)
                        if not last:
                            p.mm(pM[hs, cc * C:(cc + 1) * C], Nm[hb_][o_][hs, cc, :], Mm[hb_][o_][hs, cc, :], True, True,
                                 r=[("Mm", hb_, o_), ("Nm", hb_, o_)], w=[pkM], tile_position=tp)
                stage1.append((pN, pkN, pM, pkM))
            for hb_ in range(2):
                pN, pkN, pM, pkM = stage1[hb_]
                p.cp("dve", Nm[hb_][n_][:], pN[:, 0:HB * C].rearrange("p (c t) -> p c t", c=HB), r=[pkN], w=[("Nm", hb_, n_)])
                if not last:
                    p.cp("act", Mm[hb_][n_][:], pM[:, 0:HB * C].rearrange("p (c t) -> p c t", c=HB), r=[pkM], w=[("Mm", hb_, n_)])
            stage2 = []
            for hb_ in range(2):
                pP, pkP = pp.get()
                pQ, pkQ = (None, None) if last else pp.get()
                for cc in range(HB):
                    for h in range(2):
                        hs = slice(64 * h, 64 * h + 64); tp = (64 * h, 64 * h)
                        p.mm(pP[hs, cc * C:(cc + 1) * C], Qq[hb_][o_][hs, cc, :], Nm[hb_][n_][hs, cc, :], True, True,
                             r=[("Qq", hb_, o_), ("Nm", hb_, n_)], w=[pkP], tile_position=tp)
                        if not last:
                            p.mm(pQ[hs, cc * C:(cc + 1) * C], Pq[hb_][o_][hs, cc, :], Mm[hb_][n_][hs, cc, :], True, True,
                                 r=[("Pq", hb_, o_), ("Mm", hb_, n_)], w=[pkQ], tile_position=tp)
                stage2.append((pP, pkP, pQ, pkQ))
            for hb_ in range(2):
                pP, pkP, pQ, pkQ = stage2[hb_]
                chs = slice(hb_ * HB, (hb_ + 1) * HB)
                dstP = TmT[:, chs, :] if last else Pq[hb_][n_][:]
                kdP = "TmT" if last else ("Pq", hb_, n_)
                p.tt("dve", dstP, Pq[hb_][o_][:], pP[:, 0:HB * C].rearrange("p (c t) -> p c t", c=HB), ALU.add,
                     r=[pkP, ("Pq", hb_, o_)], w=[kdP])
                if not last:
                    p.tt("dve", Qq[hb_][n_][:], Qq[hb_][o_][:], pQ[:, 0:HB * C].rearrange("p (c t) -> p c t", c=HB), ALU.add,
                         r=[pkQ, ("Qq", hb_, o_)], w=[("Qq", hb_, n_)])
        pY, pkY = psD, "psD"
        for ch in range(NCH):
            hi = hidx[0] % 2; ho = 1 - hi; hidx[0] += 1
            pX, pkX = pp.get()
            for h in range(2):
                hs = slice(64 * h, 64 * h + 64); tp = (64 * h, 64 * h)
                p.mm(pX[hs, 0:C], A1[hs, ch, 0:C], Vt[hs, ch, :], True, False, r=["A1", "Vt"], w=[pkX], tile_position=tp)
                p.mm(pX[hs, 0:C], AR[hs, ch, 0, :], Hb[hi][hs, :], False, True, r=["AR", ("Hb", hi)], w=[pkX], tile_position=tp)
            p.cp("act", Xb[:], pX[:, 0:C], r=[pkX], w=["Xb"])
            pU, pkU = pp.get()
            for h in range(2):
                hs = slice(64 * h, 64 * h + 64); tp = (64 * h, 64 * h)
                p.mm(pU[hs, 0:C], TmT[hs, ch, :], Xb[hs, :], True, True, r=["TmT", "Xb"], w=[pkU], tile_position=tp)
            p.cp("act", Ub[:], pU[:, 0:C], r=[pkU], w=["Ub"])
            pH, pkH = pp.get()
            for h in range(2):
                hs = slice(64 * h, 64 * h + 64); tp = (64 * h, 64 * h)
                p.mm(pH[hs, 0:C], Kh[hs, ch, :], Vt[hs, ch, :], True, False, r=["Kh", "Vt"], w=[pkH], tile_position=tp)
                p.mm(pH[hs, 0:C], Bh[hs, ch, :], Ub[hs, :], False, True, r=["Bh", "Ub"], w=[pkH], tile_position=tp)
            gC = E1[:, ch * C + C - 1:ch * C + C]
            p.stt(Hb[ho][:], Hf[:], gC, pH[:, 0:C], ALU.mult, ALU.add, r=["Hf", "E1", pkH], w=[("Hb", ho)])
            for h in range(2):
                hs = slice(64 * h, 64 * h + 64); tp = (64 * h, 64 * h)
                yo_ = pY[hs, ch * C:(ch + 1) * C]
                p.mm(yo_, Hb[hi][hs, :], AR[hs, ch, 1, :], True, False, r=[("Hb", hi), "AR"], w=[pkY], tile_position=tp)
                p.mm(yo_, Vt[hs, ch, :], A1[hs, ch, C:2 * C], False, False, r=["Vt", "A1"], w=[pkY], tile_position=tp)
                p.mm(yo_, Ub[hs, :], A2m[hs, ch, C:2 * C], False, True, r=["Ub", "A2m"], w=[pkY], tile_position=tp)
            p.stt(Hf[:], Hf[:], gC, pH[:, 0:C], ALU.mult, ALU.add, r=["Hf", "E1", pkH], w=["Hf"])
        p.cp("act", Yf[:], pY[:], r=[pkY], w=["Yf"])
        pm_, pkm = pp.get()
        p.mm(pm_[:], bones[:], Yf[:], True, True, r=["bones", "Yf"], w=[pkm])
        p.stt(yc[:], pm_[:], -1.0 / C, Yf[:], ALU.mult, ALU.add, r=[pkm, "Yf"], w=["yc"])
        p.tt("pool", ycq[:], yc[:], yc[:], ALU.mult, r=["yc"], w=["ycq"])
        pv_, pkv = pp.get()
        p.mm(pv_[:], bones[:], ycq[:], True, True, r=["bones", "ycq"], w=[pkv])
        p.ts("dve", rs[:], pv_[:], 1.0 / C, ALU.mult, GN_EPS, ALU.add, r=[pkv], w=["rsA"])
        p.act(rs[:], rs[:], AF.Sqrt, r=["rsA"], w=["rsA"])
        p.op("dve", lambda e: e.reciprocal(out=rs[:], in_=rs[:]), r=["rsA"], w=["rsA"])
        p.tt("dve", yc[:], yc[:], rs[:], ALU.mult, r=["yc", "rsA"], w=["yc"])
        p.ts("dve", yc[:], yc[:], LNW, ALU.mult, LNB, ALU.add, r=["yc", "CV"], w=["yc"])
        p.tt("pool", yc[:], yc[:], bon[:], ALU.add, r=["yc", "bon"], w=["yc"])
        p.tt("dve", yo[:], yc[:], gv[:], ALU.mult, r=["yc", "gv"], w=["yo"])
        p.dma("sp", yT[0:128, n * TB:(n + 1) * TB], yo[:], s_st, r=["yo"], w=[("yT", 0, n)])
        pO, pkO = psD, "psD"
        for i in range(4):
            kv0 = 0
            if n == 0:
                kv0 = TB - i * 128
            nk = 640 - kv0
            nkt = nk // 128
            for h in range(2):
                hs = slice(64 * h, 64 * h + 64)
                pS1, pkS1 = pp.get(); pS2, pkS2 = pp.get()
                lo = i * 128 + kv0
                n1 = min(nk, 512)
                p.mm(pS1[:, 0:n1], qT[hs, i * 128:(i + 1) * 128], Kring[hs, lo:lo + n1], True, True, r=["qT", "Kring"], w=[pkS1])
                p.stt(S[:, kv0:kv0 + n1], pS1[:, 0:n1], 0.125, ATB[:, h, kv0:kv0 + n1], ALU.mult, ALU.add, r=[pkS1, "ATB"], w=["S_att"])
                if nk > 512:
                    p.mm(pS2[:, 0:128], qT[hs, i * 128:(i + 1) * 128], Kring[hs, lo + 512:lo + 640], True, True, r=["qT", "Kring"], w=[pkS2])
                    p.stt(S[:, 512:640], pS2[:, 0:128], 0.125, ATB[:, h, 512:640], ALU.mult, ALU.add, r=[pkS2, "ATB"], w=["S_att"])
                p.op("dve", lambda e, kv0=kv0: e.tensor_reduce(out=ast[:, 0:1], in_=S[:, kv0:640], axis=AX.X, op=ALU.max), r=["S_att"], w=["ast"])
                p.ts("dve", ast[:, 1:2], ast[:, 0:1], -1.0, ALU.mult, r=["ast"], w=["ast"])
                p.act(Pe[:, kv0:640], S[:, kv0:640], AF.Exp, r=["S_att", "ast"], w=["Pe", "ast"], bias=ast[:, 1:2], accum_out=ast[:, 2:3])
                p.op("dve", lambda e: e.reciprocal(out=ast[:, 3:4], in_=ast[:, 2:3]), r=["ast"], w=["ast"])
                p.ts("dve", Pn[:, kv0:640], Pe[:, kv0:640], ast[:, 3:4], ALU.mult, r=["Pe", "ast"], w=["Pn"])
                ptr, pktr = pp.get()
                ptrb = ptr.bitcast(BF16)
                for kt in range(nkt):
                    c0 = kv0 + kt * 128
                    p.tr(ptrb[:, kt * 128:(kt + 1) * 128], Pn[:, c0:c0 + 128], c["identb"][:], r=["Pn", "identb"], w=[pktr])
                p.cp("act", PT[:, 0:nkt, :], ptrb[:, 0:nkt * 128].rearrange("p (k q) -> p k q", k=nkt), r=[pktr], w=["PT"])
                vt0 = i + (kv0 // 128)
                for kt in range(nkt):
                    p.mm(pO[hs, i * 128:(i + 1) * 128], Vring[:, vt0 + kt, 64 * h:64 * h + 64], PT[:, kt, :], kt == 0, kt == nkt - 1,
                         r=["Vring", "PT"], w=[pkO], tile_position=(0, 64 * h))
        p.cp("act", ao[:], pO[:], r=[pkO], w=["ao"])
        p.dma("sp", yT[128:256, n * TB:(n + 1) * TB], ao[:], s_st, r=["ao"], w=[("yT", 1, n)])
    p.finalize(final_sems=[s_st])
    return p


C = 64
TB = 512
NCH = TB // C
DEC = 0.6065306597126334
GN_EPS = 64e-5
NBLK = 16
FIRST_FULL = 11
KV_BLOCK = 10
HB = 4


def emit_A(p, ph, c, pools, psD, psO, dr_in, yscr, nblk=NBLK, first_full=FIRST_FULL, kv_block=KV_BLOCK, nhp=4):
    xb = dr_in["xb"]; wsel4 = dr_in["wsel"]; g_mix = dr_in["g_mix"]; cv4 = dr_in["cv"]
    w24 = dr_in["w2"]; a24 = dr_in["a2"]; g24 = dr_in["g2"]; attb4 = dr_in["attb"]; amask = dr_in["amask"]
    s_ld = p.sem("a_ld"); s_x = [p.sem(f"a_x{i}") for i in range(4)]; s_w = [p.sem("a_w0"), p.sem("a_w1")]
    s_st = p.sem("a_st")
    s_hs = p.sem("a_hs"); s_hl = [p.sem("a_hl0"), p.sem("a_hl1")]
    hTs = dr_in["hTs"]
    sb = ph.sb
    pp1, pp2, s3 = pools

    gcol = sb("gcol", [128, NK], F32)
    p.dma("sp", gcol[:], g_mix.rearrange("(k p) -> p k", p=128), s_ld, w=["gcol"], allow_slow_non_contiguous=True)
    bones = sb("bones", [128, 128], F32)
    p.memset("pool", bones[:], 0.0, w=["bones"])
    p.memset("pool", bones[0:64, 0:64], 1.0, w=["bones"])
    p.memset("pool", bones[64:128, 64:128], 1.0, w=["bones"])
    rmask = sb("rmask", [128, TB], F32)
    p.memset("pool", rmask[:], 1.0, w=["rmask"])
    p.memset("pool", rmask.rearrange("p (c t) -> p c t", t=C)[:, :, 0:1], 0.0, w=["rmask"])
    mU = sb("mU", [128, HB, 128], F32); mL = sb("mL", [128, HB, 64], F32); idn = sb("idn", [128, HB, 64], F32)
    for t_, name in ((mU, "mU"), (mL, "mL"), (idn, "idn")):
        p.memset("pool", t_[:], 1.0, w=[name])
    for h in range(2):
        hs = slice(64 * h, 64 * h + 64)
        p.op("pool", lambda e, hs=hs: e.affine_select(out=mU[hs, :, 0:64], in_=mU[hs, :, 0:64], pattern=[[0, HB], [1, 64]],
                                                      compare_op=ALU.is_gt, fill=0.0, base=0, channel_multiplier=-1), r=["mU"], w=["mU"])
        p.op("pool", lambda e, hs=hs: e.affine_select(out=mU[hs, :, 64:128], in_=mU[hs, :, 64:128], pattern=[[0, HB], [1, 64]],
                                                      compare_op=ALU.is_ge, fill=0.0, base=0, channel_multiplier=-1), r=["mU"], w=["mU"])
        p.op("pool", lambda e, hs=hs: e.affine_select(out=mL[hs, :, :], in_=mL[hs, :, :], pattern=[[0, HB], [-1, 64]],
                                                      compare_op=ALU.is_gt, fill=0.0, base=0, channel_multiplier=1), r=["mL"], w=["mL"])
        p.op("pool", lambda e, hs=hs: e.affine_select(out=idn[hs, :, :], in_=idn[hs, :, :], pattern=[[0, HB], [-1, 64]],
                                                      compare_op=ALU.is_equal, fill=0.0, base=0, channel_multiplier=1), r=["idn"], w=["idn"])
    AMK = sb("AMK", [128, 4, 640], F32)
    p.dma("sp", AMK[:], amask, s_ld, w=["AMK"])

    def t5(name, dt=F32, n=TB):
        return sb(name, [128, n], dt)
    CV = sb("CV", [128, 16], F32)
    MU = lambda ct: CV[:, ct:ct + 1]
    W0, A0, KK, KA, RK, LNW, LNB = [CV[:, 5 + i:6 + i] for i in range(7)]
    OMKA = CV[:, 12:13]
    W2 = sb("W2", [128, 128], F32); A2 = sb("A2", [128, 128], F32); G2 = sb("G2", [128, 128], F32)
    p.memset("pool", W2[:], 0.0, w=["W2"])
    p.memset("pool", A2[:], 0.0, w=["A2"])
    ATB = sb("ATB", [128, 2, 640], F32)
    WB = sb("WB", [128, NK, 1024], BF16)
    wst = [sb(f"wstA{i}", [128, NK, 128], F32) for i in range(2)]
    XTbig = sb("XTbig", [128, 4, D], F32)
    XT = [XTbig[:, i, :] for i in range(4)]
    hTalt = XTbig[:, 0:2, :].rearrange("p a d -> p (a d)").bitcast(BF16).rearrange("p (k t) -> p k t", k=NK)
    sq = sb("sqA", [128, D], BF16); ssb2 = [sb(f"ssA{i}", [128, 4], F32) for i in range(2)]; hb2 = [sb(f"hbA{i}", [128, D], BF16) for i in range(2)]
    hT = sb("hTA", [128, NK, TB], BF16)
    PJ = [[sb(f"PJ{i}_{ct}", [128, 1 + TB], F32) for ct in range(5)] for i in range(2)]
    qT = t5("qT", BF16)
    Kring = sb("Kring", [128, 2 * TB], BF16)
    Vring = sb("Vring", [128, 8, 128], BF16)
    tmp = t5("tmpA"); Pm = [t5(f"Pm{ct}") for ct in range(5)]
    sgm = t5("sgm"); av = t5("av"); gv2 = [t5("gv0"), t5("gv1")]
    kk = t5("kk"); kmod = t5("kmod"); beta = t5("beta"); bon2 = [t5("bon0"), t5("bon1")]
    cs = t5("cs"); csd = t5("csd"); dcs = t5("dcs")
    E12 = [t5("E1_0"), t5("E1_1")]; E2 = t5("E2"); E3 = t5("E3"); E4 = t5("E4")
    txw, ktxw = E2, "E2"; sxg, ksxg = E3, "E3"; kk2, kkk2 = csd, "csd"; rn, krn = dcs, "dcs"
    Yf, kYf = t5("Yf"), "Yf"; yc, kyc = t5("yc"), "yc"; ycq, kycq = t5("ycq"), "ycq"; rs, krs = t5("rsA"), "rsA"
    AR2 = [sb(f"AR{i}", [128, NCH, 2, C], BF16) for i in range(2)]
    for i in range(2):
        p.memset("pool", AR2[i][:], 0.0, w=[("AR", i)])
    Kt2 = [t5(f"Kt{i}", BF16) for i in range(2)]; Bt2 = [t5(f"Bt{i}", BF16) for i in range(2)]; Khc2 = [t5(f"Khc{i}", BF16) for i in range(2)]
    Bhc2 = [t5(f"Bhc{i}", BF16) for i in range(2)]; vB2 = [t5(f"vB{i}", BF16) for i in range(2)]
    Vt2 = [sb(f"Vt{i}", [128, NCH, C], BF16) for i in range(2)]; Kh2 = [sb(f"Kh{i}", [128, NCH, C], BF16) for i in range(2)]
    Bh2 = [sb(f"Bh{i}", [128, NCH, C], BF16) for i in range(2)]
    A12 = [sb(f"A1_{i}", [128, NCH, 128], BF16) for i in range(2)]; A2m2 = [sb(f"A2m_{i}", [128, NCH, 128], BF16) for i in range(2)]
    NS = [[sb(f"NS{hb_}_{i}", [128, HB, 2, C], BF16) for i in range(2)] for hb_ in range(2)]
    Mm = [[sb(f"Mm{hb_}_{i}", [128, HB, C], BF16) for i in range(2)] for hb_ in range(2)]
    TmT2 = [sb(f"TmT{i}", [128, NCH, C], BF16) for i in range(2)]
    Hf = sb("Hf", [128, C], F32); Hb = [sb(f"Hb{i}", [128, C], BF16) for i in range(2)]
    Xb = sb("Xb", [128, C], BF16); Ub = sb("Ub", [128, C], BF16)
    yo = t5("yo", BF16)
    S = sb("S_att", [128, 640], F32); Pe = sb("Pe", [128, 640], F32); Pn = sb("Pn", [128, 640], BF16)
    PT = sb("PT", [128, 5, 128], BF16)
    ast = sb("ast", [128, 4], F32)
    ao = t5("ao", BF16)

    xv = xb.rearrange("(n p) d -> p n d", p=128)
    xcnt = [0]

    def load_x_tile(n, i):
        j = xcnt[0] % 4; xcnt[0] += 1
        p.dma("sp", XT[j], xv[:, n * 4 + i, :], s_x[j], w=[("XT", j)])
        return j

    for hp in range(nhp):
        p.dma("sp", CV[:], cv4[hp], s_ld, w=["CV"])
        p.ts("dve", OMKA, KA, -1.0, ALU.mult, 1.0, ALU.add, r=["CV"], w=["CV"])
        p.dma("sp", W2[0:64, :], w24[hp], s_ld, w=["W2"])
        p.dma("sp", A2[64:128, :], a24[hp], s_ld, w=["A2"])
        p.dma("sp", G2[:], g24[hp], s_ld, w=["G2"])
        for h in range(2):
            p.dma("sp", ATB[:, h, :], attb4[hp, h], s_ld, w=["ATB"])
        wv = wsel4[hp].rearrange("(k p) c -> p k c", p=128)
        for j in range(8):
            b_ = j % 2
            p.dma("sp", wst[b_][:], wv[:, :, j * 128:(j + 1) * 128], s_w[b_], w=[("wstA", b_)])
            for dk in range(NK):
                if dk % 2:
                    p.act(WB[:, dk, j * 128:(j + 1) * 128], wst[b_][:, dk, :], AF.Copy, r=[("wstA", b_), "gcol"], w=["WB"], scale=gcol[:, dk:dk + 1])
                else:
                    p.ts("dve", WB[:, dk, j * 128:(j + 1) * 128], wst[b_][:, dk, :], gcol[:, dk:dk + 1], ALU.mult,
                         r=[("wstA", b_), "gcol"], w=["WB"])
        p.memset("pool", Hf[:], 0.0, w=["Hf"])
        p.memset("pool", Hb[0][:], 0.0, w=[("Hb", 0)])
        for ct in range(5):
            p.memset("pool", PJ[0][ct][:, 0:1], 0.0, w=[("PJ", 0, ct)])
        hidx = 0
        share = hp > 0

        def load_hT(n):
            if n % 2 == 0:
                p.dma("sp", hT[:].rearrange("p k t -> p (k t)"), hTs[n], s_hl[0], r=[("hTs", n)], w=["hTA"])
            else:
                p.dma("sp", hTalt.rearrange("p k t -> p (k t)"), hTs[n], s_hl[1], r=[("hTs", n)], w=["hTB", ("XT", 0), ("XT", 1)])
        if share:
            load_hT(0)
        else:
            nxt = [load_x_tile(0, i) for i in range(4)]
        for n in range(nblk):
            b = n % 2
            full = n >= first_full
            kvb = full or n == kv_block
            if share:
                hTc, khT = (hT, "hTA") if n % 2 == 0 else (hTalt, "hTB")
                if n + 1 < nblk:
                    load_hT(n + 1)
            else:
                hTc, khT = hT, "hTA"
                cur_x = nxt
            AR, kAR = AR2[b], ("AR", b); E1, kE1 = E12[b], ("E1", b); bon, kbon = bon2[b], ("bon", b); gv, kgv = gv2[b], ("gv", b)
            Vt, kVt = Vt2[b], ("Vt", b); Kh, kKh = Kh2[b], ("Kh", b); Bh, kBh = Bh2[b], ("Bh", b)
            A1, kA1 = A12[b], ("A1", b); A2m, kA2m = A2m2[b], ("A2m", b); TmT, kTmT = TmT2[b], ("TmT", b)
            Kt, kKt = Kt2[b], ("Kt", b); Bt, kBt = Bt2[b], ("Bt", b); Khc, kKhc = Khc2[b], ("Khc", b)
            Bhc, kBhc = Bhc2[b], ("Bhc", b); vB, kvB = vB2[b], ("vB", b)
            nxt = []
            for i in range(0 if share else 4):
                j = cur_x[i]
                xt = XT[j]
                ssb = ssb2[i % 2]; hb = hb2[i % 2]; kss = ("ssA", i % 2); khb = ("hbA", i % 2)
                p.act(sq[:], xt, AF.Square, r=[("XT", j)], w=[kss], accum_out=ssb[:, 0:1])
                p.ts("dve", ssb[:, 1:2], ssb[:, 0:1], 1.0 / D, ALU.mult, NORM_EPS, ALU.add, r=[kss], w=[kss])
                p.act(ssb[:, 2:3], ssb[:, 1:2], AF.Sqrt, r=[kss], w=[kss])
                p.op("dve", lambda e, ssb=ssb: e.reciprocal(out=ssb[:, 3:4], in_=ssb[:, 2:3]), r=[kss], w=[kss])
                p.ts("dve", hb[:], xt, ssb[:, 3:4], ALU.mult, r=[("XT", j), kss], w=[khb])
                if n + 1 < nblk:
                    nxt.append(load_x_tile(n + 1, i))
                pt, pk = pp1.get()
                ptb = pt.bitcast(BF16)
                for dk in range(NK):
                    p.tr(ptb[:, dk * 128:(dk + 1) * 128], hb[:, dk * 128:(dk + 1) * 128], c["identb"][:], r=[khb, "identb"], w=[pk])
                p.cp("act", hT[:, :, i * 128:(i + 1) * 128], ptb.rearrange("p (k t) -> p k t", k=NK), r=[pk], w=["hTA"])
            if not share and nhp > 1:
                p.dma("sp", hTs[n], hT[:].rearrange("p k t -> p (k t)"), s_hs, r=["hTA"], w=[("hTs", n)])
            if full:
                p.cp("pool", Kring[:, 0:TB], Kring[:, TB:2 * TB], r=["Kring"], w=["Kring"])
                p.cp("pool", Vring[:, 0:4, :], Vring[:, 4:8, :], r=["Vring"], w=["Vring"])
            cts = [0, 1, 2, 3, 4, 5, 6] if full else ([1, 2, 3, 6] if kvb else [1, 2, 3])
            for ct in cts:
                pt, pk = pp1.get()
                for dk in range(NK):
                    p.mm(pt[:], WB[:, dk, ct * 128:(ct + 1) * 128], hTc[:, dk, :], dk == 0, dk == NK - 1, r=["WB", khT], w=[pk])
                if ct < 5:
                    p.cp("act" if ct % 2 else "dve", PJ[b][ct][:, 1:1 + TB], pt[:], r=[pk], w=[("PJ", b, ct)])
                elif ct == 5:
                    p.cp("act", qT[:], pt[:], r=[pk], w=["qT"])
                else:
                    p.cp("dve", Kring[:, TB:2 * TB], pt[:], r=[pk], w=["Kring"])
            if kvb:
                pt, pk = pp1.get()
                for i in range(4):
                    for dk in range(NK):
                        p.mm(pt[:, i * 128:(i + 1) * 128], hTc[:, dk, i * 128:(i + 1) * 128], WB[:, dk, 7 * 128:8 * 128], dk == 0, dk == NK - 1,
                             r=["WB", khT], w=[pk])
                p.cp("act", Vring[:, 4:8, :], pt.rearrange("p (i v) -> p i v", i=4), r=[pk], w=["Vring"])
            mcts = [0, 1, 2, 3, 4] if full else [1, 2, 3]
            for ct in range(5):
                if ct in mcts:
                    cur = PJ[b][ct][:, 1:1 + TB]; prv = PJ[b][ct][:, 0:TB]
                    p.tt("pool", Pm[ct][:], prv, cur, ALU.subtract, r=[("PJ", b, ct)], w=[("Pm", ct)])
                    p.stt(Pm[ct][:], Pm[ct][:], MU(ct), cur, ALU.mult, ALU.add, r=[("Pm", ct), ("PJ", b, ct), "CV"], w=[("Pm", ct)])
                    p.cp("pool", PJ[1 - b][ct][:, 0:1], PJ[b][ct][:, TB:TB + 1], r=[("PJ", b, ct)], w=[("PJ", 1 - b, ct)])
                elif n + 1 == first_full:
                    pt, pk = pp1.get()
                    for dk in range(NK):
                        p.mm(pt[:, 0:1], WB[:, dk, ct * 128:(ct + 1) * 128], hTc[:, dk, TB - 1:TB], dk == 0, dk == NK - 1, r=["WB", khT], w=[pk])
                    p.cp("act", PJ[1 - b][ct][:, 0:1], pt[:, 0:1], r=[pk], w=[("PJ", 1 - b, ct)])
            r_, k_, v_ = Pm[0], Pm[1], Pm[2]
            p.act(txw[:], Pm[3][:], AF.Tanh, r=[("Pm", 3)], w=[ktxw])
            pw_, pkw = pp1.get()
            p.mm(pw_[:], W2[:], txw[:], True, True, r=["W2", ktxw], w=[pkw])
            p.act(sgm[:], pw_[:], AF.Sigmoid, r=[pkw, "CV"], w=["sgm"], bias=W0)
            pa_, pka = pp1.get()
            p.mm(pa_[:], A2[:], Pm[3][:], True, True, r=["A2", ("Pm", 3)], w=[pka])
            p.act(av[:], pa_[:], AF.Sigmoid, r=[pka, "CV"], w=["av"], bias=A0)
            if full:
                p.act(sxg[:], Pm[4][:], AF.Sigmoid, r=[("Pm", 4)], w=[ksxg])
                pg_, pkg = pp1.get()
                p.mm(pg_[:], G2[:], sxg[:], True, True, r=["G2", ksxg], w=[pkg])
                p.cp("act", gv[:], pg_[:], r=[pkg], w=[kgv])
            p.ts("dve", kk[:], k_[:], KK, ALU.mult, r=[("Pm", 1), "CV"], w=["kk"])
            p.tt("pool", kk2[:], kk[:], kk[:], ALU.mult, r=["kk"], w=[kkk2])
            pn_, pkn = pp1.get()
            p.mm(pn_[:], bones[:], kk2[:], True, True, r=["bones", kkk2], w=[pkn])
            p.act(rn[:], pn_[:], AF.Sqrt, r=[pkn], w=[krn])
            p.ts("dve", rn[:], rn[:], 1e-12, ALU.max, r=[krn], w=[krn])
            p.op("dve", lambda e: e.reciprocal(out=rn[:], in_=rn[:]), r=[krn], w=[krn])
            p.tt("dve", kk[:], kk[:], rn[:], ALU.mult, r=["kk", krn], w=["kk"])
            p.ts("dve", kmod[:], av[:], KA, ALU.mult, OMKA, ALU.add, r=["av", "CV"], w=["kmod"])
            p.tt("dve", kmod[:], k_[:], kmod[:], ALU.mult, r=[("Pm", 1), "kmod"], w=["kmod"])
            p.tt("pool", beta[:], kk[:], av[:], ALU.mult, r=["kk", "av"], w=["beta"])
            if full:
                p.stt(bon[:], r_[:], RK, kmod[:], ALU.mult, ALU.mult, r=[("Pm", 0), "kmod", "CV"], w=[kbon])
                pb_, pkb = pp1.get()
                p.mm(pb_[:], bones[:], bon[:], True, True, r=["bones", kbon], w=[pkb])
                p.tt("dve", bon[:], v_[:], pb_[:], ALU.mult, r=[("Pm", 2), pkb], w=[kbon])
            p.op("dve", lambda e: e.tensor_tensor_scan(out=cs[:], data0=rmask[:], data1=sgm[:], initial=0.0, op0=ALU.mult, op1=ALU.add),
                 r=["rmask", "sgm"], w=["cs"])
            p.tt("pool", csd[:], cs[:], sgm[:], ALU.subtract, r=["cs", "sgm"], w=["csd"])
            cs3 = cs.rearrange("p (c t) -> p c t", t=C)
            p.tt("pool", dcs.rearrange("p (c t) -> p c t", t=C), cs3[:, :, C - 1:C].to_broadcast([128, NCH, C]), cs3, ALU.subtract,
                 r=["cs"], w=["dcs"])
            p.act(E1[:], cs[:], AF.Exp, r=["cs"], w=[kE1], scale=-DEC)
            p.act(E2[:], cs[:], AF.Exp, r=["cs"], w=["E2"], scale=DEC)
            p.act(E3[:], csd[:], AF.Exp, r=["csd"], w=["E3"], scale=-DEC)
            p.act(E4[:], dcs[:], AF.Exp, r=["dcs"], w=["E4"], scale=-DEC)
            c3 = lambda ap: ap.rearrange("p (c t) -> p c t", t=C)
            p.stt(AR[:, :, 0, :], c3(kk), -1.0, c3(E3), ALU.mult, ALU.mult, r=["kk", "E3"], w=[kAR])
            if full:
                p.tt("dve", AR[:, :, 1, :], c3(r_), c3(E1), ALU.mult, r=[("Pm", 0), kE1], w=[kAR])
            p.tt("pool", Kt[:], kmod[:], E2[:], ALU.mult, r=["kmod", "E2"], w=[kKt])
            p.tt("pool", Bt[:], beta[:], E2[:], ALU.mult, r=["beta", "E2"], w=[kBt])
            p.tt("dve", Khc[:], kmod[:], E4[:], ALU.mult, r=["kmod", "E4"], w=[kKhc])
            p.tt("pool", Bhc[:], beta[:], E4[:], ALU.mult, r=["beta", "E4"], w=[kBhc])
            p.cp("act", vB[:], v_[:], r=[("Pm", 2)], w=[kvB])
            for src, dst, ks, kd in ((vB, Vt, kvB, kVt), (Khc, Kh, kKhc, kKh), (Bhc, Bh, kBhc, kBh)):
                pt, pk = pp2.get()
                ptb = pt.bitcast(BF16)
                for ch in range(NCH):
                    for h in range(2):
                        hs = slice(64 * h, 64 * h + 64)
                        p.tr(ptb[hs, ch * C:(ch + 1) * C], src[hs, ch * C:(ch + 1) * C], c["identb"][hs, hs], r=[ks, "identb"], w=[pk],
                             tile_position=(64 * h, 64 * h))
                p.cp("act", dst.rearrange("p c t -> p (c t)"), ptb[:, 0:TB], r=[pk], w=[kd])
            for hb_ in range(2):
                p1, pk1 = pp2.get(); p2, pk2 = pp2.get(); p3, pk3 = pp2.get()
                for cc in range(HB):
                    ch = hb_ * HB + cc
                    for which in range(3):
                        for h in range(2):
                            hs = slice(64 * h, 64 * h + 64)
                            tp = (64 * h, 64 * h)
                            arh = AR[hs, ch, :, :].rearrange("p a t -> p (a t)")
                            if which == 0:
                                p.mm(p1[hs, cc * 128:(cc + 1) * 128], Kt[hs, ch * C:(ch + 1) * C], arh, True, True, r=[kKt, kAR], w=[pk1], tile_position=tp)
                            elif which == 1:
                                p.mm(p2[hs, cc * 128:(cc + 1) * 128], Bt[hs, ch * C:(ch + 1) * C], arh, True, True, r=[kBt, kAR], w=[pk2], tile_position=tp)
                            else:
                                p.mm(p3[hs, cc * C:(cc + 1) * C], AR[hs, ch, 0, :], Bt[hs, ch * C:(ch + 1) * C], True, True, r=[kAR, kBt], w=[pk3], tile_position=tp)
                chs = slice(hb_ * HB, (hb_ + 1) * HB)
                p.tt("dve", A1[:, chs, :], p1.rearrange("p (c t) -> p c t", c=HB), mU[:], ALU.mult, r=[pk1, "mU"], w=[kA1])
                p.tt("dve", A2m[:, chs, :], p2.rearrange("p (c t) -> p c t", c=HB), mU[:], ALU.mult, r=[pk2, "mU"], w=[kA2m])
                p.tt("dve", Mm[hb_][0][:], p3[:, 0:HB * C].rearrange("p (c t) -> p c t", c=HB), mL[:], ALU.mult, r=[pk3, "mL"], w=[("Mm", hb_, 0)])
                p.cp("pool", NS[hb_][0][:, :, 0, :], A2m[:, chs, 0:C], r=[kA2m], w=[("NS", hb_, 0)])
            for lvl in range(0, 6):
                o_ = lvl % 2; n_ = (lvl + 1) % 2
                first = lvl == 0; last = lvl == 5
                stage = []
                pBb, pkBb = (None, None) if last else pp2.get()
                for hb_ in range(2):
                    pA, pkA = pp2.get()
                    pB, pkB = (None, None) if last else (pBb[:, hb_ * HB * C:(hb_ + 1) * HB * C], pkBb)
                    for cc in range(HB):
                        for h in range(2):
                            hs = slice(64 * h, 64 * h + 64); tp = (64 * h, 64 * h)
                            if first:
                                rhs = NS[hb_][o_][hs, cc, 0, :]; wd = C
                            elif last:
                                rhs = NS[hb_][o_][hs, cc, 1, :]; wd = C
                            else:
                                rhs = NS[hb_][o_][hs, cc, :, :].rearrange("p a t -> p (a t)"); wd = 2 * C
                            p.mm(pA[hs, cc * 128:cc * 128 + wd], Mm[hb_][o_][hs, cc, :], rhs, True, True,
                                 r=[("Mm", hb_, o_), ("NS", hb_, o_)], w=[pkA], tile_position=tp)
                        if not last:
                            for h in range(2):
                                hs = slice(64 * h, 64 * h + 64); tp = (64 * h, 64 * h)
                                p.mm(pB[hs, cc * C:(cc + 1) * C], NS[hb_][o_][hs, cc, 0, :], Mm[hb_][o_][hs, cc, :], True, True,
                                     r=[("Mm", hb_, o_), ("NS", hb_, o_)], w=[pkB], tile_position=tp)
                    stage.append((pA, pkA, pB, pkB))
                for hb_ in range(2):
                    pA, pkA, pB, pkB = stage[hb_]
                    chs = slice(hb_ * HB, (hb_ + 1) * HB)
                    pA3 = pA.rearrange("p (c a t) -> p c a t", c=HB, a=2)
                    if last:
                        p.tt("dve", TmT[:, chs, :], NS[hb_][o_][:, :, 1, :], pA3[:, :, 0, :], ALU.add, r=[pkA, ("NS", hb_, o_)], w=[kTmT])
                        continue
                    p.cp("act", Mm[hb_][n_][:], pB[:, 0:HB * C].rearrange("p (c t) -> p c t", c=HB), r=[pkB], w=[("Mm", hb_, n_)])
                    p.cp("dve", NS[hb_][n_][:, :, 0, :], pA3[:, :, 0, :], r=[pkA], w=[("NS", hb_, n_)])
                    if first:
                        p.tt("pool", NS[hb_][n_][:, :, 1, :], NS[hb_][o_][:, :, 0, :], idn[:], ALU.add, r=[("NS", hb_, o_), "idn"], w=[("NS", hb_, n_)])
                    else:
                        p.tt("dve", NS[hb_][n_][:, :, 1, :], NS[hb_][o_][:, :, 1, :], pA3[:, :, 1, :], ALU.add,
                             r=[pkA, ("NS", hb_, o_)], w=[("NS", hb_, n_)])
            pY, pkY = psD, "psD"
            for ch in range(NCH):
                hi = hidx % 2; ho = 1 - hi; hidx += 1
                pX, pkX = s3[:, 0:C], "s3"
                for h in range(2):
                    hs = slice(64 * h, 64 * h + 64); tp = (64 * h, 64 * h)
                    p.mm(pX[hs, 0:C], A1[hs, ch, 0:C], Vt[hs, ch, :], True, False, r=[kA1, kVt], w=[pkX], tile_position=tp)
                for h in range(2):
                    hs = slice(64 * h, 64 * h + 64); tp = (64 * h, 64 * h)
                    p.mm(pX[hs, 0:C], AR[hs, ch, 0, :], Hb[hi][hs, :], False, True, r=[kAR, ("Hb", hi)], w=[pkX], tile_position=tp)
                p.cp("act", Xb[:], pX[:, 0:C], r=[pkX], w=["Xb"])
                pU, pkU = s3[:, 0:C], "s3"
                for h in range(2):
                    hs = slice(64 * h, 64 * h + 64); tp = (64 * h, 64 * h)
                    p.mm(pU[hs, 0:C], TmT[hs, ch, :], Xb[hs, :], True, True, r=[kTmT, "Xb"], w=[pkU], tile_position=tp)
                p.cp("act", Ub[:], pU[:, 0:C], r=[pkU], w=["Ub"])
                pH, pkH = (s3[:, 0:C], "s3") if full else (psD[:, 0:C], "psD")
                for h in range(2):
                    hs = slice(64 * h, 64 * h + 64); tp = (64 * h, 64 * h)
                    p.mm(pH[hs, 0:C], Kh[hs, ch, :], Vt[hs, ch, :], True, False, r=[kKh, kVt], w=[pkH], tile_position=tp)
                for h in range(2):
                    hs = slice(64 * h, 64 * h + 64); tp = (64 * h, 64 * h)
                    p.mm(pH[hs, 0:C], Bh[hs, ch, :], Ub[hs, :], False, True, r=[kBh, "Ub"], w=[pkH], tile_position=tp)
                gC = E1[:, ch * C + C - 1:ch * C + C]
                p.stt(Hb[ho][:], Hf[:], gC, pH[:, 0:C], ALU.mult, ALU.add, r=["Hf", kE1, pkH], w=[("Hb", ho)])
                if full:
                    for which in range(3):
                        for h in range(2):
                            hs = slice(64 * h, 64 * h + 64); tp = (64 * h, 64 * h)
                            yo_ = pY[hs, ch * C:(ch + 1) * C]
                            if which == 0:
                                p.mm(yo_, Hb[hi][hs, :], AR[hs, ch, 1, :], True, False, r=[("Hb", hi), kAR], w=[pkY], tile_position=tp)
                            elif which == 1:
                                p.mm(yo_, Vt[hs, ch, :], A1[hs, ch, C:2 * C], False, False, r=[kVt, kA1], w=[pkY], tile_position=tp)
                            else:
                                p.mm(yo_, Ub[hs, :], A2m[hs, ch, C:2 * C], False, True, r=["Ub", kA2m], w=[pkY], tile_position=tp)
                p.stt(Hf[:], Hf[:], gC, pH[:, 0:C], ALU.mult, ALU.add, r=["Hf", kE1, pkH], w=["Hf"])
            if not full:
                continue
            if n == first_full:
                ycol0, ysrc0, ylen = 0, TB - 128, 128
            else:
                ycol0, ysrc0, ylen = 128 + (n - first_full - 1) * TB, 0, TB
            p.cp("act", Yf[:], pY[:], r=[pkY], w=[kYf])
            pm_, pkm = pp1.get()
            p.mm(pm_[:], bones[:], Yf[:], True, True, r=["bones", kYf], w=[pkm])
            p.stt(yc[:], pm_[:], -1.0 / C, Yf[:], ALU.mult, ALU.add, r=[pkm, kYf], w=[kyc])
            p.tt("pool", ycq[:], yc[:], yc[:], ALU.mult, r=[kyc], w=[kycq])
            pv_, pkv = pp1.get()
            p.mm(pv_[:], bones[:], ycq[:], True, True, r=["bones", kycq], w=[pkv])
            p.ts("dve", rs[:], pv_[:], 1.0 / C, ALU.mult, GN_EPS, ALU.add, r=[pkv], w=[krs])
            p.act(rs[:], rs[:], AF.Sqrt, r=[krs], w=[krs])
            p.op("dve", lambda e: e.reciprocal(out=rs[:], in_=rs[:]), r=[krs], w=[krs])
            p.tt("dve", yc[:], yc[:], rs[:], ALU.mult, r=[kyc, krs], w=[kyc])
            p.ts("dve", yc[:], yc[:], LNW, ALU.mult, LNB, ALU.add, r=[kyc, "CV"], w=[kyc])
            p.tt("pool", yc[:], yc[:], bon[:], ALU.add, r=[kyc, kbon], w=[kyc])
            p.tt("dve", yo[:], yc[:], gv[:], ALU.mult, r=[kyc, kgv], w=["yo"])
            p.dma("sp", yscr[hp * 128:(hp + 1) * 128, ycol0:ycol0 + ylen], yo[:, ysrc0:ysrc0 + ylen], s_st, r=["yo"], w=[("yscr", hp, n, 0)])
            pO, pkO = psO, "psO"
            tiles = [3] if n == first_full else [0, 1, 2, 3]
            for i in tiles:
                for h in range(2):
                    hs = slice(64 * h, 64 * h + 64)
                    pS1, pkS1 = pp2.get(); pS2, pkS2 = pp2.get()
                    lo = i * 128
                    p.mm(pS1[:], qT[hs, i * 128:(i + 1) * 128], Kring[hs, lo:lo + 512], True, True, r=["qT", "Kring"], w=[pkS1])
                    p.stt(S[:, 0:512], pS1[:], 0.125, ATB[:, h, 0:512], ALU.mult, ALU.add, r=[pkS1, "ATB"], w=["S_att"])
                    p.mm(pS2[:, 0:128], qT[hs, i * 128:(i + 1) * 128], Kring[hs, lo + 512:lo + 640], True, True, r=["qT", "Kring"], w=[pkS2])
                    p.stt(S[:, 512:640], pS2[:, 0:128], 0.125, ATB[:, h, 512:640], ALU.mult, ALU.add, r=[pkS2, "ATB"], w=["S_att"])
                    if n == first_full + 1:
                        p.tt("pool", S[:], S[:], AMK[:, i, :], ALU.add, r=["S_att", "AMK"], w=["S_att"])
                    p.op("dve", lambda e: e.tensor_reduce(out=ast[:, 0:1], in_=S[:], axis=AX.X, op=ALU.max), r=["S_att"], w=["ast"])
                    p.ts("dve", ast[:, 1:2], ast[:, 0:1], -1.0, ALU.mult, r=["ast"], w=["ast"])
                    p.act(Pe[:], S[:], AF.Exp, r=["S_att", "ast"], w=["Pe", "ast"], bias=ast[:, 1:2], accum_out=ast[:, 2:3])
                    p.op("dve", lambda e: e.reciprocal(out=ast[:, 3:4], in_=ast[:, 2:3]), r=["ast"], w=["ast"])
                    p.ts("dve", Pn[:], Pe[:], ast[:, 3:4], ALU.mult, r=["Pe", "ast"], w=["Pn"])
                    ptr, pktr = pp2.get()
                    ptrb = ptr.bitcast(BF16)
                    for kt in range(5):
                        p.tr(ptrb[:, kt * 128:(kt + 1) * 128], Pn[:, kt * 128:(kt + 1) * 128], c["identb"][:], r=["Pn", "identb"], w=[pktr])
                    p.cp("act", PT[:], ptrb[:, 0:640].rearrange("p (k q) -> p k q", k=5), r=[pktr], w=["PT"])
                    for kt in range(5):
                        p.mm(pO[hs, i * 128:(i + 1) * 128], Vring[:, i + kt, 64 * h:64 * h + 64], PT[:, kt, :], kt == 0, kt == 4,
                             r=["Vring", "PT"], w=[pkO], tile_position=(0, 64 * h))
            p.cp("act", ao[:, ysrc0:ysrc0 + ylen], pO[:, ysrc0:ysrc0 + ylen], r=[pkO], w=["ao"])
            p.dma("sp", yscr[512 + hp * 128:512 + (hp + 1) * 128, ycol0:ycol0 + ylen], ao[:, ysrc0:ysrc0 + ylen], s_st, r=["ao"], w=[("yscr", hp, n, 1)])


def build_fused(nc):
    p = Prog(nc)
    dr = lambda name, shape, dt=F32, kind="ExternalInput": nc.dram_tensor(name, list(shape), dt, kind=kind).ap()
    A = dict(xb=dr("xb", [NBLK * TB, D]), wsel=dr("wsel", [4, D, 1024]), g_mix=dr("g_mix", [D]), cv=dr("cv", [4, 128, 16]),
             w2=dr("w2", [4, 64, 128]), a2=dr("a2", [4, 64, 128]), g2=dr("g2", [4, 128, 128]), attb=dr("attb", [4, 2, 128, 640]),
             amask=dr("amask", [128, 4, 640]))
    T = {name: dr(name, shape) for name, shape in B_INPUTS}
    T["out"] = dr("out", [2048, D], kind="ExternalOutput")
    yscr = nc.dram_tensor("yscr", [D, 2176], BF16).ap()
    A["hTs"] = nc.dram_tensor("hTs", [NBLK, 128, NK * TB], BF16).ap()
    T["yT"] = yscr
    T["xin"] = A["xb"][NBLK * TB - 2176:NBLK * TB, :]
    c = make_consts(p)
    banks = [p.ps(f"psb{i}", [128, 512], F32) for i in range(6)]
    psD = p.ps("psD", [128, 512], F32)
    psO = p.ps("psO", [128, 512], F32)

    def mkpool(idx):
        q = PsumPool.__new__(PsumPool)
        q.t = [banks[i] for i in idx]; q.keys = [f"psb{i}" for i in idx]; q.i = 0; q.n = len(idx)
        return q
    pools = (mkpool([0, 1]), mkpool([2, 3, 4]), banks[5])
    pp = mkpool([0, 1, 2, 3, 4, 5])
    with Phase(p) as ph:
        emit_A(p, ph, c, pools, psD, psO, A, yscr)
    pp.t.extend([psD, psO]); pp.keys.extend(["psD", "psO"]); pp.n = 8
    s_st = emit_B(p, c, pp, T, 17)
    p.finalize(final_sems=[s_st])
    return p


RW = 512
RWKV_COLS = 1792


def att_bias_tile(rel_bias_h):
    q = np.arange(128)[:, None]; kc = np.arange(640)[None, :]
    qc, qi = q // 64, q % 64
    kcb, ki = kc // 64, kc % 64
    inband = (kcb >= qc) & (kcb <= qc + 8)
    kb = (kcb - qc) * 64 + ki
    rel = qi + 512 - kb
    idx = np.clip(rel, -128, 128) + 128
    out = np.where(inband, rel_bias_h[np.clip(idx, 0, 256)], np.float32(-30000.0)).astype(np.float32)
    return out


def prep_A(inputs, b, hp, T=8192):
    f = lambda k: np.asarray(inputs[k], np.float32)
    w_in = f("l0_w_in")
    hs = slice(128 * hp, 128 * hp + 128)
    cols = np.concatenate([np.arange(0, 512)[hs], np.arange(512, 1024)[hs], np.arange(1024, 1536)[hs],
                           np.arange(1536, 1664), np.arange(1664, 1792),
                           np.arange(1792, 2304)[hs], np.arange(2304, 2816)[hs], np.arange(2816, 3328)[hs]])
    mu = f("l0_shift_mu")
    cv = np.zeros((128, 16), np.float32)
    for ct in range(5):
        cv[:, ct] = mu[cols[ct * 128:(ct + 1) * 128]]
    for i, k in enumerate(["l0_w0", "l0_a0", "l0_k_k", "l0_k_a", "l0_r_k", "l0_lnx_w", "l0_lnx_b"]):
        cv[:, 5 + i] = f(k)[hs]
    rb = f("l0_rel_bias")
    return dict(
        xb=np.ascontiguousarray(f("x")[b, :T]), wsel=np.ascontiguousarray(w_in[:, cols]), g_mix=f("l0_norm_mix"), cv=cv,
        w2=np.ascontiguousarray(f("l0_w2")[:, hs]), a2=np.ascontiguousarray(f("l0_a2")[:, hs]), g2=np.ascontiguousarray(f("l0_g2")[:, hs]),
        attb=np.stack([att_bias_tile(rb[2 * hp]), att_bias_tile(rb[2 * hp + 1])]))


def _prep_fused(inputs, c, shared):
    f = lambda k: np.asarray(inputs[k], np.float32)
    b, q = c // 4, c % 4
    x = f("x")
    n_real = (q + 1) * 2048
    xb = np.zeros((8192, 1024), np.float32)
    xb[8192 - n_real:] = x[b, :n_real]
    amask = np.zeros((128, 4, 640), np.float32)
    hm = np.ones((128, 1), np.float32)
    if q == 0:
        hm[:] = 0.0
        for i in range(4):
            amask[:, i, :512 - 128 * i] = -30000.0
    d = dict(shared)
    d.update(xb=xb, amask=amask, hmask=hm)
    return d


def _shared_inputs(inputs):
    f = lambda k: np.asarray(inputs[k], np.float32)
    per = [prep_A(inputs, 0, hp, 8) for hp in range(4)]
    sh = dict(wsel=np.stack([d["wsel"] for d in per]), cv=np.stack([d["cv"] for d in per]),
              w2=np.stack([d["w2"] for d in per]), a2=np.stack([d["a2"] for d in per]), g2=np.stack([d["g2"] for d in per]),
              attb=np.stack([d["attb"] for d in per]), g_mix=f("l0_norm_mix"),
              w_out=f("l0_w_out"), g_ffn0=f("l0_norm_ffn"), up0=f("l0_ffn_up"), dn0=f("l0_ffn_down"),
              g_mix1=f("l1_norm_mix"), pw1=f("l1_pw1"), pw1_b=f("l1_pw1_b"), dw=f("l1_dw"), dw_b=f("l1_dw_b"),
              ln_w=f("l1_ln_w"), ln_b=f("l1_ln_b"), pw2=f("l1_pw2"), pw2_b=f("l1_pw2_b"),
              g_ffn1=f("l1_norm_ffn"), up1=f("l1_ffn_up"), dn1=f("l1_ffn_down"), g_fin=f("final_norm"))
    return sh


def kernel(**inputs):
    nc = bass.Bass("TRN2", target_bir_lowering=False)
    build_fused(nc)
    shared = _shared_inputs(inputs)
    maps = [_prep_fused(inputs, c, shared) for c in range(8)]
    res = run_bass_kernel_spmd(nc, maps, core_ids=list(range(8)))
    out = np.concatenate([np.asarray(res.results[c]["out"]) for c in range(8)], axis=0)
    return out.reshape(2, 8192, 1024).astype(np.float32)
```
